# Optimizing a Trainium2 kernel written in Bass

```python
import jax, jax.numpy as jnp
from jax import lax
import numpy as np

D_MODEL = 2048
BATCH = 1
SEQ = 16384
DEPTH = 2

CHUNK = 64
NORM_EPS = 1e-6
CONV_K = 4
GDN_HEADS = 8
GDN_DK = 128
GDN_DV = 128
GDN_QK = GDN_HEADS * GDN_DK
GDN_V = GDN_HEADS * GDN_DV
ATT_HEADS = 8
ATT_DH = 128
ATT_W = ATT_HEADS * ATT_DH
BAND_CHUNKS = 9
REL_CLIP = 256
SSD_DINNER = 2 * D_MODEL
SSD_HEADDIM = 64
SSD_HEADS = SSD_DINNER // SSD_HEADDIM
SSD_GROUPS = 8
SSD_HPG = SSD_HEADS // SSD_GROUPS
SSD_DSTATE = 128
SSD_CONV = SSD_DINNER + 2 * SSD_GROUPS * SSD_DSTATE
D_FF = 4 * D_MODEL

AB_SPLITS = (GDN_QK, GDN_QK, GDN_V, GDN_V, GDN_HEADS, GDN_HEADS, ATT_W, ATT_W, ATT_W)
AB_IN = sum(AB_SPLITS)
AB_OUT = GDN_V + ATT_W
SSD_SPLITS = (SSD_DINNER, SSD_CONV, SSD_HEADS)
SSD_IN = sum(SSD_SPLITS)
N_EVEN = (DEPTH + 1) // 2
N_ODD = DEPTH // 2

kernel_name = "hybrid_gdn_bandattn_ssd_trunk"


def split_sizes(x, sizes):
    return jnp.split(x, np.cumsum(sizes)[:-1].tolist(), axis=-1)


def rms_norm(x, w):
    xf = x.astype(jnp.float32)
    y = xf * lax.rsqrt(jnp.mean(xf * xf, axis=-1, keepdims=True) + NORM_EPS)
    return (y * w.astype(jnp.float32)).astype(x.dtype)


def l2_norm(x):
    xf = x.astype(jnp.float32)
    return xf * lax.rsqrt(jnp.sum(xf * xf, axis=-1, keepdims=True) + NORM_EPS)


def causal_dwconv(x, w):
    k = w.shape[0]
    return lax.conv_general_dilated(
        x, w[:, None, :].astype(x.dtype), window_strides=(1,), padding=[(k - 1, 0)],
        dimension_numbers=('NWC', 'WIO', 'NWC'), feature_group_count=x.shape[-1])


def gated_deltanet(q, k, v, z, b_raw, a_raw, conv_w, a_log, dt_bias, norm_w):
    dtype = q.dtype
    b_, t_, _ = q.shape
    n = t_ // CHUNK
    f32 = jnp.float32
    qkv = jax.nn.silu(causal_dwconv(jnp.concatenate([q, k, v], axis=-1), conv_w))
    q, k, v = split_sizes(qkv, (GDN_QK, GDN_QK, GDN_V))
    q = l2_norm(q.reshape(b_, t_, GDN_HEADS, GDN_DK)) * (GDN_DK ** -0.5)
    k = l2_norm(k.reshape(b_, t_, GDN_HEADS, GDN_DK))
    v = v.reshape(b_, t_, GDN_HEADS, GDN_DV).astype(f32)
    beta = jax.nn.sigmoid(b_raw.astype(f32))
    g = -jnp.exp(a_log.astype(f32)) * jax.nn.softplus(a_raw.astype(f32) + dt_bias.astype(f32))

    def to_chunks(t):
        return t.reshape(b_, n, CHUNK, GDN_HEADS, -1).transpose(0, 3, 1, 2, 4)

    qc, kc, vc = to_chunks(q), to_chunks(k), to_chunks(v)
    bc = beta.reshape(b_, n, CHUNK, GDN_HEADS).transpose(0, 3, 1, 2)
    gcum = jnp.cumsum(g.reshape(b_, n, CHUNK, GDN_HEADS).transpose(0, 3, 1, 2), axis=-1)
    causal = jnp.tril(jnp.ones((CHUNK, CHUNK), dtype=bool))
    strict = jnp.tril(jnp.ones((CHUNK, CHUNK), dtype=bool), k=-1)
    decay = jnp.exp(jnp.where(causal, gcum[..., :, None] - gcum[..., None, :], -jnp.inf))
    kb = kc * bc[..., None]
    lower = jnp.where(strict, jnp.einsum('bhnid,bhnjd->bhnij', kb, kc) * decay, 0.0)
    eye = jnp.eye(CHUNK, dtype=f32)
    tmat = lax.linalg.triangular_solve(eye + lower, jnp.broadcast_to(eye, lower.shape),
                                       left_side=True, lower=True, unit_diagonal=True)
    u = jnp.einsum('bhnij,bhnje->bhnie', tmat, vc * bc[..., None])
    w = jnp.einsum('bhnij,bhnjd->bhnid', tmat, kb * jnp.exp(gcum)[..., None])
    qk = jnp.where(causal, jnp.einsum('bhnid,bhnjd->bhnij', qc, kc) * decay, 0.0)
    q_dec = qc * jnp.exp(gcum)[..., None]
    k_dec = kc * jnp.exp(gcum[..., -1:] - gcum)[..., None]
    g_last = jnp.exp(gcum[..., -1])

    def step(state, inp):
        w_n, u_n, qk_n, qd_n, kd_n, gl_n = inp
        v_new = u_n - jnp.einsum('bhld,bhde->bhle', w_n, state)
        o_n = jnp.einsum('bhld,bhde->bhle', qd_n, state) + jnp.einsum('bhls,bhse->bhle', qk_n, v_new)
        state = state * gl_n[..., None, None] + jnp.einsum('bhld,bhle->bhde', kd_n, v_new)
        return state, o_n

    xs = tuple(jnp.moveaxis(t, 2, 0) for t in (w, u, qk, q_dec, k_dec, g_last))
    s0 = jnp.zeros((b_, GDN_HEADS, GDN_DK, GDN_DV), f32)
    _, o = lax.scan(step, s0, xs)
    o = o.transpose(1, 0, 3, 2, 4).reshape(b_, t_, GDN_HEADS, GDN_DV)
    o = rms_norm(o, norm_w) * jax.nn.silu(z.reshape(b_, t_, GDN_HEADS, GDN_DV).astype(f32))
    return o.reshape(b_, t_, GDN_V).astype(dtype)


def chunk_band_attention(q, k, v, q_norm_w, k_norm_w, rel_bias):
    dtype = q.dtype
    b_, t_, _ = q.shape
    n = t_ // CHUNK
    f32 = jnp.float32

    def heads(t):
        return t.reshape(b_, n, CHUNK, ATT_HEADS, ATT_DH)

    qh = rms_norm(heads(q), q_norm_w).astype(f32) * (ATT_DH ** -0.5)
    kh = rms_norm(heads(k), k_norm_w).astype(f32)
    vh = heads(v).astype(f32)
    pad = [(0, 0), (BAND_CHUNKS - 1, 0), (0, 0), (0, 0), (0, 0)]
    kp = jnp.pad(kh, pad)
    vp = jnp.pad(vh, pad)
    scores = jnp.concatenate(
        [jnp.einsum('bnqhd,bnkhd->bhnqk', qh, kp[:, o:o + n]) for o in range(BAND_CHUNKS)],
        axis=-1)
    qi = jnp.arange(CHUNK)
    kj = jnp.arange(BAND_CHUNKS * CHUNK)
    rel = (kj[None, :] - (BAND_CHUNKS - 1) * CHUNK) - qi[:, None]
    bias = rel_bias.astype(f32)[:, jnp.clip(rel, -REL_CLIP, REL_CLIP) + REL_CLIP]
    chunk_idx = jnp.arange(n)[:, None] + jnp.arange(BAND_CHUNKS)[None, :] - (BAND_CHUNKS - 1)
    valid = jnp.repeat(chunk_idx >= 0, CHUNK, axis=1)
    scores = jnp.where(valid[None, None, :, None, :], scores + bias[None, :, None], -jnp.inf)
    p = jax.nn.softmax(scores, axis=-1)
    out = jnp.einsum('bhnqk,bnkhd->bnqhd', p[..., :CHUNK], vp[:, 0:n])
    for o in range(1, BAND_CHUNKS):
        out = out + jnp.einsum('bhnqk,bnkhd->bnqhd', p[..., o * CHUNK:(o + 1) * CHUNK], vp[:, o:o + n])
    return out.reshape(b_, t_, ATT_W).astype(dtype)


def gdn_attn_mixer(h, w_in, gdn_conv_w, gdn_a_log, gdn_dt_bias, gdn_norm_w,
                   q_norm_w, k_norm_w, rel_bias, w_out):
    proj = jnp.einsum('btd,de->bte', h, w_in)
    gq, gk, gv, gz, gb, ga, aq, ak, av = split_sizes(proj, AB_SPLITS)
    o_a = gated_deltanet(gq, gk, gv, gz, gb, ga, gdn_conv_w, gdn_a_log, gdn_dt_bias, gdn_norm_w)
    o_b = chunk_band_attention(aq, ak, av, q_norm_w, k_norm_w, rel_bias)
    return jnp.einsum('bte,ed->btd', jnp.concatenate([o_a, o_b], axis=-1), w_out)


def mamba2_ssd(h, w_in, conv_w, conv_b, dt_bias, a_log, d_skip, norm_w, w_out):
    dtype = h.dtype
    b_, t_, _ = h.shape
    n = t_ // CHUNK
    f32 = jnp.float32
    proj = jnp.einsum('btd,de->bte', h, w_in)
    z, xbc, dt_raw = split_sizes(proj, SSD_SPLITS)
    xbc = jax.nn.silu(causal_dwconv(xbc, conv_w) + conv_b)
    xs, bm, cm = split_sizes(xbc, (SSD_DINNER, SSD_GROUPS * SSD_DSTATE, SSD_GROUPS * SSD_DSTATE))
    x = xs.astype(f32).reshape(b_, n, CHUNK, SSD_GROUPS, SSD_HPG, SSD_HEADDIM)
    bm = bm.astype(f32).reshape(b_, n, CHUNK, SSD_GROUPS, SSD_DSTATE)
    cm = cm.astype(f32).reshape(b_, n, CHUNK, SSD_GROUPS, SSD_DSTATE)
    dt = jax.nn.softplus(dt_raw.astype(f32) + dt_bias.astype(f32)).reshape(b_, n, CHUNK, SSD_GROUPS, SSD_HPG)
    a = -jnp.exp(a_log.astype(f32)).reshape(SSD_GROUPS, SSD_HPG)
    acum = jnp.cumsum(dt * a, axis=2)
    xdt = x * dt[..., None]
    causal = jnp.tril(jnp.ones((CHUNK, CHUNK), dtype=bool))
    a_t = jnp.moveaxis(acum, 2, -1)
    decay = jnp.exp(jnp.where(causal, a_t[..., :, None] - a_t[..., None, :], -jnp.inf))
    cb = jnp.einsum('bnlgs,bnmgs->bnglm', cm, bm)
    y_diag = jnp.einsum('bngelm,bnmgep->bnlgep', cb[:, :, :, None] * decay, xdt)

    def step(state, inp):
        c_n, b_n, x_n, a_n = inp
        y_off = jnp.einsum('blgs,bgeps->blgep', c_n, state) * jnp.exp(a_n)[..., None]
        w_dec = jnp.exp(a_n[:, -1:] - a_n)
        state = (state * jnp.exp(a_n[:, -1])[..., None, None]
                 + jnp.einsum('blgs,blgep->bgeps', b_n, x_n * w_dec[..., None]))
        return state, y_off

    s0 = jnp.zeros((b_, SSD_GROUPS, SSD_HPG, SSD_HEADDIM, SSD_DSTATE), f32)
    xs_scan = tuple(jnp.moveaxis(t, 1, 0) for t in (cm, bm, xdt, acum))
    _, y_off = lax.scan(step, s0, xs_scan)
    y = y_diag + jnp.moveaxis(y_off, 0, 1) + x * d_skip.astype(f32).reshape(SSD_GROUPS, SSD_HPG, 1)
    y = y.reshape(b_, t_, SSD_DINNER) * jax.nn.silu(z.astype(f32))
    y = rms_norm(y.reshape(b_, t_, SSD_GROUPS, -1), norm_w.reshape(SSD_GROUPS, -1)).reshape(b_, t_, SSD_DINNER)
    return jnp.einsum('bte,ed->btd', y.astype(dtype), w_out)


def sq_relu_mlp(h, w1, w2):
    a = jax.nn.relu(jnp.einsum('btd,df->btf', h, w1))
    return jnp.einsum('btf,fd->btd', a * a, w2)


def _dt_bias(key, shape):
    dt = jnp.exp(jax.random.uniform(key, shape) * (jnp.log(0.1) - jnp.log(0.001)) + jnp.log(0.001))
    return dt + jnp.log(-jnp.expm1(-dt))


def setup_inputs(seed: int = 0) -> dict:
    key = jax.random.key(seed)
    ks = iter(jax.random.split(key, 32))
    nrm = lambda shape, s: jax.random.normal(next(ks), shape, jnp.float32) * s
    gain = lambda shape: 1.0 + 0.02 * jax.random.normal(next(ks), shape, jnp.float32)
    return {
        "x": nrm((BATCH, SEQ, D_MODEL), 1.0),
        "c": nrm((BATCH, D_MODEL), 1.0),
        "mod_w": nrm((DEPTH, D_MODEL, 6 * D_MODEL), 0.5 * D_MODEL ** -0.5),
        "mod_b": nrm((DEPTH, 6 * D_MODEL), 0.01),
        "norm_mix_w": gain((DEPTH, D_MODEL)),
        "norm_mlp_w": gain((DEPTH, D_MODEL)),
        "mlp_w1": nrm((DEPTH, D_MODEL, D_FF), D_MODEL ** -0.5),
        "mlp_w2": nrm((DEPTH, D_FF, D_MODEL), D_FF ** -0.5),
        "ab_w_in": nrm((N_EVEN, D_MODEL, AB_IN), D_MODEL ** -0.5),
        "gdn_conv_w": nrm((N_EVEN, CONV_K, 2 * GDN_QK + GDN_V), CONV_K ** -0.5),
        "gdn_a_log": jnp.log(jax.random.uniform(next(ks), (N_EVEN, GDN_HEADS), jnp.float32, 1.0, 16.0)),
        "gdn_dt_bias": _dt_bias(next(ks), (N_EVEN, GDN_HEADS)),
        "gdn_norm_w": gain((N_EVEN, GDN_DV)),
        "attn_q_norm_w": gain((N_EVEN, ATT_DH)),
        "attn_k_norm_w": gain((N_EVEN, ATT_DH)),
        "attn_rel_bias": nrm((N_EVEN, ATT_HEADS, 2 * REL_CLIP + 1), 0.1),
        "ab_w_out": nrm((N_EVEN, AB_OUT, D_MODEL), AB_OUT ** -0.5),
        "ssd_w_in": nrm((N_ODD, D_MODEL, SSD_IN), D_MODEL ** -0.5),
        "ssd_conv_w": nrm((N_ODD, CONV_K, SSD_CONV), CONV_K ** -0.5),
        "ssd_conv_b": nrm((N_ODD, SSD_CONV), 0.01),
        "ssd_dt_bias": _dt_bias(next(ks), (N_ODD, SSD_HEADS)),
        "ssd_a_log": jnp.log(jax.random.uniform(next(ks), (N_ODD, SSD_HEADS), jnp.float32, 1.0, 16.0)),
        "ssd_d": gain((N_ODD, SSD_HEADS)),
        "ssd_norm_w": gain((N_ODD, SSD_DINNER)),
        "ssd_w_out": nrm((N_ODD, SSD_DINNER, D_MODEL), SSD_DINNER ** -0.5),
    }


def reference(x, c, mod_w, mod_b, norm_mix_w, norm_mlp_w, mlp_w1, mlp_w2,
              ab_w_in, gdn_conv_w, gdn_a_log, gdn_dt_bias, gdn_norm_w,
              attn_q_norm_w, attn_k_norm_w, attn_rel_bias, ab_w_out,
              ssd_w_in, ssd_conv_w, ssd_conv_b, ssd_dt_bias, ssd_a_log, ssd_d,
              ssd_norm_w, ssd_w_out):
    c_act = jax.nn.silu(c)
    for layer in range(DEPTH):
        mod = jnp.einsum('bd,de->be', c_act, mod_w[layer]) + mod_b[layer]
        sh1, sc1, g1, sh2, sc2, g2 = [m[:, None, :] for m in jnp.split(mod, 6, axis=-1)]
        h = rms_norm(x, norm_mix_w[layer]) * (1.0 + sc1) + sh1
        i = layer // 2
        if layer % 2 == 0:
            mix = gdn_attn_mixer(h, ab_w_in[i], gdn_conv_w[i], gdn_a_log[i], gdn_dt_bias[i],
                                 gdn_norm_w[i], attn_q_norm_w[i], attn_k_norm_w[i],
                                 attn_rel_bias[i], ab_w_out[i])
        else:
            mix = mamba2_ssd(h, ssd_w_in[i], ssd_conv_w[i], ssd_conv_b[i], ssd_dt_bias[i],
                             ssd_a_log[i], ssd_d[i], ssd_norm_w[i], ssd_w_out[i])
        x = x + g1 * mix
        h = rms_norm(x, norm_mlp_w[layer]) * (1.0 + sc2) + sh2
        x = x + g2 * sq_relu_mlp(h, mlp_w1[layer], mlp_w2[layer])
    return x
```

```python
import numpy as np
from contextlib import ExitStack
import concourse.bass as bass
import concourse.mybir as mybir
from concourse.bass_utils import run_bass_kernel_spmd

F32 = mybir.dt.float32
BF16 = mybir.dt.bfloat16
AF = mybir.ActivationFunctionType
ALU = mybir.AluOpType
AX = mybir.AxisListType

ENGS = ["pe", "dve", "act", "pool", "sp"]
EPOCH = 30000
SAME_ENGINE_SYNC = True


class Prog:
    def __init__(self, nc, stack, n_dma_sems=12):
        self.nc = nc
        self.stack = stack
        self.ops = {e: [] for e in ENGS}
        self.known = {e: {} for e in ENGS}
        self.tiles = {}
        self.semid = 0
        self.cur = {}
        for e in ["pe", "dve", "act", "pool"]:
            self._new_sem(e)
        self.dma = {}
        for q in ["sp", "pool", "act"]:
            sems = []
            for j in range(n_dma_sems):
                s = stack.enter_context(nc.semaphore(f"dq_{q}_{j}"))
                self.semid += 1
                sems.append([s, self.semid, 0])
            self.dma[q] = [sems, 0]
        self.out_events = []

    def _new_sem(self, e):
        s = self.stack.enter_context(self.nc.semaphore(f"s_{e}_{self.semid}"))
        self.semid += 1
        self.cur[e] = [s, self.semid, 0]

    def sb(self, name, shape, dtype=F32):
        return self.stack.enter_context(self.nc.sbuf_tensor("sb_" + name, list(shape), dtype))

    def ps(self, name, shape, dtype=F32):
        return self.stack.enter_context(self.nc.psum_tensor("pp_" + name, list(shape), dtype))

    def _st(self, k):
        st = self.tiles.get(k)
        if st is None:
            st = {"w": None, "r": {}}
            self.tiles[k] = st
        return st

    def _collect(self, eng, reads, writes, same_sync):
        waits = {}

        def add(ev):
            if ev is None:
                return
            s, sid, val = ev
            if sid == self.cur.get(eng, [None, -1])[1] and not same_sync:
                return
            if self.known[eng].get(sid, 0) >= val:
                return
            if sid not in waits or waits[sid][2] < val:
                waits[sid] = ev

        for k in reads:
            add(self._st(k)["w"])
        for k in writes:
            st = self._st(k)
            add(st["w"])
            for ev in st["r"].values():
                add(ev)
        for sid, ev in waits.items():
            self.ops[eng].append(("wait", ev[0], ev[2]))
            self.known[eng][sid] = ev[2]

    def _record(self, ev, reads, writes):
        for k in reads:
            st = self._st(k)
            st["r"][ev[1]] = ev
        for k in writes:
            st = self._st(k)
            st["w"] = ev
            st["r"] = {}

    def op(self, eng, fn, reads=(), writes=(), same_sync=None):
        if same_sync is None:
            same_sync = SAME_ENGINE_SYNC and eng != "pe"
        self._collect(eng, reads, writes, same_sync)
        c = self.cur[eng]
        if c[2] >= EPOCH:
            self._new_sem(eng)
            c = self.cur[eng]
        c[2] += 1
        ev = (c[0], c[1], c[2])
        self.ops[eng].append(("op", fn, c[0], 1))
        self._record(ev, reads, writes)
        return ev

    def dma_op(self, q, out, in_, reads=(), writes=(), is_output=False, **kw):
        sems, idx = self.dma[q]
        slot = sems[idx % len(sems)]
        self.dma[q][1] = idx + 1
        s, sid, uses = slot
        if uses > 0 and self.known[q].get(sid, 0) < 16 * uses:
            self.ops[q].append(("wait", s, 16 * uses))
            self.known[q][sid] = 16 * uses
        self._collect(q, reads, writes, True)
        slot[2] = uses + 1
        ev = (s, sid, 16 * (uses + 1))

        def fn(e, out=out, in_=in_, kw=kw):
            return e.dma_start(out=out, in_=in_, **kw)

        self.ops[q].append(("op", fn, s, 16))
        self._record(ev, reads, writes)
        if is_output:
            self.out_events.append(ev)
        return ev

    def finish(self):
        best = {}
        for ev in self.out_events:
            if ev[1] not in best or best[ev[1]][2] < ev[2]:
                best[ev[1]] = ev
        for ev in best.values():
            self.ops["sp"].append(("wait", ev[0], ev[2]))
        nc = self.nc
        ops = self.ops

        def replay(name, e):
            for o in ops[name]:
                if o[0] == "wait":
                    e.wait_ge(o[1], o[2])
                else:
                    ins = o[1](e)
                    ins.then_inc(o[2], o[3])

        with nc.Block() as block:
            @block.tensor
            def _(e):
                replay("pe", e)

            @block.vector
            def _(e):
                replay("dve", e)

            @block.scalar
            def _(e):
                replay("act", e)

            @block.gpsimd
            def _(e):
                replay("pool", e)

            @block.sync
            def _(e):
                replay("sp", e)

    def n_ops(self):
        return {e: len(v) for e, v in self.ops.items()}


D = 2048
KT = 16
TT = 512
EPS = 1e-6


def new_nc():
    return bass.Bass("TRN2", target_bir_lowering=False)


def dram_in(nc, name, shape, dtype=F32):
    return nc.dram_tensor(name, list(shape), dtype, kind="ExternalInput").ap()


def dram_out(nc, name, shape, dtype=F32):
    return nc.dram_tensor(name, list(shape), dtype, kind="ExternalOutput").ap()


class Ctx:
    pass


def setup_common(P):
    C = Ctx()
    C.ones_bf = P.sb("ones_bf", [128, 128], BF16)
    P.op("pool", lambda e: e.memset(C.ones_bf[:], 1.0), writes=["ones_bf"])
    C.ps = [P.ps(f"ps{i}", [128, 512]) for i in range(8)]
    C.rot = 0
    C.ev = 0
    return C


def emit_mod_cols(P, modt, nwt, which):
    acol = P.sb("acol%d" % which, [128, 16])
    s_sh = 0 if which == 0 else 3
    P.op("dve", lambda e: e.scalar_tensor_tensor(out=acol[:], in0=modt[:, (s_sh + 1) * 16:(s_sh + 2) * 16], scalar=1.0,
                                                 in1=nwt[:], op0=ALU.add, op1=ALU.mult),
         reads=["modt", "nwt%d" % which], writes=["acol"])
    return acol, modt[:, s_sh * 16:(s_sh + 1) * 16]


def make_scr(P):
    scr = {}
    scr["rstd"] = P.sb("rstd", [128, 512])
    scr["tmp"] = [P.sb("tmp0", [128, 512]), P.sb("tmp1", [128, 512])]
    scr["eps"] = P.sb("epst", [128, 1])
    P.op("pool", lambda e: e.memset(scr["eps"][:], EPS), writes=["eps"])
    return scr


def emit_dense(P, C, w_dram, K, ncols, rhs_fn, rhs_keys, ntt, evac_fn, wbufs, wq="pool", col0=0):
    nk = K // 128
    gcols = 8192 // nk
    ngroups = (ncols + gcols - 1) // gcols
    for g in range(ngroups):
        c0 = g * gcols
        csz = min(gcols, ncols - c0)
        wb = wbufs[C.rot_w % 2]
        wk = "wbuf%d" % (C.rot_w % 2)
        C.rot_w += 1
        wv = wb[:, 0:nk * csz].rearrange("p (k n) -> p k n", k=nk)
        src = w_dram[:, col0 + c0:col0 + c0 + csz].rearrange("(k p) n -> p k n", p=128)
        P.dma_op(wq, wv, src, writes=[wk])
        nm = (csz + 127) // 128
        for mi in range(nm):
            msz = min(128, csz - mi * 128)
            m = (c0 // 128) + mi
            for tt in range(ntt):
                bank = C.rot % 6
                C.rot += 1
                pst = C.ps[bank]

                def mm(e, wv=wv, mi=mi, msz=msz, tt=tt, pst=pst):
                    ins = None
                    for kt in range(nk):
                        ins = e.matmul(pst[0:msz, :], lhsT=wv[:, kt, mi * 128:mi * 128 + msz], rhs=rhs_fn(kt, tt),
                                       start=(kt == 0), stop=(kt == nk - 1))
                    return ins
                P.op("pe", mm, reads=[wk] + rhs_keys(tt), writes=["ps%d" % bank])
                evac_fn(m, msz, tt, pst, "ps%d" % bank)


def build_stageA(NC, TOK=2048):
    nc = new_nc()
    xT = dram_in(nc, "xT", [D, TOK])
    w = dram_in(nc, "w", [D, NC])
    nw = dram_in(nc, "nw", [128, 16])
    modl = dram_in(nc, "modl", [128, 96])
    out = dram_out(nc, "projT", [NC, TOK])
    ntt = TOK // TT
    with ExitStack() as st:
        P = Prog(nc, st)
        C = setup_common(P)
        C.rot_w = 0
        scr = make_scr(P)
        modt = P.sb("modt", [128, 96])
        nwt = P.sb("nwt", [128, 16])
        P.dma_op("sp", modt[:], modl, writes=["modt"])
        P.dma_op("sp", nwt[:], nw, writes=["nwt0"])
        acol, shcol = emit_mod_cols(P, modt, nwt, 0)
        P.tiles["shcol"] = P.tiles["modt"]
        xt = P.sb("xt", [128, 16, TT])
        sq = P.sb("sq", [128, 16, TT], BF16)
        hT = P.sb("hT", [128, 16, TOK], BF16)
        wbufs = [P.sb("wb0", [128, 8192], BF16), P.sb("wb1", [128, 8192], BF16)]
        ot = [P.sb("ot0", [128, TOK]), P.sb("ot1", [128, TOK])]
        xsrc = xT.rearrange("(k p) t -> p k t", p=128)
        for tt in range(ntt):
            P.dma_op("sp", xt[:], xsrc[:, :, tt * TT:(tt + 1) * TT], writes=["xt"])
            emit_adaln_xt(P, C, xt, hT[:, :, tt * TT:(tt + 1) * TT], "hT%d" % tt, acol, shcol, sq, scr)

        state = {"n": 0}

        def evac(m, msz, tt, pst, pk):
            b = (m % 2)
            o = ot[b]
            okey = "ot%d_%d" % (b, tt)
            eng = "dve" if C.ev % 2 == 0 else "act"
            C.ev += 1
            if eng == "dve":
                P.op("dve", lambda e: e.tensor_copy(out=o[0:msz, tt * TT:(tt + 1) * TT], in_=pst[0:msz, :]), reads=[pk], writes=[okey])
            else:
                P.op("act", lambda e: e.activation(out=o[0:msz, tt * TT:(tt + 1) * TT], in_=pst[0:msz, :], func=AF.Copy), reads=[pk], writes=[okey])
            if tt == ntt - 1:
                P.dma_op("sp", out[m * 128:m * 128 + msz, :], o[0:msz, :], reads=["ot%d_%d" % (b, t) for t in range(ntt)], is_output=True)

        emit_dense(P, C, w, D, NC, lambda kt, tt: hT[:, kt, tt * TT:(tt + 1) * TT], lambda tt: ["hT%d" % tt], ntt, evac, wbufs)
        P.finish()
        print("stageA ops", P.n_ops())
    return nc


def emit_adaln_xt(P, C, xt, ht, hk, acol, shcol, sq, scr):
    class _K:
        pass
    emit_adaln2(P, C, xt, ["xt"], ht, hk, acol, shcol, sq, "sq", scr)


def emit_adaln2(P, C, xt, xkeys, ht, hk, a_col, sh_col, sq, sqk, scr):
    nbank = C.ps[7]
    for q in range(4):
        if q % 2 == 0:
            P.op("act", lambda e, q=q: e.activation(out=sq[:, 4 * q:4 * q + 4, :], in_=xt[:, 4 * q:4 * q + 4, :], func=AF.Square),
                 reads=xkeys, writes=[sqk + str(q)])
        else:
            P.op("pool", lambda e, q=q: e.tensor_tensor(out=sq[:, 4 * q:4 * q + 4, :], in0=xt[:, 4 * q:4 * q + 4, :],
                                                      in1=xt[:, 4 * q:4 * q + 4, :], op=ALU.mult),
                 reads=xkeys, writes=[sqk + str(q)])

    def mm(e):
        ins = None
        for kt in range(KT):
            ins = e.matmul(nbank[:, :], lhsT=C.ones_bf[:], rhs=sq[:, kt, :], start=(kt == 0), stop=(kt == KT - 1))
        return ins
    P.op("pe", mm, reads=["ones_bf"] + [sqk + str(q) for q in range(4)], writes=["ps7"])
    rstd = scr["rstd"]
    P.op("act", lambda e: e.activation(out=rstd[:], in_=nbank[:, :], func=AF.Sqrt, bias=scr["eps"][:, 0:1], scale=1.0 / D),
         reads=["ps7", "eps"], writes=["rstd"])
    P.op("dve", lambda e: e.reciprocal(out=rstd[:], in_=rstd[:]), reads=["rstd"], writes=["rstd"])
    for kt in range(KT):
        tmp = scr["tmp"][kt % 2]
        tk = "tmp%d" % (kt % 2)
        P.op("dve", lambda e, kt=kt, tmp=tmp: e.tensor_tensor(out=tmp[:], in0=xt[:, kt, :], in1=rstd[:], op=ALU.mult),
             reads=xkeys + ["rstd"], writes=[tk])
        P.op("act", lambda e, kt=kt, tmp=tmp: e.activation(out=ht[:, kt, :], in_=tmp[:], func=AF.Identity,
                                                         bias=sh_col[:, kt:kt + 1], scale=a_col[:, kt:kt + 1]),
             reads=[tk, "acol", "modt"], writes=[hk])


def build_stageC(KO, TOK=2048):
    nc = new_nc()
    xT = dram_in(nc, "xT", [D, TOK])
    oT = dram_in(nc, "oT", [KO, TOK])
    wo = dram_in(nc, "wo", [KO, D])
    w1 = dram_in(nc, "w1", [D, 4 * D])
    w2 = dram_in(nc, "w2", [4 * D, D])
    nw = dram_in(nc, "nw", [128, 16])
    modl = dram_in(nc, "modl", [128, 96])
    out = dram_out(nc, "x2T", [D, TOK])
    ntt = TOK // TT
    nko = KO // 128
    with ExitStack() as st:
        P = Prog(nc, st)
        C = setup_common(P)
        C.rot_w = 0
        scr = make_scr(P)
        modt = P.sb("modt", [128, 96])
        nwt = P.sb("nwt", [128, 16])
        P.dma_op("sp", modt[:], modl, writes=["modt"])
        P.dma_op("sp", nwt[:], nw, writes=["nwt1"])
        acol, shcol = emit_mod_cols(P, modt, nwt, 1)
        g1 = modt[:, 32:48]
        g2 = modt[:, 80:96]
        xt = P.sb("xt", [128, 16, TT])
        ob = P.sb("ob", [128, max(nko, 16), TT], BF16)
        aT = P.sb("aT", [128, 64, TT], BF16)
        wbufs = [P.sb("wb0", [128, 8192], BF16), P.sb("wb1", [128, 8192], BF16)]
        ot = [P.sb("ot0", [128, TT]), P.sb("ot1", [128, TT])]
        rl = [P.sb("rl0", [128, TT]), P.sb("rl1", [128, TT])]
        xsrc = xT.rearrange("(k p) t -> p k t", p=128)
        osrc = oT.rearrange("(k p) t -> p k t", p=128)
        odst = out.rearrange("(k p) t -> p k t", p=128)
        for tt in range(ntt):
            tsl = slice(tt * TT, (tt + 1) * TT)
            P.dma_op("sp", xt[:], xsrc[:, :, tsl], writes=["xt"])
            P.dma_op("pool", ob[:, 0:nko, :], osrc[:, :, tsl], writes=["ob"])

            def evac1(m, msz, t_, pst, pk):
                P.op("dve", lambda e: e.scalar_tensor_tensor(out=xt[:, m, :], in0=pst[:, :], scalar=g1[:, m:m + 1], in1=xt[:, m, :],
                                                             op0=ALU.mult, op1=ALU.add),
                     reads=[pk, "modt", "xt"], writes=["x1_%d" % m])
            emit_dense(P, C, wo, KO, D, lambda kt, t_: ob[:, kt, :], lambda t_: ["ob"], 1, evac1, wbufs)
            x1keys = ["x1_%d" % m for m in range(16)]
            emit_adaln2(P, C, xt, x1keys, ob[:, 0:16, :], "ob", acol, shcol, aT[:, 0:16, :], "aT", scr)

            def evac2(f, msz, t_, pst, pk):
                r = rl[f % 2]
                rk = "rl%d" % (f % 2)
                P.op("act", lambda e: e.activation(out=r[:], in_=pst[:, :], func=AF.Square), reads=[pk], writes=[rk])
                P.op("dve", lambda e: e.scalar_tensor_tensor(out=aT[:, f, :], in0=pst[:, :], scalar=0.0, in1=r[:],
                                                             op0=ALU.is_gt, op1=ALU.mult),
                     reads=[pk, rk], writes=["aT_%d" % f])
            emit_dense(P, C, w1, D, 4 * D, lambda kt, t_: ob[:, kt, :], lambda t_: ["ob"], 1, evac2, wbufs)
            akeys = ["aT_%d" % f for f in range(64)]

            def evac3(m, msz, t_, pst, pk):
                o = ot[m % 2]
                okey = "ot%d" % (m % 2)
                P.op("dve", lambda e: e.scalar_tensor_tensor(out=o[:], in0=pst[:, :], scalar=g2[:, m:m + 1], in1=xt[:, m, :],
                                                             op0=ALU.mult, op1=ALU.add),
                     reads=[pk, "modt", "x1_%d" % m], writes=[okey])
                P.dma_op("sp", odst[:, m, tsl], o[:], reads=[okey], is_output=True)
            emit_dense(P, C, w2, 4 * D, D, lambda kt, t_: aT[:, kt, :], lambda t_: akeys, 1, evac3, wbufs)
            _fold(P, "xt", x1keys)
            _fold(P, "aT0", akeys); _fold(P, "aT1", akeys); _fold(P, "aT2", akeys); _fold(P, "aT3", akeys)
        P.finish()
        print("stageC ops", P.n_ops())
    return nc


def _fold(P, dst, keys):
    d = P._st(dst)
    for k in keys:
        s = P._st(k)
        if s["w"] is not None:
            d["r"][("w", s["w"][1])] = s["w"] if ("w", s["w"][1]) not in d["r"] or d["r"][("w", s["w"][1])][2] < s["w"][2] else d["r"][("w", s["w"][1])]
        for sid, ev in s["r"].items():
            key = ("r", ev[1])
            if key not in d["r"] or d["r"][key][2] < ev[2]:
                d["r"][key] = ev


PH = 99
SUB = 0
ST = 512
NEG = -30000.0


def host_consts():
    c = {}
    c["ident"] = np.eye(128, dtype=np.float32)
    i = np.arange(64)[:, None]
    j = np.arange(64)[None, :]
    sl = (i > j).astype(np.float32)
    su = (j > i).astype(np.float32)
    ui = (j >= i).astype(np.float32)
    c["m_sl"] = np.tile(sl, (1, 8))
    c["m_su"] = np.tile(su, (1, 8))
    c["m_ui"] = np.tile(ui, (1, 8))
    c["ibig"] = np.tile(np.eye(64, dtype=np.float32), (1, 8))
    rm = np.ones((1, 512), np.float32)
    rm[0, ::64] = 0.0
    c["rmask"] = rm
    return c


def rel_bias_toeplitz(rb_h):
    q = np.arange(128)[:, None]
    cc = np.arange(640)[None, :]
    rel = cc - 512 - q
    return np.ascontiguousarray(rb_h[np.clip(rel, -256, 256) + 256]).astype(np.float32)


class AttnState:
    pass


def attn_setup(P, nc, C):
    A = AttnState()
    A.qw = P.sb("a_qw", [128, 1]); A.kw = P.sb("a_kw", [128, 1])
    A.bias = P.sb("a_bias", [128, 640])
    A.kbuf = P.sb("a_kbuf", [128, 1024])
    A.vtp = P.sb("a_vtp", [128, 8, 128])
    A.q = P.sb("a_q", [128, ST]); A.k = P.sb("a_k", [128, ST]); A.v = P.sb("a_v", [128, ST])
    A.sq = P.sb("a_sq", [128, ST])
    A.rs = P.sb("a_rs", [128, ST])
    A.sc = P.sb("a_sc", [128, 640])
    A.pt = P.sb("a_pt", [128, 640])
    A.ob = P.sb("a_ob", [128, ST])
    A.mx = P.sb("a_mx", [128, 1]); A.sm = P.sb("a_sm", [128, 1])
    return A


def attn_init(P, C, A, d):
    P.dma_op("sp", A.qw[:], d["qw"], writes=["a_qw"])
    P.dma_op("sp", A.kw[:], d["kw"], writes=["a_kw"])
    P.dma_op("sp", A.bias[:], d["bias"], writes=["a_bias"])
    P.op("pool", lambda e: e.memset(A.bias[0:64, 576:640], NEG), reads=[], writes=["a_bias"])
    P.op("pool", lambda e: e.memset(A.bias[64:128, 0:64], NEG), reads=[], writes=["a_bias"])
    P.op("pool", lambda e: e.memset(A.kbuf[:, 0:512], 0.0), writes=["a_kbuf"])
    P.op("pool", lambda e: e.memset(A.vtp[:, 0:4, :], 0.0), writes=["a_vtp"])


def attn_supertile(P, C, A, s, pT, out, T):
    t0 = s * ST
    tsl = slice(t0, t0 + ST)
    P.dma_op("sp", A.q[:], pT[512:640, tsl], writes=["a_q"])
    P.dma_op("sp", A.k[:], pT[640:768, tsl], writes=["a_k"])
    P.dma_op("sp", A.v[:], pT[768:896, tsl], writes=["a_v"])
    psB = C.psB
    for nm, x, w, dst, dk, sc_, bs_ in (("q", A.q, A.qw, A.q, "a_q", 1.0, "eps128"), ("k", A.k, A.kw, A.kbuf, "a_kbuf", 1.0 / 128, "eps")):
        P.op("act", lambda e, x=x: e.activation(out=A.sq[:], in_=x[:], func=AF.Square), reads=["a_" + nm], writes=["a_sq"])
        P.op("pe", lambda e: e.matmul(psB[:, :], lhsT=C.ones_f[:], rhs=A.sq[:], start=True, stop=True), reads=["ones_f", "a_sq"], writes=["psB"])
        P.op("act", lambda e, sc_=sc_, bs_=bs_: e.activation(out=A.rs[:], in_=psB[:, :], func=AF.Sqrt, bias=C.cst[bs_][:, 0:1], scale=sc_),
             reads=["psB", "cst"], writes=["a_rs"])
        P.op("dve", lambda e: e.reciprocal(out=A.rs[:], in_=A.rs[:]), reads=["a_rs"], writes=["a_rs"])
        if nm == "q":
            P.op("dve", lambda e: e.scalar_tensor_tensor(out=A.q[:], in0=A.q[:], scalar=A.qw[:, 0:1], in1=A.rs[:], op0=ALU.mult, op1=ALU.mult),
                 reads=["a_q", "a_qw", "a_rs"], writes=["a_q"])
        else:
            P.op("dve", lambda e: e.scalar_tensor_tensor(out=A.kbuf[:, 512:1024], in0=A.k[:], scalar=A.kw[:, 0:1], in1=A.rs[:], op0=ALU.mult, op1=ALU.mult),
                 reads=["a_k", "a_kw", "a_rs"], writes=["a_kbuf"])
    def tr(e):
        ins = None
        for b in range(4):
            ins = e.transpose(psB[:, b * 128:(b + 1) * 128], A.v[:, b * 128:(b + 1) * 128], C.ident[:])
        return ins
    P.op("pe", tr, reads=["a_v", "ident"], writes=["psB"])
    P.op("act", lambda e: e.activation(out=A.vtp[:, 4:8, :], in_=psB[:, :].rearrange("p (b e) -> p b e", b=4), func=AF.Copy),
         reads=["psB"], writes=["a_vtp"])
    psA = C.psA
    for j in range(4):
        def mm(e, j=j):
            e.matmul(psA[:, 0:512], lhsT=A.q[:, 128 * j:128 * j + 128], rhs=A.kbuf[:, 128 * j:128 * j + 512], start=True, stop=True)
            return e.matmul(psA[:, 512:640], lhsT=A.q[:, 128 * j:128 * j + 128], rhs=A.kbuf[:, 128 * j + 512:128 * j + 640], start=True, stop=True)
        P.op("pe", mm, reads=["a_q", "a_kbuf"], writes=["psA"])
        P.op("dve", lambda e: e.tensor_tensor(out=A.sc[:], in0=psA[:, 0:640], in1=A.bias[:], op=ALU.add), reads=["psA", "a_bias"], writes=["a_sc"])
        if s == 0:
            ninv = 512 - 128 * j
            P.op("pool", lambda e, ninv=ninv: e.memset(A.sc[:, 0:ninv], NEG), writes=["a_sc"])
        P.op("dve", lambda e: e.tensor_reduce(out=A.mx[:], in_=A.sc[:], axis=AX.X, op=ALU.max), reads=["a_sc"], writes=["a_mx"])
        P.op("dve", lambda e: e.tensor_scalar(out=A.mx[:], in0=A.mx[:], scalar1=-1.0, scalar2=None, op0=ALU.mult), reads=["a_mx"], writes=["a_mx"])
        P.op("act", lambda e: e.activation(out=A.sc[:], in_=A.sc[:], func=AF.Exp, bias=A.mx[:, 0:1], scale=1.0, accum_out=A.sm[:, 0:1]),
             reads=["a_sc", "a_mx"], writes=["a_sc", "a_sm"])
        P.op("dve", lambda e: e.reciprocal(out=A.sm[:], in_=A.sm[:]), reads=["a_sm"], writes=["a_sm"])
        P.op("dve", lambda e: e.tensor_scalar(out=A.sc[:], in0=A.sc[:], scalar1=A.sm[:, 0:1], scalar2=None, op0=ALU.mult), reads=["a_sc", "a_sm"], writes=["a_sc"])

        def trp(e):
            ins = None
            for kb in range(5):
                ins = e.transpose(psA[:, kb * 128:(kb + 1) * 128], A.sc[:, kb * 128:(kb + 1) * 128], C.ident[:])
            return ins
        P.op("pe", trp, reads=["a_sc", "ident"], writes=["psA"])
        P.op("act", lambda e: e.activation(out=A.pt[:], in_=psA[:, 0:640], func=AF.Copy), reads=["psA"], writes=["a_pt"])

        def mo(e, j=j):
            ins = None
            for kb in range(5):
                ins = e.matmul(psB[:, 0:128], lhsT=A.vtp[:, j + kb, :], rhs=A.pt[:, kb * 128:(kb + 1) * 128], start=(kb == 0), stop=(kb == 4))
            return ins
        P.op("pe", mo, reads=["a_vtp", "a_pt"], writes=["psB"])
        P.op("dve", lambda e, j=j: e.tensor_copy(out=A.ob[:, 128 * j:128 * j + 128], in_=psB[:, 0:128]), reads=["psB"], writes=["a_ob"])
    P.dma_op("sp", out[128:256, tsl], A.ob[:], reads=["a_ob"], is_output=True)
    P.op("pool", lambda e: e.tensor_copy(out=A.kbuf[:, 0:512], in_=A.kbuf[:, 512:1024]), reads=["a_kbuf"], writes=["a_kbuf"])
    P.op("pool", lambda e: e.tensor_copy(out=A.vtp[:, 0:4, :], in_=A.vtp[:, 4:8, :]), reads=["a_vtp"], writes=["a_vtp"])


def mixer_common(P, nc, C, cd):
    C.ones_f = P.sb("ones_f", [128, 128])
    P.op("pool", lambda e: e.memset(C.ones_f[:], 1.0), writes=["ones_f"])
    C.ident = P.sb("ident_sb", [128, 128])
    P.dma_op("sp", C.ident[:], cd["ident"], writes=["ident"])
    C.cst = {}
    for nm, v in (("eps", EPS), ("eps128", 128 * EPS), ("one", 1.0), ("eps512", EPS)):
        t = P.sb("cst_" + nm, [128, 1])
        P.op("pool", lambda e, t=t, v=v: e.memset(t[:], v), writes=["cst"])
        C.cst[nm] = t
    C.psA = P.ps("psA", [128, 1024])
    C.psB = P.ps("psB", [128, 512])


def build_attn_only(T):
    nc = new_nc()
    pT = dram_in(nc, "pT", [896, T])
    d = {"qw": dram_in(nc, "qw", [128, 1]), "kw": dram_in(nc, "kw", [128, 1]), "bias": dram_in(nc, "bias", [128, 640])}
    cd = {"ident": dram_in(nc, "ident", [128, 128])}
    out = dram_out(nc, "oT", [256, T])
    with ExitStack() as st:
        P = Prog(nc, st)
        C = Ctx()
        mixer_common(P, nc, C, cd)
        A = attn_setup(P, nc, C)
        attn_init(P, C, A, d)
        for s in range(T // ST):
            attn_supertile(P, C, A, s, pT, out, T)
        P.finish()
        print("attn ops", P.n_ops())
    return nc


class GS:
    pass


def gdn_setup(P, nc, C):
    G = GS()
    def sb(n, shp):
        return P.sb("g_" + n, shp)
    G.cw = sb("cw", [128, 12]); G.alog = sb("alog", [1, 1]); G.dtb = sb("dtb", [1, 1]); G.nA = sb("nA", [1, 1])
    G.nw = sb("nw", [128, 1])
    G.raw = [sb("raw%d" % i, [128, ST + 3]) for i in range(3)]
    G.z = sb("z", [128, ST]); G.gb = sb("gb", [1, ST]); G.ga = sb("ga", [1, ST])
    G.x = [sb("x%d" % i, [128, ST]) for i in range(3)]
    G.sq = sb("sq", [128, ST]); G.rs = sb("rs", [128, ST])
    G.beta = sb("beta", [1, ST]); G.g = sb("g", [1, ST]); G.gc = sb("gc", [1, ST]); G.ngc = sb("ngc", [1, ST])
    G.rmask = sb("rmask", [1, ST])
    G.beta_bc = sb("beta_bc", [128, ST]); G.gc_bc = sb("gc_bc", [128, ST]); G.egc_bc = sb("egc_bc", [128, ST]); G.ekd_bc = sb("ekd_bc", [128, ST])
    G.kb = sb("kb", [128, ST]); G.vb = sb("vb", [128, ST]); G.kbg = sb("kbg", [128, ST]); G.kd = sb("kd", [128, ST]); G.qd = sb("qd", [128, ST])
    for n in ("m_sl", "m_su", "m_ui", "ibig", "E1", "E2", "Dl", "Du", "Dq", "Pm", "Qm", "Ym", "qkT"):
        setattr(G, n, sb(n, [64, ST]))
    G.vb_tp = sb("vb_tp", [64, 1024]); G.kbg_tp = sb("kbg_tp", [64, 1024]); G.kd_tp = sb("kd_tp", [64, 1024]); G.u = sb("u", [64, 1024])
    G.wT = sb("wT", [128, ST]); G.S = sb("S", [128, 128]); G.vnew = sb("vnew", [64, 128])
    G.o = sb("o", [128, ST]); G.zs = sb("zs", [128, ST]); G.t1 = sb("t1", [128, ST]); G.res = sb("res", [128, ST])
    C.psC = P.ps("psC", [128, 1024]); C.psD = P.ps("psD", [128, 512]); C.psE = P.ps("psE", [128, 512]); C.psF = P.ps("psF", [128, 512])
    return G


def gdn_init(P, C, G, d, cd):
    for n, t in (("cw", G.cw), ("alog", G.alog), ("dtb", G.dtb), ("nw", G.nw)):
        P.dma_op("sp", t[:], d["g_" + n], writes=["g_" + n])
    for n in ("m_sl", "m_su", "m_ui", "ibig", "rmask"):
        P.dma_op("sp", getattr(G, n)[:], cd[n], writes=["g_" + n])
    P.op("act", lambda e: e.activation(out=G.nA[:], in_=G.alog[:], func=AF.Exp), reads=["g_alog"], writes=["g_nA"])
    P.op("dve", lambda e: e.tensor_scalar(out=G.nA[:], in0=G.nA[:], scalar1=-1.0, scalar2=None, op0=ALU.mult), reads=["g_nA"], writes=["g_nA"])
    P.op("pool", lambda e: e.memset(G.S[:], 0.0), writes=["g_S"])
    for i in range(3):
        P.op("pool", lambda e, i=i: e.memset(G.raw[i][:, 0:3], 0.0), writes=["g_raw%d" % i])


def gdn_supertile(P, C, G, s, pT, gates, out, T):
    t0 = s * ST
    tsl = slice(t0, t0 + ST)
    psB, psC, psD, psE, psF = C.psB, C.psC, C.psD, C.psE, C.psF
    ones_row = C.ones_f[0:1, 0:64]
    for i in range(3):
        if s == 0:
            P.dma_op("sp", G.raw[i][:, 3:ST + 3], pT[128 * i:128 * i + 128, 0:ST], writes=["g_raw%d" % i])
        else:
            P.dma_op("sp", G.raw[i][:, :], pT[128 * i:128 * i + 128, t0 - 3:t0 + ST], writes=["g_raw%d" % i])
    P.dma_op("sp", G.z[:], pT[384:512, tsl], writes=["g_z"])
    P.dma_op("sp", G.gb[:], gates[0:1, tsl], writes=["g_gb"])
    P.dma_op("sp", G.ga[:], gates[1:2, tsl], writes=["g_ga"])
    for i in range(3):
        x = G.x[i]; raw = G.raw[i]; xk = "g_x%d" % i; rk = "g_raw%d" % i
        P.op("dve", lambda e, x=x, raw=raw, i=i: e.tensor_scalar(out=x[:], in0=raw[:, 0:ST], scalar1=G.cw[:, 4 * i:4 * i + 1], scalar2=None, op0=ALU.mult),
             reads=[rk, "g_cw"], writes=[xk])
        for j in range(1, 4):
            P.op("dve", lambda e, x=x, raw=raw, i=i, j=j: e.scalar_tensor_tensor(out=x[:], in0=raw[:, j:j + ST], scalar=G.cw[:, 4 * i + j:4 * i + j + 1],
                                                                               in1=x[:], op0=ALU.mult, op1=ALU.add),
                 reads=[rk, "g_cw", xk], writes=[xk])
        P.op("act", lambda e, x=x: e.activation(out=x[:], in_=x[:], func=AF.Silu), reads=[xk], writes=[xk])
    q, k, v = G.x
    for i, (sc_, bs_) in enumerate(((128.0, "eps128"), (1.0, "eps"))):
        x = G.x[i]; xk = "g_x%d" % i
        P.op("act", lambda e, x=x: e.activation(out=G.sq[:], in_=x[:], func=AF.Square), reads=[xk], writes=["g_sq"])
        P.op("pe", lambda e: e.matmul(psB[:, :], lhsT=C.ones_f[:], rhs=G.sq[:], start=True, stop=True), reads=["ones_f", "g_sq"], writes=["psB"])
        P.op("act", lambda e, sc_=sc_, bs_=bs_: e.activation(out=G.rs[:], in_=psB[:, :], func=AF.Sqrt, bias=C.cst[bs_][:, 0:1], scale=sc_),
             reads=["psB", "cst"], writes=["g_rs"])
        P.op("dve", lambda e: e.reciprocal(out=G.rs[:], in_=G.rs[:]), reads=["g_rs"], writes=["g_rs"])
        P.op("dve", lambda e, x=x: e.tensor_tensor(out=x[:], in0=x[:], in1=G.rs[:], op=ALU.mult), reads=[xk, "g_rs"], writes=[xk])
    P.op("act", lambda e: e.activation(out=G.beta[:], in_=G.gb[:], func=AF.Exp, scale=-1.0), reads=["g_gb"], writes=["g_beta"])
    P.op("dve", lambda e: e.tensor_scalar(out=G.beta[:], in0=G.beta[:], scalar1=1.0, scalar2=None, op0=ALU.add), reads=["g_beta"], writes=["g_beta"])
    P.op("dve", lambda e: e.reciprocal(out=G.beta[:], in_=G.beta[:]), reads=["g_beta"], writes=["g_beta"])
    P.op("act", lambda e: e.activation(out=G.g[:], in_=G.ga[:], func=AF.Exp, bias=G.dtb[0:1, 0:1], scale=1.0), reads=["g_ga", "g_dtb"], writes=["g_g"])
    P.op("act", lambda e: e.activation(out=G.g[:], in_=G.g[:], func=AF.Ln, bias=C.cst["one"][0:1, 0:1], scale=1.0), reads=["g_g", "cst"], writes=["g_g"])
    P.op("dve", lambda e: e.tensor_scalar(out=G.g[:], in0=G.g[:], scalar1=G.nA[0:1, 0:1], scalar2=None, op0=ALU.mult), reads=["g_g", "g_nA"], writes=["g_g"])
    P.op("dve", lambda e: e.tensor_tensor_scan(out=G.gc[:], data0=G.rmask[:], data1=G.g[:], initial=0.0, op0=ALU.mult, op1=ALU.add),
         reads=["g_rmask", "g_g"], writes=["g_gc"])
    P.op("dve", lambda e: e.tensor_scalar(out=G.ngc[:], in0=G.gc[:], scalar1=-1.0, scalar2=None, op0=ALU.mult), reads=["g_gc"], writes=["g_ngc"])
    P.op("pe", lambda e: e.matmul(psB[:, :], lhsT=C.ones_f[0:1, :], rhs=G.beta[:], start=True, stop=True), reads=["ones_f", "g_beta"], writes=["psB"])
    P.op("act", lambda e: e.activation(out=G.beta_bc[:], in_=psB[:, :], func=AF.Copy), reads=["psB"], writes=["g_beta_bc"])
    P.op("pe", lambda e: e.matmul(psB[:, :], lhsT=C.ones_f[0:1, :], rhs=G.gc[:], start=True, stop=True), reads=["ones_f", "g_gc"], writes=["psB"])
    P.op("act", lambda e: e.activation(out=G.gc_bc[:], in_=psB[:, :], func=AF.Copy), reads=["psB"], writes=["g_gc_bc"])
    P.op("act", lambda e: e.activation(out=G.egc_bc[:], in_=G.gc_bc[:], func=AF.Exp), reads=["g_gc_bc"], writes=["g_egc_bc"])
    for c in range(8):
        P.op("act", lambda e, c=c: e.activation(out=G.ekd_bc[:, 64 * c:64 * c + 64], in_=G.gc_bc[:, 64 * c:64 * c + 64], func=AF.Exp,
                                               bias=G.gc_bc[:, 64 * c + 63:64 * c + 64], scale=-1.0),
             reads=["g_gc_bc"], writes=["g_ekd_bc"])
    P.op("dve", lambda e: e.tensor_tensor(out=G.kb[:], in0=k[:], in1=G.beta_bc[:], op=ALU.mult), reads=["g_x1", "g_beta_bc"], writes=["g_kb"])
    P.op("pool", lambda e: e.tensor_tensor(out=G.vb[:], in0=v[:], in1=G.beta_bc[:], op=ALU.mult), reads=["g_x2", "g_beta_bc"], writes=["g_vb"])
    P.op("dve", lambda e: e.tensor_tensor(out=G.kbg[:], in0=G.kb[:], in1=G.egc_bc[:], op=ALU.mult), reads=["g_kb", "g_egc_bc"], writes=["g_kbg"])
    P.op("pool", lambda e: e.tensor_tensor(out=G.kd[:], in0=k[:], in1=G.ekd_bc[:], op=ALU.mult), reads=["g_x1", "g_ekd_bc"], writes=["g_kd"])
    P.op("dve", lambda e: e.tensor_tensor(out=G.qd[:], in0=q[:], in1=G.egc_bc[:], op=ALU.mult), reads=["g_x0", "g_egc_bc"], writes=["g_qd"])

    def dec(e):
        ins = None
        for c in range(8):
            cs = slice(64 * c, 64 * c + 64)
            e.matmul(psC[0:64, 64 * c:64 * c + 64], lhsT=G.gc[0:1, cs], rhs=ones_row, start=True, stop=False)
            e.matmul(psC[0:64, 64 * c:64 * c + 64], lhsT=ones_row, rhs=G.ngc[0:1, cs], start=False, stop=True)
        for c in range(8):
            cs = slice(64 * c, 64 * c + 64)
            e.matmul(psC[0:64, 512 + 64 * c:512 + 64 * c + 64], lhsT=ones_row, rhs=G.gc[0:1, cs], start=True, stop=False)
            ins = e.matmul(psC[0:64, 512 + 64 * c:512 + 64 * c + 64], lhsT=G.ngc[0:1, cs], rhs=ones_row, start=False, stop=True)
        return ins
    P.op("pe", dec, reads=["g_gc", "g_ngc", "ones_f"], writes=["psC"])
    P.op("dve", lambda e: e.tensor_scalar(out=G.E1[:], in0=psC[0:64, 0:512], scalar1=0.0, scalar2=None, op0=ALU.min), reads=["psC"], writes=["g_E1"])
    P.op("dve", lambda e: e.tensor_scalar(out=G.E2[:], in0=psC[0:64, 512:1024], scalar1=0.0, scalar2=None, op0=ALU.min), reads=["psC"], writes=["g_E2"])
    P.op("act", lambda e: e.activation(out=G.E1[:], in_=G.E1[:], func=AF.Exp), reads=["g_E1"], writes=["g_E1"])
    P.op("act", lambda e: e.activation(out=G.E2[:], in_=G.E2[:], func=AF.Exp), reads=["g_E2"], writes=["g_E2"])
    P.op("pool", lambda e: e.tensor_tensor(out=G.Dl[:], in0=G.E1[:], in1=G.m_sl[:], op=ALU.mult), reads=["g_E1", "g_m_sl"], writes=["g_Dl"])
    P.op("pool", lambda e: e.tensor_tensor(out=G.Du[:], in0=G.E2[:], in1=G.m_su[:], op=ALU.mult), reads=["g_E2", "g_m_su"], writes=["g_Du"])
    P.op("pool", lambda e: e.tensor_tensor(out=G.Dq[:], in0=G.E2[:], in1=G.m_ui[:], op=ALU.mult), reads=["g_E2", "g_m_ui"], writes=["g_Dq"])

    def kk(e):
        ins = None
        for c in range(8):
            cs = slice(64 * c, 64 * c + 64)
            e.matmul(psC[0:64, 64 * c:64 * c + 64], lhsT=G.kb[:, cs], rhs=k[:, cs], start=True, stop=True)
        for c in range(8):
            cs = slice(64 * c, 64 * c + 64)
            ins = e.matmul(psC[0:64, 512 + 64 * c:512 + 64 * c + 64], lhsT=k[:, cs], rhs=G.kb[:, cs], start=True, stop=True)
        return ins
    P.op("pe", kk, reads=["g_kb", "g_x1"], writes=["psC"])

    def qkm(e):
        ins = None
        for c in range(8):
            cs = slice(64 * c, 64 * c + 64)
            ins = e.matmul(psD[0:64, 64 * c:64 * c + 64], lhsT=k[:, cs], rhs=q[:, cs], start=True, stop=True)
        return ins
    P.op("pe", qkm, reads=["g_x0", "g_x1"], writes=["psD"])
    P.op("dve", lambda e: e.tensor_tensor(out=G.Pm[:], in0=psC[0:64, 0:512], in1=G.Dl[:], op=ALU.mult), reads=["psC", "g_Dl"], writes=["g_Pm"])
    P.op("dve", lambda e: e.tensor_tensor(out=G.Qm[:], in0=psC[0:64, 512:1024], in1=G.Du[:], op=ALU.mult), reads=["psC", "g_Du"], writes=["g_Qm"])
    P.op("dve", lambda e: e.tensor_tensor(out=G.qkT[:], in0=psD[0:64, :], in1=G.Dq[:], op=ALU.mult), reads=["psD", "g_Dq"], writes=["g_qkT"])
    P.op("pool", lambda e: e.tensor_tensor(out=G.Ym[:], in0=G.ibig[:], in1=G.Qm[:], op=ALU.subtract), reads=["g_ibig", "g_Qm"], writes=["g_Ym"])
    for m in range(5):
        def sqr(e):
            ins = None
            for c in range(8):
                cs = slice(64 * c, 64 * c + 64)
                e.matmul(psC[0:64, 64 * c:64 * c + 64], lhsT=G.Qm[:, cs], rhs=G.Pm[:, cs], start=True, stop=True)
            for c in range(8):
                cs = slice(64 * c, 64 * c + 64)
                ins = e.matmul(psC[0:64, 512 + 64 * c:512 + 64 * c + 64], lhsT=G.Pm[:, cs], rhs=G.Qm[:, cs], start=True, stop=True)
            return ins
        P.op("pe", sqr, reads=["g_Pm", "g_Qm"], writes=["psC"])
        P.op("act", lambda e: e.activation(out=G.Pm[:], in_=psC[0:64, 0:512], func=AF.Copy), reads=["psC"], writes=["g_Pm"])
        P.op("dve", lambda e: e.tensor_copy(out=G.Qm[:], in_=psC[0:64, 512:1024]), reads=["psC"], writes=["g_Qm"])

        def yup(e):
            ins = None
            for c in range(8):
                cs = slice(64 * c, 64 * c + 64)
                e.matmul(psD[0:64, 64 * c:64 * c + 64], lhsT=G.ibig[:, 0:64], rhs=G.Ym[:, cs], start=True, stop=False)
                ins = e.matmul(psD[0:64, 64 * c:64 * c + 64], lhsT=G.Pm[:, cs], rhs=G.Ym[:, cs], start=False, stop=True)
            return ins
        P.op("pe", yup, reads=["g_ibig", "g_Ym", "g_Pm"], writes=["psD"])
        P.op("act", lambda e: e.activation(out=G.Ym[:], in_=psD[0:64, :], func=AF.Copy), reads=["psD"], writes=["g_Ym"])
    for src, dst, sk, dk in ((G.vb, G.vb_tp, "g_vb", "g_vb_tp"), (G.kbg, G.kbg_tp, "g_kbg", "g_kbg_tp"), (G.kd, G.kd_tp, "g_kd", "g_kd_tp")):
        def trp(e, src=src):
            ins = None
            for c in range(8):
                ins = e.transpose(psC[0:64, 128 * c:128 * c + 128], src[:, 64 * c:64 * c + 64], C.ident[:])
            return ins
        P.op("pe", trp, reads=[sk, "ident"], writes=["psC"])
        P.op("dve", lambda e, dst=dst: e.tensor_copy(out=dst[:], in_=psC[0:64, :]), reads=["psC"], writes=[dk])

    def um(e):
        ins = None
        for c in range(8):
            ins = e.matmul(psC[0:64, 128 * c:128 * c + 128], lhsT=G.Ym[:, 64 * c:64 * c + 64], rhs=G.vb_tp[:, 128 * c:128 * c + 128], start=True, stop=True)
        return ins
    P.op("pe", um, reads=["g_Ym", "g_vb_tp"], writes=["psC"])
    P.op("act", lambda e: e.activation(out=G.u[:], in_=psC[0:64, :], func=AF.Copy), reads=["psC"], writes=["g_u"])

    def wm(e):
        ins = None
        for c in range(8):
            ins = e.matmul(psD[:, 64 * c:64 * c + 64], lhsT=G.kbg_tp[:, 128 * c:128 * c + 128], rhs=G.Ym[:, 64 * c:64 * c + 64], start=True, stop=True)
        return ins
    P.op("pe", wm, reads=["g_Ym", "g_kbg_tp"], writes=["psD"])
    P.op("dve", lambda e: e.tensor_copy(out=G.wT[:], in_=psD[:, :]), reads=["psD"], writes=["g_wT"])
    for c in range(8):
        cs = slice(64 * c, 64 * c + 64)
        P.op("pe", lambda e, cs=cs: e.matmul(psF[0:64, 0:128], lhsT=G.wT[:, cs], rhs=G.S[:], start=True, stop=True), reads=["g_wT", "g_S"], writes=["psF_a"])
        P.op("dve", lambda e, c=c: e.tensor_tensor(out=G.vnew[:], in0=G.u[:, 128 * c:128 * c + 128], in1=psF[0:64, 0:128], op=ALU.subtract),
             reads=["g_u", "psF_a"], writes=["g_vnew"])

        def om(e, cs=cs, c=c):
            e.matmul(psE[:, cs], lhsT=G.S[:], rhs=G.qd[:, cs], start=True, stop=False)
            e.matmul(psE[:, cs], lhsT=G.vnew[:], rhs=G.qkT[:, cs], start=False, stop=True)
            return e.matmul(psF[:, 128:256], lhsT=G.kd_tp[:, 128 * c:128 * c + 128], rhs=G.vnew[:], start=True, stop=True)
        P.op("pe", om, reads=["g_S", "g_qd", "g_vnew", "g_qkT", "g_kd_tp"], writes=["psE", "psF_b"])
        P.op("dve", lambda e, c=c: e.scalar_tensor_tensor(out=G.S[:], in0=G.S[:], scalar=G.egc_bc[:, 64 * c + 63:64 * c + 64], in1=psF[:, 128:256],
                                                       op0=ALU.mult, op1=ALU.add),
             reads=["g_S", "g_egc_bc", "psF_b"], writes=["g_S"])
    P.op("act", lambda e: e.activation(out=G.o[:], in_=psE[:, :], func=AF.Copy), reads=["psE"], writes=["g_o"])
    P.op("act", lambda e: e.activation(out=G.sq[:], in_=G.o[:], func=AF.Square), reads=["g_o"], writes=["g_sq"])
    P.op("pe", lambda e: e.matmul(psB[:, :], lhsT=C.ones_f[:], rhs=G.sq[:], start=True, stop=True), reads=["ones_f", "g_sq"], writes=["psB"])
    P.op("act", lambda e: e.activation(out=G.rs[:], in_=psB[:, :], func=AF.Sqrt, bias=C.cst["eps"][:, 0:1], scale=1.0 / 128), reads=["psB", "cst"], writes=["g_rs"])
    P.op("dve", lambda e: e.reciprocal(out=G.rs[:], in_=G.rs[:]), reads=["g_rs"], writes=["g_rs"])
    P.op("act", lambda e: e.activation(out=G.zs[:], in_=G.z[:], func=AF.Silu), reads=["g_z"], writes=["g_zs"])
    P.op("dve", lambda e: e.tensor_tensor(out=G.t1[:], in0=G.o[:], in1=G.rs[:], op=ALU.mult), reads=["g_o", "g_rs"], writes=["g_t1"])
    P.op("dve", lambda e: e.scalar_tensor_tensor(out=G.res[:], in0=G.t1[:], scalar=G.nw[:, 0:1], in1=G.zs[:], op0=ALU.mult, op1=ALU.mult),
         reads=["g_t1", "g_nw", "g_zs"], writes=["g_res"])
    P.dma_op("sp", out[0:128, tsl], G.res[:], reads=["g_res"], is_output=True)


def build_B0(T, do_attn=True, do_gdn=True):
    nc = new_nc()
    pT = dram_in(nc, "pT", [896, T])
    gates = dram_in(nc, "gates", [2, T])
    d = {"qw": dram_in(nc, "qw", [128, 1]), "kw": dram_in(nc, "kw", [128, 1]), "bias": dram_in(nc, "bias", [128, 640]),
         "g_cw": dram_in(nc, "g_cw", [128, 12]), "g_alog": dram_in(nc, "g_alog", [1, 1]), "g_dtb": dram_in(nc, "g_dtb", [1, 1]),
         "g_nw": dram_in(nc, "g_nw", [128, 1])}
    cd = {"ident": dram_in(nc, "ident", [128, 128])}
    for n in ("m_sl", "m_su", "m_ui", "ibig"):
        cd[n] = dram_in(nc, n, [64, 512])
    cd["rmask"] = dram_in(nc, "rmask", [1, 512])
    out = dram_out(nc, "oT", [256, T])
    with ExitStack() as st:
        P = Prog(nc, st)
        C = Ctx()
        mixer_common(P, nc, C, cd)
        if do_attn:
            A = attn_setup(P, nc, C)
            attn_init(P, C, A, d)
        if do_gdn:
            G = gdn_setup(P, nc, C)
            gdn_init(P, C, G, d, cd)
        for s in range(T // ST):
            if do_gdn:
                gdn_supertile(P, C, G, s, pT, gates, out, T)
            if do_attn:
                attn_supertile(P, C, A, s, pT, out, T)
        P.finish()
        print("B0 ops", P.n_ops())
    return nc


def ssd_host_consts():
    c = {}
    c["ident"] = np.eye(128, dtype=np.float32)
    sp = np.zeros((8, 4, 128), np.float32)
    for j in range(4):
        sp[2 * j, j, 0:64] = 1.0
        sp[2 * j + 1, j, 64:128] = 1.0
    c["selpair"] = sp.reshape(8, 512)
    so = np.zeros((8, 8, 128), np.float32)
    for e in range(8):
        so[e, e, :] = 1.0
    c["selones"] = so.reshape(8, 1024)
    c["sel8"] = np.eye(8, dtype=np.float32)
    c["ones8"] = np.ones((8, 64), np.float32)
    rm = np.ones((8, 512), np.float32)
    rm[:, ::64] = 0.0
    c["rmask8"] = rm
    m = np.arange(64)[:, None]
    l = np.arange(64)[None, :]
    c["negmask"] = np.tile(np.where(l >= m, 0.0, NEG).astype(np.float32), (1, 8))
    return c


def ssd_setup(P, nc, C):
    S = GS()
    def sb(n, shp):
        return P.sb("s_" + n, shp)
    S.cw = sb("cw", [128, 24]); S.cb = sb("cb", [128, 6]); S.dcol = sb("dcol", [128, 4]); S.nw = sb("nw", [128, 4])
    S.dtb = sb("dtb", [8, 1]); S.alog = sb("alog", [8, 1]); S.acol = sb("acol", [8, 1])
    S.selpair = sb("selpair", [8, 512]); S.selones = sb("selones", [8, 1024]); S.sel8 = sb("sel8", [8, 8]); S.ones8 = sb("ones8", [8, 64])
    S.rmask8 = sb("rmask8", [8, 512]); S.negmask = sb("negmask", [64, 512])
    S.raw = [sb("raw%d" % i, [128, ST + 3]) for i in range(6)]
    S.z = [sb("z%d" % i, [128, ST]) for i in range(4)]
    S.xc = [sb("xc%d" % i, [128, ST]) for i in range(6)]
    S.dtr = sb("dtr", [8, ST]); S.dt = sb("dt", [8, ST]); S.acum = sb("acum", [8, ST]); S.nacum = sb("nacum", [8, ST]); S.eac = sb("eac", [8, ST])
    S.wdec = sb("wdec", [8, ST]); S.dtw = sb("dtw", [8, ST]); S.nacm = sb("nacm", [8, 8, ST])
    S.xdtT = [sb("xdtT%d" % j, [128, ST]) for j in range(4)]
    S.xdtwT = [sb("xdtwT%d" % j, [128, ST]) for j in range(4)]
    S.Cdec = sb("Cdec", [128, 8, ST]); S.glt = sb("glt", [128, 8, 8]); S.eacl = sb("eacl", [8, 8])
    S.B_tp = sb("B_tp", [64, 1024]); S.xdt_tp = sb("xdt_tp", [64, ST]); S.xdtw_tp = sb("xdtw_tp", [64, ST])
    S.E = sb("E", [64, ST]); S.cbd = sb("cbd", [64, ST]); S.y_sb = sb("y_sb", [64, ST]); S.yT = sb("yT", [128, 4, ST])
    S.ST = sb("ST", [128, ST])
    S.gt = sb("gt", [128, 4, ST]); S.sq = sb("sq", [128, 4, ST]); S.zs = sb("zs", [128, ST]); S.rs = sb("rs", [128, ST]); S.res = sb("res", [128, ST])
    C.psX = P.ps("psX", [128, 512]); C.psY = P.ps("psY", [128, 512]); C.psYY = P.ps("psYY", [128, 512]); C.psS = P.ps("psS", [128, 512])
    C.psT = C.psA; C.psV = P.ps("psV", [128, 512])
    return S


def ssd_init(P, C, S, d, cd):
    for n in ("cw", "cb", "dcol", "nw", "dtb", "alog"):
        P.dma_op("sp", getattr(S, n)[:], d["s_" + n], writes=["s_" + n])
    for n in ("selpair", "selones", "sel8", "ones8", "rmask8", "negmask"):
        P.dma_op("sp", getattr(S, n)[:], cd[n], writes=["s_" + n])
    P.op("act", lambda e: e.activation(out=S.acol[:], in_=S.alog[:], func=AF.Exp), reads=["s_alog"], writes=["s_acol"])
    P.op("dve", lambda e: e.tensor_scalar(out=S.acol[:], in0=S.acol[:], scalar1=-1.0, scalar2=None, op0=ALU.mult), reads=["s_acol"], writes=["s_acol"])
    P.op("pool", lambda e: e.memset(S.ST[:], 0.0), writes=["s_ST"])
    for i in range(6):
        P.op("pool", lambda e, i=i: e.memset(S.raw[i][:, 0:3], 0.0), writes=["s_raw%d" % i])


def ssd_supertile(P, C, S, s, pT, dtT, out, T):
    t0 = s * ST
    tsl = slice(t0, t0 + ST)
    psB, psX, psY, psYY, psS, psT, psV = C.psB, C.psX, C.psY, C.psYY, C.psS, C.psT, C.psV
    for i in range(6):
        r0 = 512 + 128 * i
        if s == 0:
            P.dma_op("sp", S.raw[i][:, 3:ST + 3], pT[r0:r0 + 128, 0:ST], writes=["s_raw%d" % i])
        else:
            P.dma_op("sp", S.raw[i][:, :], pT[r0:r0 + 128, t0 - 3:t0 + ST], writes=["s_raw%d" % i])
    for j in range(4):
        P.dma_op("sp", S.z[j][:], pT[128 * j:128 * j + 128, tsl], writes=["s_z%d" % j])
    P.dma_op("sp", S.dtr[:], dtT[:, tsl], writes=["s_dtr"])
    for i in range(6):
        x = S.xc[i]; raw = S.raw[i]; xk = "s_xc%d" % i; rk = "s_raw%d" % i
        P.op("dve", lambda e, x=x, raw=raw, i=i: e.tensor_scalar(out=x[:], in0=raw[:, 0:ST], scalar1=S.cw[:, 4 * i:4 * i + 1], scalar2=None, op0=ALU.mult),
             reads=[rk, "s_cw"], writes=[xk])
        for j in range(1, 4):
            P.op("dve", lambda e, x=x, raw=raw, i=i, j=j: e.scalar_tensor_tensor(out=x[:], in0=raw[:, j:j + ST], scalar=S.cw[:, 4 * i + j:4 * i + j + 1],
                                                                               in1=x[:], op0=ALU.mult, op1=ALU.add),
                 reads=[rk, "s_cw", xk], writes=[xk])
        P.op("act", lambda e, x=x, i=i: e.activation(out=x[:], in_=x[:], func=AF.Silu, bias=S.cb[:, i:i + 1], scale=1.0), reads=[xk, "s_cb"], writes=[xk])
    xs = S.xc[0:4]; Bc = S.xc[4]; Cc = S.xc[5]
    if PH < 1:
        return
    P.op("act", lambda e: e.activation(out=S.dt[:], in_=S.dtr[:], func=AF.Exp, bias=S.dtb[:, 0:1], scale=1.0), reads=["s_dtr", "s_dtb"], writes=["s_dt"])
    P.op("act", lambda e: e.activation(out=S.dt[:], in_=S.dt[:], func=AF.Ln, bias=C.cst["one"][0:8, 0:1], scale=1.0), reads=["s_dt", "cst"], writes=["s_dt"])
    P.op("dve", lambda e: e.tensor_scalar(out=S.nacum[:], in0=S.dt[:], scalar1=S.acol[:, 0:1], scalar2=None, op0=ALU.mult), reads=["s_dt", "s_acol"], writes=["s_nacum"])
    P.op("dve", lambda e: e.tensor_tensor_scan(out=S.acum[:], data0=S.rmask8[:], data1=S.nacum[:], initial=0.0, op0=ALU.mult, op1=ALU.add),
         reads=["s_rmask8", "s_nacum"], writes=["s_acum"])
    P.op("dve", lambda e: e.tensor_scalar(out=S.nacum[:], in0=S.acum[:], scalar1=-1.0, scalar2=None, op0=ALU.mult), reads=["s_acum"], writes=["s_nacum"])
    P.op("act", lambda e: e.activation(out=S.eac[:], in_=S.acum[:], func=AF.Exp), reads=["s_acum"], writes=["s_eac"])
    for c in range(8):
        P.op("act", lambda e, c=c: e.activation(out=S.wdec[:, 64 * c:64 * c + 64], in_=S.acum[:, 64 * c:64 * c + 64], func=AF.Exp,
                                               bias=S.acum[:, 64 * c + 63:64 * c + 64], scale=-1.0), reads=["s_acum"], writes=["s_wdec"])
    P.op("dve", lambda e: e.tensor_tensor(out=S.dtw[:], in0=S.dt[:], in1=S.wdec[:], op=ALU.mult), reads=["s_dt", "s_wdec"], writes=["s_dtw"])
    for h in range(8):
        P.op("dve", lambda e, h=h: e.tensor_scalar(out=S.nacm[:, h, :], in0=S.nacum[:], scalar1=S.sel8[:, h:h + 1], scalar2=None, op0=ALU.mult),
             reads=["s_nacum", "s_sel8"], writes=["s_nacm"])
    if PH < 2:
        return
    for j in range(4):
        P.op("pe", lambda e, j=j: e.matmul(psB[:, :], lhsT=S.selpair[:, 128 * j:128 * j + 128], rhs=S.dt[:], start=True, stop=True), reads=["s_selpair", "s_dt"], writes=["psB"])
        P.op("dve", lambda e, j=j: e.tensor_tensor(out=S.xdtT[j][:], in0=xs[j][:], in1=psB[:, :], op=ALU.mult), reads=["s_xc%d" % j, "psB"], writes=["s_xdtT%d" % j])
        P.op("pe", lambda e, j=j: e.matmul(psB[:, :], lhsT=S.selpair[:, 128 * j:128 * j + 128], rhs=S.dtw[:], start=True, stop=True), reads=["s_selpair", "s_dtw"], writes=["psB"])
        P.op("dve", lambda e, j=j: e.tensor_tensor(out=S.xdtwT[j][:], in0=xs[j][:], in1=psB[:, :], op=ALU.mult), reads=["s_xc%d" % j, "psB"], writes=["s_xdtwT%d" % j])
    if SUB == 1:
        return
    for h in range(8):
        P.op("pe", lambda e, h=h: e.matmul(psB[:, :], lhsT=S.selones[:, 128 * h:128 * h + 128], rhs=S.eac[:], start=True, stop=True), reads=["s_selones", "s_eac"], writes=["psB"])
        P.op("dve", lambda e, h=h: e.tensor_tensor(out=S.Cdec[:, h, :], in0=Cc[:], in1=psB[:, :], op=ALU.mult), reads=["s_xc5", "psB"], writes=["s_Cdec"])
    if SUB == 2:
        return
    P.op("dve", lambda e: e.tensor_copy(out=S.eacl[:], in_=S.eac[:].rearrange("p (c l) -> p c l", l=64)[:, :, 63]), reads=["s_eac"], writes=["s_eacl"])

    def glm(e):
        ins = None
        for h in range(8):
            ins = e.matmul(psX[:, 8 * h:8 * h + 8], lhsT=S.selones[:, 128 * h:128 * h + 128], rhs=S.eacl[:], start=True, stop=True)
        return ins
    P.op("pe", glm, reads=["s_selones", "s_eacl"], writes=["psX"])
    P.op("act", lambda e: e.activation(out=S.glt[:].rearrange("p h c -> p (h c)"), in_=psX[:, 0:64], func=AF.Copy), reads=["psX"], writes=["s_glt"])

    if PH < 3:
        return

    def btr(e):
        ins = None
        for c in range(8):
            ins = e.transpose(psT[0:64, 128 * c:128 * c + 128], Bc[:, 64 * c:64 * c + 64], C.ident[:])
        return ins
    P.op("pe", btr, reads=["s_xc4", "ident"], writes=["psA"])
    P.op("dve", lambda e: e.tensor_copy(out=S.B_tp[:], in_=psT[0:64, :]), reads=["psA"], writes=["s_B_tp"])
    if PH < 4:
        return
    for c in range(8):
        cs = slice(64 * c, 64 * c + 64)

        def xtr(e, cs=cs):
            ins = None
            for j in range(4):
                e.transpose(psT[0:64, 128 * j:128 * j + 128], S.xdtT[j][:, cs], C.ident[:])
            for j in range(4):
                ins = e.transpose(psT[0:64, 512 + 128 * j:512 + 128 * j + 128], S.xdtwT[j][:, cs], C.ident[:])
            return ins
        P.op("pe", xtr, reads=["s_xdtT%d" % j for j in range(4)] + ["s_xdtwT%d" % j for j in range(4)] + ["ident"], writes=["psA"])
        P.op("act", lambda e: e.activation(out=S.xdt_tp[:], in_=psT[0:64, 0:512], func=AF.Copy), reads=["psA"], writes=["s_xdt_tp"])
        P.op("dve", lambda e: e.tensor_copy(out=S.xdtw_tp[:], in_=psT[0:64, 512:1024]), reads=["psA"], writes=["s_xdtw_tp"])

        if PH < 5:
            continue

        def dm(e, cs=cs):
            ins = None
            for h in range(8):
                e.matmul(psX[0:64, 64 * h:64 * h + 64], lhsT=S.selones[:, 128 * h:128 * h + 64], rhs=S.acum[:, cs], start=True, stop=False)
                e.matmul(psX[0:64, 64 * h:64 * h + 64], lhsT=S.nacm[:, h, cs], rhs=S.ones8[:, :], start=False, stop=True)
            for h in range(8):
                ins = e.matmul(psY[0:64, 64 * h:64 * h + 64], lhsT=Bc[:, cs], rhs=Cc[:, cs], start=True, stop=True)
            return ins
        P.op("pe", dm, reads=["s_selones", "s_acum", "s_nacm", "s_ones8", "s_xc4", "s_xc5"], writes=["psX", "psY"])
        P.op("dve", lambda e: e.scalar_tensor_tensor(out=S.E[:], in0=psX[0:64, :], scalar=0.0, in1=S.negmask[:], op0=ALU.min, op1=ALU.add),
             reads=["psX", "s_negmask"], writes=["s_E"])
        P.op("act", lambda e: e.activation(out=S.E[:], in_=S.E[:], func=AF.Exp), reads=["s_E"], writes=["s_E"])
        P.op("dve", lambda e: e.tensor_tensor(out=S.cbd[:], in0=psY[0:64, :], in1=S.E[:], op=ALU.mult), reads=["psY", "s_E"], writes=["s_cbd"])

        if PH < 6:
            continue

        def ym(e, cs=cs):
            ins = None
            for h in range(8):
                hs = slice(64 * h, 64 * h + 64)
                e.matmul(psYY[0:64, hs], lhsT=S.Cdec[:, h, cs], rhs=S.ST[:, hs], start=True, stop=False)
                ins = e.matmul(psYY[0:64, hs], lhsT=S.cbd[:, hs], rhs=S.xdt_tp[:, hs], start=False, stop=True)
            return ins
        P.op("pe", ym, reads=["s_Cdec", "s_ST", "s_cbd", "s_xdt_tp"], writes=["psYY"])
        P.op("act", lambda e: e.activation(out=S.y_sb[:], in_=psYY[0:64, :], func=AF.Copy), reads=["psYY"], writes=["s_y_sb"])
        P.op("pe", lambda e, c=c: e.matmul(psS[:, :], lhsT=S.B_tp[:, 128 * c:128 * c + 128], rhs=S.xdtw_tp[:], start=True, stop=True),
             reads=["s_B_tp", "s_xdtw_tp"], writes=["psS"])
        for h in range(8):
            hs = slice(64 * h, 64 * h + 64)
            P.op("dve", lambda e, h=h, hs=hs, c=c: e.scalar_tensor_tensor(out=S.ST[:, hs], in0=S.ST[:, hs], scalar=S.glt[:, h, c:c + 1], in1=psS[:, hs],
                                                                       op0=ALU.mult, op1=ALU.add),
                 reads=["s_ST", "s_glt", "psS"], writes=["s_ST"], same_sync=False)

        if PH < 7:
            continue

        def ytr(e):
            ins = None
            for j in range(4):
                ins = e.transpose(psV[:, 64 * j:64 * j + 64], S.y_sb[:, 128 * j:128 * j + 128], C.ident[0:64, 0:64])
            return ins
        P.op("pe", ytr, reads=["s_y_sb", "ident"], writes=["psV"])
        P.op("act", lambda e, cs=cs: e.activation(out=S.yT[:, :, cs], in_=psV[:, 0:256].rearrange("p (j l) -> p j l", j=4), func=AF.Copy),
             reads=["psV"], writes=["s_yT"])
    if PH < 8:
        return
    for j in range(4):
        P.op("dve", lambda e, j=j: e.scalar_tensor_tensor(out=S.gt[:, j, :], in0=xs[j][:], scalar=S.dcol[:, j:j + 1], in1=S.yT[:, j, :], op0=ALU.mult, op1=ALU.add),
             reads=["s_xc%d" % j, "s_dcol", "s_yT"], writes=["s_gt%d" % j])
        P.op("act", lambda e, j=j: e.activation(out=S.zs[:], in_=S.z[j][:], func=AF.Silu), reads=["s_z%d" % j], writes=["s_zs"])
        P.op("dve", lambda e, j=j: e.tensor_tensor(out=S.gt[:, j, :], in0=S.gt[:, j, :], in1=S.zs[:], op=ALU.mult), reads=["s_gt%d" % j, "s_zs"], writes=["s_gt%d" % j])
        P.op("act", lambda e, j=j: e.activation(out=S.sq[:, j, :], in_=S.gt[:, j, :], func=AF.Square), reads=["s_gt%d" % j], writes=["s_sq%d" % j])

    def nm(e):
        ins = None
        for j in range(4):
            ins = e.matmul(psB[:, :], lhsT=C.ones_f[:], rhs=S.sq[:, j, :], start=(j == 0), stop=(j == 3))
        return ins
    P.op("pe", nm, reads=["ones_f"] + ["s_sq%d" % j for j in range(4)], writes=["psB"])
    P.op("act", lambda e: e.activation(out=S.rs[:], in_=psB[:, :], func=AF.Sqrt, bias=C.cst["eps"][:, 0:1], scale=1.0 / 512), reads=["psB", "cst"], writes=["s_rs"])
    P.op("dve", lambda e: e.reciprocal(out=S.rs[:], in_=S.rs[:]), reads=["s_rs"], writes=["s_rs"])
    for j in range(4):
        P.op("dve", lambda e, j=j: e.scalar_tensor_tensor(out=S.res[:], in0=S.gt[:, j, :], scalar=S.nw[:, j:j + 1], in1=S.rs[:], op0=ALU.mult, op1=ALU.mult),
             reads=["s_gt%d" % j, "s_nw", "s_rs"], writes=["s_res"])
        P.dma_op("sp", out[128 * j:128 * j + 128, tsl], S.res[:], reads=["s_res"], is_output=True)


def build_B1(T):
    nc = new_nc()
    pT = dram_in(nc, "pT", [1280, T])
    dtT = dram_in(nc, "dtT", [8, T])
    d = {"s_cw": dram_in(nc, "s_cw", [128, 24]), "s_cb": dram_in(nc, "s_cb", [128, 6]), "s_dcol": dram_in(nc, "s_dcol", [128, 4]),
         "s_nw": dram_in(nc, "s_nw", [128, 4]), "s_dtb": dram_in(nc, "s_dtb", [8, 1]), "s_alog": dram_in(nc, "s_alog", [8, 1])}
    cd = {"ident": dram_in(nc, "ident", [128, 128]), "selpair": dram_in(nc, "selpair", [8, 512]), "selones": dram_in(nc, "selones", [8, 1024]),
          "sel8": dram_in(nc, "sel8", [8, 8]), "ones8": dram_in(nc, "ones8", [8, 64]), "rmask8": dram_in(nc, "rmask8", [8, 512]),
          "negmask": dram_in(nc, "negmask", [64, 512])}
    out = dram_out(nc, "yT", [512, T])
    with ExitStack() as st:
        P = Prog(nc, st)
        C = Ctx()
        mixer_common(P, nc, C, cd)
        S = ssd_setup(P, nc, C)
        ssd_init(P, C, S, d, cd)
        for s in range(T // ST):
            ssd_supertile(P, C, S, s, pT, dtT, out, T)
        P.finish()
        print("B1 ops", P.n_ops())
    return nc

NCORES = 8
SEQ = 16384
TOKC = SEQ // NCORES

OFF = {"gq": 0, "gk": 1024, "gv": 2048, "gz": 3072, "gb": 4096, "ga": 4104, "aq": 4112, "ak": 5136, "av": 6160}


def mod_layout(modrow):
    return np.ascontiguousarray(modrow.reshape(6, 16, 128).transpose(2, 0, 1).reshape(128, 96))


def col_layout(v):
    return np.ascontiguousarray(v.reshape(16, 128).T)


def build_mod():
    nc = new_nc()
    c_in = dram_in(nc, "c_in", [128, 16])
    w_in = dram_in(nc, "w_in", [2, 2048, 1536])
    b_in = dram_in(nc, "b_in", [128, 24])
    out = dram_out(nc, "out", [128, 24])
    with ExitStack() as st:
        P = Prog(nc, st)
        ct = P.sb("ct", [128, 16]); ca = P.sb("ca", [128, 16])
        bt = P.sb("bt", [128, 24]); ot = P.sb("ot", [128, 24])
        wt = P.sb("wt", [128, 16, 1536])
        acc = P.ps("acc", [128, 512])
        P.dma_op("sp", ct[:], c_in, writes=["ct"])
        P.dma_op("sp", bt[:], b_in, writes=["bt"])
        P.op("act", lambda e: e.activation(out=ca[:], in_=ct[:], func=AF.Silu), reads=["ct"], writes=["ca"])
        for l in range(2):
            P.dma_op("sp", wt[:], w_in[l].rearrange("(kt p) n -> p kt n", p=128), writes=["wt"])

            def mm(e, l=l):
                ins = None
                for j in range(12):
                    for kt in range(16):
                        ins = e.matmul(acc[:, l * 12 + j:l * 12 + j + 1], lhsT=wt[:, kt, j * 128:(j + 1) * 128], rhs=ca[:, kt:kt + 1],
                                       start=(kt == 0), stop=(kt == 15))
                return ins
            P.op("pe", mm, reads=["wt", "ca"], writes=["acc"])
        P.op("dve", lambda e: e.tensor_tensor(out=ot[:], in0=acc[:, 0:24], in1=bt[:], op=ALU.add), reads=["acc", "bt"], writes=["ot"])
        P.dma_op("sp", out, ot[:], reads=["ot"], is_output=True)
        P.finish()
    return nc


def run(nc, in_maps):
    res = run_bass_kernel_spmd(nc, in_maps, core_ids=list(range(NCORES)))
    return res.results


def kernel(x, c, mod_w, mod_b, norm_mix_w, norm_mlp_w, mlp_w1, mlp_w2,
           ab_w_in, gdn_conv_w, gdn_a_log, gdn_dt_bias, gdn_norm_w,
           attn_q_norm_w, attn_k_norm_w, attn_rel_bias, ab_w_out,
           ssd_w_in, ssd_conv_w, ssd_conv_b, ssd_dt_bias, ssd_a_log, ssd_d,
           ssd_norm_w, ssd_w_out):
    f = lambda a: np.ascontiguousarray(np.asarray(a, dtype=np.float32))
    x = f(x); c = f(c); mod_w = f(mod_w); mod_b = f(mod_b)
    cl = np.ascontiguousarray(c.reshape(16, 128).T)
    ims = []
    for core in range(NCORES):
        sl = slice(core * 1536, (core + 1) * 1536)
        b = np.stack([mod_b[l, sl].reshape(12, 128).T for l in range(2)], axis=1).reshape(128, 24)
        ims.append({"c_in": cl, "w_in": np.ascontiguousarray(mod_w[:, :, sl]), "b_in": np.ascontiguousarray(b)})
    r = run(build_mod(), ims)
    mod = np.zeros((2, 12288), np.float32)
    for core in range(NCORES):
        o = r[core]["out"].reshape(128, 2, 12)
        for l in range(2):
            mod[l, core * 1536:(core + 1) * 1536] = o[:, l, :].T.reshape(-1)
    modl = [mod_layout(mod[l]) for l in range(2)]
    xT = [np.ascontiguousarray(x[0, cc * TOKC:(cc + 1) * TOKC].T) for cc in range(NCORES)]

    perm0 = np.concatenate([np.concatenate([OFF[n] + 128 * i + np.arange(128) for n in ("gq", "gk", "gv", "gz", "aq", "ak", "av")]) for i in range(8)]
                           + [np.arange(4096, 4112)])
    w0 = np.ascontiguousarray(f(ab_w_in)[0][:, perm0])
    nw0 = col_layout(f(norm_mix_w)[0])
    r = run(build_stageA(7184, TOKC), [{"xT": xT[cc], "w": w0, "nw": nw0, "modl": modl[0]} for cc in range(NCORES)])
    projT = [r[cc]["projT"] for cc in range(NCORES)]
    hc = host_consts()
    cw = f(gdn_conv_w)[0]
    ims = []
    for i in range(8):
        pT = np.ascontiguousarray(np.concatenate([projT[cc][896 * i:896 * (i + 1)] for cc in range(NCORES)], axis=1))
        gates = np.ascontiguousarray(np.concatenate([projT[cc][[7168 + i, 7176 + i]] for cc in range(NCORES)], axis=1))
        cwh = np.concatenate([cw[:, 1024 * j + 128 * i: 1024 * j + 128 * (i + 1)].T for j in range(3)], axis=1)
        im = {"pT": pT, "gates": gates,
              "qw": f(attn_q_norm_w)[0].reshape(128, 1).copy(), "kw": f(attn_k_norm_w)[0].reshape(128, 1).copy(),
              "bias": rel_bias_toeplitz(f(attn_rel_bias)[0, i]),
              "g_cw": np.ascontiguousarray(cwh), "g_alog": f(gdn_a_log)[0, i].reshape(1, 1).copy(), "g_dtb": f(gdn_dt_bias)[0, i].reshape(1, 1).copy(),
              "g_nw": f(gdn_norm_w)[0].reshape(128, 1).copy()}
        im.update(hc)
        ims.append(im)
    del projT
    r = run(build_B0(SEQ), ims)
    oTh = [r[i]["oT"] for i in range(8)]
    ims = []
    wo0 = f(ab_w_out)[0]; w1 = f(mlp_w1); w2 = f(mlp_w2)
    for cc in range(NCORES):
        ts = slice(cc * TOKC, (cc + 1) * TOKC)
        oT = np.ascontiguousarray(np.concatenate([oTh[i][0:128, ts] for i in range(8)] + [oTh[i][128:256, ts] for i in range(8)], axis=0))
        ims.append({"xT": xT[cc], "oT": oT, "wo": wo0, "w1": w1[0], "w2": w2[0], "nw": col_layout(f(norm_mlp_w)[0]), "modl": modl[0]})
    r = run(build_stageC(2048, TOKC), ims)
    xT = [r[cc]["x2T"] for cc in range(NCORES)]

    perm1 = np.concatenate([np.concatenate([512 * g + np.arange(512), 4096 + 512 * g + np.arange(512), 8192 + 128 * g + np.arange(128),
                                            9216 + 128 * g + np.arange(128)]) for g in range(8)] + [10240 + np.arange(64)])
    ws = np.ascontiguousarray(f(ssd_w_in)[0][:, perm1])
    nw1 = col_layout(f(norm_mix_w)[1])
    r = run(build_stageA(10304, TOKC), [{"xT": xT[cc], "w": ws, "nw": nw1, "modl": modl[1]} for cc in range(NCORES)])
    projT = [r[cc]["projT"] for cc in range(NCORES)]
    shc = ssd_host_consts()
    scw = f(ssd_conv_w)[0]; scb = f(ssd_conv_b)[0]
    ims = []
    for g in range(8):
        pT = np.ascontiguousarray(np.concatenate([projT[cc][1280 * g:1280 * (g + 1)] for cc in range(NCORES)], axis=1))
        dtT = np.ascontiguousarray(np.concatenate([projT[cc][10240 + 8 * g:10240 + 8 * (g + 1)] for cc in range(NCORES)], axis=1))
        chans = [np.arange(512 * g + 128 * j, 512 * g + 128 * (j + 1)) for j in range(4)] + [np.arange(4096 + 128 * g, 4096 + 128 * (g + 1)),
                                                                                           np.arange(5120 + 128 * g, 5120 + 128 * (g + 1))]
        cwl = np.concatenate([scw[:, ch].T for ch in chans], axis=1)
        cbl = np.stack([scb[ch] for ch in chans], axis=1)
        dsk = f(ssd_d)[0][8 * g:8 * (g + 1)]
        dcol = np.stack([np.repeat(dsk[2 * j:2 * j + 2], 64) for j in range(4)], axis=1)
        nwg = f(ssd_norm_w)[0][512 * g:512 * (g + 1)].reshape(4, 128).T
        im = {"pT": pT, "dtT": dtT, "s_cw": np.ascontiguousarray(cwl), "s_cb": np.ascontiguousarray(cbl), "s_dcol": np.ascontiguousarray(dcol),
              "s_nw": np.ascontiguousarray(nwg), "s_dtb": f(ssd_dt_bias)[0][8 * g:8 * (g + 1)].reshape(8, 1).copy(),
              "s_alog": f(ssd_a_log)[0][8 * g:8 * (g + 1)].reshape(8, 1).copy()}
        im.update(shc)
        ims.append(im)
    del projT
    r = run(build_B1(SEQ), ims)
    yTg = [r[g]["yT"] for g in range(8)]
    ims = []
    wo1 = f(ssd_w_out)[0]
    for cc in range(NCORES):
        ts = slice(cc * TOKC, (cc + 1) * TOKC)
        oT = np.ascontiguousarray(np.concatenate([yTg[g][:, ts] for g in range(8)], axis=0))
        ims.append({"xT": xT[cc], "oT": oT, "wo": wo1, "w1": w1[1], "w2": w2[1], "nw": col_layout(f(norm_mlp_w)[1]), "modl": modl[1]})
    r = run(build_stageC(4096, TOKC), ims)
    out = np.concatenate([r[cc]["x2T"].T for cc in range(NCORES)], axis=0)[None]
    return np.ascontiguousarray(out.astype(np.float32))
```

```python
import numpy as np
from contextlib import ExitStack
import concourse.bass as bass
import concourse.mybir as mybir
from concourse.bass_utils import run_bass_kernel_spmd

F32 = mybir.dt.float32
BF16 = mybir.dt.bfloat16
AF = mybir.ActivationFunctionType
ALU = mybir.AluOpType
AX = mybir.AxisListType

ENGS = ["pe", "dve", "act", "pool", "sp"]
EPOCH = 30000
SAME_ENGINE_SYNC = True


class Prog:
    def __init__(self, nc, stack, n_dma_sems=12):
        self.nc = nc
        self.stack = stack
        self.ops = {e: [] for e in ENGS}
        self.known = {e: {} for e in ENGS}
        self.tiles = {}
        self.semid = 0
        self.cur = {}
        for e in ["pe", "dve", "act", "pool"]:
            self._new_sem(e)
        self.dma = {}
        for q in ["sp", "pool", "act"]:
            sems = []
            for j in range(n_dma_sems):
                s = stack.enter_context(nc.semaphore(f"dq_{q}_{j}"))
                self.semid += 1
                sems.append([s, self.semid, 0])
            self.dma[q] = [sems, 0]
        self.out_events = []

    def _new_sem(self, e):
        s = self.stack.enter_context(self.nc.semaphore(f"s_{e}_{self.semid}"))
        self.semid += 1
        self.cur[e] = [s, self.semid, 0]

    def sb(self, name, shape, dtype=F32):
        return self.stack.enter_context(self.nc.sbuf_tensor("sb_" + name, list(shape), dtype))

    def ps(self, name, shape, dtype=F32):
        return self.stack.enter_context(self.nc.psum_tensor("pp_" + name, list(shape), dtype))

    def _st(self, k):
        st = self.tiles.get(k)
        if st is None:
            st = {"w": None, "r": {}}
            self.tiles[k] = st
        return st

    def _collect(self, eng, reads, writes, same_sync):
        waits = {}

        def add(ev):
            if ev is None:
                return
            s, sid, val = ev
            if sid == self.cur.get(eng, [None, -1])[1] and not same_sync:
                return
            if self.known[eng].get(sid, 0) >= val:
                return
            if sid not in waits or waits[sid][2] < val:
                waits[sid] = ev

        for k in reads:
            add(self._st(k)["w"])
        for k in writes:
            st = self._st(k)
            add(st["w"])
            for ev in st["r"].values():
                add(ev)
        for sid, ev in waits.items():
            self.ops[eng].append(("wait", ev[0], ev[2]))
            self.known[eng][sid] = ev[2]

    def _record(self, ev, reads, writes):
        for k in reads:
            st = self._st(k)
            st["r"][ev[1]] = ev
        for k in writes:
            st = self._st(k)
            st["w"] = ev
            st["r"] = {}

    def op(self, eng, fn, reads=(), writes=(), same_sync=None):
        if same_sync is None:
            same_sync = SAME_ENGINE_SYNC and eng != "pe"
        self._collect(eng, reads, writes, same_sync)
        c = self.cur[eng]
        if c[2] >= EPOCH:
            self._new_sem(eng)
            c = self.cur[eng]
        c[2] += 1
        ev = (c[0], c[1], c[2])
        self.ops[eng].append(("op", fn, c[0], 1))
        self._record(ev, reads, writes)
        return ev

    def dma_op(self, q, out, in_, reads=(), writes=(), is_output=False, **kw):
        sems, idx = self.dma[q]
        slot = sems[idx % len(sems)]
        self.dma[q][1] = idx + 1
        s, sid, uses = slot
        if uses > 0 and self.known[q].get(sid, 0) < 16 * uses:
            self.ops[q].append(("wait", s, 16 * uses))
            self.known[q][sid] = 16 * uses
        self._collect(q, reads, writes, True)
        slot[2] = uses + 1
        ev = (s, sid, 16 * (uses + 1))

        def fn(e, out=out, in_=in_, kw=kw):
            return e.dma_start(out=out, in_=in_, **kw)

        self.ops[q].append(("op", fn, s, 16))
        self._record(ev, reads, writes)
        if is_output:
            self.out_events.append(ev)
        return ev

    def finish(self):
        best = {}
        for ev in self.out_events:
            if ev[1] not in best or best[ev[1]][2] < ev[2]:
                best[ev[1]] = ev
        for ev in best.values():
            self.ops["sp"].append(("wait", ev[0], ev[2]))
        nc = self.nc
        ops = self.ops

        def replay(name, e):
            for o in ops[name]:
                if o[0] == "wait":
                    e.wait_ge(o[1], o[2])
                else:
                    ins = o[1](e)
                    ins.then_inc(o[2], o[3])

        with nc.Block() as block:
            @block.tensor
            def _(e):
                replay("pe", e)

            @block.vector
            def _(e):
                replay("dve", e)

            @block.scalar
            def _(e):
                replay("act", e)

            @block.gpsimd
            def _(e):
                replay("pool", e)

            @block.sync
            def _(e):
                replay("sp", e)

    def n_ops(self):
        return {e: len(v) for e, v in self.ops.items()}


D = 2048
KT = 16
TT = 512
EPS = 1e-6


def new_nc():
    return bass.Bass("TRN2", target_bir_lowering=False)


def dram_in(nc, name, shape, dtype=F32):
    return nc.dram_tensor(name, list(shape), dtype, kind="ExternalInput").ap()


def dram_out(nc, name, shape, dtype=F32):
    return nc.dram_tensor(name, list(shape), dtype, kind="ExternalOutput").ap()


class Ctx:
    pass


def setup_common(P):
    C = Ctx()
    C.ones_bf = P.sb("ones_bf", [128, 128], BF16)
    P.op("pool", lambda e: e.memset(C.ones_bf[:], 1.0), writes=["ones_bf"])
    C.ps = [P.ps(f"ps{i}", [128, 512]) for i in range(8)]
    C.rot = 0
    C.ev = 0
    return C


def emit_mod_cols(P, modt, nwt, which):
    acol = P.sb("acol%d" % which, [128, 16])
    s_sh = 0 if which == 0 else 3
    P.op("dve", lambda e: e.scalar_tensor_tensor(out=acol[:], in0=modt[:, (s_sh + 1) * 16:(s_sh + 2) * 16], scalar=1.0,
                                                 in1=nwt[:], op0=ALU.add, op1=ALU.mult),
         reads=["modt", "nwt%d" % which], writes=["acol"])
    return acol, modt[:, s_sh * 16:(s_sh + 1) * 16]


def make_scr(P):
    scr = {}
    scr["rstd"] = P.sb("rstd", [128, 512])
    scr["tmp"] = [P.sb("tmp0", [128, 512]), P.sb("tmp1", [128, 512])]
    scr["eps"] = P.sb("epst", [128, 1])
    P.op("pool", lambda e: e.memset(scr["eps"][:], EPS), writes=["eps"])
    return scr


def emit_dense(P, C, w_dram, K, ncols, rhs_fn, rhs_keys, ntt, evac_fn, wbufs, wq="pool", col0=0):
    nk = K // 128
    gcols = 8192 // nk
    ngroups = (ncols + gcols - 1) // gcols
    for g in range(ngroups):
        c0 = g * gcols
        csz = min(gcols, ncols - c0)
        wb = wbufs[C.rot_w % 2]
        wk = "wbuf%d" % (C.rot_w % 2)
        C.rot_w += 1
        wv = wb[:, 0:nk * csz].rearrange("p (k n) -> p k n", k=nk)
        src = w_dram[:, col0 + c0:col0 + c0 + csz].rearrange("(k p) n -> p k n", p=128)
        P.dma_op(wq, wv, src, writes=[wk])
        nm = (csz + 127) // 128
        for mi in range(nm):
            msz = min(128, csz - mi * 128)
            m = (c0 // 128) + mi
            for tt in range(ntt):
                bank = C.rot % 6
                C.rot += 1
                pst = C.ps[bank]

                def mm(e, wv=wv, mi=mi, msz=msz, tt=tt, pst=pst):
                    ins = None
                    for kt in range(nk):
                        ins = e.matmul(pst[0:msz, :], lhsT=wv[:, kt, mi * 128:mi * 128 + msz], rhs=rhs_fn(kt, tt),
                                       start=(kt == 0), stop=(kt == nk - 1))
                    return ins
                P.op("pe", mm, reads=[wk] + rhs_keys(tt), writes=["ps%d" % bank])
                evac_fn(m, msz, tt, pst, "ps%d" % bank)


def build_stageA(NC, TOK=2048):
    nc = new_nc()
    xT = dram_in(nc, "xT", [D, TOK])
    w = dram_in(nc, "w", [D, NC])
    nw = dram_in(nc, "nw", [128, 16])
    modl = dram_in(nc, "modl", [128, 96])
    out = dram_out(nc, "projT", [NC, TOK])
    ntt = TOK // TT
    with ExitStack() as st:
        P = Prog(nc, st)
        C = setup_common(P)
        C.rot_w = 0
        scr = make_scr(P)
        modt = P.sb("modt", [128, 96])
        nwt = P.sb("nwt", [128, 16])
        P.dma_op("sp", modt[:], modl, writes=["modt"])
        P.dma_op("sp", nwt[:], nw, writes=["nwt0"])
        acol, shcol = emit_mod_cols(P, modt, nwt, 0)
        P.tiles["shcol"] = P.tiles["modt"]
        xt = P.sb("xt", [128, 16, TT])
        sq = P.sb("sq", [128, 16, TT], BF16)
        hT = P.sb("hT", [128, 16, TOK], BF16)
        wbufs = [P.sb("wb0", [128, 8192], BF16), P.sb("wb1", [128, 8192], BF16)]
        ot = [P.sb("ot0", [128, TOK]), P.sb("ot1", [128, TOK])]
        xsrc = xT.rearrange("(k p) t -> p k t", p=128)
        for tt in range(ntt):
            P.dma_op("sp", xt[:], xsrc[:, :, tt * TT:(tt + 1) * TT], writes=["xt"])
            emit_adaln_xt(P, C, xt, hT[:, :, tt * TT:(tt + 1) * TT], "hT%d" % tt, acol, shcol, sq, scr)

        state = {"n": 0}

        def evac(m, msz, tt, pst, pk):
            b = (m % 2)
            o = ot[b]
            okey = "ot%d_%d" % (b, tt)
            eng = "dve" if C.ev % 2 == 0 else "act"
            C.ev += 1
            if eng == "dve":
                P.op("dve", lambda e: e.tensor_copy(out=o[0:msz, tt * TT:(tt + 1) * TT], in_=pst[0:msz, :]), reads=[pk], writes=[okey])
            else:
                P.op("act", lambda e: e.activation(out=o[0:msz, tt * TT:(tt + 1) * TT], in_=pst[0:msz, :], func=AF.Copy), reads=[pk], writes=[okey])
            if tt == ntt - 1:
                P.dma_op("sp", out[m * 128:m * 128 + msz, :], o[0:msz, :], reads=["ot%d_%d" % (b, t) for t in range(ntt)], is_output=True)

        emit_dense(P, C, w, D, NC, lambda kt, tt: hT[:, kt, tt * TT:(tt + 1) * TT], lambda tt: ["hT%d" % tt], ntt, evac, wbufs)
        P.finish()
        print("stageA ops", P.n_ops())
    return nc


def emit_adaln_xt(P, C, xt, ht, hk, acol, shcol, sq, scr):
    class _K:
        pass
    emit_adaln2(P, C, xt, ["xt"], ht, hk, acol, shcol, sq, "sq", scr)


def emit_adaln2(P, C, xt, xkeys, ht, hk, a_col, sh_col, sq, sqk, scr):
    nbank = C.ps[7]
    for q in range(4):
        if q % 2 == 0:
            P.op("act", lambda e, q=q: e.activation(out=sq[:, 4 * q:4 * q + 4, :], in_=xt[:, 4 * q:4 * q + 4, :], func=AF.Square),
                 reads=xkeys, writes=[sqk + str(q)])
        else:
            P.op("pool", lambda e, q=q: e.tensor_tensor(out=sq[:, 4 * q:4 * q + 4, :], in0=xt[:, 4 * q:4 * q + 4, :],
                                                      in1=xt[:, 4 * q:4 * q + 4, :], op=ALU.mult),
                 reads=xkeys, writes=[sqk + str(q)])

    def mm(e):
        ins = None
        for kt in range(KT):
            ins = e.matmul(nbank[:, :], lhsT=C.ones_bf[:], rhs=sq[:, kt, :], start=(kt == 0), stop=(kt == KT - 1))
        return ins
    P.op("pe", mm, reads=["ones_bf"] + [sqk + str(q) for q in range(4)], writes=["ps7"])
    rstd = scr["rstd"]
    P.op("act", lambda e: e.activation(out=rstd[:], in_=nbank[:, :], func=AF.Sqrt, bias=scr["eps"][:, 0:1], scale=1.0 / D),
         reads=["ps7", "eps"], writes=["rstd"])
    P.op("dve", lambda e: e.reciprocal(out=rstd[:], in_=rstd[:]), reads=["rstd"], writes=["rstd"])
    for kt in range(KT):
        tmp = scr["tmp"][kt % 2]
        tk = "tmp%d" % (kt % 2)
        P.op("dve", lambda e, kt=kt, tmp=tmp: e.tensor_tensor(out=tmp[:], in0=xt[:, kt, :], in1=rstd[:], op=ALU.mult),
             reads=xkeys + ["rstd"], writes=[tk])
        P.op("act", lambda e, kt=kt, tmp=tmp: e.activation(out=ht[:, kt, :], in_=tmp[:], func=AF.Identity,
                                                         bias=sh_col[:, kt:kt + 1], scale=a_col[:, kt:kt + 1]),
             reads=[tk, "acol", "modt"], writes=[hk])


def build_stageC(KO, TOK=2048):
    nc = new_nc()
    xT = dram_in(nc, "xT", [D, TOK])
    oT = dram_in(nc, "oT", [KO, TOK])
    wo = dram_in(nc, "wo", [KO, D])
    w1 = dram_in(nc, "w1", [D, 4 * D])
    w2 = dram_in(nc, "w2", [4 * D, D])
    nw = dram_in(nc, "nw", [128, 16])
    modl = dram_in(nc, "modl", [128, 96])
    out = dram_out(nc, "x2T", [D, TOK])
    ntt = TOK // TT
    nko = KO // 128
    with ExitStack() as st:
        P = Prog(nc, st)
        C = setup_common(P)
        C.rot_w = 0
        scr = make_scr(P)
        modt = P.sb("modt", [128, 96])
        nwt = P.sb("nwt", [128, 16])
        P.dma_op("sp", modt[:], modl, writes=["modt"])
        P.dma_op("sp", nwt[:], nw, writes=["nwt1"])
        acol, shcol = emit_mod_cols(P, modt, nwt, 1)
        g1 = modt[:, 32:48]
        g2 = modt[:, 80:96]
        xt = P.sb("xt", [128, 16, TT])
        ob = P.sb("ob", [128, max(nko, 16), TT], BF16)
        aT = P.sb("aT", [128, 64, TT], BF16)
        wbufs = [P.sb("wb0", [128, 8192], BF16), P.sb("wb1", [128, 8192], BF16)]
        ot = [P.sb("ot0", [128, TT]), P.sb("ot1", [128, TT])]
        rl = [P.sb("rl0", [128, TT]), P.sb("rl1", [128, TT])]
        xsrc = xT.rearrange("(k p) t -> p k t", p=128)
        osrc = oT.rearrange("(k p) t -> p k t", p=128)
        odst = out.rearrange("(k p) t -> p k t", p=128)
        for tt in range(ntt):
            tsl = slice(tt * TT, (tt + 1) * TT)
            P.dma_op("sp", xt[:], xsrc[:, :, tsl], writes=["xt"])
            P.dma_op("pool", ob[:, 0:nko, :], osrc[:, :, tsl], writes=["ob"])

            def evac1(m, msz, t_, pst, pk):
                P.op("dve", lambda e: e.scalar_tensor_tensor(out=xt[:, m, :], in0=pst[:, :], scalar=g1[:, m:m + 1], in1=xt[:, m, :],
                                                             op0=ALU.mult, op1=ALU.add),
                     reads=[pk, "modt", "xt"], writes=["x1_%d" % m])
            emit_dense(P, C, wo, KO, D, lambda kt, t_: ob[:, kt, :], lambda t_: ["ob"], 1, evac1, wbufs)
            x1keys = ["x1_%d" % m for m in range(16)]
            emit_adaln2(P, C, xt, x1keys, ob[:, 0:16, :], "ob", acol, shcol, aT[:, 0:16, :], "aT", scr)

            def evac2(f, msz, t_, pst, pk):
                r = rl[f % 2]
                rk = "rl%d" % (f % 2)
                P.op("act", lambda e: e.activation(out=r[:], in_=pst[:, :], func=AF.Square), reads=[pk], writes=[rk])
                P.op("dve", lambda e: e.scalar_tensor_tensor(out=aT[:, f, :], in0=pst[:, :], scalar=0.0, in1=r[:],
                                                             op0=ALU.is_gt, op1=ALU.mult),
                     reads=[pk, rk], writes=["aT_%d" % f])
            emit_dense(P, C, w1, D, 4 * D, lambda kt, t_: ob[:, kt, :], lambda t_: ["ob"], 1, evac2, wbufs)
            akeys = ["aT_%d" % f for f in range(64)]

            def evac3(m, msz, t_, pst, pk):
                o = ot[m % 2]
                okey = "ot%d" % (m % 2)
                P.op("dve", lambda e: e.scalar_tensor_tensor(out=o[:], in0=pst[:, :], scalar=g2[:, m:m + 1], in1=xt[:, m, :],
                                                             op0=ALU.mult, op1=ALU.add),
                     reads=[pk, "modt", "x1_%d" % m], writes=[okey])
                P.dma_op("sp", odst[:, m, tsl], o[:], reads=[okey], is_output=True)
            emit_dense(P, C, w2, 4 * D, D, lambda kt, t_: aT[:, kt, :], lambda t_: akeys, 1, evac3, wbufs)
            _fold(P, "xt", x1keys)
            _fold(P, "aT0", akeys); _fold(P, "aT1", akeys); _fold(P, "aT2", akeys); _fold(P, "aT3", akeys)
        P.finish()
        print("stageC ops", P.n_ops())
    return nc


def _fold(P, dst, keys):
    d = P._st(dst)
    for k in keys:
        s = P._st(k)
        if s["w"] is not None:
            d["r"][("w", s["w"][1])] = s["w"] if ("w", s["w"][1]) not in d["r"] or d["r"][("w", s["w"][1])][2] < s["w"][2] else d["r"][("w", s["w"][1])]
        for sid, ev in s["r"].items():
            key = ("r", ev[1])
            if key not in d["r"] or d["r"][key][2] < ev[2]:
                d["r"][key] = ev


PH = 99
SUB = 0
ST = 512
NEG = -30000.0


def host_consts():
    c = {}
    c["ident"] = np.eye(128, dtype=np.float32)
    i = np.arange(64)[:, None]
    j = np.arange(64)[None, :]
    sl = (i > j).astype(np.float32)
    su = (j > i).astype(np.float32)
    ui = (j >= i).astype(np.float32)
    c["m_sl"] = np.tile(sl, (1, 8))
    c["m_su"] = np.tile(su, (1, 8))
    c["m_ui"] = np.tile(ui, (1, 8))
    c["ibig"] = np.tile(np.eye(64, dtype=np.float32), (1, 8))
    rm = np.ones((1, 512), np.float32)
    rm[0, ::64] = 0.0
    c["rmask"] = rm
    return c


def rel_bias_toeplitz(rb_h):
    q = np.arange(128)[:, None]
    cc = np.arange(640)[None, :]
    rel = cc - 512 - q
    return np.ascontiguousarray(rb_h[np.clip(rel, -256, 256) + 256]).astype(np.float32)


class AttnState:
    pass


def attn_setup(P, nc, C):
    A = AttnState()
    A.qw = P.sb("a_qw", [128, 1]); A.kw = P.sb("a_kw", [128, 1])
    A.bias = P.sb("a_bias", [128, 640])
    A.kbuf = P.sb("a_kbuf", [128, 1024])
    A.vtp = P.sb("a_vtp", [128, 8, 128])
    A.q = P.sb("a_q", [128, ST]); A.k = P.sb("a_k", [128, ST]); A.v = P.sb("a_v", [128, ST])
    A.sq = P.sb("a_sq", [128, ST])
    A.rs = P.sb("a_rs", [128, ST])
    A.sc = P.sb("a_sc", [128, 640])
    A.pt = P.sb("a_pt", [128, 640])
    A.ob = P.sb("a_ob", [128, ST])
    A.mx = P.sb("a_mx", [128, 1]); A.sm = P.sb("a_sm", [128, 1])
    return A


def attn_init(P, C, A, d):
    P.dma_op("sp", A.qw[:], d["qw"], writes=["a_qw"])
    P.dma_op("sp", A.kw[:], d["kw"], writes=["a_kw"])
    P.dma_op("sp", A.bias[:], d["bias"], writes=["a_bias"])
    P.op("pool", lambda e: e.memset(A.bias[0:64, 576:640], NEG), reads=[], writes=["a_bias"])
    P.op("pool", lambda e: e.memset(A.bias[64:128, 0:64], NEG), reads=[], writes=["a_bias"])
    P.op("pool", lambda e: e.memset(A.kbuf[:, 0:512], 0.0), writes=["a_kbuf"])
    P.op("pool", lambda e: e.memset(A.vtp[:, 0:4, :], 0.0), writes=["a_vtp"])


def attn_supertile(P, C, A, s, pT, out, T):
    t0 = s * ST
    tsl = slice(t0, t0 + ST)
    P.dma_op("sp", A.q[:], pT[512:640, tsl], writes=["a_q"])
    P.dma_op("sp", A.k[:], pT[640:768, tsl], writes=["a_k"])
    P.dma_op("sp", A.v[:], pT[768:896, tsl], writes=["a_v"])
    psB = C.psB
    for nm, x, w, dst, dk, sc_, bs_ in (("q", A.q, A.qw, A.q, "a_q", 1.0, "eps128"), ("k", A.k, A.kw, A.kbuf, "a_kbuf", 1.0 / 128, "eps")):
        P.op("act", lambda e, x=x: e.activation(out=A.sq[:], in_=x[:], func=AF.Square), reads=["a_" + nm], writes=["a_sq"])
        P.op("pe", lambda e: e.matmul(psB[:, :], lhsT=C.ones_f[:], rhs=A.sq[:], start=True, stop=True), reads=["ones_f", "a_sq"], writes=["psB"])
        P.op("act", lambda e, sc_=sc_, bs_=bs_: e.activation(out=A.rs[:], in_=psB[:, :], func=AF.Sqrt, bias=C.cst[bs_][:, 0:1], scale=sc_),
             reads=["psB", "cst"], writes=["a_rs"])
        P.op("dve", lambda e: e.reciprocal(out=A.rs[:], in_=A.rs[:]), reads=["a_rs"], writes=["a_rs"])
        if nm == "q":
            P.op("dve", lambda e: e.scalar_tensor_tensor(out=A.q[:], in0=A.q[:], scalar=A.qw[:, 0:1], in1=A.rs[:], op0=ALU.mult, op1=ALU.mult),
                 reads=["a_q", "a_qw", "a_rs"], writes=["a_q"])
        else:
            P.op("dve", lambda e: e.scalar_tensor_tensor(out=A.kbuf[:, 512:1024], in0=A.k[:], scalar=A.kw[:, 0:1], in1=A.rs[:], op0=ALU.mult, op1=ALU.mult),
                 reads=["a_k", "a_kw", "a_rs"], writes=["a_kbuf"])
    yield
    def tr(e):
        ins = None
        for b in range(4):
            ins = e.transpose(psB[:, b * 128:(b + 1) * 128], A.v[:, b * 128:(b + 1) * 128], C.ident[:])
        return ins
    P.op("pe", tr, reads=["a_v", "ident"], writes=["psB"])
    P.op("act", lambda e: e.activation(out=A.vtp[:, 4:8, :], in_=psB[:, :].rearrange("p (b e) -> p b e", b=4), func=AF.Copy),
         reads=["psB"], writes=["a_vtp"])
    yield
    psA = C.psA
    for j in range(4):
        def mm(e, j=j):
            e.matmul(psA[:, 0:512], lhsT=A.q[:, 128 * j:128 * j + 128], rhs=A.kbuf[:, 128 * j:128 * j + 512], start=True, stop=True)
            return e.matmul(psA[:, 512:640], lhsT=A.q[:, 128 * j:128 * j + 128], rhs=A.kbuf[:, 128 * j + 512:128 * j + 640], start=True, stop=True)
        P.op("pe", mm, reads=["a_q", "a_kbuf"], writes=["psA"])
        P.op("dve", lambda e: e.tensor_tensor(out=A.sc[:], in0=psA[:, 0:640], in1=A.bias[:], op=ALU.add), reads=["psA", "a_bias"], writes=["a_sc"])
        if s == 0:
            ninv = 512 - 128 * j
            P.op("dve", lambda e, ninv=ninv: e.memset(A.sc[:, 0:ninv], NEG), writes=["a_sc"])
        yield
        P.op("dve", lambda e: e.tensor_reduce(out=A.mx[:], in_=A.sc[:], axis=AX.X, op=ALU.max), reads=["a_sc"], writes=["a_mx"])
        P.op("dve", lambda e: e.tensor_scalar(out=A.mx[:], in0=A.mx[:], scalar1=-1.0, scalar2=None, op0=ALU.mult), reads=["a_mx"], writes=["a_mx"])
        P.op("act", lambda e: e.activation(out=A.sc[:], in_=A.sc[:], func=AF.Exp, bias=A.mx[:, 0:1], scale=1.0, accum_out=A.sm[:, 0:1]),
             reads=["a_sc", "a_mx"], writes=["a_sc", "a_sm"])
        P.op("dve", lambda e: e.reciprocal(out=A.sm[:], in_=A.sm[:]), reads=["a_sm"], writes=["a_sm"])
        P.op("dve", lambda e: e.tensor_scalar(out=A.sc[:], in0=A.sc[:], scalar1=A.sm[:, 0:1], scalar2=None, op0=ALU.mult), reads=["a_sc", "a_sm"], writes=["a_sc"])

        yield

        def trp(e):
            ins = None
            for kb in range(5):
                ins = e.transpose(psA[:, kb * 128:(kb + 1) * 128], A.sc[:, kb * 128:(kb + 1) * 128], C.ident[:])
            return ins
        P.op("pe", trp, reads=["a_sc", "ident"], writes=["psA"])
        P.op("act", lambda e: e.activation(out=A.pt[:], in_=psA[:, 0:640], func=AF.Copy), reads=["psA"], writes=["a_pt"])

        yield

        def mo(e, j=j):
            ins = None
            for kb in range(5):
                ins = e.matmul(psB[:, 0:128], lhsT=A.vtp[:, j + kb, :], rhs=A.pt[:, kb * 128:(kb + 1) * 128], start=(kb == 0), stop=(kb == 4))
            return ins
        P.op("pe", mo, reads=["a_vtp", "a_pt"], writes=["psB"])
        P.op("dve", lambda e, j=j: e.tensor_copy(out=A.ob[:, 128 * j:128 * j + 128], in_=psB[:, 0:128]), reads=["psB"], writes=["a_ob"])
    P.dma_op("sp", out[128:256, tsl], A.ob[:], reads=["a_ob"], is_output=True)
    P.op("act", lambda e: e.activation(out=A.kbuf[:, 0:512], in_=A.kbuf[:, 512:1024], func=AF.Copy), reads=["a_kbuf"], writes=["a_kbuf"])
    P.op("act", lambda e: e.activation(out=A.vtp[:, 0:4, :], in_=A.vtp[:, 4:8, :], func=AF.Copy), reads=["a_vtp"], writes=["a_vtp"])
    yield


def mixer_common(P, nc, C, cd):
    C.ones_f = P.sb("ones_f", [128, 128])
    P.op("pool", lambda e: e.memset(C.ones_f[:], 1.0), writes=["ones_f"])
    C.ident = P.sb("ident_sb", [128, 128])
    P.dma_op("sp", C.ident[:], cd["ident"], writes=["ident"])
    C.cst = {}
    for nm, v in (("eps", EPS), ("eps128", 128 * EPS), ("one", 1.0), ("eps512", EPS)):
        t = P.sb("cst_" + nm, [128, 1])
        P.op("pool", lambda e, t=t, v=v: e.memset(t[:], v), writes=["cst"])
        C.cst[nm] = t
    C.psA = P.ps("psA", [128, 1024])
    C.psB = P.ps("psB", [128, 512])


def build_attn_only(T):
    nc = new_nc()
    pT = dram_in(nc, "pT", [896, T])
    d = {"qw": dram_in(nc, "qw", [128, 1]), "kw": dram_in(nc, "kw", [128, 1]), "bias": dram_in(nc, "bias", [128, 640])}
    cd = {"ident": dram_in(nc, "ident", [128, 128])}
    out = dram_out(nc, "oT", [256, T])
    with ExitStack() as st:
        P = Prog(nc, st)
        C = Ctx()
        mixer_common(P, nc, C, cd)
        A = attn_setup(P, nc, C)
        attn_init(P, C, A, d)
        for s in range(T // ST):
            rr([attn_supertile(P, C, A, s, pT, out, T)])
        P.finish()
        print("attn ops", P.n_ops())
    return nc


class GS:
    pass


def gdn_setup(P, nc, C):
    G = GS()
    def sb(n, shp):
        return P.sb("g_" + n, shp)
    G.cw = sb("cw", [128, 12]); G.alog = sb("alog", [1, 1]); G.dtb = sb("dtb", [1, 1]); G.nA = sb("nA", [1, 1])
    G.nw = sb("nw", [128, 1])
    G.raw = [sb("raw%d" % i, [128, ST + 3]) for i in range(3)]
    G.z = [sb("z0", [128, ST]), sb("z1", [128, ST])]; G.gb = sb("gb", [1, ST]); G.ga = sb("ga", [1, ST])
    G.sq2 = sb("sq2", [128, ST]); G.rs2 = sb("rs2", [128, ST])
    G.x = [sb("x%d" % i, [128, ST]) for i in range(3)]
    G.sq = sb("sq", [128, ST]); G.rs = sb("rs", [128, ST])
    G.beta = sb("beta", [1, ST]); G.g = sb("g", [1, ST]); G.gc = sb("gc", [1, ST]); G.ngc = sb("ngc", [1, ST])
    G.rmask = sb("rmask", [1, ST])
    G.beta_bc = sb("beta_bc", [128, ST]); G.gc_bc = sb("gc_bc", [128, ST]); G.egc_bc = [sb("egc_bc0", [128, ST]), sb("egc_bc1", [128, ST])]; G.ekd_bc = sb("ekd_bc", [128, ST])
    G.kb = sb("kb", [128, ST]); G.vb = sb("vb", [128, ST]); G.kbg = sb("kbg", [128, ST]); G.kd = sb("kd", [128, ST]); G.qd = [sb("qd0", [128, ST]), sb("qd1", [128, ST])]
    for n in ("m_sl", "m_su", "m_ui", "ibig", "E1", "E2", "Dl", "Du", "Dq", "Pm", "Qm", "Ym"):
        setattr(G, n, sb(n, [64, ST]))
    G.qkT = [sb("qkT0", [64, ST]), sb("qkT1", [64, ST])]
    G.vb_tp = sb("vb_tp", [64, 1024]); G.kbg_tp = sb("kbg_tp", [64, 1024]); G.kd_tp = [sb("kd_tp0", [64, 1024]), sb("kd_tp1", [64, 1024])]; G.u = [sb("u0", [64, 1024]), sb("u1", [64, 1024])]
    G.wT = [sb("wT0", [128, ST]), sb("wT1", [128, ST])]; G.S = sb("S", [128, 128]); G.vnew = sb("vnew", [64, 128])
    G.o = sb("o", [128, ST]); G.zs = sb("zs", [128, ST]); G.t1 = sb("t1", [128, ST]); G.res = sb("res", [128, ST])
    C.psC = P.ps("psC", [128, 1024]); C.psD = P.ps("psD", [128, 512]); C.psE = P.ps("psE", [128, 512]); C.psF = P.ps("psF", [128, 512])
    return G


def gdn_init(P, C, G, d, cd):
    for n, t in (("cw", G.cw), ("alog", G.alog), ("dtb", G.dtb), ("nw", G.nw)):
        P.dma_op("sp", t[:], d["g_" + n], writes=["g_" + n])
    for n in ("m_sl", "m_su", "m_ui", "ibig", "rmask"):
        P.dma_op("sp", getattr(G, n)[:], cd[n], writes=["g_" + n])
    P.op("act", lambda e: e.activation(out=G.nA[:], in_=G.alog[:], func=AF.Exp), reads=["g_alog"], writes=["g_nA"])
    P.op("dve", lambda e: e.tensor_scalar(out=G.nA[:], in0=G.nA[:], scalar1=-1.0, scalar2=None, op0=ALU.mult), reads=["g_nA"], writes=["g_nA"])
    P.op("pool", lambda e: e.memset(G.S[:], 0.0), writes=["g_S"])
    for i in range(3):
        P.op("pool", lambda e, i=i: e.memset(G.raw[i][:, 0:3], 0.0), writes=["g_raw%d" % i])


def gdn_pre(P, C, G, s, pT, gates, out, T):
    b = s % 2
    t0 = s * ST
    tsl = slice(t0, t0 + ST)
    psB, psC, psD, psE, psF = C.psB, C.psC, C.psD, C.psE, C.psF
    ones_row = C.ones_f[0:1, 0:64]
    for i in range(3):
        if s == 0:
            P.dma_op("sp", G.raw[i][:, 3:ST + 3], pT[128 * i:128 * i + 128, 0:ST], writes=["g_raw%d" % i])
        else:
            P.dma_op("sp", G.raw[i][:, :], pT[128 * i:128 * i + 128, t0 - 3:t0 + ST], writes=["g_raw%d" % i])
    P.dma_op("sp", G.z[b][:], pT[384:512, tsl], writes=["g_z%d" % b])
    P.dma_op("sp", G.gb[:], gates[0:1, tsl], writes=["g_gb"])
    P.dma_op("sp", G.ga[:], gates[1:2, tsl], writes=["g_ga"])
    yield
    for i in range(3):
        x = G.x[i]; raw = G.raw[i]; xk = "g_x%d" % i; rk = "g_raw%d" % i
        P.op("dve", lambda e, x=x, raw=raw, i=i: e.tensor_scalar(out=x[:], in0=raw[:, 0:ST], scalar1=G.cw[:, 4 * i:4 * i + 1], scalar2=None, op0=ALU.mult),
             reads=[rk, "g_cw"], writes=[xk])
        for j in range(1, 4):
            P.op("dve", lambda e, x=x, raw=raw, i=i, j=j: e.scalar_tensor_tensor(out=x[:], in0=raw[:, j:j + ST], scalar=G.cw[:, 4 * i + j:4 * i + j + 1],
                                                                               in1=x[:], op0=ALU.mult, op1=ALU.add),
                 reads=[rk, "g_cw", xk], writes=[xk])
        P.op("act", lambda e, x=x: e.activation(out=x[:], in_=x[:], func=AF.Silu), reads=[xk], writes=[xk])
    q, k, v = G.x
    yield
    for i, (sc_, bs_) in enumerate(((128.0, "eps128"), (1.0, "eps"))):
        x = G.x[i]; xk = "g_x%d" % i
        P.op("act", lambda e, x=x: e.activation(out=G.sq[:], in_=x[:], func=AF.Square), reads=[xk], writes=["g_sq"])
        P.op("pe", lambda e: e.matmul(psB[:, :], lhsT=C.ones_f[:], rhs=G.sq[:], start=True, stop=True), reads=["ones_f", "g_sq"], writes=["psB"])
        P.op("act", lambda e, sc_=sc_, bs_=bs_: e.activation(out=G.rs[:], in_=psB[:, :], func=AF.Sqrt, bias=C.cst[bs_][:, 0:1], scale=sc_),
             reads=["psB", "cst"], writes=["g_rs"])
        P.op("dve", lambda e: e.reciprocal(out=G.rs[:], in_=G.rs[:]), reads=["g_rs"], writes=["g_rs"])
        P.op("dve", lambda e, x=x: e.tensor_tensor(out=x[:], in0=x[:], in1=G.rs[:], op=ALU.mult), reads=[xk, "g_rs"], writes=[xk])
    yield
    P.op("act", lambda e: e.activation(out=G.beta[:], in_=G.gb[:], func=AF.Exp, scale=-1.0), reads=["g_gb"], writes=["g_beta"])
    P.op("dve", lambda e: e.tensor_scalar(out=G.beta[:], in0=G.beta[:], scalar1=1.0, scalar2=None, op0=ALU.add), reads=["g_beta"], writes=["g_beta"])
    P.op("dve", lambda e: e.reciprocal(out=G.beta[:], in_=G.beta[:]), reads=["g_beta"], writes=["g_beta"])
    P.op("act", lambda e: e.activation(out=G.g[:], in_=G.ga[:], func=AF.Exp, bias=G.dtb[0:1, 0:1], scale=1.0), reads=["g_ga", "g_dtb"], writes=["g_g"])
    P.op("act", lambda e: e.activation(out=G.g[:], in_=G.g[:], func=AF.Ln, bias=C.cst["one"][0:1, 0:1], scale=1.0), reads=["g_g", "cst"], writes=["g_g"])
    P.op("dve", lambda e: e.tensor_scalar(out=G.g[:], in0=G.g[:], scalar1=G.nA[0:1, 0:1], scalar2=None, op0=ALU.mult), reads=["g_g", "g_nA"], writes=["g_g"])
    P.op("dve", lambda e: e.tensor_tensor_scan(out=G.gc[:], data0=G.rmask[:], data1=G.g[:], initial=0.0, op0=ALU.mult, op1=ALU.add),
         reads=["g_rmask", "g_g"], writes=["g_gc"])
    P.op("dve", lambda e: e.tensor_scalar(out=G.ngc[:], in0=G.gc[:], scalar1=-1.0, scalar2=None, op0=ALU.mult), reads=["g_gc"], writes=["g_ngc"])
    yield
    P.op("pe", lambda e: e.matmul(psB[:, :], lhsT=C.ones_f[0:1, :], rhs=G.beta[:], start=True, stop=True), reads=["ones_f", "g_beta"], writes=["psB"])
    P.op("act", lambda e: e.activation(out=G.beta_bc[:], in_=psB[:, :], func=AF.Copy), reads=["psB"], writes=["g_beta_bc"])
    P.op("pe", lambda e: e.matmul(psB[:, :], lhsT=C.ones_f[0:1, :], rhs=G.gc[:], start=True, stop=True), reads=["ones_f", "g_gc"], writes=["psB"])
    P.op("act", lambda e: e.activation(out=G.gc_bc[:], in_=psB[:, :], func=AF.Copy), reads=["psB"], writes=["g_gc_bc"])
    P.op("act", lambda e: e.activation(out=G.egc_bc[b][:], in_=G.gc_bc[:], func=AF.Exp), reads=["g_gc_bc"], writes=["g_egc_bc%d" % b])
    for c in range(8):
        P.op("act", lambda e, c=c: e.activation(out=G.ekd_bc[:, 64 * c:64 * c + 64], in_=G.gc_bc[:, 64 * c:64 * c + 64], func=AF.Exp,
                                               bias=G.gc_bc[:, 64 * c + 63:64 * c + 64], scale=-1.0),
             reads=["g_gc_bc"], writes=["g_ekd_bc"])
    yield
    P.op("dve", lambda e: e.tensor_tensor(out=G.kb[:], in0=k[:], in1=G.beta_bc[:], op=ALU.mult), reads=["g_x1", "g_beta_bc"], writes=["g_kb"])
    P.op("dve", lambda e: e.tensor_tensor(out=G.vb[:], in0=v[:], in1=G.beta_bc[:], op=ALU.mult), reads=["g_x2", "g_beta_bc"], writes=["g_vb"])
    P.op("dve", lambda e: e.tensor_tensor(out=G.kbg[:], in0=G.kb[:], in1=G.egc_bc[b][:], op=ALU.mult), reads=["g_kb", "g_egc_bc%d" % b], writes=["g_kbg"])
    P.op("dve", lambda e: e.tensor_tensor(out=G.kd[:], in0=k[:], in1=G.ekd_bc[:], op=ALU.mult), reads=["g_x1", "g_ekd_bc"], writes=["g_kd"])
    P.op("dve", lambda e: e.tensor_tensor(out=G.qd[b][:], in0=q[:], in1=G.egc_bc[b][:], op=ALU.mult), reads=["g_x0", "g_egc_bc%d" % b], writes=["g_qd%d" % b])

    yield
    def dec(e):
        ins = None
        for c in range(8):
            cs = slice(64 * c, 64 * c + 64)
            e.matmul(psC[0:64, 64 * c:64 * c + 64], lhsT=G.gc[0:1, cs], rhs=ones_row, start=True, stop=False)
            e.matmul(psC[0:64, 64 * c:64 * c + 64], lhsT=ones_row, rhs=G.ngc[0:1, cs], start=False, stop=True)
        for c in range(8):
            cs = slice(64 * c, 64 * c + 64)
            e.matmul(psC[0:64, 512 + 64 * c:512 + 64 * c + 64], lhsT=ones_row, rhs=G.gc[0:1, cs], start=True, stop=False)
            ins = e.matmul(psC[0:64, 512 + 64 * c:512 + 64 * c + 64], lhsT=G.ngc[0:1, cs], rhs=ones_row, start=False, stop=True)
        return ins
    P.op("pe", dec, reads=["g_gc", "g_ngc", "ones_f"], writes=["psC"])
    P.op("dve", lambda e: e.tensor_scalar(out=G.E1[:], in0=psC[0:64, 0:512], scalar1=0.0, scalar2=None, op0=ALU.min), reads=["psC"], writes=["g_E1"])
    P.op("dve", lambda e: e.tensor_scalar(out=G.E2[:], in0=psC[0:64, 512:1024], scalar1=0.0, scalar2=None, op0=ALU.min), reads=["psC"], writes=["g_E2"])
    P.op("act", lambda e: e.activation(out=G.E1[:], in_=G.E1[:], func=AF.Exp), reads=["g_E1"], writes=["g_E1"])
    P.op("act", lambda e: e.activation(out=G.E2[:], in_=G.E2[:], func=AF.Exp), reads=["g_E2"], writes=["g_E2"])
    P.op("dve", lambda e: e.tensor_tensor(out=G.Dl[:], in0=G.E1[:], in1=G.m_sl[:], op=ALU.mult), reads=["g_E1", "g_m_sl"], writes=["g_Dl"])
    P.op("dve", lambda e: e.tensor_tensor(out=G.Du[:], in0=G.E2[:], in1=G.m_su[:], op=ALU.mult), reads=["g_E2", "g_m_su"], writes=["g_Du"])
    P.op("dve", lambda e: e.tensor_tensor(out=G.Dq[:], in0=G.E2[:], in1=G.m_ui[:], op=ALU.mult), reads=["g_E2", "g_m_ui"], writes=["g_Dq"])

    yield
    def kk(e):
        ins = None
        for c in range(8):
            cs = slice(64 * c, 64 * c + 64)
            e.matmul(psC[0:64, 64 * c:64 * c + 64], lhsT=G.kb[:, cs], rhs=k[:, cs], start=True, stop=True)
        for c in range(8):
            cs = slice(64 * c, 64 * c + 64)
            ins = e.matmul(psC[0:64, 512 + 64 * c:512 + 64 * c + 64], lhsT=k[:, cs], rhs=G.kb[:, cs], start=True, stop=True)
        return ins
    P.op("pe", kk, reads=["g_kb", "g_x1"], writes=["psC"])

    def qkm(e):
        ins = None
        for c in range(8):
            cs = slice(64 * c, 64 * c + 64)
            ins = e.matmul(psD[0:64, 64 * c:64 * c + 64], lhsT=k[:, cs], rhs=q[:, cs], start=True, stop=True)
        return ins
    P.op("pe", qkm, reads=["g_x0", "g_x1"], writes=["psD"])
    P.op("dve", lambda e: e.tensor_tensor(out=G.Pm[:], in0=psC[0:64, 0:512], in1=G.Dl[:], op=ALU.mult), reads=["psC", "g_Dl"], writes=["g_Pm"])
    P.op("dve", lambda e: e.tensor_tensor(out=G.Qm[:], in0=psC[0:64, 512:1024], in1=G.Du[:], op=ALU.mult), reads=["psC", "g_Du"], writes=["g_Qm"])
    P.op("dve", lambda e: e.tensor_tensor(out=G.qkT[b][:], in0=psD[0:64, :], in1=G.Dq[:], op=ALU.mult), reads=["psD", "g_Dq"], writes=["g_qkT%d" % b])
    P.op("dve", lambda e: e.tensor_tensor(out=G.Ym[:], in0=G.ibig[:], in1=G.Qm[:], op=ALU.subtract), reads=["g_ibig", "g_Qm"], writes=["g_Ym"])
    yield
    yield
    for m in range(5):
        def sqr(e):
            ins = None
            for c in range(8):
                cs = slice(64 * c, 64 * c + 64)
                e.matmul(psC[0:64, 64 * c:64 * c + 64], lhsT=G.Qm[:, cs], rhs=G.Pm[:, cs], start=True, stop=True)
            for c in range(8):
                cs = slice(64 * c, 64 * c + 64)
                ins = e.matmul(psC[0:64, 512 + 64 * c:512 + 64 * c + 64], lhsT=G.Pm[:, cs], rhs=G.Qm[:, cs], start=True, stop=True)
            return ins
        P.op("pe", sqr, reads=["g_Pm", "g_Qm"], writes=["psC"])
        P.op("act", lambda e: e.activation(out=G.Pm[:], in_=psC[0:64, 0:512], func=AF.Copy), reads=["psC"], writes=["g_Pm"])
        P.op("dve", lambda e: e.tensor_copy(out=G.Qm[:], in_=psC[0:64, 512:1024]), reads=["psC"], writes=["g_Qm"])
        yield

        def yup(e):
            ins = None
            for c in range(8):
                cs = slice(64 * c, 64 * c + 64)
                e.matmul(psD[0:64, 64 * c:64 * c + 64], lhsT=G.ibig[:, 0:64], rhs=G.Ym[:, cs], start=True, stop=False)
                ins = e.matmul(psD[0:64, 64 * c:64 * c + 64], lhsT=G.Pm[:, cs], rhs=G.Ym[:, cs], start=False, stop=True)
            return ins
        P.op("pe", yup, reads=["g_ibig", "g_Ym", "g_Pm"], writes=["psD"])
        P.op("act", lambda e: e.activation(out=G.Ym[:], in_=psD[0:64, :], func=AF.Copy), reads=["psD"], writes=["g_Ym"])
        yield
    yield
    for src, dst, sk, dk in ((G.vb, G.vb_tp, "g_vb", "g_vb_tp"), (G.kbg, G.kbg_tp, "g_kbg", "g_kbg_tp"), (G.kd, G.kd_tp[b], "g_kd", "g_kd_tp%d" % b)):
        def trp(e, src=src):
            ins = None
            for c in range(8):
                ins = e.transpose(psC[0:64, 128 * c:128 * c + 128], src[:, 64 * c:64 * c + 64], C.ident[:])
            return ins
        P.op("pe", trp, reads=[sk, "ident"], writes=["psC"])
        P.op("dve", lambda e, dst=dst: e.tensor_copy(out=dst[:], in_=psC[0:64, :]), reads=["psC"], writes=[dk])

    yield
    def um(e):
        ins = None
        for c in range(8):
            ins = e.matmul(psC[0:64, 128 * c:128 * c + 128], lhsT=G.Ym[:, 64 * c:64 * c + 64], rhs=G.vb_tp[:, 128 * c:128 * c + 128], start=True, stop=True)
        return ins
    P.op("pe", um, reads=["g_Ym", "g_vb_tp"], writes=["psC"])
    P.op("act", lambda e: e.activation(out=G.u[b][:], in_=psC[0:64, :], func=AF.Copy), reads=["psC"], writes=["g_u%d" % b])

    def wm(e):
        ins = None
        for c in range(8):
            ins = e.matmul(psD[:, 64 * c:64 * c + 64], lhsT=G.kbg_tp[:, 128 * c:128 * c + 128], rhs=G.Ym[:, 64 * c:64 * c + 64], start=True, stop=True)
        return ins
    P.op("pe", wm, reads=["g_Ym", "g_kbg_tp"], writes=["psD"])
    P.op("dve", lambda e: e.tensor_copy(out=G.wT[b][:], in_=psD[:, :]), reads=["psD"], writes=["g_wT%d" % b])
    yield


def gdn_scan(P, C, G, s, pT, gates, out, T):
    b = s % 2
    t0 = s * ST
    tsl = slice(t0, t0 + ST)
    psB, psC, psD, psE, psF = C.psB, C.psC, C.psD, C.psE, C.psF
    ones_row = C.ones_f[0:1, 0:64]
    for c in range(8):
        cs = slice(64 * c, 64 * c + 64)
        P.op("pe", lambda e, cs=cs: e.matmul(psF[0:64, 0:128], lhsT=G.wT[b][:, cs], rhs=G.S[:], start=True, stop=True), reads=["g_wT%d" % b, "g_S"], writes=["psF_a"])
        P.op("dve", lambda e, c=c: e.tensor_tensor(out=G.vnew[:], in0=G.u[b][:, 128 * c:128 * c + 128], in1=psF[0:64, 0:128], op=ALU.subtract),
             reads=["g_u%d" % b, "psF_a"], writes=["g_vnew"])
        yield

        def om(e, cs=cs, c=c):
            e.matmul(psE[:, cs], lhsT=G.S[:], rhs=G.qd[b][:, cs], start=True, stop=False)
            e.matmul(psE[:, cs], lhsT=G.vnew[:], rhs=G.qkT[b][:, cs], start=False, stop=True)
            return e.matmul(psF[:, 128:256], lhsT=G.kd_tp[b][:, 128 * c:128 * c + 128], rhs=G.vnew[:], start=True, stop=True)
        P.op("pe", om, reads=["g_S", "g_qd%d" % b, "g_vnew", "g_qkT%d" % b, "g_kd_tp%d" % b], writes=["psE", "psF_b"])
        P.op("dve", lambda e, c=c: e.scalar_tensor_tensor(out=G.S[:], in0=G.S[:], scalar=G.egc_bc[b][:, 64 * c + 63:64 * c + 64], in1=psF[:, 128:256],
                                                       op0=ALU.mult, op1=ALU.add),
             reads=["g_S", "g_egc_bc%d" % b, "psF_b"], writes=["g_S"])
        yield
    P.op("act", lambda e: e.activation(out=G.o[:], in_=psE[:, :], func=AF.Copy), reads=["psE"], writes=["g_o"])
    P.op("act", lambda e: e.activation(out=G.sq2[:], in_=G.o[:], func=AF.Square), reads=["g_o"], writes=["g_sq2"])
    P.op("pe", lambda e: e.matmul(psB[:, :], lhsT=C.ones_f[:], rhs=G.sq2[:], start=True, stop=True), reads=["ones_f", "g_sq2"], writes=["psB"])
    P.op("act", lambda e: e.activation(out=G.rs2[:], in_=psB[:, :], func=AF.Sqrt, bias=C.cst["eps"][:, 0:1], scale=1.0 / 128), reads=["psB", "cst"], writes=["g_rs2"])
    P.op("dve", lambda e: e.reciprocal(out=G.rs2[:], in_=G.rs2[:]), reads=["g_rs2"], writes=["g_rs2"])
    yield
    P.op("act", lambda e: e.activation(out=G.zs[:], in_=G.z[b][:], func=AF.Silu), reads=["g_z%d" % b], writes=["g_zs"])
    P.op("dve", lambda e: e.tensor_tensor(out=G.t1[:], in0=G.o[:], in1=G.rs2[:], op=ALU.mult), reads=["g_o", "g_rs2"], writes=["g_t1"])
    P.op("dve", lambda e: e.scalar_tensor_tensor(out=G.res[:], in0=G.t1[:], scalar=G.nw[:, 0:1], in1=G.zs[:], op0=ALU.mult, op1=ALU.mult),
         reads=["g_t1", "g_nw", "g_zs"], writes=["g_res"])
    P.dma_op("sp", out[0:128, tsl], G.res[:], reads=["g_res"], is_output=True)


    yield


def rr(gens):
    gens = [g for g in gens if g is not None]
    while gens:
        for g in list(gens):
            try:
                next(g)
            except StopIteration:
                gens.remove(g)


def build_B0(T, do_attn=True, do_gdn=True):
    nc = new_nc()
    pT = dram_in(nc, "pT", [896, T])
    gates = dram_in(nc, "gates", [2, T])
    d = {"qw": dram_in(nc, "qw", [128, 1]), "kw": dram_in(nc, "kw", [128, 1]), "bias": dram_in(nc, "bias", [128, 640]),
         "g_cw": dram_in(nc, "g_cw", [128, 12]), "g_alog": dram_in(nc, "g_alog", [1, 1]), "g_dtb": dram_in(nc, "g_dtb", [1, 1]),
         "g_nw": dram_in(nc, "g_nw", [128, 1])}
    cd = {"ident": dram_in(nc, "ident", [128, 128])}
    for n in ("m_sl", "m_su", "m_ui", "ibig"):
        cd[n] = dram_in(nc, n, [64, 512])
    cd["rmask"] = dram_in(nc, "rmask", [1, 512])
    out = dram_out(nc, "oT", [256, T])
    with ExitStack() as st:
        P = Prog(nc, st)
        C = Ctx()
        mixer_common(P, nc, C, cd)
        if do_attn:
            A = attn_setup(P, nc, C)
            attn_init(P, C, A, d)
        if do_gdn:
            G = gdn_setup(P, nc, C)
            gdn_init(P, C, G, d, cd)
        nst = T // ST
        if do_gdn:
            rr([gdn_pre(P, C, G, 0, pT, gates, out, T)])
        for s in range(nst):
            rr([gdn_scan(P, C, G, s, pT, gates, out, T) if do_gdn else None,
                attn_supertile(P, C, A, s, pT, out, T) if do_attn else None,
                gdn_pre(P, C, G, s + 1, pT, gates, out, T) if (do_gdn and s + 1 < nst) else None])
        P.finish()
        print("B0 ops", P.n_ops())
    return nc


def ssd_host_consts():
    c = {}
    c["ident"] = np.eye(128, dtype=np.float32)
    sp = np.zeros((8, 4, 128), np.float32)
    for j in range(4):
        sp[2 * j, j, 0:64] = 1.0
        sp[2 * j + 1, j, 64:128] = 1.0
    c["selpair"] = sp.reshape(8, 512)
    so = np.zeros((8, 8, 128), np.float32)
    for e in range(8):
        so[e, e, :] = 1.0
    c["selones"] = so.reshape(8, 1024)
    c["sel8"] = np.eye(8, dtype=np.float32)
    c["ones8"] = np.ones((8, 64), np.float32)
    rm = np.ones((8, 512), np.float32)
    rm[:, ::64] = 0.0
    c["rmask8"] = rm
    m = np.arange(64)[:, None]
    l = np.arange(64)[None, :]
    c["negmask"] = np.tile(np.where(l >= m, 0.0, NEG).astype(np.float32), (1, 8))
    sr = np.zeros((8, 8, 64), np.float32)
    for k in range(8):
        sr[k, k, :] = 1.0
    c["selrow64"] = sr.reshape(8, 512)
    return c


def ssd_setup(P, nc, C):
    S = GS()
    def sb(n, shp):
        return P.sb("s_" + n, shp)
    S.cw = sb("cw", [128, 24]); S.cb = sb("cb", [128, 6]); S.dcol = sb("dcol", [128, 4]); S.nw = sb("nw", [128, 4])
    S.dtb = sb("dtb", [8, 1]); S.alog = sb("alog", [8, 1]); S.acol = sb("acol", [8, 1])
    S.selpair = sb("selpair", [8, 512]); S.selones = sb("selones", [8, 1024]); S.sel8 = sb("sel8", [8, 8]); S.ones8 = sb("ones8", [8, 64])
    S.rmask8 = sb("rmask8", [8, 512]); S.negmask = sb("negmask", [64, 512])
    S.selrow64 = sb("selrow64", [8, 512]); S.R_all = sb("R_all", [8, 8, ST])
    S.raw = [sb("raw%d" % i, [128, ST + 3]) for i in range(6)]
    S.z = [sb("z%d" % i, [128, ST]) for i in range(4)]
    S.xc = [sb("xc%d" % i, [128, ST]) for i in range(6)]
    S.dt = sb("dt", [8, ST]); S.acum = sb("acum", [8, ST]); S.nacum = sb("nacum", [8, ST]); S.eac = sb("eac", [8, ST])
    S.dtw = sb("dtw", [8, ST])
    S.xdtT = [sb("xdtT%d" % j, [128, ST]) for j in range(4)]
    S.xdtwT = [sb("xdtwT%d" % j, [128, ST]) for j in range(4)]
    S.Cdec = sb("Cdec", [128, 8, ST]); S.glt = sb("glt", [128, 8, 8]); S.eacl = sb("eacl", [8, 8])
    S.B_tp = sb("B_tp", [64, 1024]); S.xdt_all = sb("xdt_all", [64, 8, ST]); S.xdtw_tp = sb("xdtw_tp", [64, 2, ST])
    S.E = sb("E", [64, 2, ST]); S.cbd_all = sb("cbd_all", [64, 8, ST]); S.y_sb = sb("y_sb", [64, 2, ST]); S.yT = sb("yT", [128, 4, ST])
    S.ST_all = sb("ST_all", [128, 9, ST])
    S.zs = sb("zs", [128, ST]); S.rs = sb("rs", [128, ST]); S.res = sb("res", [128, ST])
    C.psX = P.ps("psX", [128, 512]); C.psY = P.ps("psY", [128, 512]); C.psYY = P.ps("psYY", [128, 512]); C.psS = P.ps("psS", [128, 512])
    C.psV = P.ps("psV", [128, 512])
    return S


def ssd_init(P, C, S, d, cd):
    for n in ("cw", "cb", "dcol", "nw", "dtb", "alog"):
        P.dma_op("sp", getattr(S, n)[:], d["s_" + n], writes=["s_" + n])
    for n in ("selpair", "selones", "sel8", "ones8", "rmask8", "negmask", "selrow64"):
        P.dma_op("sp", getattr(S, n)[:], cd[n], writes=["s_" + n])
    P.op("act", lambda e: e.activation(out=S.acol[:], in_=S.alog[:], func=AF.Exp), reads=["s_alog"], writes=["s_acol"])
    P.op("dve", lambda e: e.tensor_scalar(out=S.acol[:], in0=S.acol[:], scalar1=-1.0, scalar2=None, op0=ALU.mult), reads=["s_acol"], writes=["s_acol"])
    P.op("pool", lambda e: e.memset(S.ST_all[:, 0, :], 0.0), writes=["s_ST0"])
    for i in range(6):
        P.op("pool", lambda e, i=i: e.memset(S.raw[i][:, 0:3], 0.0), writes=["s_raw%d" % i])


def ssd_supertile(P, C, S, s, pT, dtT, out, T):
    t0 = s * ST
    tsl = slice(t0, t0 + ST)
    psB = C.psB
    A0 = C.psA[:, 0:512]; A1 = C.psA[:, 512:1024]
    bk = {"psB": psB, "psA0": A0, "psA1": A1, "psX": C.psX, "psY": C.psY, "psYY": C.psYY, "psS": C.psS, "psV": C.psV}
    for i in range(6):
        r0 = 512 + 128 * i
        if s == 0:
            P.dma_op("sp", S.raw[i][:, 3:ST + 3], pT[r0:r0 + 128, 0:ST], writes=["s_raw%d" % i])
        else:
            P.dma_op("sp", S.raw[i][:, :], pT[r0:r0 + 128, t0 - 3:t0 + ST], writes=["s_raw%d" % i])
    for j in range(4):
        P.dma_op("sp", S.z[j][:], pT[128 * j:128 * j + 128, tsl], writes=["s_z%d" % j])
    P.dma_op("sp", S.dt[:], dtT[:, tsl], writes=["s_dt"])
    for i in range(6):
        x = S.xc[i]; raw = S.raw[i]; xk = "s_xc%d" % i; rk = "s_raw%d" % i
        ce = "dve"
        if ce == "dve":
            P.op("dve", lambda e, x=x, raw=raw, i=i: e.tensor_scalar(out=x[:], in0=raw[:, 0:ST], scalar1=S.cw[:, 4 * i:4 * i + 1], scalar2=None, op0=ALU.mult),
                 reads=[rk, "s_cw"], writes=[xk])
            for j in range(1, 4):
                P.op("dve", lambda e, x=x, raw=raw, i=i, j=j: e.scalar_tensor_tensor(out=x[:], in0=raw[:, j:j + ST], scalar=S.cw[:, 4 * i + j:4 * i + j + 1],
                                                                                   in1=x[:], op0=ALU.mult, op1=ALU.add),
                     reads=[rk, "s_cw", xk], writes=[xk])
        else:
            P.op("pool", lambda e, x=x, raw=raw, i=i: e.tensor_scalar(out=x[:], in0=raw[:, 0:ST], scalar1=S.cw[:, 4 * i:4 * i + 1], scalar2=None, op0=ALU.mult),
                 reads=[rk, "s_cw"], writes=[xk])
            for j in range(1, 4):
                P.op("pool", lambda e, raw=raw, i=i, j=j: e.tensor_scalar(out=S.zs[:], in0=raw[:, j:j + ST], scalar1=S.cw[:, 4 * i + j:4 * i + j + 1], scalar2=None, op0=ALU.mult),
                     reads=[rk, "s_cw"], writes=["s_zs"])
                P.op("pool", lambda e, x=x: e.tensor_tensor(out=x[:], in0=x[:], in1=S.zs[:], op=ALU.add), reads=[xk, "s_zs"], writes=[xk])
        P.op("act", lambda e, x=x, i=i: e.activation(out=x[:], in_=x[:], func=AF.Silu, bias=S.cb[:, i:i + 1], scale=1.0), reads=[xk, "s_cb"], writes=[xk])
    xs = S.xc[0:4]; Bc = S.xc[4]; Cc = S.xc[5]
    P.op("act", lambda e: e.activation(out=S.dt[:], in_=S.dt[:], func=AF.Exp, bias=S.dtb[:, 0:1], scale=1.0), reads=["s_dt", "s_dtb"], writes=["s_dt"])
    P.op("act", lambda e: e.activation(out=S.dt[:], in_=S.dt[:], func=AF.Ln, bias=C.cst["one"][0:8, 0:1], scale=1.0), reads=["s_dt", "cst"], writes=["s_dt"])
    P.op("dve", lambda e: e.tensor_scalar(out=S.nacum[:], in0=S.dt[:], scalar1=S.acol[:, 0:1], scalar2=None, op0=ALU.mult), reads=["s_dt", "s_acol"], writes=["s_nacum"])
    P.op("dve", lambda e: e.tensor_tensor_scan(out=S.acum[:], data0=S.rmask8[:], data1=S.nacum[:], initial=0.0, op0=ALU.mult, op1=ALU.add),
         reads=["s_rmask8", "s_nacum"], writes=["s_acum"])
    P.op("dve", lambda e: e.tensor_scalar(out=S.nacum[:], in0=S.acum[:], scalar1=-1.0, scalar2=None, op0=ALU.mult), reads=["s_acum"], writes=["s_nacum"])
    P.op("act", lambda e: e.activation(out=S.eac[:], in_=S.acum[:], func=AF.Exp), reads=["s_acum"], writes=["s_eac"])
    for c in range(8):
        P.op("act", lambda e, c=c: e.activation(out=S.dtw[:, 64 * c:64 * c + 64], in_=S.acum[:, 64 * c:64 * c + 64], func=AF.Exp,
                                               bias=S.acum[:, 64 * c + 63:64 * c + 64], scale=-1.0), reads=["s_acum"], writes=["s_dtw"])
    P.op("dve", lambda e: e.tensor_tensor(out=S.dtw[:], in0=S.dt[:], in1=S.dtw[:], op=ALU.mult), reads=["s_dt", "s_dtw"], writes=["s_dtw"])
    for h in range(8):
        P.op("dve", lambda e, h=h: e.tensor_scalar(out=S.R_all[:, h, :], in0=S.acum[:], scalar1=S.sel8[:, h:h + 1], scalar2=None, op0=ALU.mult),
             reads=["s_acum", "s_sel8"], writes=["s_R_all"], same_sync=False)
    P.op("dve", lambda e: e.tensor_copy(out=S.eacl[:], in_=S.eac[:].rearrange("p (c l) -> p c l", l=64)[:, :, 63]), reads=["s_eac"], writes=["s_eacl"])
    rot = ["psB", "psV", "psYY", "psS"]
    ri = 0
    for j in range(4):
        for src_, dst_, dkey in ((S.dt, S.xdtT[j], "s_xdtT%d" % j), (S.dtw, S.xdtwT[j], "s_xdtwT%d" % j)):
            bn = rot[ri % 4]; ri += 1
            P.op("pe", lambda e, j=j, src_=src_, bn=bn: e.matmul(bk[bn][:, :], lhsT=S.selpair[:, 128 * j:128 * j + 128], rhs=src_[:], start=True, stop=True),
                 reads=["s_selpair", "s_dt", "s_dtw"], writes=[bn])
            P.op("dve", lambda e, j=j, dst_=dst_, bn=bn: e.tensor_tensor(out=dst_[:], in0=xs[j][:], in1=bk[bn][:, :], op=ALU.mult), reads=["s_xc%d" % j, bn], writes=[dkey])
    for h in range(8):
        bn = rot[ri % 4]; ri += 1
        P.op("pe", lambda e, h=h, bn=bn: e.matmul(bk[bn][:, :], lhsT=S.selones[:, 128 * h:128 * h + 128], rhs=S.eac[:], start=True, stop=True), reads=["s_selones", "s_eac"], writes=[bn])
        eng = "dve"
        if eng == "dve":
            P.op("dve", lambda e, h=h, bn=bn: e.tensor_tensor(out=S.Cdec[:, h, :], in0=Cc[:], in1=bk[bn][:, :], op=ALU.mult), reads=["s_xc5", bn], writes=["s_Cdec%d" % h])
        else:
            P.op("act", lambda e, bn=bn: e.activation(out=S.res[:], in_=bk[bn][:, :], func=AF.Copy), reads=[bn], writes=["s_res"])
            P.op("pool", lambda e, h=h: e.tensor_tensor(out=S.Cdec[:, h, :], in0=Cc[:], in1=S.res[:], op=ALU.mult), reads=["s_xc5", "s_res"], writes=["s_Cdec%d" % h])

    def glm(e):
        ins = None
        for h in range(8):
            ins = e.matmul(C.psX[:, 8 * h:8 * h + 8], lhsT=S.selones[:, 128 * h:128 * h + 128], rhs=S.eacl[:], start=True, stop=True)
        return ins
    P.op("pe", glm, reads=["s_selones", "s_eacl"], writes=["psX"])
    P.op("act", lambda e: e.activation(out=S.glt[:].rearrange("p h c -> p (h c)"), in_=C.psX[:, 0:64], func=AF.Copy), reads=["psX"], writes=["s_glt"])
    for hf in range(2):
        bn = "psA%d" % hf

        def btr(e, hf=hf, bn=bn):
            ins = None
            for c in range(4):
                cc = 4 * hf + c
                ins = e.transpose(bk[bn][0:64, 128 * c:128 * c + 128], Bc[:, 64 * cc:64 * cc + 64], C.ident[:])
            return ins
        P.op("pe", btr, reads=["s_xc4", "ident"], writes=[bn])
        P.op("dve", lambda e, hf=hf, bn=bn: e.tensor_copy(out=S.B_tp[:, 512 * hf:512 * hf + 512], in_=bk[bn][0:64, :]), reads=[bn], writes=["s_B_tp"])
    for c in range(8):
        cs = slice(64 * c, 64 * c + 64)
        p = c % 2
        b1 = "psA%d" % p
        b2 = ["psX", "psY"][p]

        def xtr(e, cs=cs, b1=b1, b2=b2):
            ins = None
            for j in range(4):
                e.transpose(bk[b1][0:64, 128 * j:128 * j + 128], S.xdtT[j][:, cs], C.ident[:])
            for j in range(4):
                ins = e.transpose(bk[b2][0:64, 128 * j:128 * j + 128], S.xdtwT[j][:, cs], C.ident[:])
            return ins
        P.op("pe", xtr, reads=["s_xdtT%d" % j for j in range(4)] + ["s_xdtwT%d" % j for j in range(4)] + ["ident"], writes=[b1, b2])
        P.op("act", lambda e, c=c, b1=b1: e.activation(out=S.xdt_all[:, c, :], in_=bk[b1][0:64, :], func=AF.Copy), reads=[b1], writes=["s_xdt_all%d" % c])
        P.op("dve", lambda e, p=p, b2=b2: e.tensor_copy(out=S.xdtw_tp[:, p, :], in_=bk[b2][0:64, :]), reads=[b2], writes=["s_xdtw_tp%d" % p])
        b3 = ["psYY", "psS"][p]
        P.op("pe", lambda e, c=c, p=p, b3=b3: e.matmul(bk[b3][:, :], lhsT=S.B_tp[:, 128 * c:128 * c + 128], rhs=S.xdtw_tp[:, p, :], start=True, stop=True),
             reads=["s_B_tp", "s_xdtw_tp%d" % p], writes=[b3])
        for h in range(8):
            hs = slice(64 * h, 64 * h + 64)
            P.op("dve", lambda e, h=h, hs=hs, c=c, b3=b3: e.scalar_tensor_tensor(out=S.ST_all[:, c + 1, hs], in0=S.ST_all[:, c, hs], scalar=S.glt[:, h, c:c + 1],
                                                                              in1=bk[b3][:, hs], op0=ALU.mult, op1=ALU.add),
                 reads=["s_ST%d" % c, "s_glt", b3], writes=["s_ST%d" % (c + 1)], same_sync=False)
    for c in range(8):
        cs = slice(64 * c, 64 * c + 64)
        p = c % 2
        bx = ["psV", "psB"][p]
        by = "psA%d" % p

        def dm(e, cs=cs, bx=bx, by=by):
            e.matmul(bk[bx][0:64, :], lhsT=S.ones8[:, :], rhs=S.R_all[:, :, cs], start=True, stop=False)
            e.matmul(bk[bx][0:64, :], lhsT=S.nacum[:, cs], rhs=S.selrow64[:, :], start=False, stop=True)
            return e.matmul(bk[by][0:64, 0:64], lhsT=Bc[:, cs], rhs=Cc[:, cs], start=True, stop=True)
        P.op("pe", dm, reads=["s_ones8", "s_R_all", "s_nacum", "s_selrow64", "s_xc4", "s_xc5"], writes=[bx, by])
        P.op("dve", lambda e, p=p, bx=bx: e.scalar_tensor_tensor(out=S.E[:, p, :], in0=bk[bx][0:64, :], scalar=0.0, in1=S.negmask[:], op0=ALU.min, op1=ALU.add),
             reads=[bx, "s_negmask"], writes=["s_E%d" % p])
        P.op("act", lambda e, p=p: e.activation(out=S.E[:, p, :], in_=S.E[:, p, :], func=AF.Exp), reads=["s_E%d" % p], writes=["s_E%d" % p])
        for h in range(8):
            hs = slice(64 * h, 64 * h + 64)
            P.op("dve", lambda e, c=c, p=p, by=by, hs=hs: e.tensor_tensor(out=S.cbd_all[:, c, hs], in0=S.E[:, p, hs], in1=bk[by][0:64, 0:64], op=ALU.mult),
                 reads=[by, "s_E%d" % p], writes=["s_cbd%d" % c], same_sync=False)
    for c in range(8):
        cs = slice(64 * c, 64 * c + 64)
        p = c % 2
        b1 = ["psX", "psY"][p]
        b2 = ["psYY", "psS"][p]

        def ym(e, cs=cs, c=c, b1=b1):
            ins = None
            for h in range(8):
                hs = slice(64 * h, 64 * h + 64)
                e.matmul(bk[b1][0:64, hs], lhsT=S.Cdec[:, h, cs], rhs=S.ST_all[:, c, hs], start=True, stop=False)
                ins = e.matmul(bk[b1][0:64, hs], lhsT=S.cbd_all[:, c, hs], rhs=S.xdt_all[:, c, hs], start=False, stop=True)
            return ins
        P.op("pe", ym, reads=["s_Cdec%d" % h for h in range(8)] + ["s_ST%d" % c, "s_cbd%d" % c, "s_xdt_all%d" % c], writes=[b1])
        P.op("act", lambda e, p=p, b1=b1: e.activation(out=S.y_sb[:, p, :], in_=bk[b1][0:64, :], func=AF.Copy), reads=[b1], writes=["s_y_sb%d" % p])

        def ytr(e, p=p, b2=b2):
            ins = None
            for j in range(4):
                ins = e.transpose(bk[b2][:, 64 * j:64 * j + 64], S.y_sb[:, p, 128 * j:128 * j + 128], C.ident[0:64, 0:64])
            return ins
        P.op("pe", ytr, reads=["s_y_sb%d" % p, "ident"], writes=[b2])
        eng = "act" if c % 2 == 0 else "dve"
        if eng == "act":
            P.op("act", lambda e, cs=cs, b2=b2: e.activation(out=S.yT[:, :, cs], in_=bk[b2][:, 0:256].rearrange("p (j l) -> p j l", j=4), func=AF.Copy),
                 reads=[b2], writes=["s_yT"])
        else:
            P.op("dve", lambda e, cs=cs, b2=b2: e.tensor_copy(out=S.yT[:, :, cs], in_=bk[b2][:, 0:256].rearrange("p (j l) -> p j l", j=4)),
                 reads=[b2], writes=["s_yT"])
    P.op("act", lambda e: e.activation(out=S.ST_all[:, 0, :], in_=S.ST_all[:, 8, :], func=AF.Copy), reads=["s_ST8"], writes=["s_ST0"])
    for j in range(4):
        gt = S.xdtT[j]; gk = "s_xdtT%d" % j
        sq = S.xdtwT[j]; sk = "s_xdtwT%d" % j
        P.op("dve", lambda e, j=j, gt=gt: e.scalar_tensor_tensor(out=gt[:], in0=xs[j][:], scalar=S.dcol[:, j:j + 1], in1=S.yT[:, j, :], op0=ALU.mult, op1=ALU.add),
             reads=["s_xc%d" % j, "s_dcol", "s_yT"], writes=[gk])
        P.op("act", lambda e, j=j: e.activation(out=S.zs[:], in_=S.z[j][:], func=AF.Silu), reads=["s_z%d" % j], writes=["s_zs"])
        P.op("dve", lambda e, gt=gt: e.tensor_tensor(out=gt[:], in0=gt[:], in1=S.zs[:], op=ALU.mult), reads=[gk, "s_zs"], writes=[gk])
        P.op("act", lambda e, gt=gt, sq=sq: e.activation(out=sq[:], in_=gt[:], func=AF.Square), reads=[gk], writes=[sk])

    def nm(e):
        ins = None
        for j in range(4):
            ins = e.matmul(psB[:, :], lhsT=C.ones_f[:], rhs=S.xdtwT[j][:], start=(j == 0), stop=(j == 3))
        return ins
    P.op("pe", nm, reads=["ones_f"] + ["s_xdtwT%d" % j for j in range(4)], writes=["psB"])
    P.op("act", lambda e: e.activation(out=S.rs[:], in_=psB[:, :], func=AF.Sqrt, bias=C.cst["eps"][:, 0:1], scale=1.0 / 512), reads=["psB", "cst"], writes=["s_rs"])
    P.op("dve", lambda e: e.reciprocal(out=S.rs[:], in_=S.rs[:]), reads=["s_rs"], writes=["s_rs"])
    for j in range(4):
        P.op("dve", lambda e, j=j: e.scalar_tensor_tensor(out=S.res[:], in0=S.xdtT[j][:], scalar=S.nw[:, j:j + 1], in1=S.rs[:], op0=ALU.mult, op1=ALU.mult),
             reads=["s_xdtT%d" % j, "s_nw", "s_rs"], writes=["s_res"])
        P.dma_op("sp", out[128 * j:128 * j + 128, tsl], S.res[:], reads=["s_res"], is_output=True)


def build_B1(T):
    nc = new_nc()
    pT = dram_in(nc, "pT", [1280, T])
    dtT = dram_in(nc, "dtT", [8, T])
    d = {"s_cw": dram_in(nc, "s_cw", [128, 24]), "s_cb": dram_in(nc, "s_cb", [128, 6]), "s_dcol": dram_in(nc, "s_dcol", [128, 4]),
         "s_nw": dram_in(nc, "s_nw", [128, 4]), "s_dtb": dram_in(nc, "s_dtb", [8, 1]), "s_alog": dram_in(nc, "s_alog", [8, 1])}
    cd = {"ident": dram_in(nc, "ident", [128, 128]), "selpair": dram_in(nc, "selpair", [8, 512]), "selones": dram_in(nc, "selones", [8, 1024]),
          "sel8": dram_in(nc, "sel8", [8, 8]), "ones8": dram_in(nc, "ones8", [8, 64]), "rmask8": dram_in(nc, "rmask8", [8, 512]),
          "negmask": dram_in(nc, "negmask", [64, 512]), "selrow64": dram_in(nc, "selrow64", [8, 512])}
    out = dram_out(nc, "yT", [512, T])
    with ExitStack() as st:
        P = Prog(nc, st)
        C = Ctx()
        mixer_common(P, nc, C, cd)
        S = ssd_setup(P, nc, C)
        ssd_init(P, C, S, d, cd)
        for s in range(T // ST):
            ssd_supertile(P, C, S, s, pT, dtT, out, T)
        P.finish()
        print("B1 ops", P.n_ops())
    return nc

NCORES = 8
SEQ = 16384
TOKC = SEQ // NCORES

OFF = {"gq": 0, "gk": 1024, "gv": 2048, "gz": 3072, "gb": 4096, "ga": 4104, "aq": 4112, "ak": 5136, "av": 6160}


def mod_layout(modrow):
    return np.ascontiguousarray(modrow.reshape(6, 16, 128).transpose(2, 0, 1).reshape(128, 96))


def col_layout(v):
    return np.ascontiguousarray(v.reshape(16, 128).T)


def build_mod():
    nc = new_nc()
    c_in = dram_in(nc, "c_in", [128, 16])
    w_in = dram_in(nc, "w_in", [2, 2048, 1536])
    b_in = dram_in(nc, "b_in", [128, 24])
    out = dram_out(nc, "out", [128, 24])
    with ExitStack() as st:
        P = Prog(nc, st)
        ct = P.sb("ct", [128, 16]); ca = P.sb("ca", [128, 16])
        bt = P.sb("bt", [128, 24]); ot = P.sb("ot", [128, 24])
        wt = P.sb("wt", [128, 16, 1536])
        acc = P.ps("acc", [128, 512])
        P.dma_op("sp", ct[:], c_in, writes=["ct"])
        P.dma_op("sp", bt[:], b_in, writes=["bt"])
        P.op("act", lambda e: e.activation(out=ca[:], in_=ct[:], func=AF.Silu), reads=["ct"], writes=["ca"])
        for l in range(2):
            P.dma_op("sp", wt[:], w_in[l].rearrange("(kt p) n -> p kt n", p=128), writes=["wt"])

            def mm(e, l=l):
                ins = None
                for j in range(12):
                    for kt in range(16):
                        ins = e.matmul(acc[:, l * 12 + j:l * 12 + j + 1], lhsT=wt[:, kt, j * 128:(j + 1) * 128], rhs=ca[:, kt:kt + 1],
                                       start=(kt == 0), stop=(kt == 15))
                return ins
            P.op("pe", mm, reads=["wt", "ca"], writes=["acc"])
        P.op("dve", lambda e: e.tensor_tensor(out=ot[:], in0=acc[:, 0:24], in1=bt[:], op=ALU.add), reads=["acc", "bt"], writes=["ot"])
        P.dma_op("sp", out, ot[:], reads=["ot"], is_output=True)
        P.finish()
    return nc


def run(nc, in_maps):
    res = run_bass_kernel_spmd(nc, in_maps, core_ids=list(range(NCORES)))
    return res.results


def kernel(x, c, mod_w, mod_b, norm_mix_w, norm_mlp_w, mlp_w1, mlp_w2,
           ab_w_in, gdn_conv_w, gdn_a_log, gdn_dt_bias, gdn_norm_w,
           attn_q_norm_w, attn_k_norm_w, attn_rel_bias, ab_w_out,
           ssd_w_in, ssd_conv_w, ssd_conv_b, ssd_dt_bias, ssd_a_log, ssd_d,
           ssd_norm_w, ssd_w_out):
    f = lambda a: np.ascontiguousarray(np.asarray(a, dtype=np.float32))
    x = f(x); c = f(c); mod_w = f(mod_w); mod_b = f(mod_b)
    cl = np.ascontiguousarray(c.reshape(16, 128).T)
    ims = []
    for core in range(NCORES):
        sl = slice(core * 1536, (core + 1) * 1536)
        b = np.stack([mod_b[l, sl].reshape(12, 128).T for l in range(2)], axis=1).reshape(128, 24)
        ims.append({"c_in": cl, "w_in": np.ascontiguousarray(mod_w[:, :, sl]), "b_in": np.ascontiguousarray(b)})
    r = run(build_mod(), ims)
    mod = np.zeros((2, 12288), np.float32)
    for core in range(NCORES):
        o = r[core]["out"].reshape(128, 2, 12)
        for l in range(2):
            mod[l, core * 1536:(core + 1) * 1536] = o[:, l, :].T.reshape(-1)
    modl = [mod_layout(mod[l]) for l in range(2)]
    xT = [np.ascontiguousarray(x[0, cc * TOKC:(cc + 1) * TOKC].T) for cc in range(NCORES)]

    perm0 = np.concatenate([np.concatenate([OFF[n] + 128 * i + np.arange(128) for n in ("gq", "gk", "gv", "gz", "aq", "ak", "av")]) for i in range(8)]
                           + [np.arange(4096, 4112)])
    w0 = np.ascontiguousarray(f(ab_w_in)[0][:, perm0])
    nw0 = col_layout(f(norm_mix_w)[0])
    r = run(build_stageA(7184, TOKC), [{"xT": xT[cc], "w": w0, "nw": nw0, "modl": modl[0]} for cc in range(NCORES)])
    projT = [r[cc]["projT"] for cc in range(NCORES)]
    hc = host_consts()
    cw = f(gdn_conv_w)[0]
    ims = []
    for i in range(8):
        pT = np.ascontiguousarray(np.concatenate([projT[cc][896 * i:896 * (i + 1)] for cc in range(NCORES)], axis=1))
        gates = np.ascontiguousarray(np.concatenate([projT[cc][[7168 + i, 7176 + i]] for cc in range(NCORES)], axis=1))
        cwh = np.concatenate([cw[:, 1024 * j + 128 * i: 1024 * j + 128 * (i + 1)].T for j in range(3)], axis=1)
        im = {"pT": pT, "gates": gates,
              "qw": f(attn_q_norm_w)[0].reshape(128, 1).copy(), "kw": f(attn_k_norm_w)[0].reshape(128, 1).copy(),
              "bias": rel_bias_toeplitz(f(attn_rel_bias)[0, i]),
              "g_cw": np.ascontiguousarray(cwh), "g_alog": f(gdn_a_log)[0, i].reshape(1, 1).copy(), "g_dtb": f(gdn_dt_bias)[0, i].reshape(1, 1).copy(),
              "g_nw": f(gdn_norm_w)[0].reshape(128, 1).copy()}
        im.update(hc)
        ims.append(im)
    del projT
    r = run(build_B0(SEQ), ims)
    oTh = [r[i]["oT"] for i in range(8)]
    ims = []
    wo0 = f(ab_w_out)[0]; w1 = f(mlp_w1); w2 = f(mlp_w2)
    for cc in range(NCORES):
        ts = slice(cc * TOKC, (cc + 1) * TOKC)
        oT = np.ascontiguousarray(np.concatenate([oTh[i][0:128, ts] for i in range(8)] + [oTh[i][128:256, ts] for i in range(8)], axis=0))
        ims.append({"xT": xT[cc], "oT": oT, "wo": wo0, "w1": w1[0], "w2": w2[0], "nw": col_layout(f(norm_mlp_w)[0]), "modl": modl[0]})
    r = run(build_stageC(2048, TOKC), ims)
    xT = [r[cc]["x2T"] for cc in range(NCORES)]

    perm1 = np.concatenate([np.concatenate([512 * g + np.arange(512), 4096 + 512 * g + np.arange(512), 8192 + 128 * g + np.arange(128),
                                            9216 + 128 * g + np.arange(128)]) for g in range(8)] + [10240 + np.arange(64)])
    ws = np.ascontiguousarray(f(ssd_w_in)[0][:, perm1])
    nw1 = col_layout(f(norm_mix_w)[1])
    r = run(build_stageA(10304, TOKC), [{"xT": xT[cc], "w": ws, "nw": nw1, "modl": modl[1]} for cc in range(NCORES)])
    projT = [r[cc]["projT"] for cc in range(NCORES)]
    shc = ssd_host_consts()
    scw = f(ssd_conv_w)[0]; scb = f(ssd_conv_b)[0]
    ims = []
    for g in range(8):
        pT = np.ascontiguousarray(np.concatenate([projT[cc][1280 * g:1280 * (g + 1)] for cc in range(NCORES)], axis=1))
        dtT = np.ascontiguousarray(np.concatenate([projT[cc][10240 + 8 * g:10240 + 8 * (g + 1)] for cc in range(NCORES)], axis=1))
        chans = [np.arange(512 * g + 128 * j, 512 * g + 128 * (j + 1)) for j in range(4)] + [np.arange(4096 + 128 * g, 4096 + 128 * (g + 1)),
                                                                                           np.arange(5120 + 128 * g, 5120 + 128 * (g + 1))]
        cwl = np.concatenate([scw[:, ch].T for ch in chans], axis=1)
        cbl = np.stack([scb[ch] for ch in chans], axis=1)
        dsk = f(ssd_d)[0][8 * g:8 * (g + 1)]
        dcol = np.stack([np.repeat(dsk[2 * j:2 * j + 2], 64) for j in range(4)], axis=1)
        nwg = f(ssd_norm_w)[0][512 * g:512 * (g + 1)].reshape(4, 128).T
        im = {"pT": pT, "dtT": dtT, "s_cw": np.ascontiguousarray(cwl), "s_cb": np.ascontiguousarray(cbl), "s_dcol": np.ascontiguousarray(dcol),
              "s_nw": np.ascontiguousarray(nwg), "s_dtb": f(ssd_dt_bias)[0][8 * g:8 * (g + 1)].reshape(8, 1).copy(),
              "s_alog": f(ssd_a_log)[0][8 * g:8 * (g + 1)].reshape(8, 1).copy()}
        im.update(shc)
        ims.append(im)
    del projT
    r = run(build_B1(SEQ), ims)
    yTg = [r[g]["yT"] for g in range(8)]
    ims = []
    wo1 = f(ssd_w_out)[0]
    for cc in range(NCORES):
        ts = slice(cc * TOKC, (cc + 1) * TOKC)
        oT = np.ascontiguousarray(np.concatenate([yTg[g][:, ts] for g in range(8)], axis=0))
        ims.append({"xT": xT[cc], "oT": oT, "wo": wo1, "w1": w1[1], "w2": w2[1], "nw": col_layout(f(norm_mlp_w)[1]), "modl": modl[1]})
    r = run(build_stageC(4096, TOKC), ims)
    out = np.concatenate([r[cc]["x2T"].T for cc in range(NCORES)], axis=0)[None]
    return np.ascontiguousarray(out.astype(np.float32))
```

```python
import numpy as np
from contextlib import ExitStack
import concourse.bass as bass
import concourse.mybir as mybir
from concourse.bass_utils import run_bass_kernel_spmd

F32 = mybir.dt.float32
BF16 = mybir.dt.bfloat16
AF = mybir.ActivationFunctionType
ALU = mybir.AluOpType
AX = mybir.AxisListType

ENGS = ["pe", "dve", "act", "pool", "sp"]
EPOCH = 30000
SAME_ENGINE_SYNC = True


class Prog:
    def __init__(self, nc, stack, n_dma_sems=12):
        self.nc = nc
        self.stack = stack
        self.ops = {e: [] for e in ENGS}
        self.known = {e: {} for e in ENGS}
        self.tiles = {}
        self.semid = 0
        self.cur = {}
        for e in ["pe", "dve", "act", "pool"]:
            self._new_sem(e)
        self.dma = {}
        for q in ["sp", "pool", "act"]:
            sems = []
            for j in range(n_dma_sems):
                s = stack.enter_context(nc.semaphore(f"dq_{q}_{j}"))
                self.semid += 1
                sems.append([s, self.semid, 0])
            self.dma[q] = [sems, 0]
        self.out_events = []

    def _new_sem(self, e):
        s = self.stack.enter_context(self.nc.semaphore(f"s_{e}_{self.semid}"))
        self.semid += 1
        self.cur[e] = [s, self.semid, 0]

    def sb(self, name, shape, dtype=F32):
        return self.stack.enter_context(self.nc.sbuf_tensor("sb_" + name, list(shape), dtype))

    def ps(self, name, shape, dtype=F32):
        return self.stack.enter_context(self.nc.psum_tensor("pp_" + name, list(shape), dtype))

    def _st(self, k):
        st = self.tiles.get(k)
        if st is None:
            st = {"w": None, "r": {}}
            self.tiles[k] = st
        return st

    def _collect(self, eng, reads, writes, same_sync):
        waits = {}

        def add(ev):
            if ev is None:
                return
            s, sid, val = ev
            if sid == self.cur.get(eng, [None, -1])[1] and not same_sync:
                return
            if self.known[eng].get(sid, 0) >= val:
                return
            if sid not in waits or waits[sid][2] < val:
                waits[sid] = ev

        for k in reads:
            add(self._st(k)["w"])
        for k in writes:
            st = self._st(k)
            add(st["w"])
            for ev in st["r"].values():
                add(ev)
        for sid, ev in waits.items():
            self.ops[eng].append(("wait", ev[0], ev[2]))
            self.known[eng][sid] = ev[2]

    def _record(self, ev, reads, writes):
        for k in reads:
            st = self._st(k)
            st["r"][ev[1]] = ev
        for k in writes:
            st = self._st(k)
            st["w"] = ev
            st["r"] = {}

    def op(self, eng, fn, reads=(), writes=(), same_sync=None):
        if same_sync is None:
            same_sync = SAME_ENGINE_SYNC and eng != "pe"
        self._collect(eng, reads, writes, same_sync)
        c = self.cur[eng]
        if c[2] >= EPOCH:
            self._new_sem(eng)
            c = self.cur[eng]
        c[2] += 1
        ev = (c[0], c[1], c[2])
        self.ops[eng].append(("op", fn, c[0], 1))
        self._record(ev, reads, writes)
        return ev

    def dma_op(self, q, out, in_, reads=(), writes=(), is_output=False, **kw):
        sems, idx = self.dma[q]
        slot = sems[idx % len(sems)]
        self.dma[q][1] = idx + 1
        s, sid, uses = slot
        if uses > 0 and self.known[q].get(sid, 0) < 16 * uses:
            self.ops[q].append(("wait", s, 16 * uses))
            self.known[q][sid] = 16 * uses
        self._collect(q, reads, writes, True)
        slot[2] = uses + 1
        ev = (s, sid, 16 * (uses + 1))

        def fn(e, out=out, in_=in_, kw=kw):
            return e.dma_start(out=out, in_=in_, **kw)

        self.ops[q].append(("op", fn, s, 16))
        self._record(ev, reads, writes)
        if is_output:
            self.out_events.append(ev)
        return ev

    def finish(self):
        best = {}
        for ev in self.out_events:
            if ev[1] not in best or best[ev[1]][2] < ev[2]:
                best[ev[1]] = ev
        for ev in best.values():
            self.ops["sp"].append(("wait", ev[0], ev[2]))
        nc = self.nc
        ops = self.ops

        def replay(name, e):
            for o in ops[name]:
                if o[0] == "wait":
                    e.wait_ge(o[1], o[2])
                else:
                    ins = o[1](e)
                    ins.then_inc(o[2], o[3])

        with nc.Block() as block:
            @block.tensor
            def _(e):
                replay("pe", e)

            @block.vector
            def _(e):
                replay("dve", e)

            @block.scalar
            def _(e):
                replay("act", e)

            @block.gpsimd
            def _(e):
                replay("pool", e)

            @block.sync
            def _(e):
                replay("sp", e)

    def n_ops(self):
        return {e: len(v) for e, v in self.ops.items()}


D = 2048
KT = 16
TT = 512
EPS = 1e-6


def new_nc():
    return bass.Bass("TRN2", target_bir_lowering=False)


def dram_in(nc, name, shape, dtype=F32):
    return nc.dram_tensor(name, list(shape), dtype, kind="ExternalInput").ap()


def dram_out(nc, name, shape, dtype=F32):
    return nc.dram_tensor(name, list(shape), dtype, kind="ExternalOutput").ap()


class Ctx:
    pass


def setup_common(P):
    C = Ctx()
    C.ones_bf = P.sb("ones_bf", [128, 128], BF16)
    P.op("pool", lambda e: e.memset(C.ones_bf[:], 1.0), writes=["ones_bf"])
    C.ps = [P.ps(f"ps{i}", [128, 512]) for i in range(8)]
    C.rot = 0
    C.ev = 0
    return C


def emit_mod_cols(P, modt, nwt, which):
    acol = P.sb("acol%d" % which, [128, 16])
    s_sh = 0 if which == 0 else 3
    P.op("dve", lambda e: e.scalar_tensor_tensor(out=acol[:], in0=modt[:, (s_sh + 1) * 16:(s_sh + 2) * 16], scalar=1.0,
                                                 in1=nwt[:], op0=ALU.add, op1=ALU.mult),
         reads=["modt", "nwt%d" % which], writes=["acol"])
    return acol, modt[:, s_sh * 16:(s_sh + 1) * 16]


def make_scr(P):
    scr = {}
    scr["rstd"] = P.sb("rstd", [128, 512])
    scr["tmp"] = [P.sb("tmp0", [128, 512]), P.sb("tmp1", [128, 512])]
    scr["eps"] = P.sb("epst", [128, 1])
    P.op("pool", lambda e: e.memset(scr["eps"][:], EPS), writes=["eps"])
    return scr


def emit_dense(P, C, w_dram, K, ncols, rhs_fn, rhs_keys, ntt, evac_fn, wbufs, wq="pool", col0=0):
    nk = K // 128
    gcols = 8192 // nk
    ngroups = (ncols + gcols - 1) // gcols
    for g in range(ngroups):
        c0 = g * gcols
        csz = min(gcols, ncols - c0)
        wb = wbufs[C.rot_w % 2]
        wk = "wbuf%d" % (C.rot_w % 2)
        C.rot_w += 1
        wv = wb[:, 0:nk * csz].rearrange("p (k n) -> p k n", k=nk)
        src = w_dram[:, col0 + c0:col0 + c0 + csz].rearrange("(k p) n -> p k n", p=128)
        P.dma_op(wq, wv, src, writes=[wk])
        nm = (csz + 127) // 128
        for mi in range(nm):
            msz = min(128, csz - mi * 128)
            m = (c0 // 128) + mi
            for tt in range(ntt):
                bank = C.rot % 6
                C.rot += 1
                pst = C.ps[bank]

                def mm(e, wv=wv, mi=mi, msz=msz, tt=tt, pst=pst):
                    ins = None
                    for kt in range(nk):
                        ins = e.matmul(pst[0:msz, :], lhsT=wv[:, kt, mi * 128:mi * 128 + msz], rhs=rhs_fn(kt, tt),
                                       start=(kt == 0), stop=(kt == nk - 1))
                    return ins
                P.op("pe", mm, reads=[wk] + rhs_keys(tt), writes=["ps%d" % bank])
                evac_fn(m, msz, tt, pst, "ps%d" % bank)


def build_stageA(NC, TOK=2048):
    nc = new_nc()
    xT = dram_in(nc, "xT", [D, TOK])
    w = dram_in(nc, "w", [D, NC])
    nw = dram_in(nc, "nw", [128, 16])
    modl = dram_in(nc, "modl", [128, 96])
    out = dram_out(nc, "projT", [NC, TOK])
    ntt = TOK // TT
    with ExitStack() as st:
        P = Prog(nc, st)
        C = setup_common(P)
        C.rot_w = 0
        scr = make_scr(P)
        modt = P.sb("modt", [128, 96])
        nwt = P.sb("nwt", [128, 16])
        P.dma_op("sp", modt[:], modl, writes=["modt"])
        P.dma_op("sp", nwt[:], nw, writes=["nwt0"])
        acol, shcol = emit_mod_cols(P, modt, nwt, 0)
        P.tiles["shcol"] = P.tiles["modt"]
        xt = P.sb("xt", [128, 16, TT])
        sq = P.sb("sq", [128, 16, TT], BF16)
        hT = P.sb("hT", [128, 16, TOK], BF16)
        wbufs = [P.sb("wb0", [128, 8192], BF16), P.sb("wb1", [128, 8192], BF16)]
        ot = [P.sb("ot0", [128, TOK]), P.sb("ot1", [128, TOK])]
        xsrc = xT.rearrange("(k p) t -> p k t", p=128)
        for tt in range(ntt):
            P.dma_op("sp", xt[:], xsrc[:, :, tt * TT:(tt + 1) * TT], writes=["xt"])
            emit_adaln_xt(P, C, xt, hT[:, :, tt * TT:(tt + 1) * TT], "hT%d" % tt, acol, shcol, sq, scr)

        state = {"n": 0}

        def evac(m, msz, tt, pst, pk):
            b = (m % 2)
            o = ot[b]
            okey = "ot%d_%d" % (b, tt)
            eng = "dve" if C.ev % 2 == 0 else "act"
            C.ev += 1
            if eng == "dve":
                P.op("dve", lambda e: e.tensor_copy(out=o[0:msz, tt * TT:(tt + 1) * TT], in_=pst[0:msz, :]), reads=[pk], writes=[okey])
            else:
                P.op("act", lambda e: e.activation(out=o[0:msz, tt * TT:(tt + 1) * TT], in_=pst[0:msz, :], func=AF.Copy), reads=[pk], writes=[okey])
            if tt == ntt - 1:
                P.dma_op("sp", out[m * 128:m * 128 + msz, :], o[0:msz, :], reads=["ot%d_%d" % (b, t) for t in range(ntt)], is_output=True)

        emit_dense(P, C, w, D, NC, lambda kt, tt: hT[:, kt, tt * TT:(tt + 1) * TT], lambda tt: ["hT%d" % tt], ntt, evac, wbufs)
        P.finish()
        print("stageA ops", P.n_ops())
    return nc


def emit_adaln_xt(P, C, xt, ht, hk, acol, shcol, sq, scr):
    class _K:
        pass
    emit_adaln2(P, C, xt, ["xt"], ht, hk, acol, shcol, sq, "sq", scr)


def emit_adaln2(P, C, xt, xkeys, ht, hk, a_col, sh_col, sq, sqk, scr):
    nbank = C.ps[7]
    for q in range(4):
        if q % 2 == 0:
            P.op("act", lambda e, q=q: e.activation(out=sq[:, 4 * q:4 * q + 4, :], in_=xt[:, 4 * q:4 * q + 4, :], func=AF.Square),
                 reads=xkeys, writes=[sqk + str(q)])
        else:
            P.op("pool", lambda e, q=q: e.tensor_tensor(out=sq[:, 4 * q:4 * q + 4, :], in0=xt[:, 4 * q:4 * q + 4, :],
                                                      in1=xt[:, 4 * q:4 * q + 4, :], op=ALU.mult),
                 reads=xkeys, writes=[sqk + str(q)])

    def mm(e):
        ins = None
        for kt in range(KT):
            ins = e.matmul(nbank[:, :], lhsT=C.ones_bf[:], rhs=sq[:, kt, :], start=(kt == 0), stop=(kt == KT - 1))
        return ins
    P.op("pe", mm, reads=["ones_bf"] + [sqk + str(q) for q in range(4)], writes=["ps7"])
    rstd = scr["rstd"]
    P.op("act", lambda e: e.activation(out=rstd[:], in_=nbank[:, :], func=AF.Sqrt, bias=scr["eps"][:, 0:1], scale=1.0 / D),
         reads=["ps7", "eps"], writes=["rstd"])
    P.op("dve", lambda e: e.reciprocal(out=rstd[:], in_=rstd[:]), reads=["rstd"], writes=["rstd"])
    for kt in range(KT):
        tmp = scr["tmp"][kt % 2]
        tk = "tmp%d" % (kt % 2)
        P.op("dve", lambda e, kt=kt, tmp=tmp: e.tensor_tensor(out=tmp[:], in0=xt[:, kt, :], in1=rstd[:], op=ALU.mult),
             reads=xkeys + ["rstd"], writes=[tk])
        P.op("act", lambda e, kt=kt, tmp=tmp: e.activation(out=ht[:, kt, :], in_=tmp[:], func=AF.Identity,
                                                         bias=sh_col[:, kt:kt + 1], scale=a_col[:, kt:kt + 1]),
             reads=[tk, "acol", "modt"], writes=[hk])


def build_stageC(KO, TOK=2048):
    nc = new_nc()
    xT = dram_in(nc, "xT", [D, TOK])
    oT = dram_in(nc, "oT", [KO, TOK])
    wo = dram_in(nc, "wo", [KO, D])
    w1 = dram_in(nc, "w1", [D, 4 * D])
    w2 = dram_in(nc, "w2", [4 * D, D])
    nw = dram_in(nc, "nw", [128, 16])
    modl = dram_in(nc, "modl", [128, 96])
    out = dram_out(nc, "x2T", [D, TOK])
    ntt = TOK // TT
    nko = KO // 128
    with ExitStack() as st:
        P = Prog(nc, st)
        C = setup_common(P)
        C.rot_w = 0
        scr = make_scr(P)
        modt = P.sb("modt", [128, 96])
        nwt = P.sb("nwt", [128, 16])
        P.dma_op("sp", modt[:], modl, writes=["modt"])
        P.dma_op("sp", nwt[:], nw, writes=["nwt1"])
        acol, shcol = emit_mod_cols(P, modt, nwt, 1)
        g1 = modt[:, 32:48]
        g2 = modt[:, 80:96]
        xt = P.sb("xt", [128, 16, TT])
        ob = P.sb("ob", [128, max(nko, 16), TT], BF16)
        aT = P.sb("aT", [128, 64, TT], BF16)
        wbufs = [P.sb("wb0", [128, 8192], BF16), P.sb("wb1", [128, 8192], BF16)]
        ot = [P.sb("ot0", [128, TT]), P.sb("ot1", [128, TT])]
        rl = [P.sb("rl0", [128, TT]), P.sb("rl1", [128, TT])]
        xsrc = xT.rearrange("(k p) t -> p k t", p=128)
        osrc = oT.rearrange("(k p) t -> p k t", p=128)
        odst = out.rearrange("(k p) t -> p k t", p=128)
        for tt in range(ntt):
            tsl = slice(tt * TT, (tt + 1) * TT)
            P.dma_op("sp", xt[:], xsrc[:, :, tsl], writes=["xt"])
            P.dma_op("pool", ob[:, 0:nko, :], osrc[:, :, tsl], writes=["ob"])

            def evac1(m, msz, t_, pst, pk):
                P.op("dve", lambda e: e.scalar_tensor_tensor(out=xt[:, m, :], in0=pst[:, :], scalar=g1[:, m:m + 1], in1=xt[:, m, :],
                                                             op0=ALU.mult, op1=ALU.add),
                     reads=[pk, "modt", "xt"], writes=["x1_%d" % m])
            emit_dense(P, C, wo, KO, D, lambda kt, t_: ob[:, kt, :], lambda t_: ["ob"], 1, evac1, wbufs)
            x1keys = ["x1_%d" % m for m in range(16)]
            emit_adaln2(P, C, xt, x1keys, ob[:, 0:16, :], "ob", acol, shcol, aT[:, 0:16, :], "aT", scr)

            def evac2(f, msz, t_, pst, pk):
                r = rl[f % 2]
                rk = "rl%d" % (f % 2)
                P.op("act", lambda e: e.activation(out=r[:], in_=pst[:, :], func=AF.Square), reads=[pk], writes=[rk])
                P.op("dve", lambda e: e.scalar_tensor_tensor(out=aT[:, f, :], in0=pst[:, :], scalar=0.0, in1=r[:],
                                                             op0=ALU.is_gt, op1=ALU.mult),
                     reads=[pk, rk], writes=["aT_%d" % f])
            emit_dense(P, C, w1, D, 4 * D, lambda kt, t_: ob[:, kt, :], lambda t_: ["ob"], 1, evac2, wbufs)
            akeys = ["aT_%d" % f for f in range(64)]

            def evac3(m, msz, t_, pst, pk):
                o = ot[m % 2]
                okey = "ot%d" % (m % 2)
                P.op("dve", lambda e: e.scalar_tensor_tensor(out=o[:], in0=pst[:, :], scalar=g2[:, m:m + 1], in1=xt[:, m, :],
                                                             op0=ALU.mult, op1=ALU.add),
                     reads=[pk, "modt", "x1_%d" % m], writes=[okey])
                P.dma_op("sp", odst[:, m, tsl], o[:], reads=[okey], is_output=True)
            emit_dense(P, C, w2, 4 * D, D, lambda kt, t_: aT[:, kt, :], lambda t_: akeys, 1, evac3, wbufs)
            _fold(P, "xt", x1keys)
            _fold(P, "aT0", akeys); _fold(P, "aT1", akeys); _fold(P, "aT2", akeys); _fold(P, "aT3", akeys)
        P.finish()
        print("stageC ops", P.n_ops())
    return nc


def _fold(P, dst, keys):
    d = P._st(dst)
    for k in keys:
        s = P._st(k)
        if s["w"] is not None:
            d["r"][("w", s["w"][1])] = s["w"] if ("w", s["w"][1]) not in d["r"] or d["r"][("w", s["w"][1])][2] < s["w"][2] else d["r"][("w", s["w"][1])]
        for sid, ev in s["r"].items():
            key = ("r", ev[1])
            if key not in d["r"] or d["r"][key][2] < ev[2]:
                d["r"][key] = ev


PH = 99
SUB = 0
ST = 512
NEG = -30000.0


def host_consts():
    c = {}
    c["ident"] = np.eye(128, dtype=np.float32)
    i = np.arange(64)[:, None]
    j = np.arange(64)[None, :]
    sl = (i > j).astype(np.float32)
    su = (j > i).astype(np.float32)
    ui = (j >= i).astype(np.float32)
    c["m_sl"] = np.tile(sl, (1, 8))
    c["m_su"] = np.tile(su, (1, 8))
    c["m_ui"] = np.tile(ui, (1, 8))
    c["ibig"] = np.tile(np.eye(64, dtype=np.float32), (1, 8))
    rm = np.ones((1, 512), np.float32)
    rm[0, ::64] = 0.0
    c["rmask"] = rm
    return c


def rel_bias_toeplitz(rb_h):
    q = np.arange(128)[:, None]
    cc = np.arange(640)[None, :]
    rel = cc - 512 - q
    return np.ascontiguousarray(rb_h[np.clip(rel, -256, 256) + 256]).astype(np.float32)


class AttnState:
    pass


def attn_setup(P, nc, C):
    A = AttnState()
    A.qw = P.sb("a_qw", [128, 1]); A.kw = P.sb("a_kw", [128, 1])
    A.bias = P.sb("a_bias", [128, 640])
    A.kbuf = P.sb("a_kbuf", [128, 1024], BF16)
    A.vtp = P.sb("a_vtp", [128, 8, 128], BF16)
    A.qn = P.sb("a_qn", [128, ST], BF16)
    A.q = P.sb("a_q", [128, ST]); A.k = P.sb("a_k", [128, ST]); A.v = P.sb("a_v", [128, ST])
    A.sq = P.sb("a_sq", [128, ST])
    A.rs = P.sb("a_rs", [128, ST])
    A.sc = P.sb("a_sc", [128, 640])
    A.pt = P.sb("a_pt", [128, 640], BF16)
    A.ob = P.sb("a_ob", [128, ST])
    A.mx = P.sb("a_mx", [128, 1]); A.sm = P.sb("a_sm", [128, 1])
    return A


def attn_init(P, C, A, d):
    P.dma_op("sp", A.qw[:], d["qw"], writes=["a_qw"])
    P.dma_op("sp", A.kw[:], d["kw"], writes=["a_kw"])
    P.dma_op("sp", A.bias[:], d["bias"], writes=["a_bias"])
    P.op("pool", lambda e: e.memset(A.bias[0:64, 576:640], NEG), reads=[], writes=["a_bias"])
    P.op("pool", lambda e: e.memset(A.bias[64:128, 0:64], NEG), reads=[], writes=["a_bias"])
    P.op("pool", lambda e: e.memset(A.kbuf[:, 0:512], 0.0), writes=["a_kbuf"])
    P.op("pool", lambda e: e.memset(A.vtp[:, 0:4, :], 0.0), writes=["a_vtp"])


def attn_supertile(P, C, A, s, pT, out, T):
    t0 = s * ST
    tsl = slice(t0, t0 + ST)
    P.dma_op("sp", A.q[:], pT[512:640, tsl], writes=["a_q"])
    P.dma_op("sp", A.k[:], pT[640:768, tsl], writes=["a_k"])
    P.dma_op("sp", A.v[:], pT[768:896, tsl], writes=["a_v"])
    psB = C.psB
    for nm, x, w, dst, dk, sc_, bs_ in (("q", A.q, A.qw, A.q, "a_q", 1.0, "eps128"), ("k", A.k, A.kw, A.kbuf, "a_kbuf", 1.0 / 128, "eps")):
        P.op("act", lambda e, x=x: e.activation(out=A.sq[:], in_=x[:], func=AF.Square), reads=["a_" + nm], writes=["a_sq"])
        P.op("pe", lambda e: e.matmul(psB[:, :], lhsT=C.ones_f[:], rhs=A.sq[:], start=True, stop=True), reads=["ones_f", "a_sq"], writes=["psB"])
        P.op("act", lambda e, sc_=sc_, bs_=bs_: e.activation(out=A.rs[:], in_=psB[:, :], func=AF.Sqrt, bias=C.cst[bs_][:, 0:1], scale=sc_),
             reads=["psB", "cst"], writes=["a_rs"])
        P.op("dve", lambda e: e.reciprocal(out=A.rs[:], in_=A.rs[:]), reads=["a_rs"], writes=["a_rs"])
        if nm == "q":
            P.op("dve", lambda e: e.scalar_tensor_tensor(out=A.qn[:], in0=A.q[:], scalar=A.qw[:, 0:1], in1=A.rs[:], op0=ALU.mult, op1=ALU.mult),
                 reads=["a_q", "a_qw", "a_rs"], writes=["a_qn"])
        else:
            P.op("dve", lambda e: e.scalar_tensor_tensor(out=A.kbuf[:, 512:1024], in0=A.k[:], scalar=A.kw[:, 0:1], in1=A.rs[:], op0=ALU.mult, op1=ALU.mult),
                 reads=["a_k", "a_kw", "a_rs"], writes=["a_kbuf"])
    yield
    def tr(e):
        ins = None
        for b in range(4):
            ins = e.transpose(psB[:, b * 128:(b + 1) * 128], A.v[:, b * 128:(b + 1) * 128], C.ident[:])
        return ins
    P.op("pe", tr, reads=["a_v", "ident"], writes=["psB"])
    P.op("act", lambda e: e.activation(out=A.vtp[:, 4:8, :], in_=psB[:, :].rearrange("p (b e) -> p b e", b=4), func=AF.Copy),
         reads=["psB"], writes=["a_vtp"])
    yield
    psA = C.psA
    for j in range(4):
        def mm(e, j=j):
            e.matmul(psA[:, 0:512], lhsT=A.qn[:, 128 * j:128 * j + 128], rhs=A.kbuf[:, 128 * j:128 * j + 512], start=True, stop=True)
            return e.matmul(psA[:, 512:640], lhsT=A.qn[:, 128 * j:128 * j + 128], rhs=A.kbuf[:, 128 * j + 512:128 * j + 640], start=True, stop=True)
        P.op("pe", mm, reads=["a_qn", "a_kbuf"], writes=["psA"])
        P.op("dve", lambda e: e.tensor_tensor(out=A.sc[:], in0=psA[:, 0:640], in1=A.bias[:], op=ALU.add), reads=["psA", "a_bias"], writes=["a_sc"])
        if s == 0:
            ninv = 512 - 128 * j
            P.op("dve", lambda e, ninv=ninv: e.memset(A.sc[:, 0:ninv], NEG), writes=["a_sc"])
        yield
        P.op("dve", lambda e: e.tensor_reduce(out=A.mx[:], in_=A.sc[:], axis=AX.X, op=ALU.max), reads=["a_sc"], writes=["a_mx"])
        P.op("dve", lambda e: e.tensor_scalar(out=A.mx[:], in0=A.mx[:], scalar1=-1.0, scalar2=None, op0=ALU.mult), reads=["a_mx"], writes=["a_mx"])
        P.op("act", lambda e: e.activation(out=A.sc[:], in_=A.sc[:], func=AF.Exp, bias=A.mx[:, 0:1], scale=1.0, accum_out=A.sm[:, 0:1]),
             reads=["a_sc", "a_mx"], writes=["a_sc", "a_sm"])
        P.op("dve", lambda e: e.reciprocal(out=A.sm[:], in_=A.sm[:]), reads=["a_sm"], writes=["a_sm"])
        P.op("dve", lambda e: e.tensor_scalar(out=A.sc[:], in0=A.sc[:], scalar1=A.sm[:, 0:1], scalar2=None, op0=ALU.mult), reads=["a_sc", "a_sm"], writes=["a_sc"])

        yield

        def trp(e):
            ins = None
            for kb in range(5):
                ins = e.transpose(psA[:, kb * 128:(kb + 1) * 128], A.sc[:, kb * 128:(kb + 1) * 128], C.ident[:])
            return ins
        P.op("pe", trp, reads=["a_sc", "ident"], writes=["psA"])
        P.op("act", lambda e: e.activation(out=A.pt[:], in_=psA[:, 0:640], func=AF.Copy), reads=["psA"], writes=["a_pt"])

        yield

        def mo(e, j=j):
            ins = None
            for kb in range(5):
                ins = e.matmul(psB[:, 0:128], lhsT=A.vtp[:, j + kb, :], rhs=A.pt[:, kb * 128:(kb + 1) * 128], start=(kb == 0), stop=(kb == 4))
            return ins
        P.op("pe", mo, reads=["a_vtp", "a_pt"], writes=["psB"])
        P.op("dve", lambda e, j=j: e.tensor_copy(out=A.ob[:, 128 * j:128 * j + 128], in_=psB[:, 0:128]), reads=["psB"], writes=["a_ob"])
    P.dma_op("sp", out[128:256, tsl], A.ob[:], reads=["a_ob"], is_output=True)
    P.op("act", lambda e: e.activation(out=A.kbuf[:, 0:512], in_=A.kbuf[:, 512:1024], func=AF.Copy), reads=["a_kbuf"], writes=["a_kbuf"])
    P.op("act", lambda e: e.activation(out=A.vtp[:, 0:4, :], in_=A.vtp[:, 4:8, :], func=AF.Copy), reads=["a_vtp"], writes=["a_vtp"])
    yield


def mixer_common(P, nc, C, cd):
    C.ones_f = P.sb("ones_f", [128, 128])
    P.op("pool", lambda e: e.memset(C.ones_f[:], 1.0), writes=["ones_f"])
    C.ident = P.sb("ident_sb", [128, 128])
    P.dma_op("sp", C.ident[:], cd["ident"], writes=["ident"])
    C.cst = {}
    for nm, v in (("eps", EPS), ("eps128", 128 * EPS), ("one", 1.0), ("eps512", EPS)):
        t = P.sb("cst_" + nm, [128, 1])
        P.op("pool", lambda e, t=t, v=v: e.memset(t[:], v), writes=["cst"])
        C.cst[nm] = t
    C.psA = P.ps("psA", [128, 1024])
    C.psB = P.ps("psB", [128, 512])


def build_attn_only(T):
    nc = new_nc()
    pT = dram_in(nc, "pT", [896, T])
    d = {"qw": dram_in(nc, "qw", [128, 1]), "kw": dram_in(nc, "kw", [128, 1]), "bias": dram_in(nc, "bias", [128, 640])}
    cd = {"ident": dram_in(nc, "ident", [128, 128])}
    out = dram_out(nc, "oT", [256, T])
    with ExitStack() as st:
        P = Prog(nc, st)
        C = Ctx()
        mixer_common(P, nc, C, cd)
        A = attn_setup(P, nc, C)
        attn_init(P, C, A, d)
        for s in range(T // ST):
            rr([attn_supertile(P, C, A, s, pT, out, T)])
        P.finish()
        print("attn ops", P.n_ops())
    return nc


class GS:
    pass


def gdn_setup(P, nc, C):
    G = GS()
    def sb(n, shp):
        return P.sb("g_" + n, shp)
    G.cw = sb("cw", [128, 12]); G.alog = sb("alog", [1, 1]); G.dtb = sb("dtb", [1, 1]); G.nA = sb("nA", [1, 1])
    G.nw = sb("nw", [128, 1])
    G.raw = [sb("raw%d" % i, [128, ST + 3]) for i in range(3)]
    G.z = [sb("z0", [128, ST]), sb("z1", [128, ST])]; G.gb = sb("gb", [1, ST]); G.ga = sb("ga", [1, ST])
    G.sq2 = sb("sq2", [128, ST]); G.rs2 = sb("rs2", [128, ST])
    G.q_bf = P.sb("g_q_bf", [128, ST], BF16); G.k_bf = P.sb("g_k_bf", [128, ST], BF16); G.kb_bf = P.sb("g_kb_bf", [128, ST], BF16)
    G.Yf = sb("Yf", [64, ST])
    G.x = [sb("x%d" % i, [128, ST]) for i in range(3)]
    G.sq = sb("sq", [128, ST]); G.rs = sb("rs", [128, ST])
    G.beta = sb("beta", [1, ST]); G.g = sb("g", [1, ST]); G.gc = sb("gc", [1, ST]); G.ngc = sb("ngc", [1, ST])
    G.rmask = sb("rmask", [1, ST])
    G.beta_bc = sb("beta_bc", [128, ST]); G.gc_bc = sb("gc_bc", [128, ST]); G.egc_bc = [sb("egc_bc0", [128, ST]), sb("egc_bc1", [128, ST])]; G.ekd_bc = sb("ekd_bc", [128, ST])
    G.kb = sb("kb", [128, ST]); G.vb = sb("vb", [128, ST]); G.kbg = sb("kbg", [128, ST]); G.kd = sb("kd", [128, ST]); G.qd = [sb("qd0", [128, ST]), sb("qd1", [128, ST])]
    for n in ("m_sl", "m_su", "m_ui", "ibig", "E1", "E2", "Dl", "Du", "Dq"):
        setattr(G, n, sb(n, [64, ST]))
    for n in ("Pm", "Qm", "Ym"):
        setattr(G, n, P.sb("g_" + n, [64, ST], BF16))
    G.qkT = [sb("qkT0", [64, ST]), sb("qkT1", [64, ST])]
    G.vb_tp = P.sb("g_vb_tp", [64, 1024], BF16); G.kbg_tp = P.sb("g_kbg_tp", [64, 1024], BF16); G.kd_tp = [sb("kd_tp0", [64, 1024]), sb("kd_tp1", [64, 1024])]; G.u = [sb("u0", [64, 1024]), sb("u1", [64, 1024])]
    G.wT = [sb("wT0", [128, ST]), sb("wT1", [128, ST])]; G.S = sb("S", [128, 128]); G.vnew = sb("vnew", [64, 128])
    G.o = sb("o", [128, ST]); G.zs = sb("zs", [128, ST]); G.t1 = sb("t1", [128, ST]); G.res = sb("res", [128, ST])
    C.psC = P.ps("psC", [128, 1024]); C.psD = P.ps("psD", [128, 512]); C.psE = P.ps("psE", [128, 512]); C.psF = P.ps("psF", [128, 512])
    return G


def gdn_init(P, C, G, d, cd):
    for n, t in (("cw", G.cw), ("alog", G.alog), ("dtb", G.dtb), ("nw", G.nw)):
        P.dma_op("sp", t[:], d["g_" + n], writes=["g_" + n])
    for n in ("m_sl", "m_su", "m_ui", "ibig", "rmask"):
        P.dma_op("sp", getattr(G, n)[:], cd[n], writes=["g_" + n])
    P.op("act", lambda e: e.activation(out=G.nA[:], in_=G.alog[:], func=AF.Exp), reads=["g_alog"], writes=["g_nA"])
    P.op("dve", lambda e: e.tensor_scalar(out=G.nA[:], in0=G.nA[:], scalar1=-1.0, scalar2=None, op0=ALU.mult), reads=["g_nA"], writes=["g_nA"])
    P.op("pool", lambda e: e.memset(G.S[:], 0.0), writes=["g_S"])
    for i in range(3):
        P.op("pool", lambda e, i=i: e.memset(G.raw[i][:, 0:3], 0.0), writes=["g_raw%d" % i])


def gdn_pre(P, C, G, s, pT, gates, out, T):
    b = s % 2
    t0 = s * ST
    tsl = slice(t0, t0 + ST)
    psB, psC, psD, psE, psF = C.psB, C.psC, C.psD, C.psE, C.psF
    ones_row = C.ones_f[0:1, 0:64]
    for i in range(3):
        if s == 0:
            P.dma_op("sp", G.raw[i][:, 3:ST + 3], pT[128 * i:128 * i + 128, 0:ST], writes=["g_raw%d" % i])
        else:
            P.dma_op("sp", G.raw[i][:, :], pT[128 * i:128 * i + 128, t0 - 3:t0 + ST], writes=["g_raw%d" % i])
    P.dma_op("sp", G.z[b][:], pT[384:512, tsl], writes=["g_z%d" % b])
    P.dma_op("sp", G.gb[:], gates[0:1, tsl], writes=["g_gb"])
    P.dma_op("sp", G.ga[:], gates[1:2, tsl], writes=["g_ga"])
    yield
    for i in range(3):
        x = G.x[i]; raw = G.raw[i]; xk = "g_x%d" % i; rk = "g_raw%d" % i
        P.op("dve", lambda e, x=x, raw=raw, i=i: e.tensor_scalar(out=x[:], in0=raw[:, 0:ST], scalar1=G.cw[:, 4 * i:4 * i + 1], scalar2=None, op0=ALU.mult),
             reads=[rk, "g_cw"], writes=[xk])
        for j in range(1, 4):
            P.op("dve", lambda e, x=x, raw=raw, i=i, j=j: e.scalar_tensor_tensor(out=x[:], in0=raw[:, j:j + ST], scalar=G.cw[:, 4 * i + j:4 * i + j + 1],
                                                                               in1=x[:], op0=ALU.mult, op1=ALU.add),
                 reads=[rk, "g_cw", xk], writes=[xk])
        P.op("act", lambda e, x=x: e.activation(out=x[:], in_=x[:], func=AF.Silu), reads=[xk], writes=[xk])
    q, k, v = G.x
    yield
    for i, (sc_, bs_) in enumerate(((128.0, "eps128"), (1.0, "eps"))):
        x = G.x[i]; xk = "g_x%d" % i
        P.op("act", lambda e, x=x: e.activation(out=G.sq[:], in_=x[:], func=AF.Square), reads=[xk], writes=["g_sq"])
        P.op("pe", lambda e: e.matmul(psB[:, :], lhsT=C.ones_f[:], rhs=G.sq[:], start=True, stop=True), reads=["ones_f", "g_sq"], writes=["psB"])
        P.op("act", lambda e, sc_=sc_, bs_=bs_: e.activation(out=G.rs[:], in_=psB[:, :], func=AF.Sqrt, bias=C.cst[bs_][:, 0:1], scale=sc_),
             reads=["psB", "cst"], writes=["g_rs"])
        P.op("dve", lambda e: e.reciprocal(out=G.rs[:], in_=G.rs[:]), reads=["g_rs"], writes=["g_rs"])
        P.op("dve", lambda e, x=x: e.tensor_tensor(out=x[:], in0=x[:], in1=G.rs[:], op=ALU.mult), reads=[xk, "g_rs"], writes=[xk])
    yield
    P.op("act", lambda e: e.activation(out=G.beta[:], in_=G.gb[:], func=AF.Exp, scale=-1.0), reads=["g_gb"], writes=["g_beta"])
    P.op("dve", lambda e: e.tensor_scalar(out=G.beta[:], in0=G.beta[:], scalar1=1.0, scalar2=None, op0=ALU.add), reads=["g_beta"], writes=["g_beta"])
    P.op("dve", lambda e: e.reciprocal(out=G.beta[:], in_=G.beta[:]), reads=["g_beta"], writes=["g_beta"])
    P.op("act", lambda e: e.activation(out=G.g[:], in_=G.ga[:], func=AF.Exp, bias=G.dtb[0:1, 0:1], scale=1.0), reads=["g_ga", "g_dtb"], writes=["g_g"])
    P.op("act", lambda e: e.activation(out=G.g[:], in_=G.g[:], func=AF.Ln, bias=C.cst["one"][0:1, 0:1], scale=1.0), reads=["g_g", "cst"], writes=["g_g"])
    P.op("dve", lambda e: e.tensor_scalar(out=G.g[:], in0=G.g[:], scalar1=G.nA[0:1, 0:1], scalar2=None, op0=ALU.mult), reads=["g_g", "g_nA"], writes=["g_g"])
    P.op("dve", lambda e: e.tensor_tensor_scan(out=G.gc[:], data0=G.rmask[:], data1=G.g[:], initial=0.0, op0=ALU.mult, op1=ALU.add),
         reads=["g_rmask", "g_g"], writes=["g_gc"])
    P.op("dve", lambda e: e.tensor_scalar(out=G.ngc[:], in0=G.gc[:], scalar1=-1.0, scalar2=None, op0=ALU.mult), reads=["g_gc"], writes=["g_ngc"])
    yield
    P.op("pe", lambda e: e.matmul(psB[:, :], lhsT=C.ones_f[0:1, :], rhs=G.beta[:], start=True, stop=True), reads=["ones_f", "g_beta"], writes=["psB"])
    P.op("act", lambda e: e.activation(out=G.beta_bc[:], in_=psB[:, :], func=AF.Copy), reads=["psB"], writes=["g_beta_bc"])
    P.op("pe", lambda e: e.matmul(psB[:, :], lhsT=C.ones_f[0:1, :], rhs=G.gc[:], start=True, stop=True), reads=["ones_f", "g_gc"], writes=["psB"])
    P.op("act", lambda e: e.activation(out=G.gc_bc[:], in_=psB[:, :], func=AF.Copy), reads=["psB"], writes=["g_gc_bc"])
    P.op("act", lambda e: e.activation(out=G.egc_bc[b][:], in_=G.gc_bc[:], func=AF.Exp), reads=["g_gc_bc"], writes=["g_egc_bc%d" % b])
    for c in range(8):
        P.op("act", lambda e, c=c: e.activation(out=G.ekd_bc[:, 64 * c:64 * c + 64], in_=G.gc_bc[:, 64 * c:64 * c + 64], func=AF.Exp,
                                               bias=G.gc_bc[:, 64 * c + 63:64 * c + 64], scale=-1.0),
             reads=["g_gc_bc"], writes=["g_ekd_bc"])
    yield
    P.op("dve", lambda e: e.tensor_tensor(out=G.kb[:], in0=k[:], in1=G.beta_bc[:], op=ALU.mult), reads=["g_x1", "g_beta_bc"], writes=["g_kb"])
    P.op("dve", lambda e: e.tensor_tensor(out=G.vb[:], in0=v[:], in1=G.beta_bc[:], op=ALU.mult), reads=["g_x2", "g_beta_bc"], writes=["g_vb"])
    P.op("act", lambda e: e.activation(out=G.kb_bf[:], in_=G.kb[:], func=AF.Copy), reads=["g_kb"], writes=["g_kb_bf"])
    P.op("act", lambda e: e.activation(out=G.k_bf[:], in_=k[:], func=AF.Copy), reads=["g_x1"], writes=["g_k_bf"])
    P.op("act", lambda e: e.activation(out=G.q_bf[:], in_=q[:], func=AF.Copy), reads=["g_x0"], writes=["g_q_bf"])
    P.op("dve", lambda e: e.tensor_tensor(out=G.kbg[:], in0=G.kb[:], in1=G.egc_bc[b][:], op=ALU.mult), reads=["g_kb", "g_egc_bc%d" % b], writes=["g_kbg"])
    P.op("dve", lambda e: e.tensor_tensor(out=G.kd[:], in0=k[:], in1=G.ekd_bc[:], op=ALU.mult), reads=["g_x1", "g_ekd_bc"], writes=["g_kd"])
    P.op("dve", lambda e: e.tensor_tensor(out=G.qd[b][:], in0=q[:], in1=G.egc_bc[b][:], op=ALU.mult), reads=["g_x0", "g_egc_bc%d" % b], writes=["g_qd%d" % b])

    yield
    def dec(e):
        ins = None
        for c in range(8):
            cs = slice(64 * c, 64 * c + 64)
            e.matmul(psC[0:64, 64 * c:64 * c + 64], lhsT=G.gc[0:1, cs], rhs=ones_row, start=True, stop=False)
            e.matmul(psC[0:64, 64 * c:64 * c + 64], lhsT=ones_row, rhs=G.ngc[0:1, cs], start=False, stop=True)
        for c in range(8):
            cs = slice(64 * c, 64 * c + 64)
            e.matmul(psC[0:64, 512 + 64 * c:512 + 64 * c + 64], lhsT=ones_row, rhs=G.gc[0:1, cs], start=True, stop=False)
            ins = e.matmul(psC[0:64, 512 + 64 * c:512 + 64 * c + 64], lhsT=G.ngc[0:1, cs], rhs=ones_row, start=False, stop=True)
        return ins
    P.op("pe", dec, reads=["g_gc", "g_ngc", "ones_f"], writes=["psC"])
    P.op("dve", lambda e: e.tensor_scalar(out=G.E1[:], in0=psC[0:64, 0:512], scalar1=0.0, scalar2=None, op0=ALU.min), reads=["psC"], writes=["g_E1"])
    P.op("dve", lambda e: e.tensor_scalar(out=G.E2[:], in0=psC[0:64, 512:1024], scalar1=0.0, scalar2=None, op0=ALU.min), reads=["psC"], writes=["g_E2"])
    P.op("act", lambda e: e.activation(out=G.E1[:], in_=G.E1[:], func=AF.Exp), reads=["g_E1"], writes=["g_E1"])
    P.op("act", lambda e: e.activation(out=G.E2[:], in_=G.E2[:], func=AF.Exp), reads=["g_E2"], writes=["g_E2"])
    P.op("dve", lambda e: e.tensor_tensor(out=G.Dl[:], in0=G.E1[:], in1=G.m_sl[:], op=ALU.mult), reads=["g_E1", "g_m_sl"], writes=["g_Dl"])
    P.op("dve", lambda e: e.tensor_tensor(out=G.Du[:], in0=G.E2[:], in1=G.m_su[:], op=ALU.mult), reads=["g_E2", "g_m_su"], writes=["g_Du"])
    P.op("dve", lambda e: e.tensor_tensor(out=G.Dq[:], in0=G.E2[:], in1=G.m_ui[:], op=ALU.mult), reads=["g_E2", "g_m_ui"], writes=["g_Dq"])

    yield
    def kk(e):
        ins = None
        for c in range(8):
            cs = slice(64 * c, 64 * c + 64)
            e.matmul(psC[0:64, 64 * c:64 * c + 64], lhsT=G.kb_bf[:, cs], rhs=G.k_bf[:, cs], start=True, stop=True)
        for c in range(8):
            cs = slice(64 * c, 64 * c + 64)
            ins = e.matmul(psC[0:64, 512 + 64 * c:512 + 64 * c + 64], lhsT=G.k_bf[:, cs], rhs=G.kb_bf[:, cs], start=True, stop=True)
        return ins
    P.op("pe", kk, reads=["g_kb_bf", "g_k_bf"], writes=["psC"])

    def qkm(e):
        ins = None
        for c in range(8):
            cs = slice(64 * c, 64 * c + 64)
            ins = e.matmul(psD[0:64, 64 * c:64 * c + 64], lhsT=G.k_bf[:, cs], rhs=G.q_bf[:, cs], start=True, stop=True)
        return ins
    P.op("pe", qkm, reads=["g_q_bf", "g_k_bf"], writes=["psD"])
    P.op("dve", lambda e: e.tensor_tensor(out=G.Pm[:], in0=psC[0:64, 0:512], in1=G.Dl[:], op=ALU.mult), reads=["psC", "g_Dl"], writes=["g_Pm"])
    P.op("dve", lambda e: e.tensor_tensor(out=G.Qm[:], in0=psC[0:64, 512:1024], in1=G.Du[:], op=ALU.mult), reads=["psC", "g_Du"], writes=["g_Qm"])
    P.op("dve", lambda e: e.tensor_tensor(out=G.qkT[b][:], in0=psD[0:64, :], in1=G.Dq[:], op=ALU.mult), reads=["psD", "g_Dq"], writes=["g_qkT%d" % b])
    P.op("dve", lambda e: e.tensor_tensor(out=G.Yf[:], in0=G.ibig[:], in1=G.Qm[:], op=ALU.subtract), reads=["g_ibig", "g_Qm"], writes=["g_Yf"])
    P.op("act", lambda e: e.activation(out=G.Ym[:], in_=G.Yf[:], func=AF.Copy), reads=["g_Yf"], writes=["g_Ym"])
    yield
    yield
    for m in range(5):
        def sqr(e):
            ins = None
            for c in range(8):
                cs = slice(64 * c, 64 * c + 64)
                e.matmul(psC[0:64, 64 * c:64 * c + 64], lhsT=G.Qm[:, cs], rhs=G.Pm[:, cs], start=True, stop=True)
            for c in range(8):
                cs = slice(64 * c, 64 * c + 64)
                ins = e.matmul(psC[0:64, 512 + 64 * c:512 + 64 * c + 64], lhsT=G.Pm[:, cs], rhs=G.Qm[:, cs], start=True, stop=True)
            return ins
        P.op("pe", sqr, reads=["g_Pm", "g_Qm"], writes=["psC"])
        P.op("act", lambda e: e.activation(out=G.Pm[:], in_=psC[0:64, 0:512], func=AF.Copy), reads=["psC"], writes=["g_Pm"])
        P.op("dve", lambda e: e.tensor_copy(out=G.Qm[:], in_=psC[0:64, 512:1024]), reads=["psC"], writes=["g_Qm"])
        yield

        def yup(e):
            ins = None
            for c in range(8):
                cs = slice(64 * c, 64 * c + 64)
                ins = e.matmul(psD[0:64, 64 * c:64 * c + 64], lhsT=G.Pm[:, cs], rhs=G.Ym[:, cs], start=True, stop=True)
            return ins
        P.op("pe", yup, reads=["g_Ym", "g_Pm"], writes=["psD"])
        P.op("dve", lambda e: e.tensor_tensor(out=G.Yf[:], in0=G.Yf[:], in1=psD[0:64, :], op=ALU.add), reads=["psD", "g_Yf"], writes=["g_Yf"])
        P.op("act", lambda e: e.activation(out=G.Ym[:], in_=G.Yf[:], func=AF.Copy), reads=["g_Yf"], writes=["g_Ym"])
        yield
    yield
    for src, dst, sk, dk in ((G.vb, G.vb_tp, "g_vb", "g_vb_tp"), (G.kbg, G.kbg_tp, "g_kbg", "g_kbg_tp"), (G.kd, G.kd_tp[b], "g_kd", "g_kd_tp%d" % b)):
        def trp(e, src=src):
            ins = None
            for c in range(8):
                ins = e.transpose(psC[0:64, 128 * c:128 * c + 128], src[:, 64 * c:64 * c + 64], C.ident[:])
            return ins
        P.op("pe", trp, reads=[sk, "ident"], writes=["psC"])
        P.op("dve", lambda e, dst=dst: e.tensor_copy(out=dst[:], in_=psC[0:64, :]), reads=["psC"], writes=[dk])

    yield
    def um(e):
        ins = None
        for c in range(8):
            ins = e.matmul(psC[0:64, 128 * c:128 * c + 128], lhsT=G.Ym[:, 64 * c:64 * c + 64], rhs=G.vb_tp[:, 128 * c:128 * c + 128], start=True, stop=True)
        return ins
    P.op("pe", um, reads=["g_Ym", "g_vb_tp"], writes=["psC"])
    P.op("act", lambda e: e.activation(out=G.u[b][:], in_=psC[0:64, :], func=AF.Copy), reads=["psC"], writes=["g_u%d" % b])

    def wm(e):
        ins = None
        for c in range(8):
            ins = e.matmul(psD[:, 64 * c:64 * c + 64], lhsT=G.kbg_tp[:, 128 * c:128 * c + 128], rhs=G.Ym[:, 64 * c:64 * c + 64], start=True, stop=True)
        return ins
    P.op("pe", wm, reads=["g_Ym", "g_kbg_tp"], writes=["psD"])
    P.op("dve", lambda e: e.tensor_copy(out=G.wT[b][:], in_=psD[:, :]), reads=["psD"], writes=["g_wT%d" % b])
    yield


def gdn_scan(P, C, G, s, pT, gates, out, T):
    b = s % 2
    t0 = s * ST
    tsl = slice(t0, t0 + ST)
    psB, psC, psD, psE, psF = C.psB, C.psC, C.psD, C.psE, C.psF
    ones_row = C.ones_f[0:1, 0:64]
    for c in range(8):
        cs = slice(64 * c, 64 * c + 64)
        P.op("pe", lambda e, cs=cs: e.matmul(psF[0:64, 0:128], lhsT=G.wT[b][:, cs], rhs=G.S[:], start=True, stop=True), reads=["g_wT%d" % b, "g_S"], writes=["psF_a"])
        P.op("dve", lambda e, c=c: e.tensor_tensor(out=G.vnew[:], in0=G.u[b][:, 128 * c:128 * c + 128], in1=psF[0:64, 0:128], op=ALU.subtract),
             reads=["g_u%d" % b, "psF_a"], writes=["g_vnew"])
        yield

        def om(e, cs=cs, c=c):
            e.matmul(psE[:, cs], lhsT=G.S[:], rhs=G.qd[b][:, cs], start=True, stop=False)
            e.matmul(psE[:, cs], lhsT=G.vnew[:], rhs=G.qkT[b][:, cs], start=False, stop=True)
            return e.matmul(psF[:, 128:256], lhsT=G.kd_tp[b][:, 128 * c:128 * c + 128], rhs=G.vnew[:], start=True, stop=True)
        P.op("pe", om, reads=["g_S", "g_qd%d" % b, "g_vnew", "g_qkT%d" % b, "g_kd_tp%d" % b], writes=["psE", "psF_b"])
        P.op("dve", lambda e, c=c: e.scalar_tensor_tensor(out=G.S[:], in0=G.S[:], scalar=G.egc_bc[b][:, 64 * c + 63:64 * c + 64], in1=psF[:, 128:256],
                                                       op0=ALU.mult, op1=ALU.add),
             reads=["g_S", "g_egc_bc%d" % b, "psF_b"], writes=["g_S"])
        yield
    P.op("act", lambda e: e.activation(out=G.o[:], in_=psE[:, :], func=AF.Copy), reads=["psE"], writes=["g_o"])
    P.op("act", lambda e: e.activation(out=G.sq2[:], in_=G.o[:], func=AF.Square), reads=["g_o"], writes=["g_sq2"])
    P.op("pe", lambda e: e.matmul(psB[:, :], lhsT=C.ones_f[:], rhs=G.sq2[:], start=True, stop=True), reads=["ones_f", "g_sq2"], writes=["psB"])
    P.op("act", lambda e: e.activation(out=G.rs2[:], in_=psB[:, :], func=AF.Sqrt, bias=C.cst["eps"][:, 0:1], scale=1.0 / 128), reads=["psB", "cst"], writes=["g_rs2"])
    P.op("dve", lambda e: e.reciprocal(out=G.rs2[:], in_=G.rs2[:]), reads=["g_rs2"], writes=["g_rs2"])
    yield
    P.op("act", lambda e: e.activation(out=G.zs[:], in_=G.z[b][:], func=AF.Silu), reads=["g_z%d" % b], writes=["g_zs"])
    P.op("dve", lambda e: e.tensor_tensor(out=G.t1[:], in0=G.o[:], in1=G.rs2[:], op=ALU.mult), reads=["g_o", "g_rs2"], writes=["g_t1"])
    P.op("dve", lambda e: e.scalar_tensor_tensor(out=G.res[:], in0=G.t1[:], scalar=G.nw[:, 0:1], in1=G.zs[:], op0=ALU.mult, op1=ALU.mult),
         reads=["g_t1", "g_nw", "g_zs"], writes=["g_res"])
    P.dma_op("sp", out[0:128, tsl], G.res[:], reads=["g_res"], is_output=True)


    yield


def rr(gens):
    gens = [g for g in gens if g is not None]
    while gens:
        for g in list(gens):
            try:
                next(g)
            except StopIteration:
                gens.remove(g)


def build_B0(T, do_attn=True, do_gdn=True):
    nc = new_nc()
    pT = dram_in(nc, "pT", [896, T])
    gates = dram_in(nc, "gates", [2, T])
    d = {"qw": dram_in(nc, "qw", [128, 1]), "kw": dram_in(nc, "kw", [128, 1]), "bias": dram_in(nc, "bias", [128, 640]),
         "g_cw": dram_in(nc, "g_cw", [128, 12]), "g_alog": dram_in(nc, "g_alog", [1, 1]), "g_dtb": dram_in(nc, "g_dtb", [1, 1]),
         "g_nw": dram_in(nc, "g_nw", [128, 1])}
    cd = {"ident": dram_in(nc, "ident", [128, 128])}
    for n in ("m_sl", "m_su", "m_ui", "ibig"):
        cd[n] = dram_in(nc, n, [64, 512])
    cd["rmask"] = dram_in(nc, "rmask", [1, 512])
    out = dram_out(nc, "oT", [256, T])
    with ExitStack() as st:
        P = Prog(nc, st)
        C = Ctx()
        mixer_common(P, nc, C, cd)
        if do_attn:
            A = attn_setup(P, nc, C)
            attn_init(P, C, A, d)
        if do_gdn:
            G = gdn_setup(P, nc, C)
            gdn_init(P, C, G, d, cd)
        nst = T // ST
        if do_gdn:
            rr([gdn_pre(P, C, G, 0, pT, gates, out, T)])
        for s in range(nst):
            rr([gdn_scan(P, C, G, s, pT, gates, out, T) if do_gdn else None,
                attn_supertile(P, C, A, s, pT, out, T) if do_attn else None,
                gdn_pre(P, C, G, s + 1, pT, gates, out, T) if (do_gdn and s + 1 < nst) else None])
        P.finish()
        print("B0 ops", P.n_ops())
    return nc


def ssd_host_consts():
    c = {}
    c["ident"] = np.eye(128, dtype=np.float32)
    sp = np.zeros((8, 4, 128), np.float32)
    for j in range(4):
        sp[2 * j, j, 0:64] = 1.0
        sp[2 * j + 1, j, 64:128] = 1.0
    c["selpair"] = sp.reshape(8, 512)
    so = np.zeros((8, 8, 128), np.float32)
    for e in range(8):
        so[e, e, :] = 1.0
    c["selones"] = so.reshape(8, 1024)
    c["sel8"] = np.eye(8, dtype=np.float32)
    c["ones8"] = np.ones((8, 64), np.float32)
    rm = np.ones((8, 512), np.float32)
    rm[:, ::64] = 0.0
    c["rmask8"] = rm
    m = np.arange(64)[:, None]
    l = np.arange(64)[None, :]
    c["negmask"] = np.tile(np.where(l >= m, 0.0, NEG).astype(np.float32), (1, 8))
    sr = np.zeros((8, 8, 64), np.float32)
    for k in range(8):
        sr[k, k, :] = 1.0
    c["selrow64"] = sr.reshape(8, 512)
    return c


def ssd_setup(P, nc, C):
    S = GS()
    def sb(n, shp):
        return P.sb("s_" + n, shp)
    S.cw = sb("cw", [128, 24]); S.cb = sb("cb", [128, 6]); S.dcol = sb("dcol", [128, 4]); S.nw = sb("nw", [128, 4])
    S.dtb = sb("dtb", [8, 1]); S.alog = sb("alog", [8, 1]); S.acol = sb("acol", [8, 1])
    S.selpair = sb("selpair", [8, 512]); S.selones = sb("selones", [8, 1024]); S.sel8 = sb("sel8", [8, 8]); S.ones8 = sb("ones8", [8, 64])
    S.rmask8 = sb("rmask8", [8, 512]); S.negmask = sb("negmask", [64, 512])
    S.selrow64 = sb("selrow64", [8, 512])
    S.R_hi = P.sb("s_R_hi", [8, 8, ST], BF16); S.R_lo = P.sb("s_R_lo", [8, 8, ST], BF16)
    S.ac_hi = P.sb("s_ac_hi", [8, ST], BF16); S.ac_lo = P.sb("s_ac_lo", [8, ST], BF16); S.nac_hi = P.sb("s_nac_hi", [8, ST], BF16); S.nac_lo = P.sb("s_nac_lo", [8, ST], BF16)
    S.selrow_bf = P.sb("s_selrow_bf", [8, 512], BF16); S.ones8_bf = P.sb("s_ones8_bf", [8, 64], BF16)
    S.selpair_bf = P.sb("s_selpair_bf", [8, 512], BF16); S.selones_bf = P.sb("s_selones_bf", [8, 1024], BF16)
    S.dt_bf = P.sb("s_dt_bf", [8, ST], BF16); S.dtw_bf = P.sb("s_dtw_bf", [8, ST], BF16); S.eac_bf = P.sb("s_eac_bf", [8, ST], BF16)
    S.tmpS = sb("tmpS", [128, ST])
    S.raw = [sb("raw%d" % i, [128, ST + 3]) for i in range(6)]
    S.z = [sb("z%d" % i, [128, ST]) for i in range(4)]
    S.xc = [sb("xc%d" % i, [128, ST]) for i in range(6)]
    S.dt = sb("dt", [8, ST]); S.acum = sb("acum", [8, ST]); S.nacum = sb("nacum", [8, ST]); S.eac = sb("eac", [8, ST])
    S.dtw = sb("dtw", [8, ST])
    S.xdtT = [sb("xdtT%d" % j, [128, ST]) for j in range(4)]
    S.xdtwT = [sb("xdtwT%d" % j, [128, ST]) for j in range(4)]
    S.Cdec = P.sb("s_Cdec", [128, 8, ST], BF16); S.glt = sb("glt", [128, 8, 8]); S.eacl = sb("eacl", [8, 8])
    S.B_tp = P.sb("s_B_tp", [64, 1024], BF16); S.xdt_all = P.sb("s_xdt_all", [64, 8, ST], BF16); S.xdtw_tp = P.sb("s_xdtw_tp", [64, 2, ST], BF16)
    S.ST_bf = P.sb("s_ST_bf", [128, 8, ST], BF16); S.Bc_bf = P.sb("s_Bc_bf", [128, ST], BF16); S.Cc_bf = P.sb("s_Cc_bf", [128, ST], BF16)
    S.E = sb("E", [64, 2, ST]); S.cbd_all = P.sb("s_cbd_all", [64, 8, ST], BF16); S.y_sb = sb("y_sb", [64, 2, ST]); S.yT = sb("yT", [128, 4, ST])
    S.ST_all = sb("ST_all", [128, 9, ST])
    S.zs = sb("zs", [128, ST]); S.rs = sb("rs", [128, ST]); S.res = sb("res", [128, ST])
    C.psX = P.ps("psX", [128, 512]); C.psY = P.ps("psY", [128, 512]); C.psYY = P.ps("psYY", [128, 512]); C.psS = P.ps("psS", [128, 512])
    C.psV = P.ps("psV", [128, 512])
    return S


def ssd_init(P, C, S, d, cd):
    for n in ("cw", "cb", "dcol", "nw", "dtb", "alog"):
        P.dma_op("sp", getattr(S, n)[:], d["s_" + n], writes=["s_" + n])
    for n in ("selpair", "selones", "sel8", "ones8", "rmask8", "negmask", "selrow64"):
        P.dma_op("sp", getattr(S, n)[:], cd[n], writes=["s_" + n])
    P.op("act", lambda e: e.activation(out=S.acol[:], in_=S.alog[:], func=AF.Exp), reads=["s_alog"], writes=["s_acol"])
    P.op("dve", lambda e: e.tensor_scalar(out=S.acol[:], in0=S.acol[:], scalar1=-1.0, scalar2=None, op0=ALU.mult), reads=["s_acol"], writes=["s_acol"])
    P.op("pool", lambda e: e.memset(S.ST_all[:, 0, :], 0.0), writes=["s_ST0"])
    for src_, dst_, k_ in ((S.selrow64, S.selrow_bf, "selrow64"), (S.ones8, S.ones8_bf, "ones8"), (S.selpair, S.selpair_bf, "selpair"), (S.selones, S.selones_bf, "selones")):
        P.op("dve", lambda e, src_=src_, dst_=dst_: e.tensor_copy(out=dst_[:], in_=src_[:]), reads=["s_" + k_], writes=["s_" + k_ + "_bf"])
    for i in range(6):
        P.op("pool", lambda e, i=i: e.memset(S.raw[i][:, 0:3], 0.0), writes=["s_raw%d" % i])


def ssd_supertile(P, C, S, s, pT, dtT, out, T):
    t0 = s * ST
    tsl = slice(t0, t0 + ST)
    psB = C.psB
    A0 = C.psA[:, 0:512]; A1 = C.psA[:, 512:1024]
    bk = {"psB": psB, "psA0": A0, "psA1": A1, "psX": C.psX, "psY": C.psY, "psYY": C.psYY, "psS": C.psS, "psV": C.psV}
    for i in range(6):
        r0 = 512 + 128 * i
        if s == 0:
            P.dma_op("sp", S.raw[i][:, 3:ST + 3], pT[r0:r0 + 128, 0:ST], writes=["s_raw%d" % i])
        else:
            P.dma_op("sp", S.raw[i][:, :], pT[r0:r0 + 128, t0 - 3:t0 + ST], writes=["s_raw%d" % i])
    for j in range(4):
        P.dma_op("sp", S.z[j][:], pT[128 * j:128 * j + 128, tsl], writes=["s_z%d" % j])
    P.dma_op("sp", S.dt[:], dtT[:, tsl], writes=["s_dt"])
    for i in range(6):
        x = S.xc[i]; raw = S.raw[i]; xk = "s_xc%d" % i; rk = "s_raw%d" % i
        ce = "dve"
        if ce == "dve":
            P.op("dve", lambda e, x=x, raw=raw, i=i: e.tensor_scalar(out=x[:], in0=raw[:, 0:ST], scalar1=S.cw[:, 4 * i:4 * i + 1], scalar2=None, op0=ALU.mult),
                 reads=[rk, "s_cw"], writes=[xk])
            for j in range(1, 4):
                P.op("dve", lambda e, x=x, raw=raw, i=i, j=j: e.scalar_tensor_tensor(out=x[:], in0=raw[:, j:j + ST], scalar=S.cw[:, 4 * i + j:4 * i + j + 1],
                                                                                   in1=x[:], op0=ALU.mult, op1=ALU.add),
                     reads=[rk, "s_cw", xk], writes=[xk])
        else:
            P.op("pool", lambda e, x=x, raw=raw, i=i: e.tensor_scalar(out=x[:], in0=raw[:, 0:ST], scalar1=S.cw[:, 4 * i:4 * i + 1], scalar2=None, op0=ALU.mult),
                 reads=[rk, "s_cw"], writes=[xk])
            for j in range(1, 4):
                P.op("pool", lambda e, raw=raw, i=i, j=j: e.tensor_scalar(out=S.zs[:], in0=raw[:, j:j + ST], scalar1=S.cw[:, 4 * i + j:4 * i + j + 1], scalar2=None, op0=ALU.mult),
                     reads=[rk, "s_cw"], writes=["s_zs"])
                P.op("pool", lambda e, x=x: e.tensor_tensor(out=x[:], in0=x[:], in1=S.zs[:], op=ALU.add), reads=[xk, "s_zs"], writes=[xk])
        P.op("act", lambda e, x=x, i=i: e.activation(out=x[:], in_=x[:], func=AF.Silu, bias=S.cb[:, i:i + 1], scale=1.0), reads=[xk, "s_cb"], writes=[xk])
    xs = S.xc[0:4]; Bc = S.xc[4]; Cc = S.xc[5]
    P.op("act", lambda e: e.activation(out=S.Bc_bf[:], in_=Bc[:], func=AF.Copy), reads=["s_xc4"], writes=["s_Bc_bf"])
    P.op("act", lambda e: e.activation(out=S.Cc_bf[:], in_=Cc[:], func=AF.Copy), reads=["s_xc5"], writes=["s_Cc_bf"])
    P.op("act", lambda e: e.activation(out=S.dt[:], in_=S.dt[:], func=AF.Exp, bias=S.dtb[:, 0:1], scale=1.0), reads=["s_dt", "s_dtb"], writes=["s_dt"])
    P.op("act", lambda e: e.activation(out=S.dt[:], in_=S.dt[:], func=AF.Ln, bias=C.cst["one"][0:8, 0:1], scale=1.0), reads=["s_dt", "cst"], writes=["s_dt"])
    P.op("dve", lambda e: e.tensor_scalar(out=S.nacum[:], in0=S.dt[:], scalar1=S.acol[:, 0:1], scalar2=None, op0=ALU.mult), reads=["s_dt", "s_acol"], writes=["s_nacum"])
    P.op("dve", lambda e: e.tensor_tensor_scan(out=S.acum[:], data0=S.rmask8[:], data1=S.nacum[:], initial=0.0, op0=ALU.mult, op1=ALU.add),
         reads=["s_rmask8", "s_nacum"], writes=["s_acum"])
    P.op("dve", lambda e: e.tensor_scalar(out=S.nacum[:], in0=S.acum[:], scalar1=-1.0, scalar2=None, op0=ALU.mult), reads=["s_acum"], writes=["s_nacum"])
    P.op("act", lambda e: e.activation(out=S.eac[:], in_=S.acum[:], func=AF.Exp), reads=["s_acum"], writes=["s_eac"])
    for c in range(8):
        P.op("act", lambda e, c=c: e.activation(out=S.dtw[:, 64 * c:64 * c + 64], in_=S.acum[:, 64 * c:64 * c + 64], func=AF.Exp,
                                               bias=S.acum[:, 64 * c + 63:64 * c + 64], scale=-1.0), reads=["s_acum"], writes=["s_dtw"])
    P.op("dve", lambda e: e.tensor_tensor(out=S.dtw[:], in0=S.dt[:], in1=S.dtw[:], op=ALU.mult), reads=["s_dt", "s_dtw"], writes=["s_dtw"])
    P.op("dve", lambda e: e.tensor_copy(out=S.ac_hi[:], in_=S.acum[:]), reads=["s_acum"], writes=["s_ac_hi"])
    P.op("dve", lambda e: e.tensor_tensor(out=S.ac_lo[:], in0=S.acum[:], in1=S.ac_hi[:], op=ALU.subtract), reads=["s_acum", "s_ac_hi"], writes=["s_ac_lo"])
    P.op("dve", lambda e: e.tensor_scalar(out=S.nac_hi[:], in0=S.ac_hi[:], scalar1=-1.0, scalar2=None, op0=ALU.mult), reads=["s_ac_hi"], writes=["s_nac_hi"])
    P.op("dve", lambda e: e.tensor_scalar(out=S.nac_lo[:], in0=S.ac_lo[:], scalar1=-1.0, scalar2=None, op0=ALU.mult), reads=["s_ac_lo"], writes=["s_nac_lo"])
    selb = S.sel8[:, :].unsqueeze(2).to_broadcast([8, 8, ST])
    P.op("dve", lambda e: e.tensor_tensor(out=S.R_hi[:], in0=S.ac_hi[:, :].unsqueeze(1).to_broadcast([8, 8, ST]), in1=selb, op=ALU.mult),
         reads=["s_ac_hi", "s_sel8"], writes=["s_R_hi"])
    P.op("dve", lambda e: e.tensor_tensor(out=S.R_lo[:], in0=S.ac_lo[:, :].unsqueeze(1).to_broadcast([8, 8, ST]), in1=selb, op=ALU.mult),
         reads=["s_ac_lo", "s_sel8"], writes=["s_R_lo"])
    P.op("act", lambda e: e.activation(out=S.dt_bf[:], in_=S.dt[:], func=AF.Copy), reads=["s_dt"], writes=["s_dt_bf"])
    P.op("act", lambda e: e.activation(out=S.dtw_bf[:], in_=S.dtw[:], func=AF.Copy), reads=["s_dtw"], writes=["s_dtw_bf"])
    P.op("act", lambda e: e.activation(out=S.eac_bf[:], in_=S.eac[:], func=AF.Copy), reads=["s_eac"], writes=["s_eac_bf"])
    P.op("dve", lambda e: e.tensor_copy(out=S.eacl[:], in_=S.eac[:].rearrange("p (c l) -> p c l", l=64)[:, :, 63]), reads=["s_eac"], writes=["s_eacl"])
    rot = ["psB", "psV", "psYY", "psS"]
    ri = 0
    for j in range(4):
        for src_, dst_, dkey in ((S.dt_bf, S.xdtT[j], "s_xdtT%d" % j), (S.dtw_bf, S.xdtwT[j], "s_xdtwT%d" % j)):
            bn = rot[ri % 4]; ri += 1
            P.op("pe", lambda e, j=j, src_=src_, bn=bn: e.matmul(bk[bn][:, :], lhsT=S.selpair_bf[:, 128 * j:128 * j + 128], rhs=src_[:], start=True, stop=True),
                 reads=["s_selpair_bf", "s_dt_bf", "s_dtw_bf"], writes=[bn])
            P.op("dve", lambda e, j=j, dst_=dst_, bn=bn: e.tensor_tensor(out=dst_[:], in0=xs[j][:], in1=bk[bn][:, :], op=ALU.mult), reads=["s_xc%d" % j, bn], writes=[dkey])
    for h in range(8):
        bn = rot[ri % 4]; ri += 1
        P.op("pe", lambda e, h=h, bn=bn: e.matmul(bk[bn][:, :], lhsT=S.selones_bf[:, 128 * h:128 * h + 128], rhs=S.eac_bf[:], start=True, stop=True), reads=["s_selones_bf", "s_eac_bf"], writes=[bn])
        eng = "dve"
        if eng == "dve":
            P.op("dve", lambda e, h=h, bn=bn: e.tensor_tensor(out=S.Cdec[:, h, :], in0=Cc[:], in1=bk[bn][:, :], op=ALU.mult), reads=["s_xc5", bn], writes=["s_Cdec%d" % h])
        else:
            P.op("act", lambda e, bn=bn: e.activation(out=S.res[:], in_=bk[bn][:, :], func=AF.Copy), reads=[bn], writes=["s_res"])
            P.op("pool", lambda e, h=h: e.tensor_tensor(out=S.Cdec[:, h, :], in0=Cc[:], in1=S.res[:], op=ALU.mult), reads=["s_xc5", "s_res"], writes=["s_Cdec%d" % h])

    def glm(e):
        ins = None
        for h in range(8):
            ins = e.matmul(C.psX[:, 8 * h:8 * h + 8], lhsT=S.selones[:, 128 * h:128 * h + 128], rhs=S.eacl[:], start=True, stop=True)
        return ins
    P.op("pe", glm, reads=["s_selones", "s_eacl"], writes=["psX"])
    P.op("act", lambda e: e.activation(out=S.glt[:].rearrange("p h c -> p (h c)"), in_=C.psX[:, 0:64], func=AF.Copy), reads=["psX"], writes=["s_glt"])
    for hf in range(2):
        bn = "psA%d" % hf

        def btr(e, hf=hf, bn=bn):
            ins = None
            for c in range(4):
                cc = 4 * hf + c
                ins = e.transpose(bk[bn][0:64, 128 * c:128 * c + 128], Bc[:, 64 * cc:64 * cc + 64], C.ident[:])
            return ins
        P.op("pe", btr, reads=["s_xc4", "ident"], writes=[bn])
        P.op("dve", lambda e, hf=hf, bn=bn: e.tensor_copy(out=S.B_tp[:, 512 * hf:512 * hf + 512], in_=bk[bn][0:64, :]), reads=[bn], writes=["s_B_tp"])
    for c in range(8):
        cs = slice(64 * c, 64 * c + 64)
        p = c % 2
        b1 = "psA%d" % p
        b2 = ["psX", "psY"][p]

        def xtr(e, cs=cs, b1=b1, b2=b2):
            ins = None
            for j in range(4):
                e.transpose(bk[b1][0:64, 128 * j:128 * j + 128], S.xdtT[j][:, cs], C.ident[:])
            for j in range(4):
                ins = e.transpose(bk[b2][0:64, 128 * j:128 * j + 128], S.xdtwT[j][:, cs], C.ident[:])
            return ins
        P.op("pe", xtr, reads=["s_xdtT%d" % j for j in range(4)] + ["s_xdtwT%d" % j for j in range(4)] + ["ident"], writes=[b1, b2])
        P.op("act", lambda e, c=c, b1=b1: e.activation(out=S.xdt_all[:, c, :], in_=bk[b1][0:64, :], func=AF.Copy), reads=[b1], writes=["s_xdt_all%d" % c])
        P.op("dve", lambda e, p=p, b2=b2: e.tensor_copy(out=S.xdtw_tp[:, p, :], in_=bk[b2][0:64, :]), reads=[b2], writes=["s_xdtw_tp%d" % p])
        b3 = ["psYY", "psS"][p]
        P.op("pe", lambda e, c=c, p=p, b3=b3: e.matmul(bk[b3][:, :], lhsT=S.B_tp[:, 128 * c:128 * c + 128], rhs=S.xdtw_tp[:, p, :], start=True, stop=True),
             reads=["s_B_tp", "s_xdtw_tp%d" % p], writes=[b3])
        P.op("dve", lambda e, c=c: e.tensor_tensor(out=S.tmpS[:].rearrange("s (h p) -> s h p", h=8), in0=S.ST_all[:, c, :].rearrange("s (h p) -> s h p", h=8),
                                                   in1=S.glt[:, :, c:c + 1].to_broadcast([128, 8, 64]), op=ALU.mult),
             reads=["s_ST%d" % c, "s_glt"], writes=["s_tmpS"])
        P.op("dve", lambda e, c=c, b3=b3: e.tensor_tensor(out=S.ST_all[:, c + 1, :], in0=S.tmpS[:], in1=bk[b3][:, :], op=ALU.add),
             reads=["s_tmpS", b3], writes=["s_ST%d" % (c + 1)])
    for c in range(8):
        cs = slice(64 * c, 64 * c + 64)
        p = c % 2
        bx = ["psV", "psB"][p]
        by = "psA%d" % p

        def dm(e, cs=cs, bx=bx, by=by):
            e.matmul(bk[bx][0:64, :], lhsT=S.ones8_bf[:, :], rhs=S.R_hi[:, :, cs], start=True, stop=False)
            e.matmul(bk[bx][0:64, :], lhsT=S.ones8_bf[:, :], rhs=S.R_lo[:, :, cs], start=False, stop=False)
            e.matmul(bk[bx][0:64, :], lhsT=S.nac_hi[:, cs], rhs=S.selrow_bf[:, :], start=False, stop=False)
            e.matmul(bk[bx][0:64, :], lhsT=S.nac_lo[:, cs], rhs=S.selrow_bf[:, :], start=False, stop=True)
            return e.matmul(bk[by][0:64, 0:64], lhsT=S.Bc_bf[:, cs], rhs=S.Cc_bf[:, cs], start=True, stop=True)
        P.op("pe", dm, reads=["s_ones8_bf", "s_R_hi", "s_R_lo", "s_nac_hi", "s_nac_lo", "s_selrow64_bf", "s_Bc_bf", "s_Cc_bf"], writes=[bx, by])
        P.op("dve", lambda e, p=p, bx=bx: e.scalar_tensor_tensor(out=S.E[:, p, :], in0=bk[bx][0:64, :], scalar=0.0, in1=S.negmask[:], op0=ALU.min, op1=ALU.add),
             reads=[bx, "s_negmask"], writes=["s_E%d" % p])
        P.op("act", lambda e, p=p: e.activation(out=S.E[:, p, :], in_=S.E[:, p, :], func=AF.Exp), reads=["s_E%d" % p], writes=["s_E%d" % p])
        P.op("dve", lambda e, c=c, p=p, by=by: e.tensor_tensor(out=S.cbd_all[:, c, :].rearrange("m (h l) -> m h l", h=8),
                                                               in0=S.E[:, p, :].rearrange("m (h l) -> m h l", h=8),
                                                               in1=bk[by][0:64, 0:64].unsqueeze(1).to_broadcast([64, 8, 64]), op=ALU.mult),
             reads=[by, "s_E%d" % p], writes=["s_cbd%d" % c])
    for c in range(8):
        cs = slice(64 * c, 64 * c + 64)
        p = c % 2
        b1 = ["psX", "psY"][p]
        b2 = ["psYY", "psS"][p]

        def ym(e, cs=cs, c=c, b1=b1):
            ins = None
            for h in range(8):
                hs = slice(64 * h, 64 * h + 64)
                e.matmul(bk[b1][0:64, hs], lhsT=S.Cdec[:, h, cs], rhs=S.ST_bf[:, c, hs], start=True, stop=False)
                ins = e.matmul(bk[b1][0:64, hs], lhsT=S.cbd_all[:, c, hs], rhs=S.xdt_all[:, c, hs], start=False, stop=True)
            return ins
        P.op("act", lambda e, c=c: e.activation(out=S.ST_bf[:, c, :], in_=S.ST_all[:, c, :], func=AF.Copy), reads=["s_ST%d" % c], writes=["s_STbf%d" % c])
        P.op("pe", ym, reads=["s_Cdec%d" % h for h in range(8)] + ["s_STbf%d" % c, "s_cbd%d" % c, "s_xdt_all%d" % c], writes=[b1])
        P.op("act", lambda e, p=p, b1=b1: e.activation(out=S.y_sb[:, p, :], in_=bk[b1][0:64, :], func=AF.Copy), reads=[b1], writes=["s_y_sb%d" % p])

        def ytr(e, p=p, b2=b2):
            ins = None
            for j in range(4):
                ins = e.transpose(bk[b2][:, 64 * j:64 * j + 64], S.y_sb[:, p, 128 * j:128 * j + 128], C.ident[0:64, 0:64])
            return ins
        P.op("pe", ytr, reads=["s_y_sb%d" % p, "ident"], writes=[b2])
        eng = "act" if c % 2 == 0 else "dve"
        if eng == "act":
            P.op("act", lambda e, cs=cs, b2=b2: e.activation(out=S.yT[:, :, cs], in_=bk[b2][:, 0:256].rearrange("p (j l) -> p j l", j=4), func=AF.Copy),
                 reads=[b2], writes=["s_yT"])
        else:
            P.op("dve", lambda e, cs=cs, b2=b2: e.tensor_copy(out=S.yT[:, :, cs], in_=bk[b2][:, 0:256].rearrange("p (j l) -> p j l", j=4)),
                 reads=[b2], writes=["s_yT"])
    P.op("act", lambda e: e.activation(out=S.ST_all[:, 0, :], in_=S.ST_all[:, 8, :], func=AF.Copy), reads=["s_ST8"], writes=["s_ST0"])
    for j in range(4):
        gt = S.xdtT[j]; gk = "s_xdtT%d" % j
        sq = S.xdtwT[j]; sk = "s_xdtwT%d" % j
        P.op("dve", lambda e, j=j, gt=gt: e.scalar_tensor_tensor(out=gt[:], in0=xs[j][:], scalar=S.dcol[:, j:j + 1], in1=S.yT[:, j, :], op0=ALU.mult, op1=ALU.add),
             reads=["s_xc%d" % j, "s_dcol", "s_yT"], writes=[gk])
        P.op("act", lambda e, j=j: e.activation(out=S.zs[:], in_=S.z[j][:], func=AF.Silu), reads=["s_z%d" % j], writes=["s_zs"])
        P.op("dve", lambda e, gt=gt: e.tensor_tensor(out=gt[:], in0=gt[:], in1=S.zs[:], op=ALU.mult), reads=[gk, "s_zs"], writes=[gk])
        P.op("act", lambda e, gt=gt, sq=sq: e.activation(out=sq[:], in_=gt[:], func=AF.Square), reads=[gk], writes=[sk])

    def nm(e):
        ins = None
        for j in range(4):
            ins = e.matmul(psB[:, :], lhsT=C.ones_f[:], rhs=S.xdtwT[j][:], start=(j == 0), stop=(j == 3))
        return ins
    P.op("pe", nm, reads=["ones_f"] + ["s_xdtwT%d" % j for j in range(4)], writes=["psB"])
    P.op("act", lambda e: e.activation(out=S.rs[:], in_=psB[:, :], func=AF.Sqrt, bias=C.cst["eps"][:, 0:1], scale=1.0 / 512), reads=["psB", "cst"], writes=["s_rs"])
    P.op("dve", lambda e: e.reciprocal(out=S.rs[:], in_=S.rs[:]), reads=["s_rs"], writes=["s_rs"])
    for j in range(4):
        P.op("dve", lambda e, j=j: e.scalar_tensor_tensor(out=S.res[:], in0=S.xdtT[j][:], scalar=S.nw[:, j:j + 1], in1=S.rs[:], op0=ALU.mult, op1=ALU.mult),
             reads=["s_xdtT%d" % j, "s_nw", "s_rs"], writes=["s_res"])
        P.dma_op("sp", out[128 * j:128 * j + 128, tsl], S.res[:], reads=["s_res"], is_output=True)


def build_B1(T):
    nc = new_nc()
    pT = dram_in(nc, "pT", [1280, T])
    dtT = dram_in(nc, "dtT", [8, T])
    d = {"s_cw": dram_in(nc, "s_cw", [128, 24]), "s_cb": dram_in(nc, "s_cb", [128, 6]), "s_dcol": dram_in(nc, "s_dcol", [128, 4]),
         "s_nw": dram_in(nc, "s_nw", [128, 4]), "s_dtb": dram_in(nc, "s_dtb", [8, 1]), "s_alog": dram_in(nc, "s_alog", [8, 1])}
    cd = {"ident": dram_in(nc, "ident", [128, 128]), "selpair": dram_in(nc, "selpair", [8, 512]), "selones": dram_in(nc, "selones", [8, 1024]),
          "sel8": dram_in(nc, "sel8", [8, 8]), "ones8": dram_in(nc, "ones8", [8, 64]), "rmask8": dram_in(nc, "rmask8", [8, 512]),
          "negmask": dram_in(nc, "negmask", [64, 512]), "selrow64": dram_in(nc, "selrow64", [8, 512])}
    out = dram_out(nc, "yT", [512, T])
    with ExitStack() as st:
        P = Prog(nc, st)
        C = Ctx()
        mixer_common(P, nc, C, cd)
        S = ssd_setup(P, nc, C)
        ssd_init(P, C, S, d, cd)
        for s in range(T // ST):
            ssd_supertile(P, C, S, s, pT, dtT, out, T)
        P.finish()
        print("B1 ops", P.n_ops())
    return nc

NCORES = 8
SEQ = 16384
TOKC = SEQ // NCORES

OFF = {"gq": 0, "gk": 1024, "gv": 2048, "gz": 3072, "gb": 4096, "ga": 4104, "aq": 4112, "ak": 5136, "av": 6160}


def mod_layout(modrow):
    return np.ascontiguousarray(modrow.reshape(6, 16, 128).transpose(2, 0, 1).reshape(128, 96))


def col_layout(v):
    return np.ascontiguousarray(v.reshape(16, 128).T)


def build_mod():
    nc = new_nc()
    c_in = dram_in(nc, "c_in", [128, 16])
    w_in = dram_in(nc, "w_in", [2, 2048, 1536])
    b_in = dram_in(nc, "b_in", [128, 24])
    out = dram_out(nc, "out", [128, 24])
    with ExitStack() as st:
        P = Prog(nc, st)
        ct = P.sb("ct", [128, 16]); ca = P.sb("ca", [128, 16])
        bt = P.sb("bt", [128, 24]); ot = P.sb("ot", [128, 24])
        wt = P.sb("wt", [128, 16, 1536])
        acc = P.ps("acc", [128, 512])
        P.dma_op("sp", ct[:], c_in, writes=["ct"])
        P.dma_op("sp", bt[:], b_in, writes=["bt"])
        P.op("act", lambda e: e.activation(out=ca[:], in_=ct[:], func=AF.Silu), reads=["ct"], writes=["ca"])
        for l in range(2):
            P.dma_op("sp", wt[:], w_in[l].rearrange("(kt p) n -> p kt n", p=128), writes=["wt"])

            def mm(e, l=l):
                ins = None
                for j in range(12):
                    for kt in range(16):
                        ins = e.matmul(acc[:, l * 12 + j:l * 12 + j + 1], lhsT=wt[:, kt, j * 128:(j + 1) * 128], rhs=ca[:, kt:kt + 1],
                                       start=(kt == 0), stop=(kt == 15))
                return ins
            P.op("pe", mm, reads=["wt", "ca"], writes=["acc"])
        P.op("dve", lambda e: e.tensor_tensor(out=ot[:], in0=acc[:, 0:24], in1=bt[:], op=ALU.add), reads=["acc", "bt"], writes=["ot"])
        P.dma_op("sp", out, ot[:], reads=["ot"], is_output=True)
        P.finish()
    return nc


def run(nc, in_maps):
    res = run_bass_kernel_spmd(nc, in_maps, core_ids=list(range(NCORES)))
    return res.results


def kernel(x, c, mod_w, mod_b, norm_mix_w, norm_mlp_w, mlp_w1, mlp_w2,
           ab_w_in, gdn_conv_w, gdn_a_log, gdn_dt_bias, gdn_norm_w,
           attn_q_norm_w, attn_k_norm_w, attn_rel_bias, ab_w_out,
           ssd_w_in, ssd_conv_w, ssd_conv_b, ssd_dt_bias, ssd_a_log, ssd_d,
           ssd_norm_w, ssd_w_out):
    f = lambda a: np.ascontiguousarray(np.asarray(a, dtype=np.float32))
    x = f(x); c = f(c); mod_w = f(mod_w); mod_b = f(mod_b)
    cl = np.ascontiguousarray(c.reshape(16, 128).T)
    ims = []
    for core in range(NCORES):
        sl = slice(core * 1536, (core + 1) * 1536)
        b = np.stack([mod_b[l, sl].reshape(12, 128).T for l in range(2)], axis=1).reshape(128, 24)
        ims.append({"c_in": cl, "w_in": np.ascontiguousarray(mod_w[:, :, sl]), "b_in": np.ascontiguousarray(b)})
    r = run(build_mod(), ims)
    mod = np.zeros((2, 12288), np.float32)
    for core in range(NCORES):
        o = r[core]["out"].reshape(128, 2, 12)
        for l in range(2):
            mod[l, core * 1536:(core + 1) * 1536] = o[:, l, :].T.reshape(-1)
    modl = [mod_layout(mod[l]) for l in range(2)]
    xT = [np.ascontiguousarray(x[0, cc * TOKC:(cc + 1) * TOKC].T) for cc in range(NCORES)]

    perm0 = np.concatenate([np.concatenate([OFF[n] + 128 * i + np.arange(128) for n in ("gq", "gk", "gv", "gz", "aq", "ak", "av")]) for i in range(8)]
                           + [np.arange(4096, 4112)])
    w0 = np.ascontiguousarray(f(ab_w_in)[0][:, perm0])
    nw0 = col_layout(f(norm_mix_w)[0])
    r = run(build_stageA(7184, TOKC), [{"xT": xT[cc], "w": w0, "nw": nw0, "modl": modl[0]} for cc in range(NCORES)])
    projT = [r[cc]["projT"] for cc in range(NCORES)]
    hc = host_consts()
    cw = f(gdn_conv_w)[0]
    ims = []
    for i in range(8):
        pT = np.ascontiguousarray(np.concatenate([projT[cc][896 * i:896 * (i + 1)] for cc in range(NCORES)], axis=1))
        gates = np.ascontiguousarray(np.concatenate([projT[cc][[7168 + i, 7176 + i]] for cc in range(NCORES)], axis=1))
        cwh = np.concatenate([cw[:, 1024 * j + 128 * i: 1024 * j + 128 * (i + 1)].T for j in range(3)], axis=1)
        im = {"pT": pT, "gates": gates,
              "qw": f(attn_q_norm_w)[0].reshape(128, 1).copy(), "kw": f(attn_k_norm_w)[0].reshape(128, 1).copy(),
              "bias": rel_bias_toeplitz(f(attn_rel_bias)[0, i]),
              "g_cw": np.ascontiguousarray(cwh), "g_alog": f(gdn_a_log)[0, i].reshape(1, 1).copy(), "g_dtb": f(gdn_dt_bias)[0, i].reshape(1, 1).copy(),
              "g_nw": f(gdn_norm_w)[0].reshape(128, 1).copy()}
        im.update(hc)
        ims.append(im)
    del projT
    r = run(build_B0(SEQ), ims)
    oTh = [r[i]["oT"] for i in range(8)]
    ims = []
    wo0 = f(ab_w_out)[0]; w1 = f(mlp_w1); w2 = f(mlp_w2)
    for cc in range(NCORES):
        ts = slice(cc * TOKC, (cc + 1) * TOKC)
        oT = np.ascontiguousarray(np.concatenate([oTh[i][0:128, ts] for i in range(8)] + [oTh[i][128:256, ts] for i in range(8)], axis=0))
        ims.append({"xT": xT[cc], "oT": oT, "wo": wo0, "w1": w1[0], "w2": w2[0], "nw": col_layout(f(norm_mlp_w)[0]), "modl": modl[0]})
    r = run(build_stageC(2048, TOKC), ims)
    xT = [r[cc]["x2T"] for cc in range(NCORES)]

    perm1 = np.concatenate([np.concatenate([512 * g + np.arange(512), 4096 + 512 * g + np.arange(512), 8192 + 128 * g + np.arange(128),
                                            9216 + 128 * g + np.arange(128)]) for g in range(8)] + [10240 + np.arange(64)])
    ws = np.ascontiguousarray(f(ssd_w_in)[0][:, perm1])
    nw1 = col_layout(f(norm_mix_w)[1])
    r = run(build_stageA(10304, TOKC), [{"xT": xT[cc], "w": ws, "nw": nw1, "modl": modl[1]} for cc in range(NCORES)])
    projT = [r[cc]["projT"] for cc in range(NCORES)]
    shc = ssd_host_consts()
    scw = f(ssd_conv_w)[0]; scb = f(ssd_conv_b)[0]
    ims = []
    for g in range(8):
        pT = np.ascontiguousarray(np.concatenate([projT[cc][1280 * g:1280 * (g + 1)] for cc in range(NCORES)], axis=1))
        dtT = np.ascontiguousarray(np.concatenate([projT[cc][10240 + 8 * g:10240 + 8 * (g + 1)] for cc in range(NCORES)], axis=1))
        chans = [np.arange(512 * g + 128 * j, 512 * g + 128 * (j + 1)) for j in range(4)] + [np.arange(4096 + 128 * g, 4096 + 128 * (g + 1)),
                                                                                           np.arange(5120 + 128 * g, 5120 + 128 * (g + 1))]
        cwl = np.concatenate([scw[:, ch].T for ch in chans], axis=1)
        cbl = np.stack([scb[ch] for ch in chans], axis=1)
        dsk = f(ssd_d)[0][8 * g:8 * (g + 1)]
        dcol = np.stack([np.repeat(dsk[2 * j:2 * j + 2], 64) for j in range(4)], axis=1)
        nwg = f(ssd_norm_w)[0][512 * g:512 * (g + 1)].reshape(4, 128).T
        im = {"pT": pT, "dtT": dtT, "s_cw": np.ascontiguousarray(cwl), "s_cb": np.ascontiguousarray(cbl), "s_dcol": np.ascontiguousarray(dcol),
              "s_nw": np.ascontiguousarray(nwg), "s_dtb": f(ssd_dt_bias)[0][8 * g:8 * (g + 1)].reshape(8, 1).copy(),
              "s_alog": f(ssd_a_log)[0][8 * g:8 * (g + 1)].reshape(8, 1).copy()}
        im.update(shc)
        ims.append(im)
    del projT
    r = run(build_B1(SEQ), ims)
    yTg = [r[g]["yT"] for g in range(8)]
    ims = []
    wo1 = f(ssd_w_out)[0]
    for cc in range(NCORES):
        ts = slice(cc * TOKC, (cc + 1) * TOKC)
        oT = np.ascontiguousarray(np.concatenate([yTg[g][:, ts] for g in range(8)], axis=0))
        ims.append({"xT": xT[cc], "oT": oT, "wo": wo1, "w1": w1[1], "w2": w2[1], "nw": col_layout(f(norm_mlp_w)[1]), "modl": modl[1]})
    r = run(build_stageC(4096, TOKC), ims)
    out = np.concatenate([r[cc]["x2T"].T for cc in range(NCORES)], axis=0)[None]
    return np.ascontiguousarray(out.astype(np.float32))
```

```python
import numpy as np
from contextlib import ExitStack
import concourse.bass as bass
import concourse.mybir as mybir
from concourse.bass_utils import run_bass_kernel_spmd

F32 = mybir.dt.float32
BF16 = mybir.dt.bfloat16
AF = mybir.ActivationFunctionType
ALU = mybir.AluOpType
AX = mybir.AxisListType

ENGS = ["pe", "dve", "act", "pool", "sp"]
EPOCH = 30000
SAME_ENGINE_SYNC = True


class Prog:
    def __init__(self, nc, stack, n_dma_sems=12):
        self.nc = nc
        self.stack = stack
        self.ops = {e: [] for e in ENGS}
        self.known = {e: {} for e in ENGS}
        self.tiles = {}
        self.semid = 0
        self.cur = {}
        for e in ["pe", "dve", "act", "pool"]:
            self._new_sem(e)
        self.dma = {}
        for q in ["sp", "pool", "act"]:
            sems = []
            for j in range(n_dma_sems):
                s = stack.enter_context(nc.semaphore(f"dq_{q}_{j}"))
                self.semid += 1
                sems.append([s, self.semid, 0])
            self.dma[q] = [sems, 0]
        self.out_events = []

    def _new_sem(self, e):
        s = self.stack.enter_context(self.nc.semaphore(f"s_{e}_{self.semid}"))
        self.semid += 1
        self.cur[e] = [s, self.semid, 0]

    def sb(self, name, shape, dtype=F32):
        return self.stack.enter_context(self.nc.sbuf_tensor("sb_" + name, list(shape), dtype))

    def ps(self, name, shape, dtype=F32):
        return self.stack.enter_context(self.nc.psum_tensor("pp_" + name, list(shape), dtype))

    def _st(self, k):
        st = self.tiles.get(k)
        if st is None:
            st = {"w": None, "r": {}}
            self.tiles[k] = st
        return st

    def _collect(self, eng, reads, writes, same_sync):
        waits = {}

        def add(ev):
            if ev is None:
                return
            s, sid, val = ev
            if sid == self.cur.get(eng, [None, -1])[1] and not same_sync:
                return
            if self.known[eng].get(sid, 0) >= val:
                return
            if sid not in waits or waits[sid][2] < val:
                waits[sid] = ev

        for k in reads:
            add(self._st(k)["w"])
        for k in writes:
            st = self._st(k)
            add(st["w"])
            for ev in st["r"].values():
                add(ev)
        for sid, ev in waits.items():
            self.ops[eng].append(("wait", ev[0], ev[2]))
            self.known[eng][sid] = ev[2]

    def _record(self, ev, reads, writes):
        for k in reads:
            st = self._st(k)
            st["r"][ev[1]] = ev
        for k in writes:
            st = self._st(k)
            st["w"] = ev
            st["r"] = {}

    def op(self, eng, fn, reads=(), writes=(), same_sync=None):
        if same_sync is None:
            same_sync = SAME_ENGINE_SYNC and eng != "pe"
        self._collect(eng, reads, writes, same_sync)
        c = self.cur[eng]
        if c[2] >= EPOCH:
            self._new_sem(eng)
            c = self.cur[eng]
        c[2] += 1
        ev = (c[0], c[1], c[2])
        self.ops[eng].append(("op", fn, c[0], 1))
        self._record(ev, reads, writes)
        return ev

    def dma_op(self, q, out, in_, reads=(), writes=(), is_output=False, **kw):
        sems, idx = self.dma[q]
        slot = sems[idx % len(sems)]
        self.dma[q][1] = idx + 1
        s, sid, uses = slot
        if uses > 0 and self.known[q].get(sid, 0) < 16 * uses:
            self.ops[q].append(("wait", s, 16 * uses))
            self.known[q][sid] = 16 * uses
        self._collect(q, reads, writes, True)
        slot[2] = uses + 1
        ev = (s, sid, 16 * (uses + 1))

        def fn(e, out=out, in_=in_, kw=kw):
            return e.dma_start(out=out, in_=in_, **kw)

        self.ops[q].append(("op", fn, s, 16))
        self._record(ev, reads, writes)
        if is_output:
            self.out_events.append(ev)
        return ev

    def finish(self):
        best = {}
        for ev in self.out_events:
            if ev[1] not in best or best[ev[1]][2] < ev[2]:
                best[ev[1]] = ev
        for ev in best.values():
            self.ops["sp"].append(("wait", ev[0], ev[2]))
        nc = self.nc
        ops = self.ops

        def replay(name, e):
            for o in ops[name]:
                if o[0] == "wait":
                    e.wait_ge(o[1], o[2])
                else:
                    ins = o[1](e)
                    ins.then_inc(o[2], o[3])

        with nc.Block() as block:
            @block.tensor
            def _(e):
                replay("pe", e)

            @block.vector
            def _(e):
                replay("dve", e)

            @block.scalar
            def _(e):
                replay("act", e)

            @block.gpsimd
            def _(e):
                replay("pool", e)

            @block.sync
            def _(e):
                replay("sp", e)

    def n_ops(self):
        return {e: len(v) for e, v in self.ops.items()}


D = 2048
KT = 16
TT = 512
EPS = 1e-6


def new_nc():
    return bass.Bass("TRN2", target_bir_lowering=False)


def dram_in(nc, name, shape, dtype=F32):
    return nc.dram_tensor(name, list(shape), dtype, kind="ExternalInput").ap()


def dram_out(nc, name, shape, dtype=F32):
    return nc.dram_tensor(name, list(shape), dtype, kind="ExternalOutput").ap()


class Ctx:
    pass


def setup_common(P):
    C = Ctx()
    C.ones_bf = P.sb("ones_bf", [128, 128], BF16)
    P.op("pool", lambda e: e.memset(C.ones_bf[:], 1.0), writes=["ones_bf"])
    C.ps = [P.ps(f"ps{i}", [128, 512]) for i in range(8)]
    C.rot = 0
    C.ev = 0
    return C


def emit_mod_cols(P, modt, nwt, which):
    acol = P.sb("acol%d" % which, [128, 16])
    s_sh = 0 if which == 0 else 3
    P.op("dve", lambda e: e.scalar_tensor_tensor(out=acol[:], in0=modt[:, (s_sh + 1) * 16:(s_sh + 2) * 16], scalar=1.0,
                                                 in1=nwt[:], op0=ALU.add, op1=ALU.mult),
         reads=["modt", "nwt%d" % which], writes=["acol"])
    return acol, modt[:, s_sh * 16:(s_sh + 1) * 16]


def make_scr(P):
    scr = {}
    scr["rstd"] = P.sb("rstd", [128, 512])
    scr["tmp"] = [P.sb("tmp0", [128, 512]), P.sb("tmp1", [128, 512])]
    scr["eps"] = P.sb("epst", [128, 1])
    P.op("pool", lambda e: e.memset(scr["eps"][:], EPS), writes=["eps"])
    return scr


def emit_dense(P, C, w_dram, K, ncols, rhs_fn, rhs_keys, ntt, evac_fn, wbufs, wq="pool", col0=0):
    nk = K // 128
    gcols = 8192 // nk
    ngroups = (ncols + gcols - 1) // gcols
    for g in range(ngroups):
        c0 = g * gcols
        csz = min(gcols, ncols - c0)
        wb = wbufs[C.rot_w % 2]
        wk = "wbuf%d" % (C.rot_w % 2)
        C.rot_w += 1
        wv = wb[:, 0:nk * csz].rearrange("p (k n) -> p k n", k=nk)
        src = w_dram[:, col0 + c0:col0 + c0 + csz].rearrange("(k p) n -> p k n", p=128)
        P.dma_op(wq, wv, src, writes=[wk])
        nm = (csz + 127) // 128
        for mi in range(nm):
            msz = min(128, csz - mi * 128)
            m = (c0 // 128) + mi
            for tt in range(ntt):
                bank = C.rot % 6
                C.rot += 1
                pst = C.ps[bank]

                def mm(e, wv=wv, mi=mi, msz=msz, tt=tt, pst=pst):
                    ins = None
                    for kt in range(nk):
                        ins = e.matmul(pst[0:msz, :], lhsT=wv[:, kt, mi * 128:mi * 128 + msz], rhs=rhs_fn(kt, tt),
                                       start=(kt == 0), stop=(kt == nk - 1))
                    return ins
                P.op("pe", mm, reads=[wk] + rhs_keys(tt), writes=["ps%d" % bank])
                evac_fn(m, msz, tt, pst, "ps%d" % bank)


def build_stageA(NC, TOK=2048):
    nc = new_nc()
    xT = dram_in(nc, "xT", [D, TOK])
    w = dram_in(nc, "w", [D, NC])
    nw = dram_in(nc, "nw", [128, 16])
    modl = dram_in(nc, "modl", [128, 96])
    out = dram_out(nc, "projT", [NC, TOK])
    ntt = TOK // TT
    with ExitStack() as st:
        P = Prog(nc, st)
        C = setup_common(P)
        C.rot_w = 0
        scr = make_scr(P)
        modt = P.sb("modt", [128, 96])
        nwt = P.sb("nwt", [128, 16])
        P.dma_op("sp", modt[:], modl, writes=["modt"])
        P.dma_op("sp", nwt[:], nw, writes=["nwt0"])
        acol, shcol = emit_mod_cols(P, modt, nwt, 0)
        P.tiles["shcol"] = P.tiles["modt"]
        xt = P.sb("xt", [128, 16, TT])
        sq = P.sb("sq", [128, 16, TT], BF16)
        hT = P.sb("hT", [128, 16, TOK], BF16)
        wbufs = [P.sb("wb0", [128, 8192], BF16), P.sb("wb1", [128, 8192], BF16)]
        ot = [P.sb("ot0", [128, TOK]), P.sb("ot1", [128, TOK])]
        xsrc = xT.rearrange("(k p) t -> p k t", p=128)
        for tt in range(ntt):
            P.dma_op("sp", xt[:], xsrc[:, :, tt * TT:(tt + 1) * TT], writes=["xt"])
            emit_adaln_xt(P, C, xt, hT[:, :, tt * TT:(tt + 1) * TT], "hT%d" % tt, acol, shcol, sq, scr)

        state = {"n": 0}

        def evac(m, msz, tt, pst, pk):
            b = (m % 2)
            o = ot[b]
            okey = "ot%d_%d" % (b, tt)
            eng = "dve" if C.ev % 2 == 0 else "act"
            C.ev += 1
            if eng == "dve":
                P.op("dve", lambda e: e.tensor_copy(out=o[0:msz, tt * TT:(tt + 1) * TT], in_=pst[0:msz, :]), reads=[pk], writes=[okey])
            else:
                P.op("act", lambda e: e.activation(out=o[0:msz, tt * TT:(tt + 1) * TT], in_=pst[0:msz, :], func=AF.Copy), reads=[pk], writes=[okey])
            if tt == ntt - 1:
                P.dma_op("sp", out[m * 128:m * 128 + msz, :], o[0:msz, :], reads=["ot%d_%d" % (b, t) for t in range(ntt)], is_output=True)

        emit_dense(P, C, w, D, NC, lambda kt, tt: hT[:, kt, tt * TT:(tt + 1) * TT], lambda tt: ["hT%d" % tt], ntt, evac, wbufs)
        P.finish()
        print("stageA ops", P.n_ops())
    return nc


def emit_adaln_xt(P, C, xt, ht, hk, acol, shcol, sq, scr):
    class _K:
        pass
    emit_adaln2(P, C, xt, ["xt"], ht, hk, acol, shcol, sq, "sq", scr)


def emit_adaln2(P, C, xt, xkeys, ht, hk, a_col, sh_col, sq, sqk, scr):
    nbank = C.ps[7]
    for q in range(4):
        if q % 2 == 0:
            P.op("act", lambda e, q=q: e.activation(out=sq[:, 4 * q:4 * q + 4, :], in_=xt[:, 4 * q:4 * q + 4, :], func=AF.Square),
                 reads=xkeys, writes=[sqk + str(q)])
        else:
            P.op("pool", lambda e, q=q: e.tensor_tensor(out=sq[:, 4 * q:4 * q + 4, :], in0=xt[:, 4 * q:4 * q + 4, :],
                                                      in1=xt[:, 4 * q:4 * q + 4, :], op=ALU.mult),
                 reads=xkeys, writes=[sqk + str(q)])

    def mm(e):
        ins = None
        for kt in range(KT):
            ins = e.matmul(nbank[:, :], lhsT=C.ones_bf[:], rhs=sq[:, kt, :], start=(kt == 0), stop=(kt == KT - 1))
        return ins
    P.op("pe", mm, reads=["ones_bf"] + [sqk + str(q) for q in range(4)], writes=["ps7"])
    rstd = scr["rstd"]
    P.op("act", lambda e: e.activation(out=rstd[:], in_=nbank[:, :], func=AF.Sqrt, bias=scr["eps"][:, 0:1], scale=1.0 / D),
         reads=["ps7", "eps"], writes=["rstd"])
    P.op("dve", lambda e: e.reciprocal(out=rstd[:], in_=rstd[:]), reads=["rstd"], writes=["rstd"])
    for kt in range(KT):
        tmp = scr["tmp"][kt % 2]
        tk = "tmp%d" % (kt % 2)
        P.op("dve", lambda e, kt=kt, tmp=tmp: e.tensor_tensor(out=tmp[:], in0=xt[:, kt, :], in1=rstd[:], op=ALU.mult),
             reads=xkeys + ["rstd"], writes=[tk])
        P.op("act", lambda e, kt=kt, tmp=tmp: e.activation(out=ht[:, kt, :], in_=tmp[:], func=AF.Identity,
                                                         bias=sh_col[:, kt:kt + 1], scale=a_col[:, kt:kt + 1]),
             reads=[tk, "acol", "modt"], writes=[hk])


def build_stageC(KO, TOK=2048):
    nc = new_nc()
    xT = dram_in(nc, "xT", [D, TOK])
    oT = dram_in(nc, "oT", [KO, TOK])
    wo = dram_in(nc, "wo", [KO, D])
    w1 = dram_in(nc, "w1", [D, 4 * D])
    w2 = dram_in(nc, "w2", [4 * D, D])
    nw = dram_in(nc, "nw", [128, 16])
    modl = dram_in(nc, "modl", [128, 96])
    out = dram_out(nc, "x2T", [D, TOK])
    ntt = TOK // TT
    nko = KO // 128
    with ExitStack() as st:
        P = Prog(nc, st)
        C = setup_common(P)
        C.rot_w = 0
        scr = make_scr(P)
        modt = P.sb("modt", [128, 96])
        nwt = P.sb("nwt", [128, 16])
        P.dma_op("sp", modt[:], modl, writes=["modt"])
        P.dma_op("sp", nwt[:], nw, writes=["nwt1"])
        acol, shcol = emit_mod_cols(P, modt, nwt, 1)
        g1 = modt[:, 32:48]
        g2 = modt[:, 80:96]
        xt = P.sb("xt", [128, 16, TT])
        ob = P.sb("ob", [128, max(nko, 16), TT], BF16)
        aT = P.sb("aT", [128, 64, TT], BF16)
        wbufs = [P.sb("wb0", [128, 8192], BF16), P.sb("wb1", [128, 8192], BF16)]
        ot = [P.sb("ot0", [128, TT]), P.sb("ot1", [128, TT])]
        rl = [P.sb("rl0", [128, TT]), P.sb("rl1", [128, TT])]
        xsrc = xT.rearrange("(k p) t -> p k t", p=128)
        osrc = oT.rearrange("(k p) t -> p k t", p=128)
        odst = out.rearrange("(k p) t -> p k t", p=128)
        for tt in range(ntt):
            tsl = slice(tt * TT, (tt + 1) * TT)
            P.dma_op("sp", xt[:], xsrc[:, :, tsl], writes=["xt"])
            P.dma_op("pool", ob[:, 0:nko, :], osrc[:, :, tsl], writes=["ob"])

            def evac1(m, msz, t_, pst, pk):
                P.op("dve", lambda e: e.scalar_tensor_tensor(out=xt[:, m, :], in0=pst[:, :], scalar=g1[:, m:m + 1], in1=xt[:, m, :],
                                                             op0=ALU.mult, op1=ALU.add),
                     reads=[pk, "modt", "xt"], writes=["x1_%d" % m])
            emit_dense(P, C, wo, KO, D, lambda kt, t_: ob[:, kt, :], lambda t_: ["ob"], 1, evac1, wbufs)
            x1keys = ["x1_%d" % m for m in range(16)]
            emit_adaln2(P, C, xt, x1keys, ob[:, 0:16, :], "ob", acol, shcol, aT[:, 0:16, :], "aT", scr)

            def evac2(f, msz, t_, pst, pk):
                r = rl[f % 2]
                rk = "rl%d" % (f % 2)
                P.op("act", lambda e: e.activation(out=r[:], in_=pst[:, :], func=AF.Square), reads=[pk], writes=[rk])
                P.op("dve", lambda e: e.scalar_tensor_tensor(out=aT[:, f, :], in0=pst[:, :], scalar=0.0, in1=r[:],
                                                             op0=ALU.is_gt, op1=ALU.mult),
                     reads=[pk, rk], writes=["aT_%d" % f])
            emit_dense(P, C, w1, D, 4 * D, lambda kt, t_: ob[:, kt, :], lambda t_: ["ob"], 1, evac2, wbufs)
            akeys = ["aT_%d" % f for f in range(64)]

            def evac3(m, msz, t_, pst, pk):
                o = ot[m % 2]
                okey = "ot%d" % (m % 2)
                P.op("dve", lambda e: e.scalar_tensor_tensor(out=o[:], in0=pst[:, :], scalar=g2[:, m:m + 1], in1=xt[:, m, :],
                                                             op0=ALU.mult, op1=ALU.add),
                     reads=[pk, "modt", "x1_%d" % m], writes=[okey])
                P.dma_op("sp", odst[:, m, tsl], o[:], reads=[okey], is_output=True)
            emit_dense(P, C, w2, 4 * D, D, lambda kt, t_: aT[:, kt, :], lambda t_: akeys, 1, evac3, wbufs)
            _fold(P, "xt", x1keys)
            _fold(P, "aT0", akeys); _fold(P, "aT1", akeys); _fold(P, "aT2", akeys); _fold(P, "aT3", akeys)
        P.finish()
        print("stageC ops", P.n_ops())
    return nc


def _fold(P, dst, keys):
    d = P._st(dst)
    for k in keys:
        s = P._st(k)
        if s["w"] is not None:
            d["r"][("w", s["w"][1])] = s["w"] if ("w", s["w"][1]) not in d["r"] or d["r"][("w", s["w"][1])][2] < s["w"][2] else d["r"][("w", s["w"][1])]
        for sid, ev in s["r"].items():
            key = ("r", ev[1])
            if key not in d["r"] or d["r"][key][2] < ev[2]:
                d["r"][key] = ev


def build_stageC2(KO, TOK=2048, CT=1024):
    nc = new_nc()
    xT = dram_in(nc, "xT", [D, TOK])
    oT = dram_in(nc, "oT", [KO, TOK])
    wo = dram_in(nc, "wo", [KO, D])
    w1 = dram_in(nc, "w1", [D, 4 * D])
    w2 = dram_in(nc, "w2", [4 * D, D])
    nw = dram_in(nc, "nw", [128, 16])
    modl = dram_in(nc, "modl", [128, 96])
    out = dram_out(nc, "x2T", [D, TOK])
    nct = TOK // CT
    nsub = CT // TT
    nkh = KO // D
    with ExitStack() as st:
        P = Prog(nc, st)
        C = setup_common(P)
        C.rot_w = 0
        scr = make_scr(P)
        modt = P.sb("modt", [128, 96])
        nwt = P.sb("nwt", [128, 16])
        P.dma_op("sp", modt[:], modl, writes=["modt"])
        P.dma_op("sp", nwt[:], nw, writes=["nwt1"])
        acol, shcol = emit_mod_cols(P, modt, nwt, 1)
        g1 = modt[:, 32:48]
        g2 = modt[:, 80:96]
        xt = P.sb("xt", [128, 16, CT])
        ob = P.sb("ob", [128, 16, CT], BF16)
        aT = P.sb("aT", [128, 16, CT], BF16)
        wbufs = [P.sb("wb0", [128, 8192], BF16), P.sb("wb1", [128, 8192], BF16)]
        rl = [P.sb("rl0", [128, TT]), P.sb("rl1", [128, TT])]
        xsrc = xT.rearrange("(k p) t -> p k t", p=128)
        osrc = oT.rearrange("(k p) t -> p k t", p=128)
        odst = out.rearrange("(k p) t -> p k t", p=128)
        for ct in range(nct):
            csl = slice(ct * CT, (ct + 1) * CT)
            for hh in range(nsub):
                P.dma_op("sp", xt[:, :, hh * TT:(hh + 1) * TT], xsrc[:, :, ct * CT + hh * TT:ct * CT + (hh + 1) * TT], writes=["xt%d" % hh])
            xkeys = [["x_%d_%d" % (m, hh) for m in range(16)] for hh in range(nsub)]
            for hh in range(nsub):
                for m in range(16):
                    P.tiles["x_%d_%d" % (m, hh)] = {"w": P._st("xt%d" % hh)["w"], "r": {}}
            for kh in range(nkh):
                for hh in range(nsub):
                    P.dma_op("pool", ob[:, :, hh * TT:(hh + 1) * TT], osrc[:, kh * 16:(kh + 1) * 16, ct * CT + hh * TT:ct * CT + (hh + 1) * TT], writes=["ob%d" % hh])

                def evac1(m, msz, hh, pst, pk):
                    hs = slice(hh * TT, (hh + 1) * TT)
                    P.op("dve", lambda e: e.scalar_tensor_tensor(out=xt[:, m, hs], in0=pst[:, :], scalar=g1[:, m:m + 1], in1=xt[:, m, hs],
                                                                 op0=ALU.mult, op1=ALU.add),
                         reads=[pk, "modt", "x_%d_%d" % (m, hh)], writes=["x_%d_%d" % (m, hh)])
                emit_dense(P, C, wo[kh * D:(kh + 1) * D, :], D, D, lambda kt, hh: ob[:, kt, hh * TT:(hh + 1) * TT], lambda hh: ["ob%d" % hh], nsub, evac1, wbufs)
            for hh in range(nsub):
                hs = slice(hh * TT, (hh + 1) * TT)
                emit_adaln2(P, C, xt[:, :, hs], xkeys[hh], ob[:, :, hs], "ob%d" % hh, acol, shcol, aT[:, :, hs], "aTsq%d_" % hh, scr)
            for fq in range(4):
                def evac2(f, msz, hh, pst, pk):
                    hs = slice(hh * TT, (hh + 1) * TT)
                    r = rl[C.ev % 2]
                    rk = "rl%d" % (C.ev % 2)
                    C.ev += 1
                    wr = ["aT_%d_%d" % (f, hh)]
                    if f < 4:
                        wr = wr + ["aTsq%d_%d" % (hh, q) for q in range(4)]
                    P.op("act", lambda e: e.activation(out=r[:], in_=pst[:, :], func=AF.Square), reads=[pk], writes=[rk])
                    P.op("dve", lambda e: e.scalar_tensor_tensor(out=aT[:, f, hs], in0=pst[:, :], scalar=0.0, in1=r[:],
                                                                 op0=ALU.is_gt, op1=ALU.mult),
                         reads=[pk, rk], writes=wr)
                emit_dense(P, C, w1, D, D, lambda kt, hh: ob[:, kt, hh * TT:(hh + 1) * TT], lambda hh: ["ob%d" % hh], nsub, evac2, wbufs, col0=fq * D)

                def evac3(m, msz, hh, pst, pk):
                    hs = slice(hh * TT, (hh + 1) * TT)
                    P.op("dve", lambda e: e.scalar_tensor_tensor(out=xt[:, m, hs], in0=pst[:, :], scalar=g2[:, m:m + 1], in1=xt[:, m, hs],
                                                                 op0=ALU.mult, op1=ALU.add),
                         reads=[pk, "modt", "x_%d_%d" % (m, hh)], writes=["x_%d_%d" % (m, hh)])
                    if fq == 3:
                        P.dma_op("sp", odst[:, m, ct * CT + hh * TT:ct * CT + (hh + 1) * TT], xt[:, m, hs], reads=["x_%d_%d" % (m, hh)], is_output=True)
                emit_dense(P, C, w2[fq * D:(fq + 1) * D, :], D, D, lambda kt, hh: aT[:, kt, hh * TT:(hh + 1) * TT],
                           lambda hh: ["aT_%d_%d" % (f, hh) for f in range(16)], nsub, evac3, wbufs)
            for hh in range(nsub):
                _fold(P, "xt%d" % hh, xkeys[hh])
                for q in range(4):
                    _fold(P, "aTsq%d_%d" % (hh, q), ["aT_%d_%d" % (f, hh) for f in range(16)])
        P.finish()
        print("stageC2 ops", P.n_ops())
    return nc


PH = 99
SUB = 0
ST = 512
NEG = -30000.0


def host_consts():
    c = {}
    c["ident"] = np.eye(128, dtype=np.float32)
    i = np.arange(64)[:, None]
    j = np.arange(64)[None, :]
    sl = (i > j).astype(np.float32)
    su = (j > i).astype(np.float32)
    ui = (j >= i).astype(np.float32)
    c["m_sl"] = np.tile(sl, (1, 8))
    c["m_su"] = np.tile(su, (1, 8))
    c["m_ui"] = np.tile(ui, (1, 8))
    c["ibig"] = np.tile(np.eye(64, dtype=np.float32), (1, 8))
    rm = np.ones((1, 512), np.float32)
    rm[0, ::64] = 0.0
    c["rmask"] = rm
    return c


def rel_bias_toeplitz(rb_h):
    q = np.arange(128)[:, None]
    cc = np.arange(640)[None, :]
    rel = cc - 512 - q
    return np.ascontiguousarray(rb_h[np.clip(rel, -256, 256) + 256]).astype(np.float32)


class AttnState:
    pass


def attn_setup(P, nc, C):
    A = AttnState()
    A.qw = P.sb("a_qw", [128, 1]); A.kw = P.sb("a_kw", [128, 1])
    A.bias = P.sb("a_bias", [128, 640])
    A.kbuf = P.sb("a_kbuf", [128, 1024], BF16)
    A.vtp = P.sb("a_vtp", [128, 8, 128], BF16)
    A.qn = P.sb("a_qn", [128, ST], BF16)
    A.q = P.sb("a_q", [128, ST]); A.k = P.sb("a_k", [128, ST]); A.v = P.sb("a_v", [128, ST])
    A.sq = P.sb("a_sq", [128, ST])
    A.rs = P.sb("a_rs", [128, ST])
    A.sc = P.sb("a_sc", [128, 640])
    A.pt = P.sb("a_pt", [128, 640], BF16)
    A.ob = P.sb("a_ob", [128, ST])
    A.mx = P.sb("a_mx", [128, 1]); A.sm = P.sb("a_sm", [128, 1])
    return A


def attn_init(P, C, A, d):
    P.dma_op("sp", A.qw[:], d["qw"], writes=["a_qw"])
    P.dma_op("sp", A.kw[:], d["kw"], writes=["a_kw"])
    P.dma_op("sp", A.bias[:], d["bias"], writes=["a_bias"])
    P.op("pool", lambda e: e.memset(A.bias[0:64, 576:640], NEG), reads=[], writes=["a_bias"])
    P.op("pool", lambda e: e.memset(A.bias[64:128, 0:64], NEG), reads=[], writes=["a_bias"])
    P.op("pool", lambda e: e.memset(A.kbuf[:, 0:512], 0.0), writes=["a_kbuf"])
    P.op("pool", lambda e: e.memset(A.vtp[:, 0:4, :], 0.0), writes=["a_vtp"])


def attn_supertile(P, C, A, s, pT, out, T):
    t0 = s * ST
    tsl = slice(t0, t0 + ST)
    P.dma_op("sp", A.q[:], pT[512:640, tsl], writes=["a_q"])
    P.dma_op("sp", A.k[:], pT[640:768, tsl], writes=["a_k"])
    P.dma_op("sp", A.v[:], pT[768:896, tsl], writes=["a_v"])
    psB = C.psB
    for nm, x, w, dst, dk, sc_, bs_ in (("q", A.q, A.qw, A.q, "a_q", 1.0, "eps128"), ("k", A.k, A.kw, A.kbuf, "a_kbuf", 1.0 / 128, "eps")):
        P.op("act", lambda e, x=x: e.activation(out=A.sq[:], in_=x[:], func=AF.Square), reads=["a_" + nm], writes=["a_sq"])
        P.op("pe", lambda e: e.matmul(psB[:, :], lhsT=C.ones_f[:], rhs=A.sq[:], start=True, stop=True), reads=["ones_f", "a_sq"], writes=["psB"])
        P.op("act", lambda e, sc_=sc_, bs_=bs_: e.activation(out=A.rs[:], in_=psB[:, :], func=AF.Sqrt, bias=C.cst[bs_][:, 0:1], scale=sc_),
             reads=["psB", "cst"], writes=["a_rs"])
        P.op("dve", lambda e: e.reciprocal(out=A.rs[:], in_=A.rs[:]), reads=["a_rs"], writes=["a_rs"])
        if nm == "q":
            P.op("dve", lambda e: e.scalar_tensor_tensor(out=A.qn[:], in0=A.q[:], scalar=A.qw[:, 0:1], in1=A.rs[:], op0=ALU.mult, op1=ALU.mult),
                 reads=["a_q", "a_qw", "a_rs"], writes=["a_qn"])
        else:
            P.op("dve", lambda e: e.scalar_tensor_tensor(out=A.kbuf[:, 512:1024], in0=A.k[:], scalar=A.kw[:, 0:1], in1=A.rs[:], op0=ALU.mult, op1=ALU.mult),
                 reads=["a_k", "a_kw", "a_rs"], writes=["a_kbuf"])
    yield
    def tr(e):
        ins = None
        for b in range(4):
            ins = e.transpose(psB[:, b * 128:(b + 1) * 128], A.v[:, b * 128:(b + 1) * 128], C.ident[:])
        return ins
    P.op("pe", tr, reads=["a_v", "ident"], writes=["psB"])
    P.op("act", lambda e: e.activation(out=A.vtp[:, 4:8, :], in_=psB[:, :].rearrange("p (b e) -> p b e", b=4), func=AF.Copy),
         reads=["psB"], writes=["a_vtp"])
    yield
    psA = C.psA
    for j in range(4):
        def mm(e, j=j):
            e.matmul(psA[:, 0:512], lhsT=A.qn[:, 128 * j:128 * j + 128], rhs=A.kbuf[:, 128 * j:128 * j + 512], start=True, stop=True)
            return e.matmul(psA[:, 512:640], lhsT=A.qn[:, 128 * j:128 * j + 128], rhs=A.kbuf[:, 128 * j + 512:128 * j + 640], start=True, stop=True)
        P.op("pe", mm, reads=["a_qn", "a_kbuf"], writes=["psA"])
        P.op("dve", lambda e: e.tensor_tensor(out=A.sc[:], in0=psA[:, 0:640], in1=A.bias[:], op=ALU.add), reads=["psA", "a_bias"], writes=["a_sc"])
        if s == 0:
            ninv = 512 - 128 * j
            P.op("dve", lambda e, ninv=ninv: e.memset(A.sc[:, 0:ninv], NEG), writes=["a_sc"])
        yield
        P.op("dve", lambda e: e.tensor_reduce(out=A.mx[:], in_=A.sc[:], axis=AX.X, op=ALU.max), reads=["a_sc"], writes=["a_mx"])
        P.op("dve", lambda e: e.tensor_scalar(out=A.mx[:], in0=A.mx[:], scalar1=-1.0, scalar2=None, op0=ALU.mult), reads=["a_mx"], writes=["a_mx"])
        P.op("act", lambda e: e.activation(out=A.sc[:], in_=A.sc[:], func=AF.Exp, bias=A.mx[:, 0:1], scale=1.0, accum_out=A.sm[:, 0:1]),
             reads=["a_sc", "a_mx"], writes=["a_sc", "a_sm"])
        P.op("dve", lambda e: e.reciprocal(out=A.sm[:], in_=A.sm[:]), reads=["a_sm"], writes=["a_sm"])
        P.op("dve", lambda e: e.tensor_scalar(out=A.sc[:], in0=A.sc[:], scalar1=A.sm[:, 0:1], scalar2=None, op0=ALU.mult), reads=["a_sc", "a_sm"], writes=["a_sc"])

        yield

        def trp(e):
            ins = None
            for kb in range(5):
                ins = e.transpose(psA[:, kb * 128:(kb + 1) * 128], A.sc[:, kb * 128:(kb + 1) * 128], C.ident[:])
            return ins
        P.op("pe", trp, reads=["a_sc", "ident"], writes=["psA"])
        P.op("act", lambda e: e.activation(out=A.pt[:], in_=psA[:, 0:640], func=AF.Copy), reads=["psA"], writes=["a_pt"])

        yield

        def mo(e, j=j):
            ins = None
            for kb in range(5):
                ins = e.matmul(psB[:, 0:128], lhsT=A.vtp[:, j + kb, :], rhs=A.pt[:, kb * 128:(kb + 1) * 128], start=(kb == 0), stop=(kb == 4))
            return ins
        P.op("pe", mo, reads=["a_vtp", "a_pt"], writes=["psB"])
        P.op("dve", lambda e, j=j: e.tensor_copy(out=A.ob[:, 128 * j:128 * j + 128], in_=psB[:, 0:128]), reads=["psB"], writes=["a_ob"])
    P.dma_op("sp", out[128:256, tsl], A.ob[:], reads=["a_ob"], is_output=True)
    P.op("act", lambda e: e.activation(out=A.kbuf[:, 0:512], in_=A.kbuf[:, 512:1024], func=AF.Copy), reads=["a_kbuf"], writes=["a_kbuf"])
    P.op("act", lambda e: e.activation(out=A.vtp[:, 0:4, :], in_=A.vtp[:, 4:8, :], func=AF.Copy), reads=["a_vtp"], writes=["a_vtp"])
    yield


def mixer_common(P, nc, C, cd):
    C.ones_f = P.sb("ones_f", [128, 128])
    P.op("pool", lambda e: e.memset(C.ones_f[:], 1.0), writes=["ones_f"])
    C.ident = P.sb("ident_sb", [128, 128])
    P.dma_op("sp", C.ident[:], cd["ident"], writes=["ident"])
    C.cst = {}
    for nm, v in (("eps", EPS), ("eps128", 128 * EPS), ("one", 1.0), ("eps512", EPS)):
        t = P.sb("cst_" + nm, [128, 1])
        P.op("pool", lambda e, t=t, v=v: e.memset(t[:], v), writes=["cst"])
        C.cst[nm] = t
    C.psA = P.ps("psA", [128, 1024])
    C.psB = P.ps("psB", [128, 512])


def build_attn_only(T):
    nc = new_nc()
    pT = dram_in(nc, "pT", [896, T])
    d = {"qw": dram_in(nc, "qw", [128, 1]), "kw": dram_in(nc, "kw", [128, 1]), "bias": dram_in(nc, "bias", [128, 640])}
    cd = {"ident": dram_in(nc, "ident", [128, 128])}
    out = dram_out(nc, "oT", [256, T])
    with ExitStack() as st:
        P = Prog(nc, st)
        C = Ctx()
        mixer_common(P, nc, C, cd)
        A = attn_setup(P, nc, C)
        attn_init(P, C, A, d)
        for s in range(T // ST):
            rr([attn_supertile(P, C, A, s, pT, out, T)])
        P.finish()
        print("attn ops", P.n_ops())
    return nc


class GS:
    pass


def gdn_setup(P, nc, C):
    G = GS()
    def sb(n, shp):
        return P.sb("g_" + n, shp)
    G.cw = sb("cw", [128, 12]); G.alog = sb("alog", [1, 1]); G.dtb = sb("dtb", [1, 1]); G.nA = sb("nA", [1, 1])
    G.nw = sb("nw", [128, 1])
    G.raw = [sb("raw%d" % i, [128, ST + 3]) for i in range(3)]
    G.z = [sb("z0", [128, ST]), sb("z1", [128, ST])]; G.gb = sb("gb", [1, ST]); G.ga = sb("ga", [1, ST])
    G.sq2 = sb("sq2", [128, ST]); G.rs2 = sb("rs2", [128, ST])
    G.q_bf = P.sb("g_q_bf", [128, ST], BF16); G.k_bf = P.sb("g_k_bf", [128, ST], BF16); G.kb_bf = P.sb("g_kb_bf", [128, ST], BF16)
    G.Yf = sb("Yf", [64, ST])
    G.x = [sb("x%d" % i, [128, ST]) for i in range(3)]
    G.sq = sb("sq", [128, ST]); G.rs = sb("rs", [128, ST])
    G.beta = sb("beta", [1, ST]); G.g = sb("g", [1, ST]); G.gc = sb("gc", [1, ST]); G.ngc = sb("ngc", [1, ST])
    G.rmask = sb("rmask", [1, ST])
    G.beta_bc = sb("beta_bc", [128, ST]); G.gc_bc = sb("gc_bc", [128, ST]); G.egc_bc = [sb("egc_bc0", [128, ST]), sb("egc_bc1", [128, ST])]; G.ekd_bc = sb("ekd_bc", [128, ST])
    G.kb = sb("kb", [128, ST]); G.vb = sb("vb", [128, ST]); G.kbg = sb("kbg", [128, ST]); G.kd = sb("kd", [128, ST]); G.qd = [sb("qd0", [128, ST]), sb("qd1", [128, ST])]
    for n in ("m_sl", "m_su", "m_ui", "ibig", "E1", "E2", "Dl", "Du", "Dq"):
        setattr(G, n, sb(n, [64, ST]))
    for n in ("Pm", "Qm", "Ym"):
        setattr(G, n, P.sb("g_" + n, [64, ST], BF16))
    G.qkT = [sb("qkT0", [64, ST]), sb("qkT1", [64, ST])]
    G.vb_tp = P.sb("g_vb_tp", [64, 1024], BF16); G.kbg_tp = P.sb("g_kbg_tp", [64, 1024], BF16); G.kd_tp = [sb("kd_tp0", [64, 1024]), sb("kd_tp1", [64, 1024])]; G.u = [sb("u0", [64, 1024]), sb("u1", [64, 1024])]
    G.wT = [sb("wT0", [128, ST]), sb("wT1", [128, ST])]; G.S = sb("S", [128, 128]); G.vnew = sb("vnew", [64, 128])
    G.o = sb("o", [128, ST]); G.zs = sb("zs", [128, ST]); G.t1 = sb("t1", [128, ST]); G.res = sb("res", [128, ST])
    C.psC = P.ps("psC", [128, 1024]); C.psD = P.ps("psD", [128, 512]); C.psE = P.ps("psE", [128, 512]); C.psF = P.ps("psF", [128, 512])
    return G


def gdn_init(P, C, G, d, cd):
    for n, t in (("cw", G.cw), ("alog", G.alog), ("dtb", G.dtb), ("nw", G.nw)):
        P.dma_op("sp", t[:], d["g_" + n], writes=["g_" + n])
    for n in ("m_sl", "m_su", "m_ui", "ibig", "rmask"):
        P.dma_op("sp", getattr(G, n)[:], cd[n], writes=["g_" + n])
    P.op("act", lambda e: e.activation(out=G.nA[:], in_=G.alog[:], func=AF.Exp), reads=["g_alog"], writes=["g_nA"])
    P.op("dve", lambda e: e.tensor_scalar(out=G.nA[:], in0=G.nA[:], scalar1=-1.0, scalar2=None, op0=ALU.mult), reads=["g_nA"], writes=["g_nA"])
    P.op("pool", lambda e: e.memset(G.S[:], 0.0), writes=["g_S"])
    for i in range(3):
        P.op("pool", lambda e, i=i: e.memset(G.raw[i][:, 0:3], 0.0), writes=["g_raw%d" % i])


def gdn_pre(P, C, G, s, pT, gates, out, T):
    b = s % 2
    t0 = s * ST
    tsl = slice(t0, t0 + ST)
    psB, psC, psD, psE, psF = C.psB, C.psC, C.psD, C.psE, C.psF
    ones_row = C.ones_f[0:1, 0:64]
    for i in range(3):
        if s == 0:
            P.dma_op("sp", G.raw[i][:, 3:ST + 3], pT[128 * i:128 * i + 128, 0:ST], writes=["g_raw%d" % i])
        else:
            P.dma_op("sp", G.raw[i][:, :], pT[128 * i:128 * i + 128, t0 - 3:t0 + ST], writes=["g_raw%d" % i])
    P.dma_op("sp", G.z[b][:], pT[384:512, tsl], writes=["g_z%d" % b])
    P.dma_op("sp", G.gb[:], gates[0:1, tsl], writes=["g_gb"])
    P.dma_op("sp", G.ga[:], gates[1:2, tsl], writes=["g_ga"])
    yield
    for i in range(3):
        x = G.x[i]; raw = G.raw[i]; xk = "g_x%d" % i; rk = "g_raw%d" % i
        P.op("dve", lambda e, x=x, raw=raw, i=i: e.tensor_scalar(out=x[:], in0=raw[:, 0:ST], scalar1=G.cw[:, 4 * i:4 * i + 1], scalar2=None, op0=ALU.mult),
             reads=[rk, "g_cw"], writes=[xk])
        for j in range(1, 4):
            P.op("dve", lambda e, x=x, raw=raw, i=i, j=j: e.scalar_tensor_tensor(out=x[:], in0=raw[:, j:j + ST], scalar=G.cw[:, 4 * i + j:4 * i + j + 1],
                                                                               in1=x[:], op0=ALU.mult, op1=ALU.add),
                 reads=[rk, "g_cw", xk], writes=[xk])
        P.op("act", lambda e, x=x: e.activation(out=x[:], in_=x[:], func=AF.Silu), reads=[xk], writes=[xk])
    q, k, v = G.x
    yield
    for i, (sc_, bs_) in enumerate(((128.0, "eps128"), (1.0, "eps"))):
        x = G.x[i]; xk = "g_x%d" % i
        P.op("act", lambda e, x=x: e.activation(out=G.sq[:], in_=x[:], func=AF.Square), reads=[xk], writes=["g_sq"])
        P.op("pe", lambda e: e.matmul(psB[:, :], lhsT=C.ones_f[:], rhs=G.sq[:], start=True, stop=True), reads=["ones_f", "g_sq"], writes=["psB"])
        P.op("act", lambda e, sc_=sc_, bs_=bs_: e.activation(out=G.rs[:], in_=psB[:, :], func=AF.Sqrt, bias=C.cst[bs_][:, 0:1], scale=sc_),
             reads=["psB", "cst"], writes=["g_rs"])
        P.op("dve", lambda e: e.reciprocal(out=G.rs[:], in_=G.rs[:]), reads=["g_rs"], writes=["g_rs"])
        P.op("dve", lambda e, x=x: e.tensor_tensor(out=x[:], in0=x[:], in1=G.rs[:], op=ALU.mult), reads=[xk, "g_rs"], writes=[xk])
    yield
    P.op("act", lambda e: e.activation(out=G.beta[:], in_=G.gb[:], func=AF.Exp, scale=-1.0), reads=["g_gb"], writes=["g_beta"])
    P.op("dve", lambda e: e.tensor_scalar(out=G.beta[:], in0=G.beta[:], scalar1=1.0, scalar2=None, op0=ALU.add), reads=["g_beta"], writes=["g_beta"])
    P.op("dve", lambda e: e.reciprocal(out=G.beta[:], in_=G.beta[:]), reads=["g_beta"], writes=["g_beta"])
    P.op("act", lambda e: e.activation(out=G.g[:], in_=G.ga[:], func=AF.Exp, bias=G.dtb[0:1, 0:1], scale=1.0), reads=["g_ga", "g_dtb"], writes=["g_g"])
    P.op("act", lambda e: e.activation(out=G.g[:], in_=G.g[:], func=AF.Ln, bias=C.cst["one"][0:1, 0:1], scale=1.0), reads=["g_g", "cst"], writes=["g_g"])
    P.op("dve", lambda e: e.tensor_scalar(out=G.g[:], in0=G.g[:], scalar1=G.nA[0:1, 0:1], scalar2=None, op0=ALU.mult), reads=["g_g", "g_nA"], writes=["g_g"])
    P.op("dve", lambda e: e.tensor_tensor_scan(out=G.gc[:], data0=G.rmask[:], data1=G.g[:], initial=0.0, op0=ALU.mult, op1=ALU.add),
         reads=["g_rmask", "g_g"], writes=["g_gc"])
    P.op("dve", lambda e: e.tensor_scalar(out=G.ngc[:], in0=G.gc[:], scalar1=-1.0, scalar2=None, op0=ALU.mult), reads=["g_gc"], writes=["g_ngc"])
    yield
    P.op("pe", lambda e: e.matmul(psB[:, :], lhsT=C.ones_f[0:1, :], rhs=G.beta[:], start=True, stop=True), reads=["ones_f", "g_beta"], writes=["psB"])
    P.op("act", lambda e: e.activation(out=G.beta_bc[:], in_=psB[:, :], func=AF.Copy), reads=["psB"], writes=["g_beta_bc"])
    P.op("pe", lambda e: e.matmul(psB[:, :], lhsT=C.ones_f[0:1, :], rhs=G.gc[:], start=True, stop=True), reads=["ones_f", "g_gc"], writes=["psB"])
    P.op("act", lambda e: e.activation(out=G.gc_bc[:], in_=psB[:, :], func=AF.Copy), reads=["psB"], writes=["g_gc_bc"])
    P.op("act", lambda e: e.activation(out=G.egc_bc[b][:], in_=G.gc_bc[:], func=AF.Exp), reads=["g_gc_bc"], writes=["g_egc_bc%d" % b])
    for c in range(8):
        P.op("act", lambda e, c=c: e.activation(out=G.ekd_bc[:, 64 * c:64 * c + 64], in_=G.gc_bc[:, 64 * c:64 * c + 64], func=AF.Exp,
                                               bias=G.gc_bc[:, 64 * c + 63:64 * c + 64], scale=-1.0),
             reads=["g_gc_bc"], writes=["g_ekd_bc"])
    yield
    P.op("dve", lambda e: e.tensor_tensor(out=G.kb[:], in0=k[:], in1=G.beta_bc[:], op=ALU.mult), reads=["g_x1", "g_beta_bc"], writes=["g_kb"])
    P.op("dve", lambda e: e.tensor_tensor(out=G.vb[:], in0=v[:], in1=G.beta_bc[:], op=ALU.mult), reads=["g_x2", "g_beta_bc"], writes=["g_vb"])
    P.op("act", lambda e: e.activation(out=G.kb_bf[:], in_=G.kb[:], func=AF.Copy), reads=["g_kb"], writes=["g_kb_bf"])
    P.op("act", lambda e: e.activation(out=G.k_bf[:], in_=k[:], func=AF.Copy), reads=["g_x1"], writes=["g_k_bf"])
    P.op("act", lambda e: e.activation(out=G.q_bf[:], in_=q[:], func=AF.Copy), reads=["g_x0"], writes=["g_q_bf"])
    P.op("dve", lambda e: e.tensor_tensor(out=G.kbg[:], in0=G.kb[:], in1=G.egc_bc[b][:], op=ALU.mult), reads=["g_kb", "g_egc_bc%d" % b], writes=["g_kbg"])
    P.op("dve", lambda e: e.tensor_tensor(out=G.kd[:], in0=k[:], in1=G.ekd_bc[:], op=ALU.mult), reads=["g_x1", "g_ekd_bc"], writes=["g_kd"])
    P.op("dve", lambda e: e.tensor_tensor(out=G.qd[b][:], in0=q[:], in1=G.egc_bc[b][:], op=ALU.mult), reads=["g_x0", "g_egc_bc%d" % b], writes=["g_qd%d" % b])

    yield
    def dec(e):
        ins = None
        for c in range(8):
            cs = slice(64 * c, 64 * c + 64)
            e.matmul(psC[0:64, 64 * c:64 * c + 64], lhsT=G.gc[0:1, cs], rhs=ones_row, start=True, stop=False)
            e.matmul(psC[0:64, 64 * c:64 * c + 64], lhsT=ones_row, rhs=G.ngc[0:1, cs], start=False, stop=True)
        for c in range(8):
            cs = slice(64 * c, 64 * c + 64)
            e.matmul(psC[0:64, 512 + 64 * c:512 + 64 * c + 64], lhsT=ones_row, rhs=G.gc[0:1, cs], start=True, stop=False)
            ins = e.matmul(psC[0:64, 512 + 64 * c:512 + 64 * c + 64], lhsT=G.ngc[0:1, cs], rhs=ones_row, start=False, stop=True)
        return ins
    P.op("pe", dec, reads=["g_gc", "g_ngc", "ones_f"], writes=["psC"])
    P.op("dve", lambda e: e.tensor_scalar(out=G.E1[:], in0=psC[0:64, 0:512], scalar1=0.0, scalar2=None, op0=ALU.min), reads=["psC"], writes=["g_E1"])
    P.op("dve", lambda e: e.tensor_scalar(out=G.E2[:], in0=psC[0:64, 512:1024], scalar1=0.0, scalar2=None, op0=ALU.min), reads=["psC"], writes=["g_E2"])
    P.op("act", lambda e: e.activation(out=G.E1[:], in_=G.E1[:], func=AF.Exp), reads=["g_E1"], writes=["g_E1"])
    P.op("act", lambda e: e.activation(out=G.E2[:], in_=G.E2[:], func=AF.Exp), reads=["g_E2"], writes=["g_E2"])
    P.op("dve", lambda e: e.tensor_tensor(out=G.Dl[:], in0=G.E1[:], in1=G.m_sl[:], op=ALU.mult), reads=["g_E1", "g_m_sl"], writes=["g_Dl"])
    P.op("dve", lambda e: e.tensor_tensor(out=G.Du[:], in0=G.E2[:], in1=G.m_su[:], op=ALU.mult), reads=["g_E2", "g_m_su"], writes=["g_Du"])
    P.op("dve", lambda e: e.tensor_tensor(out=G.Dq[:], in0=G.E2[:], in1=G.m_ui[:], op=ALU.mult), reads=["g_E2", "g_m_ui"], writes=["g_Dq"])

    yield
    def kk(e):
        ins = None
        for c in range(8):
            cs = slice(64 * c, 64 * c + 64)
            e.matmul(psC[0:64, 64 * c:64 * c + 64], lhsT=G.kb_bf[:, cs], rhs=G.k_bf[:, cs], start=True, stop=True)
        for c in range(8):
            cs = slice(64 * c, 64 * c + 64)
            ins = e.matmul(psC[0:64, 512 + 64 * c:512 + 64 * c + 64], lhsT=G.k_bf[:, cs], rhs=G.kb_bf[:, cs], start=True, stop=True)
        return ins
    P.op("pe", kk, reads=["g_kb_bf", "g_k_bf"], writes=["psC"])

    def qkm(e):
        ins = None
        for c in range(8):
            cs = slice(64 * c, 64 * c + 64)
            ins = e.matmul(psD[0:64, 64 * c:64 * c + 64], lhsT=G.k_bf[:, cs], rhs=G.q_bf[:, cs], start=True, stop=True)
        return ins
    P.op("pe", qkm, reads=["g_q_bf", "g_k_bf"], writes=["psD"])
    P.op("dve", lambda e: e.tensor_tensor(out=G.Pm[:], in0=psC[0:64, 0:512], in1=G.Dl[:], op=ALU.mult), reads=["psC", "g_Dl"], writes=["g_Pm"])
    P.op("dve", lambda e: e.tensor_tensor(out=G.Qm[:], in0=psC[0:64, 512:1024], in1=G.Du[:], op=ALU.mult), reads=["psC", "g_Du"], writes=["g_Qm"])
    P.op("dve", lambda e: e.tensor_tensor(out=G.qkT[b][:], in0=psD[0:64, :], in1=G.Dq[:], op=ALU.mult), reads=["psD", "g_Dq"], writes=["g_qkT%d" % b])
    P.op("dve", lambda e: e.tensor_tensor(out=G.Yf[:], in0=G.ibig[:], in1=G.Qm[:], op=ALU.subtract), reads=["g_ibig", "g_Qm"], writes=["g_Yf"])
    P.op("act", lambda e: e.activation(out=G.Ym[:], in_=G.Yf[:], func=AF.Copy), reads=["g_Yf"], writes=["g_Ym"])
    yield
    yield
    for m in range(5):
        def sqr(e):
            ins = None
            for c in range(8):
                cs = slice(64 * c, 64 * c + 64)
                e.matmul(psC[0:64, 64 * c:64 * c + 64], lhsT=G.Qm[:, cs], rhs=G.Pm[:, cs], start=True, stop=True)
            for c in range(8):
                cs = slice(64 * c, 64 * c + 64)
                ins = e.matmul(psC[0:64, 512 + 64 * c:512 + 64 * c + 64], lhsT=G.Pm[:, cs], rhs=G.Qm[:, cs], start=True, stop=True)
            return ins
        P.op("pe", sqr, reads=["g_Pm", "g_Qm"], writes=["psC"])
        P.op("act", lambda e: e.activation(out=G.Pm[:], in_=psC[0:64, 0:512], func=AF.Copy), reads=["psC"], writes=["g_Pm"])
        P.op("dve", lambda e: e.tensor_copy(out=G.Qm[:], in_=psC[0:64, 512:1024]), reads=["psC"], writes=["g_Qm"])
        yield

        def yup(e):
            ins = None
            for c in range(8):
                cs = slice(64 * c, 64 * c + 64)
                ins = e.matmul(psD[0:64, 64 * c:64 * c + 64], lhsT=G.Pm[:, cs], rhs=G.Ym[:, cs], start=True, stop=True)
            return ins
        P.op("pe", yup, reads=["g_Ym", "g_Pm"], writes=["psD"])
        P.op("dve", lambda e: e.tensor_tensor(out=G.Yf[:], in0=G.Yf[:], in1=psD[0:64, :], op=ALU.add), reads=["psD", "g_Yf"], writes=["g_Yf"])
        P.op("act", lambda e: e.activation(out=G.Ym[:], in_=G.Yf[:], func=AF.Copy), reads=["g_Yf"], writes=["g_Ym"])
        yield
    yield
    for src, dst, sk, dk in ((G.vb, G.vb_tp, "g_vb", "g_vb_tp"), (G.kbg, G.kbg_tp, "g_kbg", "g_kbg_tp"), (G.kd, G.kd_tp[b], "g_kd", "g_kd_tp%d" % b)):
        def trp(e, src=src):
            ins = None
            for c in range(8):
                ins = e.transpose(psC[0:64, 128 * c:128 * c + 128], src[:, 64 * c:64 * c + 64], C.ident[:])
            return ins
        P.op("pe", trp, reads=[sk, "ident"], writes=["psC"])
        P.op("dve", lambda e, dst=dst: e.tensor_copy(out=dst[:], in_=psC[0:64, :]), reads=["psC"], writes=[dk])

    yield
    def um(e):
        ins = None
        for c in range(8):
            ins = e.matmul(psC[0:64, 128 * c:128 * c + 128], lhsT=G.Ym[:, 64 * c:64 * c + 64], rhs=G.vb_tp[:, 128 * c:128 * c + 128], start=True, stop=True)
        return ins
    P.op("pe", um, reads=["g_Ym", "g_vb_tp"], writes=["psC"])
    P.op("act", lambda e: e.activation(out=G.u[b][:], in_=psC[0:64, :], func=AF.Copy), reads=["psC"], writes=["g_u%d" % b])

    def wm(e):
        ins = None
        for c in range(8):
            ins = e.matmul(psD[:, 64 * c:64 * c + 64], lhsT=G.kbg_tp[:, 128 * c:128 * c + 128], rhs=G.Ym[:, 64 * c:64 * c + 64], start=True, stop=True)
        return ins
    P.op("pe", wm, reads=["g_Ym", "g_kbg_tp"], writes=["psD"])
    P.op("dve", lambda e: e.tensor_copy(out=G.wT[b][:], in_=psD[:, :]), reads=["psD"], writes=["g_wT%d" % b])
    yield


def gdn_scan(P, C, G, s, pT, gates, out, T):
    b = s % 2
    t0 = s * ST
    tsl = slice(t0, t0 + ST)
    psB, psC, psD, psE, psF = C.psB, C.psC, C.psD, C.psE, C.psF
    ones_row = C.ones_f[0:1, 0:64]
    for c in range(8):
        cs = slice(64 * c, 64 * c + 64)
        P.op("pe", lambda e, cs=cs: e.matmul(psF[0:64, 0:128], lhsT=G.wT[b][:, cs], rhs=G.S[:], start=True, stop=True), reads=["g_wT%d" % b, "g_S"], writes=["psF_a"])
        P.op("dve", lambda e, c=c: e.tensor_tensor(out=G.vnew[:], in0=G.u[b][:, 128 * c:128 * c + 128], in1=psF[0:64, 0:128], op=ALU.subtract),
             reads=["g_u%d" % b, "psF_a"], writes=["g_vnew"])
        yield

        def om(e, cs=cs, c=c):
            e.matmul(psE[:, cs], lhsT=G.S[:], rhs=G.qd[b][:, cs], start=True, stop=False)
            e.matmul(psE[:, cs], lhsT=G.vnew[:], rhs=G.qkT[b][:, cs], start=False, stop=True)
            return e.matmul(psF[:, 128:256], lhsT=G.kd_tp[b][:, 128 * c:128 * c + 128], rhs=G.vnew[:], start=True, stop=True)
        P.op("pe", om, reads=["g_S", "g_qd%d" % b, "g_vnew", "g_qkT%d" % b, "g_kd_tp%d" % b], writes=["psE", "psF_b"])
        P.op("dve", lambda e, c=c: e.scalar_tensor_tensor(out=G.S[:], in0=G.S[:], scalar=G.egc_bc[b][:, 64 * c + 63:64 * c + 64], in1=psF[:, 128:256],
                                                       op0=ALU.mult, op1=ALU.add),
             reads=["g_S", "g_egc_bc%d" % b, "psF_b"], writes=["g_S"])
        yield
    P.op("act", lambda e: e.activation(out=G.o[:], in_=psE[:, :], func=AF.Copy), reads=["psE"], writes=["g_o"])
    P.op("act", lambda e: e.activation(out=G.sq2[:], in_=G.o[:], func=AF.Square), reads=["g_o"], writes=["g_sq2"])
    P.op("pe", lambda e: e.matmul(psB[:, :], lhsT=C.ones_f[:], rhs=G.sq2[:], start=True, stop=True), reads=["ones_f", "g_sq2"], writes=["psB"])
    P.op("act", lambda e: e.activation(out=G.rs2[:], in_=psB[:, :], func=AF.Sqrt, bias=C.cst["eps"][:, 0:1], scale=1.0 / 128), reads=["psB", "cst"], writes=["g_rs2"])
    P.op("dve", lambda e: e.reciprocal(out=G.rs2[:], in_=G.rs2[:]), reads=["g_rs2"], writes=["g_rs2"])
    yield
    P.op("act", lambda e: e.activation(out=G.zs[:], in_=G.z[b][:], func=AF.Silu), reads=["g_z%d" % b], writes=["g_zs"])
    P.op("dve", lambda e: e.tensor_tensor(out=G.t1[:], in0=G.o[:], in1=G.rs2[:], op=ALU.mult), reads=["g_o", "g_rs2"], writes=["g_t1"])
    P.op("dve", lambda e: e.scalar_tensor_tensor(out=G.res[:], in0=G.t1[:], scalar=G.nw[:, 0:1], in1=G.zs[:], op0=ALU.mult, op1=ALU.mult),
         reads=["g_t1", "g_nw", "g_zs"], writes=["g_res"])
    P.dma_op("sp", out[0:128, tsl], G.res[:], reads=["g_res"], is_output=True)


    yield


def rr(gens):
    gens = [g for g in gens if g is not None]
    while gens:
        for g in list(gens):
            try:
                next(g)
            except StopIteration:
                gens.remove(g)


def build_B0(T, do_attn=True, do_gdn=True):
    nc = new_nc()
    pT = dram_in(nc, "pT", [896, T])
    gates = dram_in(nc, "gates", [2, T])
    d = {"qw": dram_in(nc, "qw", [128, 1]), "kw": dram_in(nc, "kw", [128, 1]), "bias": dram_in(nc, "bias", [128, 640]),
         "g_cw": dram_in(nc, "g_cw", [128, 12]), "g_alog": dram_in(nc, "g_alog", [1, 1]), "g_dtb": dram_in(nc, "g_dtb", [1, 1]),
         "g_nw": dram_in(nc, "g_nw", [128, 1])}
    cd = {"ident": dram_in(nc, "ident", [128, 128])}
    for n in ("m_sl", "m_su", "m_ui", "ibig"):
        cd[n] = dram_in(nc, n, [64, 512])
    cd["rmask"] = dram_in(nc, "rmask", [1, 512])
    out = dram_out(nc, "oT", [256, T])
    with ExitStack() as st:
        P = Prog(nc, st)
        C = Ctx()
        mixer_common(P, nc, C, cd)
        if do_attn:
            A = attn_setup(P, nc, C)
            attn_init(P, C, A, d)
        if do_gdn:
            G = gdn_setup(P, nc, C)
            gdn_init(P, C, G, d, cd)
        nst = T // ST
        if do_gdn:
            rr([gdn_pre(P, C, G, 0, pT, gates, out, T)])
        for s in range(nst):
            rr([gdn_scan(P, C, G, s, pT, gates, out, T) if do_gdn else None,
                attn_supertile(P, C, A, s, pT, out, T) if do_attn else None,
                gdn_pre(P, C, G, s + 1, pT, gates, out, T) if (do_gdn and s + 1 < nst) else None])
        P.finish()
        print("B0 ops", P.n_ops())
    return nc


def ssd_host_consts():
    c = {}
    c["ident"] = np.eye(128, dtype=np.float32)
    sp = np.zeros((8, 4, 128), np.float32)
    for j in range(4):
        sp[2 * j, j, 0:64] = 1.0
        sp[2 * j + 1, j, 64:128] = 1.0
    c["selpair"] = sp.reshape(8, 512)
    so = np.zeros((8, 8, 128), np.float32)
    for e in range(8):
        so[e, e, :] = 1.0
    c["selones"] = so.reshape(8, 1024)
    c["sel8"] = np.eye(8, dtype=np.float32)
    c["ones8"] = np.ones((8, 64), np.float32)
    rm = np.ones((8, 512), np.float32)
    rm[:, ::64] = 0.0
    c["rmask8"] = rm
    m = np.arange(64)[:, None]
    l = np.arange(64)[None, :]
    c["negmask"] = np.tile(np.where(l >= m, 0.0, NEG).astype(np.float32), (1, 8))
    sr = np.zeros((8, 8, 64), np.float32)
    for k in range(8):
        sr[k, k, :] = 1.0
    c["selrow64"] = sr.reshape(8, 512)
    return c


def ssd_setup(P, nc, C):
    S = GS()
    def sb(n, shp):
        return P.sb("s_" + n, shp)
    S.cw = sb("cw", [128, 24]); S.cb = sb("cb", [128, 6]); S.dcol = sb("dcol", [128, 4]); S.nw = sb("nw", [128, 4])
    S.dtb = sb("dtb", [8, 1]); S.alog = sb("alog", [8, 1]); S.acol = sb("acol", [8, 1])
    S.selpair = sb("selpair", [8, 512]); S.selones = sb("selones", [8, 1024]); S.sel8 = sb("sel8", [8, 8]); S.ones8 = sb("ones8", [8, 64])
    S.rmask8 = sb("rmask8", [8, 512]); S.negmask = sb("negmask", [64, 512])
    S.selrow64 = sb("selrow64", [8, 512])
    S.R_hi = P.sb("s_R_hi", [8, 8, ST], BF16); S.R_lo = P.sb("s_R_lo", [8, 8, ST], BF16)
    S.ac_hi = P.sb("s_ac_hi", [8, ST], BF16); S.ac_lo = P.sb("s_ac_lo", [8, ST], BF16); S.nac_hi = P.sb("s_nac_hi", [8, ST], BF16); S.nac_lo = P.sb("s_nac_lo", [8, ST], BF16)
    S.selrow_bf = P.sb("s_selrow_bf", [8, 512], BF16); S.ones8_bf = P.sb("s_ones8_bf", [8, 64], BF16)
    S.selpair_bf = P.sb("s_selpair_bf", [8, 512], BF16); S.selones_bf = P.sb("s_selones_bf", [8, 1024], BF16)
    S.dt_bf = P.sb("s_dt_bf", [8, ST], BF16); S.dtw_bf = P.sb("s_dtw_bf", [8, ST], BF16); S.eac_bf = P.sb("s_eac_bf", [8, ST], BF16)
    S.tmpS = sb("tmpS", [128, ST])
    S.raw = [sb("raw%d" % i, [128, ST + 3]) for i in range(6)]
    S.z = [sb("z%d" % i, [128, ST]) for i in range(4)]
    S.xc = [sb("xc%d" % i, [128, ST]) for i in range(6)]
    S.dt = sb("dt", [8, ST]); S.acum = sb("acum", [8, ST]); S.nacum = sb("nacum", [8, ST]); S.eac = sb("eac", [8, ST])
    S.dtw = sb("dtw", [8, ST])
    S.xdtT = [sb("xdtT%d" % j, [128, ST]) for j in range(4)]
    S.xdtwT = [sb("xdtwT%d" % j, [128, ST]) for j in range(4)]
    S.Cdec = P.sb("s_Cdec", [128, 8, ST], BF16); S.glt = sb("glt", [128, 8, 8]); S.eacl = sb("eacl", [8, 8])
    S.B_tp = P.sb("s_B_tp", [64, 1024], BF16); S.xdt_all = P.sb("s_xdt_all", [64, 8, ST], BF16); S.xdtw_tp = P.sb("s_xdtw_tp", [64, 2, ST], BF16)
    S.ST_bf = P.sb("s_ST_bf", [128, 8, ST], BF16); S.Bc_bf = P.sb("s_Bc_bf", [128, ST], BF16); S.Cc_bf = P.sb("s_Cc_bf", [128, ST], BF16)
    S.E = sb("E", [64, 2, ST]); S.cbd_all = P.sb("s_cbd_all", [64, 8, ST], BF16); S.y_sb = sb("y_sb", [64, 2, ST]); S.yT = sb("yT", [128, 4, ST])
    S.ST_all = sb("ST_all", [128, 9, ST])
    S.zs = sb("zs", [128, ST]); S.rs = sb("rs", [128, ST]); S.res = sb("res", [128, ST])
    C.psX = P.ps("psX", [128, 512]); C.psY = P.ps("psY", [128, 512]); C.psYY = P.ps("psYY", [128, 512]); C.psS = P.ps("psS", [128, 512])
    C.psV = P.ps("psV", [128, 512])
    return S


def ssd_init(P, C, S, d, cd):
    for n in ("cw", "cb", "dcol", "nw", "dtb", "alog"):
        P.dma_op("sp", getattr(S, n)[:], d["s_" + n], writes=["s_" + n])
    for n in ("selpair", "selones", "sel8", "ones8", "rmask8", "negmask", "selrow64"):
        P.dma_op("sp", getattr(S, n)[:], cd[n], writes=["s_" + n])
    P.op("act", lambda e: e.activation(out=S.acol[:], in_=S.alog[:], func=AF.Exp), reads=["s_alog"], writes=["s_acol"])
    P.op("dve", lambda e: e.tensor_scalar(out=S.acol[:], in0=S.acol[:], scalar1=-1.0, scalar2=None, op0=ALU.mult), reads=["s_acol"], writes=["s_acol"])
    P.op("pool", lambda e: e.memset(S.ST_all[:, 0, :], 0.0), writes=["s_ST0"])
    for src_, dst_, k_ in ((S.selrow64, S.selrow_bf, "selrow64"), (S.ones8, S.ones8_bf, "ones8"), (S.selpair, S.selpair_bf, "selpair"), (S.selones, S.selones_bf, "selones")):
        P.op("dve", lambda e, src_=src_, dst_=dst_: e.tensor_copy(out=dst_[:], in_=src_[:]), reads=["s_" + k_], writes=["s_" + k_ + "_bf"])
    for i in range(6):
        P.op("pool", lambda e, i=i: e.memset(S.raw[i][:, 0:3], 0.0), writes=["s_raw%d" % i])


def ssd_supertile(P, C, S, s, pT, dtT, out, T):
    t0 = s * ST
    tsl = slice(t0, t0 + ST)
    psB = C.psB
    A0 = C.psA[:, 0:512]; A1 = C.psA[:, 512:1024]
    bk = {"psB": psB, "psA0": A0, "psA1": A1, "psX": C.psX, "psY": C.psY, "psYY": C.psYY, "psS": C.psS, "psV": C.psV}
    for i in range(6):
        r0 = 512 + 128 * i
        if s == 0:
            P.dma_op("sp", S.raw[i][:, 3:ST + 3], pT[r0:r0 + 128, 0:ST], writes=["s_raw%d" % i])
        else:
            P.dma_op("sp", S.raw[i][:, :], pT[r0:r0 + 128, t0 - 3:t0 + ST], writes=["s_raw%d" % i])
    for j in range(4):
        P.dma_op("sp", S.z[j][:], pT[128 * j:128 * j + 128, tsl], writes=["s_z%d" % j])
    P.dma_op("sp", S.dt[:], dtT[:, tsl], writes=["s_dt"])
    for i in range(6):
        x = S.xc[i]; raw = S.raw[i]; xk = "s_xc%d" % i; rk = "s_raw%d" % i
        ce = "dve"
        if ce == "dve":
            P.op("dve", lambda e, x=x, raw=raw, i=i: e.tensor_scalar(out=x[:], in0=raw[:, 0:ST], scalar1=S.cw[:, 4 * i:4 * i + 1], scalar2=None, op0=ALU.mult),
                 reads=[rk, "s_cw"], writes=[xk])
            for j in range(1, 4):
                P.op("dve", lambda e, x=x, raw=raw, i=i, j=j: e.scalar_tensor_tensor(out=x[:], in0=raw[:, j:j + ST], scalar=S.cw[:, 4 * i + j:4 * i + j + 1],
                                                                                   in1=x[:], op0=ALU.mult, op1=ALU.add),
                     reads=[rk, "s_cw", xk], writes=[xk])
        else:
            P.op("pool", lambda e, x=x, raw=raw, i=i: e.tensor_scalar(out=x[:], in0=raw[:, 0:ST], scalar1=S.cw[:, 4 * i:4 * i + 1], scalar2=None, op0=ALU.mult),
                 reads=[rk, "s_cw"], writes=[xk])
            for j in range(1, 4):
                P.op("pool", lambda e, raw=raw, i=i, j=j: e.tensor_scalar(out=S.zs[:], in0=raw[:, j:j + ST], scalar1=S.cw[:, 4 * i + j:4 * i + j + 1], scalar2=None, op0=ALU.mult),
                     reads=[rk, "s_cw"], writes=["s_zs"])
                P.op("pool", lambda e, x=x: e.tensor_tensor(out=x[:], in0=x[:], in1=S.zs[:], op=ALU.add), reads=[xk, "s_zs"], writes=[xk])
        P.op("act", lambda e, x=x, i=i: e.activation(out=x[:], in_=x[:], func=AF.Silu, bias=S.cb[:, i:i + 1], scale=1.0), reads=[xk, "s_cb"], writes=[xk])
    xs = S.xc[0:4]; Bc = S.xc[4]; Cc = S.xc[5]
    P.op("act", lambda e: e.activation(out=S.Bc_bf[:], in_=Bc[:], func=AF.Copy), reads=["s_xc4"], writes=["s_Bc_bf"])
    P.op("act", lambda e: e.activation(out=S.Cc_bf[:], in_=Cc[:], func=AF.Copy), reads=["s_xc5"], writes=["s_Cc_bf"])
    P.op("act", lambda e: e.activation(out=S.dt[:], in_=S.dt[:], func=AF.Exp, bias=S.dtb[:, 0:1], scale=1.0), reads=["s_dt", "s_dtb"], writes=["s_dt"])
    P.op("act", lambda e: e.activation(out=S.dt[:], in_=S.dt[:], func=AF.Ln, bias=C.cst["one"][0:8, 0:1], scale=1.0), reads=["s_dt", "cst"], writes=["s_dt"])
    P.op("dve", lambda e: e.tensor_scalar(out=S.nacum[:], in0=S.dt[:], scalar1=S.acol[:, 0:1], scalar2=None, op0=ALU.mult), reads=["s_dt", "s_acol"], writes=["s_nacum"])
    P.op("dve", lambda e: e.tensor_tensor_scan(out=S.acum[:], data0=S.rmask8[:], data1=S.nacum[:], initial=0.0, op0=ALU.mult, op1=ALU.add),
         reads=["s_rmask8", "s_nacum"], writes=["s_acum"])
    P.op("dve", lambda e: e.tensor_scalar(out=S.nacum[:], in0=S.acum[:], scalar1=-1.0, scalar2=None, op0=ALU.mult), reads=["s_acum"], writes=["s_nacum"])
    P.op("act", lambda e: e.activation(out=S.eac[:], in_=S.acum[:], func=AF.Exp), reads=["s_acum"], writes=["s_eac"])
    for c in range(8):
        P.op("act", lambda e, c=c: e.activation(out=S.dtw[:, 64 * c:64 * c + 64], in_=S.acum[:, 64 * c:64 * c + 64], func=AF.Exp,
                                               bias=S.acum[:, 64 * c + 63:64 * c + 64], scale=-1.0), reads=["s_acum"], writes=["s_dtw"])
    P.op("dve", lambda e: e.tensor_tensor(out=S.dtw[:], in0=S.dt[:], in1=S.dtw[:], op=ALU.mult), reads=["s_dt", "s_dtw"], writes=["s_dtw"])
    P.op("dve", lambda e: e.tensor_copy(out=S.ac_hi[:], in_=S.acum[:]), reads=["s_acum"], writes=["s_ac_hi"])
    P.op("dve", lambda e: e.tensor_tensor(out=S.ac_lo[:], in0=S.acum[:], in1=S.ac_hi[:], op=ALU.subtract), reads=["s_acum", "s_ac_hi"], writes=["s_ac_lo"])
    P.op("dve", lambda e: e.tensor_scalar(out=S.nac_hi[:], in0=S.ac_hi[:], scalar1=-1.0, scalar2=None, op0=ALU.mult), reads=["s_ac_hi"], writes=["s_nac_hi"])
    P.op("dve", lambda e: e.tensor_scalar(out=S.nac_lo[:], in0=S.ac_lo[:], scalar1=-1.0, scalar2=None, op0=ALU.mult), reads=["s_ac_lo"], writes=["s_nac_lo"])
    selb = S.sel8[:, :].unsqueeze(2).to_broadcast([8, 8, ST])
    P.op("dve", lambda e: e.tensor_tensor(out=S.R_hi[:], in0=S.ac_hi[:, :].unsqueeze(1).to_broadcast([8, 8, ST]), in1=selb, op=ALU.mult),
         reads=["s_ac_hi", "s_sel8"], writes=["s_R_hi"])
    P.op("dve", lambda e: e.tensor_tensor(out=S.R_lo[:], in0=S.ac_lo[:, :].unsqueeze(1).to_broadcast([8, 8, ST]), in1=selb, op=ALU.mult),
         reads=["s_ac_lo", "s_sel8"], writes=["s_R_lo"])
    P.op("act", lambda e: e.activation(out=S.dt_bf[:], in_=S.dt[:], func=AF.Copy), reads=["s_dt"], writes=["s_dt_bf"])
    P.op("act", lambda e: e.activation(out=S.dtw_bf[:], in_=S.dtw[:], func=AF.Copy), reads=["s_dtw"], writes=["s_dtw_bf"])
    P.op("act", lambda e: e.activation(out=S.eac_bf[:], in_=S.eac[:], func=AF.Copy), reads=["s_eac"], writes=["s_eac_bf"])
    P.op("dve", lambda e: e.tensor_copy(out=S.eacl[:], in_=S.eac[:].rearrange("p (c l) -> p c l", l=64)[:, :, 63]), reads=["s_eac"], writes=["s_eacl"])
    rot = ["psB", "psV", "psYY", "psS"]
    ri = 0
    for j in range(4):
        for src_, dst_, dkey in ((S.dt_bf, S.xdtT[j], "s_xdtT%d" % j), (S.dtw_bf, S.xdtwT[j], "s_xdtwT%d" % j)):
            bn = rot[ri % 4]; ri += 1
            P.op("pe", lambda e, j=j, src_=src_, bn=bn: e.matmul(bk[bn][:, :], lhsT=S.selpair_bf[:, 128 * j:128 * j + 128], rhs=src_[:], start=True, stop=True),
                 reads=["s_selpair_bf", "s_dt_bf", "s_dtw_bf"], writes=[bn])
            P.op("dve", lambda e, j=j, dst_=dst_, bn=bn: e.tensor_tensor(out=dst_[:], in0=xs[j][:], in1=bk[bn][:, :], op=ALU.mult), reads=["s_xc%d" % j, bn], writes=[dkey])
    for h in range(8):
        bn = rot[ri % 4]; ri += 1
        P.op("pe", lambda e, h=h, bn=bn: e.matmul(bk[bn][:, :], lhsT=S.selones_bf[:, 128 * h:128 * h + 128], rhs=S.eac_bf[:], start=True, stop=True), reads=["s_selones_bf", "s_eac_bf"], writes=[bn])
        eng = "dve"
        if eng == "dve":
            P.op("dve", lambda e, h=h, bn=bn: e.tensor_tensor(out=S.Cdec[:, h, :], in0=Cc[:], in1=bk[bn][:, :], op=ALU.mult), reads=["s_xc5", bn], writes=["s_Cdec%d" % h])
        else:
            P.op("act", lambda e, bn=bn: e.activation(out=S.res[:], in_=bk[bn][:, :], func=AF.Copy), reads=[bn], writes=["s_res"])
            P.op("pool", lambda e, h=h: e.tensor_tensor(out=S.Cdec[:, h, :], in0=Cc[:], in1=S.res[:], op=ALU.mult), reads=["s_xc5", "s_res"], writes=["s_Cdec%d" % h])

    def glm(e):
        ins = None
        for h in range(8):
            ins = e.matmul(C.psX[:, 8 * h:8 * h + 8], lhsT=S.selones[:, 128 * h:128 * h + 128], rhs=S.eacl[:], start=True, stop=True)
        return ins
    P.op("pe", glm, reads=["s_selones", "s_eacl"], writes=["psX"])
    P.op("act", lambda e: e.activation(out=S.glt[:].rearrange("p h c -> p (h c)"), in_=C.psX[:, 0:64], func=AF.Copy), reads=["psX"], writes=["s_glt"])
    for hf in range(2):
        bn = "psA%d" % hf

        def btr(e, hf=hf, bn=bn):
            ins = None
            for c in range(4):
                cc = 4 * hf + c
                ins = e.transpose(bk[bn][0:64, 128 * c:128 * c + 128], Bc[:, 64 * cc:64 * cc + 64], C.ident[:])
            return ins
        P.op("pe", btr, reads=["s_xc4", "ident"], writes=[bn])
        P.op("dve", lambda e, hf=hf, bn=bn: e.tensor_copy(out=S.B_tp[:, 512 * hf:512 * hf + 512], in_=bk[bn][0:64, :]), reads=[bn], writes=["s_B_tp"])
    for c in range(8):
        cs = slice(64 * c, 64 * c + 64)
        p = c % 2
        b1 = "psA%d" % p
        b2 = ["psX", "psY"][p]

        def xtr(e, cs=cs, b1=b1, b2=b2):
            ins = None
            for j in range(4):
                e.transpose(bk[b1][0:64, 128 * j:128 * j + 128], S.xdtT[j][:, cs], C.ident[:])
            for j in range(4):
                ins = e.transpose(bk[b2][0:64, 128 * j:128 * j + 128], S.xdtwT[j][:, cs], C.ident[:])
            return ins
        P.op("pe", xtr, reads=["s_xdtT%d" % j for j in range(4)] + ["s_xdtwT%d" % j for j in range(4)] + ["ident"], writes=[b1, b2])
        P.op("act", lambda e, c=c, b1=b1: e.activation(out=S.xdt_all[:, c, :], in_=bk[b1][0:64, :], func=AF.Copy), reads=[b1], writes=["s_xdt_all%d" % c])
        P.op("dve", lambda e, p=p, b2=b2: e.tensor_copy(out=S.xdtw_tp[:, p, :], in_=bk[b2][0:64, :]), reads=[b2], writes=["s_xdtw_tp%d" % p])
        b3 = ["psYY", "psS"][p]
        P.op("pe", lambda e, c=c, p=p, b3=b3: e.matmul(bk[b3][:, :], lhsT=S.B_tp[:, 128 * c:128 * c + 128], rhs=S.xdtw_tp[:, p, :], start=True, stop=True),
             reads=["s_B_tp", "s_xdtw_tp%d" % p], writes=[b3])
        P.op("dve", lambda e, c=c: e.tensor_tensor(out=S.tmpS[:].rearrange("s (h p) -> s h p", h=8), in0=S.ST_all[:, c, :].rearrange("s (h p) -> s h p", h=8),
                                                   in1=S.glt[:, :, c:c + 1].to_broadcast([128, 8, 64]), op=ALU.mult),
             reads=["s_ST%d" % c, "s_glt"], writes=["s_tmpS"])
        P.op("dve", lambda e, c=c, b3=b3: e.tensor_tensor(out=S.ST_all[:, c + 1, :], in0=S.tmpS[:], in1=bk[b3][:, :], op=ALU.add),
             reads=["s_tmpS", b3], writes=["s_ST%d" % (c + 1)])
    for c in range(8):
        cs = slice(64 * c, 64 * c + 64)
        p = c % 2
        bx = ["psV", "psB"][p]
        by = "psA%d" % p

        def dm(e, cs=cs, bx=bx, by=by):
            e.matmul(bk[bx][0:64, :], lhsT=S.ones8_bf[:, :], rhs=S.R_hi[:, :, cs], start=True, stop=False)
            e.matmul(bk[bx][0:64, :], lhsT=S.ones8_bf[:, :], rhs=S.R_lo[:, :, cs], start=False, stop=False)
            e.matmul(bk[bx][0:64, :], lhsT=S.nac_hi[:, cs], rhs=S.selrow_bf[:, :], start=False, stop=False)
            e.matmul(bk[bx][0:64, :], lhsT=S.nac_lo[:, cs], rhs=S.selrow_bf[:, :], start=False, stop=True)
            return e.matmul(bk[by][0:64, 0:64], lhsT=S.Bc_bf[:, cs], rhs=S.Cc_bf[:, cs], start=True, stop=True)
        P.op("pe", dm, reads=["s_ones8_bf", "s_R_hi", "s_R_lo", "s_nac_hi", "s_nac_lo", "s_selrow64_bf", "s_Bc_bf", "s_Cc_bf"], writes=[bx, by])
        P.op("dve", lambda e, p=p, bx=bx: e.scalar_tensor_tensor(out=S.E[:, p, :], in0=bk[bx][0:64, :], scalar=0.0, in1=S.negmask[:], op0=ALU.min, op1=ALU.add),
             reads=[bx, "s_negmask"], writes=["s_E%d" % p])
        P.op("act", lambda e, p=p: e.activation(out=S.E[:, p, :], in_=S.E[:, p, :], func=AF.Exp), reads=["s_E%d" % p], writes=["s_E%d" % p])
        P.op("dve", lambda e, c=c, p=p, by=by: e.tensor_tensor(out=S.cbd_all[:, c, :].rearrange("m (h l) -> m h l", h=8),
                                                               in0=S.E[:, p, :].rearrange("m (h l) -> m h l", h=8),
                                                               in1=bk[by][0:64, 0:64].unsqueeze(1).to_broadcast([64, 8, 64]), op=ALU.mult),
             reads=[by, "s_E%d" % p], writes=["s_cbd%d" % c])
    for c in range(8):
        cs = slice(64 * c, 64 * c + 64)
        p = c % 2
        b1 = ["psX", "psY"][p]
        b2 = ["psYY", "psS"][p]

        def ym(e, cs=cs, c=c, b1=b1):
            ins = None
            for h in range(8):
                hs = slice(64 * h, 64 * h + 64)
                e.matmul(bk[b1][0:64, hs], lhsT=S.Cdec[:, h, cs], rhs=S.ST_bf[:, c, hs], start=True, stop=False)
                ins = e.matmul(bk[b1][0:64, hs], lhsT=S.cbd_all[:, c, hs], rhs=S.xdt_all[:, c, hs], start=False, stop=True)
            return ins
        P.op("act", lambda e, c=c: e.activation(out=S.ST_bf[:, c, :], in_=S.ST_all[:, c, :], func=AF.Copy), reads=["s_ST%d" % c], writes=["s_STbf%d" % c])
        P.op("pe", ym, reads=["s_Cdec%d" % h for h in range(8)] + ["s_STbf%d" % c, "s_cbd%d" % c, "s_xdt_all%d" % c], writes=[b1])
        P.op("act", lambda e, p=p, b1=b1: e.activation(out=S.y_sb[:, p, :], in_=bk[b1][0:64, :], func=AF.Copy), reads=[b1], writes=["s_y_sb%d" % p])

        def ytr(e, p=p, b2=b2):
            ins = None
            for j in range(4):
                ins = e.transpose(bk[b2][:, 64 * j:64 * j + 64], S.y_sb[:, p, 128 * j:128 * j + 128], C.ident[0:64, 0:64])
            return ins
        P.op("pe", ytr, reads=["s_y_sb%d" % p, "ident"], writes=[b2])
        eng = "act" if c % 2 == 0 else "dve"
        if eng == "act":
            P.op("act", lambda e, cs=cs, b2=b2: e.activation(out=S.yT[:, :, cs], in_=bk[b2][:, 0:256].rearrange("p (j l) -> p j l", j=4), func=AF.Copy),
                 reads=[b2], writes=["s_yT"])
        else:
            P.op("dve", lambda e, cs=cs, b2=b2: e.tensor_copy(out=S.yT[:, :, cs], in_=bk[b2][:, 0:256].rearrange("p (j l) -> p j l", j=4)),
                 reads=[b2], writes=["s_yT"])
    P.op("act", lambda e: e.activation(out=S.ST_all[:, 0, :], in_=S.ST_all[:, 8, :], func=AF.Copy), reads=["s_ST8"], writes=["s_ST0"])
    for j in range(4):
        gt = S.xdtT[j]; gk = "s_xdtT%d" % j
        sq = S.xdtwT[j]; sk = "s_xdtwT%d" % j
        P.op("dve", lambda e, j=j, gt=gt: e.scalar_tensor_tensor(out=gt[:], in0=xs[j][:], scalar=S.dcol[:, j:j + 1], in1=S.yT[:, j, :], op0=ALU.mult, op1=ALU.add),
             reads=["s_xc%d" % j, "s_dcol", "s_yT"], writes=[gk])
        P.op("act", lambda e, j=j: e.activation(out=S.zs[:], in_=S.z[j][:], func=AF.Silu), reads=["s_z%d" % j], writes=["s_zs"])
        P.op("dve", lambda e, gt=gt: e.tensor_tensor(out=gt[:], in0=gt[:], in1=S.zs[:], op=ALU.mult), reads=[gk, "s_zs"], writes=[gk])
        P.op("act", lambda e, gt=gt, sq=sq: e.activation(out=sq[:], in_=gt[:], func=AF.Square), reads=[gk], writes=[sk])

    def nm(e):
        ins = None
        for j in range(4):
            ins = e.matmul(psB[:, :], lhsT=C.ones_f[:], rhs=S.xdtwT[j][:], start=(j == 0), stop=(j == 3))
        return ins
    P.op("pe", nm, reads=["ones_f"] + ["s_xdtwT%d" % j for j in range(4)], writes=["psB"])
    P.op("act", lambda e: e.activation(out=S.rs[:], in_=psB[:, :], func=AF.Sqrt, bias=C.cst["eps"][:, 0:1], scale=1.0 / 512), reads=["psB", "cst"], writes=["s_rs"])
    P.op("dve", lambda e: e.reciprocal(out=S.rs[:], in_=S.rs[:]), reads=["s_rs"], writes=["s_rs"])
    for j in range(4):
        P.op("dve", lambda e, j=j: e.scalar_tensor_tensor(out=S.res[:], in0=S.xdtT[j][:], scalar=S.nw[:, j:j + 1], in1=S.rs[:], op0=ALU.mult, op1=ALU.mult),
             reads=["s_xdtT%d" % j, "s_nw", "s_rs"], writes=["s_res"])
        P.dma_op("sp", out[128 * j:128 * j + 128, tsl], S.res[:], reads=["s_res"], is_output=True)


def build_B1(T):
    nc = new_nc()
    pT = dram_in(nc, "pT", [1280, T])
    dtT = dram_in(nc, "dtT", [8, T])
    d = {"s_cw": dram_in(nc, "s_cw", [128, 24]), "s_cb": dram_in(nc, "s_cb", [128, 6]), "s_dcol": dram_in(nc, "s_dcol", [128, 4]),
         "s_nw": dram_in(nc, "s_nw", [128, 4]), "s_dtb": dram_in(nc, "s_dtb", [8, 1]), "s_alog": dram_in(nc, "s_alog", [8, 1])}
    cd = {"ident": dram_in(nc, "ident", [128, 128]), "selpair": dram_in(nc, "selpair", [8, 512]), "selones": dram_in(nc, "selones", [8, 1024]),
          "sel8": dram_in(nc, "sel8", [8, 8]), "ones8": dram_in(nc, "ones8", [8, 64]), "rmask8": dram_in(nc, "rmask8", [8, 512]),
          "negmask": dram_in(nc, "negmask", [64, 512]), "selrow64": dram_in(nc, "selrow64", [8, 512])}
    out = dram_out(nc, "yT", [512, T])
    with ExitStack() as st:
        P = Prog(nc, st)
        C = Ctx()
        mixer_common(P, nc, C, cd)
        S = ssd_setup(P, nc, C)
        ssd_init(P, C, S, d, cd)
        for s in range(T // ST):
            ssd_supertile(P, C, S, s, pT, dtT, out, T)
        P.finish()
        print("B1 ops", P.n_ops())
    return nc

NCORES = 8
SEQ = 16384
TOKC = SEQ // NCORES

OFF = {"gq": 0, "gk": 1024, "gv": 2048, "gz": 3072, "gb": 4096, "ga": 4104, "aq": 4112, "ak": 5136, "av": 6160}


def mod_layout(modrow):
    return np.ascontiguousarray(modrow.reshape(6, 16, 128).transpose(2, 0, 1).reshape(128, 96))


def col_layout(v):
    return np.ascontiguousarray(v.reshape(16, 128).T)


def build_mod():
    nc = new_nc()
    c_in = dram_in(nc, "c_in", [128, 16])
    w_in = dram_in(nc, "w_in", [2, 2048, 1536])
    b_in = dram_in(nc, "b_in", [128, 24])
    out = dram_out(nc, "out", [128, 24])
    with ExitStack() as st:
        P = Prog(nc, st)
        ct = P.sb("ct", [128, 16]); ca = P.sb("ca", [128, 16])
        bt = P.sb("bt", [128, 24]); ot = P.sb("ot", [128, 24])
        wt = P.sb("wt", [128, 16, 1536])
        acc = P.ps("acc", [128, 512])
        P.dma_op("sp", ct[:], c_in, writes=["ct"])
        P.dma_op("sp", bt[:], b_in, writes=["bt"])
        P.op("act", lambda e: e.activation(out=ca[:], in_=ct[:], func=AF.Silu), reads=["ct"], writes=["ca"])
        for l in range(2):
            P.dma_op("sp", wt[:], w_in[l].rearrange("(kt p) n -> p kt n", p=128), writes=["wt"])

            def mm(e, l=l):
                ins = None
                for j in range(12):
                    for kt in range(16):
                        ins = e.matmul(acc[:, l * 12 + j:l * 12 + j + 1], lhsT=wt[:, kt, j * 128:(j + 1) * 128], rhs=ca[:, kt:kt + 1],
                                       start=(kt == 0), stop=(kt == 15))
                return ins
            P.op("pe", mm, reads=["wt", "ca"], writes=["acc"])
        P.op("dve", lambda e: e.tensor_tensor(out=ot[:], in0=acc[:, 0:24], in1=bt[:], op=ALU.add), reads=["acc", "bt"], writes=["ot"])
        P.dma_op("sp", out, ot[:], reads=["ot"], is_output=True)
        P.finish()
    return nc


def run(nc, in_maps):
    res = run_bass_kernel_spmd(nc, in_maps, core_ids=list(range(NCORES)))
    return res.results


def kernel(x, c, mod_w, mod_b, norm_mix_w, norm_mlp_w, mlp_w1, mlp_w2,
           ab_w_in, gdn_conv_w, gdn_a_log, gdn_dt_bias, gdn_norm_w,
           attn_q_norm_w, attn_k_norm_w, attn_rel_bias, ab_w_out,
           ssd_w_in, ssd_conv_w, ssd_conv_b, ssd_dt_bias, ssd_a_log, ssd_d,
           ssd_norm_w, ssd_w_out):
    f = lambda a: np.ascontiguousarray(np.asarray(a, dtype=np.float32))
    x = f(x); c = f(c); mod_w = f(mod_w); mod_b = f(mod_b)
    cl = np.ascontiguousarray(c.reshape(16, 128).T)
    ims = []
    for core in range(NCORES):
        sl = slice(core * 1536, (core + 1) * 1536)
        b = np.stack([mod_b[l, sl].reshape(12, 128).T for l in range(2)], axis=1).reshape(128, 24)
        ims.append({"c_in": cl, "w_in": np.ascontiguousarray(mod_w[:, :, sl]), "b_in": np.ascontiguousarray(b)})
    r = run(build_mod(), ims)
    mod = np.zeros((2, 12288), np.float32)
    for core in range(NCORES):
        o = r[core]["out"].reshape(128, 2, 12)
        for l in range(2):
            mod[l, core * 1536:(core + 1) * 1536] = o[:, l, :].T.reshape(-1)
    modl = [mod_layout(mod[l]) for l in range(2)]
    xT = [np.ascontiguousarray(x[0, cc * TOKC:(cc + 1) * TOKC].T) for cc in range(NCORES)]

    perm0 = np.concatenate([np.concatenate([OFF[n] + 128 * i + np.arange(128) for n in ("gq", "gk", "gv", "gz", "aq", "ak", "av")]) for i in range(8)]
                           + [np.arange(4096, 4112)])
    w0 = np.ascontiguousarray(f(ab_w_in)[0][:, perm0])
    nw0 = col_layout(f(norm_mix_w)[0])
    r = run(build_stageA(7184, TOKC), [{"xT": xT[cc], "w": w0, "nw": nw0, "modl": modl[0]} for cc in range(NCORES)])
    projT = [r[cc]["projT"] for cc in range(NCORES)]
    hc = host_consts()
    cw = f(gdn_conv_w)[0]
    ims = []
    for i in range(8):
        pT = np.ascontiguousarray(np.concatenate([projT[cc][896 * i:896 * (i + 1)] for cc in range(NCORES)], axis=1))
        gates = np.ascontiguousarray(np.concatenate([projT[cc][[7168 + i, 7176 + i]] for cc in range(NCORES)], axis=1))
        cwh = np.concatenate([cw[:, 1024 * j + 128 * i: 1024 * j + 128 * (i + 1)].T for j in range(3)], axis=1)
        im = {"pT": pT, "gates": gates,
              "qw": f(attn_q_norm_w)[0].reshape(128, 1).copy(), "kw": f(attn_k_norm_w)[0].reshape(128, 1).copy(),
              "bias": rel_bias_toeplitz(f(attn_rel_bias)[0, i]),
              "g_cw": np.ascontiguousarray(cwh), "g_alog": f(gdn_a_log)[0, i].reshape(1, 1).copy(), "g_dtb": f(gdn_dt_bias)[0, i].reshape(1, 1).copy(),
              "g_nw": f(gdn_norm_w)[0].reshape(128, 1).copy()}
        im.update(hc)
        ims.append(im)
    del projT
    r = run(build_B0(SEQ), ims)
    oTh = [r[i]["oT"] for i in range(8)]
    ims = []
    wo0 = f(ab_w_out)[0]; w1 = f(mlp_w1); w2 = f(mlp_w2)
    for cc in range(NCORES):
        ts = slice(cc * TOKC, (cc + 1) * TOKC)
        oT = np.ascontiguousarray(np.concatenate([oTh[i][0:128, ts] for i in range(8)] + [oTh[i][128:256, ts] for i in range(8)], axis=0))
        ims.append({"xT": xT[cc], "oT": oT, "wo": wo0, "w1": w1[0], "w2": w2[0], "nw": col_layout(f(norm_mlp_w)[0]), "modl": modl[0]})
    r = run(build_stageC2(2048, TOKC), ims)
    xT = [r[cc]["x2T"] for cc in range(NCORES)]

    perm1 = np.concatenate([np.concatenate([512 * g + np.arange(512), 4096 + 512 * g + np.arange(512), 8192 + 128 * g + np.arange(128),
                                            9216 + 128 * g + np.arange(128)]) for g in range(8)] + [10240 + np.arange(64)])
    ws = np.ascontiguousarray(f(ssd_w_in)[0][:, perm1])
    nw1 = col_layout(f(norm_mix_w)[1])
    r = run(build_stageA(10304, TOKC), [{"xT": xT[cc], "w": ws, "nw": nw1, "modl": modl[1]} for cc in range(NCORES)])
    projT = [r[cc]["projT"] for cc in range(NCORES)]
    shc = ssd_host_consts()
    scw = f(ssd_conv_w)[0]; scb = f(ssd_conv_b)[0]
    ims = []
    for g in range(8):
        pT = np.ascontiguousarray(np.concatenate([projT[cc][1280 * g:1280 * (g + 1)] for cc in range(NCORES)], axis=1))
        dtT = np.ascontiguousarray(np.concatenate([projT[cc][10240 + 8 * g:10240 + 8 * (g + 1)] for cc in range(NCORES)], axis=1))
        chans = [np.arange(512 * g + 128 * j, 512 * g + 128 * (j + 1)) for j in range(4)] + [np.arange(4096 + 128 * g, 4096 + 128 * (g + 1)),
                                                                                           np.arange(5120 + 128 * g, 5120 + 128 * (g + 1))]
        cwl = np.concatenate([scw[:, ch].T for ch in chans], axis=1)
        cbl = np.stack([scb[ch] for ch in chans], axis=1)
        dsk = f(ssd_d)[0][8 * g:8 * (g + 1)]
        dcol = np.stack([np.repeat(dsk[2 * j:2 * j + 2], 64) for j in range(4)], axis=1)
        nwg = f(ssd_norm_w)[0][512 * g:512 * (g + 1)].reshape(4, 128).T
        im = {"pT": pT, "dtT": dtT, "s_cw": np.ascontiguousarray(cwl), "s_cb": np.ascontiguousarray(cbl), "s_dcol": np.ascontiguousarray(dcol),
              "s_nw": np.ascontiguousarray(nwg), "s_dtb": f(ssd_dt_bias)[0][8 * g:8 * (g + 1)].reshape(8, 1).copy(),
              "s_alog": f(ssd_a_log)[0][8 * g:8 * (g + 1)].reshape(8, 1).copy()}
        im.update(shc)
        ims.append(im)
    del projT
    r = run(build_B1(SEQ), ims)
    yTg = [r[g]["yT"] for g in range(8)]
    ims = []
    wo1 = f(ssd_w_out)[0]
    for cc in range(NCORES):
        ts = slice(cc * TOKC, (cc + 1) * TOKC)
        oT = np.ascontiguousarray(np.concatenate([yTg[g][:, ts] for g in range(8)], axis=0))
        ims.append({"xT": xT[cc], "oT": oT, "wo": wo1, "w1": w1[1], "w2": w2[1], "nw": col_layout(f(norm_mlp_w)[1]), "modl": modl[1]})
    r = run(build_stageC2(4096, TOKC), ims)
    out = np.concatenate([r[cc]["x2T"].T for cc in range(NCORES)], axis=0)[None]
    return np.ascontiguousarray(out.astype(np.float32))
```

```python
import numpy as np
from contextlib import ExitStack
import concourse.bass as bass
import concourse.mybir as mybir
from concourse.bass_utils import run_bass_kernel_spmd

F32 = mybir.dt.float32
BF16 = mybir.dt.bfloat16
AF = mybir.ActivationFunctionType
ALU = mybir.AluOpType
AX = mybir.AxisListType

ENGS = ["pe", "dve", "act", "pool", "sp"]
EPOCH = 30000
SAME_ENGINE_SYNC = True


class Prog:
    def __init__(self, nc, stack, n_dma_sems=12):
        self.nc = nc
        self.stack = stack
        self.ops = {e: [] for e in ENGS}
        self.known = {e: {} for e in ENGS}
        self.tiles = {}
        self.semid = 0
        self.cur = {}
        for e in ["pe", "dve", "act", "pool"]:
            self._new_sem(e)
        self.dma = {}
        for q in ["sp", "pool", "act"]:
            sems = []
            for j in range(n_dma_sems):
                s = stack.enter_context(nc.semaphore(f"dq_{q}_{j}"))
                self.semid += 1
                sems.append([s, self.semid, 0])
            self.dma[q] = [sems, 0]
        self.out_events = []

    def _new_sem(self, e):
        s = self.stack.enter_context(self.nc.semaphore(f"s_{e}_{self.semid}"))
        self.semid += 1
        self.cur[e] = [s, self.semid, 0]

    def sb(self, name, shape, dtype=F32):
        return self.stack.enter_context(self.nc.sbuf_tensor("sb_" + name, list(shape), dtype))

    def ps(self, name, shape, dtype=F32):
        return self.stack.enter_context(self.nc.psum_tensor("pp_" + name, list(shape), dtype))

    def _st(self, k):
        st = self.tiles.get(k)
        if st is None:
            st = {"w": None, "r": {}}
            self.tiles[k] = st
        return st

    def _collect(self, eng, reads, writes, same_sync):
        waits = {}

        def add(ev):
            if ev is None:
                return
            s, sid, val = ev
            if sid == self.cur.get(eng, [None, -1])[1] and not same_sync:
                return
            if self.known[eng].get(sid, 0) >= val:
                return
            if sid not in waits or waits[sid][2] < val:
                waits[sid] = ev

        for k in reads:
            add(self._st(k)["w"])
        for k in writes:
            st = self._st(k)
            add(st["w"])
            for ev in st["r"].values():
                add(ev)
        for sid, ev in waits.items():
            self.ops[eng].append(("wait", ev[0], ev[2]))
            self.known[eng][sid] = ev[2]

    def _record(self, ev, reads, writes):
        for k in reads:
            st = self._st(k)
            st["r"][ev[1]] = ev
        for k in writes:
            st = self._st(k)
            st["w"] = ev
            st["r"] = {}

    def op(self, eng, fn, reads=(), writes=(), same_sync=None):
        if same_sync is None:
            same_sync = SAME_ENGINE_SYNC and eng != "pe"
        self._collect(eng, reads, writes, same_sync)
        c = self.cur[eng]
        if c[2] >= EPOCH:
            self._new_sem(eng)
            c = self.cur[eng]
        c[2] += 1
        ev = (c[0], c[1], c[2])
        self.ops[eng].append(("op", fn, c[0], 1))
        self._record(ev, reads, writes)
        return ev

    def dma_op(self, q, out, in_, reads=(), writes=(), is_output=False, **kw):
        sems, idx = self.dma[q]
        slot = sems[idx % len(sems)]
        self.dma[q][1] = idx + 1
        s, sid, uses = slot
        if uses > 0 and self.known[q].get(sid, 0) < 16 * uses:
            self.ops[q].append(("wait", s, 16 * uses))
            self.known[q][sid] = 16 * uses
        self._collect(q, reads, writes, True)
        slot[2] = uses + 1
        ev = (s, sid, 16 * (uses + 1))

        def fn(e, out=out, in_=in_, kw=kw):
            return e.dma_start(out=out, in_=in_, **kw)

        self.ops[q].append(("op", fn, s, 16))
        self._record(ev, reads, writes)
        if is_output:
            self.out_events.append(ev)
        return ev

    def finish(self):
        best = {}
        for ev in self.out_events:
            if ev[1] not in best or best[ev[1]][2] < ev[2]:
                best[ev[1]] = ev
        for ev in best.values():
            self.ops["sp"].append(("wait", ev[0], ev[2]))
        nc = self.nc
        ops = self.ops

        def replay(name, e):
            for o in ops[name]:
                if o[0] == "wait":
                    e.wait_ge(o[1], o[2])
                else:
                    ins = o[1](e)
                    ins.then_inc(o[2], o[3])

        with nc.Block() as block:
            @block.tensor
            def _(e):
                replay("pe", e)

            @block.vector
            def _(e):
                replay("dve", e)

            @block.scalar
            def _(e):
                replay("act", e)

            @block.gpsimd
            def _(e):
                replay("pool", e)

            @block.sync
            def _(e):
                replay("sp", e)

    def n_ops(self):
        return {e: len(v) for e, v in self.ops.items()}


D = 2048
KT = 16
TT = 512
EPS = 1e-6


def new_nc():
    return bass.Bass("TRN2", target_bir_lowering=False)


def dram_in(nc, name, shape, dtype=F32):
    return nc.dram_tensor(name, list(shape), dtype, kind="ExternalInput").ap()


def dram_out(nc, name, shape, dtype=F32):
    return nc.dram_tensor(name, list(shape), dtype, kind="ExternalOutput").ap()


class Ctx:
    pass


def setup_common(P):
    C = Ctx()
    C.ones_bf = P.sb("ones_bf", [128, 128], BF16)
    P.op("pool", lambda e: e.memset(C.ones_bf[:], 1.0), writes=["ones_bf"])
    C.ps = [P.ps(f"ps{i}", [128, 512]) for i in range(8)]
    C.rot = 0
    C.ev = 0
    return C


def emit_mod_cols(P, modt, nwt, which):
    acol = P.sb("acol%d" % which, [128, 16])
    s_sh = 0 if which == 0 else 3
    P.op("dve", lambda e: e.scalar_tensor_tensor(out=acol[:], in0=modt[:, (s_sh + 1) * 16:(s_sh + 2) * 16], scalar=1.0,
                                                 in1=nwt[:], op0=ALU.add, op1=ALU.mult),
         reads=["modt", "nwt%d" % which], writes=["acol"])
    return acol, modt[:, s_sh * 16:(s_sh + 1) * 16]


def make_scr(P):
    scr = {}
    scr["rstd"] = P.sb("rstd", [128, 512])
    scr["tmp"] = [P.sb("tmp0", [128, 512]), P.sb("tmp1", [128, 512])]
    scr["eps"] = P.sb("epst", [128, 1])
    P.op("pool", lambda e: e.memset(scr["eps"][:], EPS), writes=["eps"])
    return scr


def emit_dense(P, C, w_dram, K, ncols, rhs_fn, rhs_keys, ntt, evac_fn, wbufs, wq="pool", col0=0):
    nk = K // 128
    gcols = 8192 // nk
    ngroups = (ncols + gcols - 1) // gcols
    for g in range(ngroups):
        c0 = g * gcols
        csz = min(gcols, ncols - c0)
        wb = wbufs[C.rot_w % 2]
        wk = "wbuf%d" % (C.rot_w % 2)
        C.rot_w += 1
        wv = wb[:, 0:nk * csz].rearrange("p (k n) -> p k n", k=nk)
        src = w_dram[:, col0 + c0:col0 + c0 + csz].rearrange("(k p) n -> p k n", p=128)
        P.dma_op(wq, wv, src, writes=[wk])
        nm = (csz + 127) // 128
        for mi in range(nm):
            msz = min(128, csz - mi * 128)
            m = (c0 // 128) + mi
            for tt in range(ntt):
                bank = C.rot % 6
                C.rot += 1
                pst = C.ps[bank]

                def mm(e, wv=wv, mi=mi, msz=msz, tt=tt, pst=pst):
                    ins = None
                    for kt in range(nk):
                        ins = e.matmul(pst[0:msz, :], lhsT=wv[:, kt, mi * 128:mi * 128 + msz], rhs=rhs_fn(kt, tt),
                                       start=(kt == 0), stop=(kt == nk - 1))
                    return ins
                P.op("pe", mm, reads=[wk] + rhs_keys(tt), writes=["ps%d" % bank])
                evac_fn(m, msz, tt, pst, "ps%d" % bank)


def build_stageA(NC, TOK=2048):
    nc = new_nc()
    xT = dram_in(nc, "xT", [D, TOK])
    w = dram_in(nc, "w", [D, NC])
    nw = dram_in(nc, "nw", [128, 16])
    modl = dram_in(nc, "modl", [128, 96])
    out = dram_out(nc, "projT", [NC, TOK])
    ntt = TOK // TT
    with ExitStack() as st:
        P = Prog(nc, st)
        C = setup_common(P)
        C.rot_w = 0
        scr = make_scr(P)
        modt = P.sb("modt", [128, 96])
        nwt = P.sb("nwt", [128, 16])
        P.dma_op("sp", modt[:], modl, writes=["modt"])
        P.dma_op("sp", nwt[:], nw, writes=["nwt0"])
        acol, shcol = emit_mod_cols(P, modt, nwt, 0)
        P.tiles["shcol"] = P.tiles["modt"]
        xt = P.sb("xt", [128, 16, TT])
        sq = P.sb("sq", [128, 16, TT], BF16)
        hT = P.sb("hT", [128, 16, TOK], BF16)
        wbufs = [P.sb("wb0", [128, 8192], BF16), P.sb("wb1", [128, 8192], BF16)]
        ot = [P.sb("ot0", [128, TOK]), P.sb("ot1", [128, TOK])]
        xsrc = xT.rearrange("(k p) t -> p k t", p=128)
        for tt in range(ntt):
            P.dma_op("sp", xt[:], xsrc[:, :, tt * TT:(tt + 1) * TT], writes=["xt"])
            emit_adaln_xt(P, C, xt, hT[:, :, tt * TT:(tt + 1) * TT], "hT%d" % tt, acol, shcol, sq, scr)

        state = {"n": 0}

        def evac(m, msz, tt, pst, pk):
            b = (m % 2)
            o = ot[b]
            okey = "ot%d_%d" % (b, tt)
            eng = "dve" if C.ev % 2 == 0 else "act"
            C.ev += 1
            if eng == "dve":
                P.op("dve", lambda e: e.tensor_copy(out=o[0:msz, tt * TT:(tt + 1) * TT], in_=pst[0:msz, :]), reads=[pk], writes=[okey])
            else:
                P.op("act", lambda e: e.activation(out=o[0:msz, tt * TT:(tt + 1) * TT], in_=pst[0:msz, :], func=AF.Copy), reads=[pk], writes=[okey])
            if tt == ntt - 1:
                P.dma_op("sp", out[m * 128:m * 128 + msz, :], o[0:msz, :], reads=["ot%d_%d" % (b, t) for t in range(ntt)], is_output=True)

        emit_dense(P, C, w, D, NC, lambda kt, tt: hT[:, kt, tt * TT:(tt + 1) * TT], lambda tt: ["hT%d" % tt], ntt, evac, wbufs)
        P.finish()
        print("stageA ops", P.n_ops())
    return nc


def emit_adaln_xt(P, C, xt, ht, hk, acol, shcol, sq, scr):
    class _K:
        pass
    emit_adaln2(P, C, xt, ["xt"], ht, hk, acol, shcol, sq, "sq", scr)


def emit_adaln2(P, C, xt, xkeys, ht, hk, a_col, sh_col, sq, sqk, scr):
    nbank = C.ps[7]
    for q in range(4):
        if q % 2 == 0:
            P.op("act", lambda e, q=q: e.activation(out=sq[:, 4 * q:4 * q + 4, :], in_=xt[:, 4 * q:4 * q + 4, :], func=AF.Square),
                 reads=xkeys, writes=[sqk + str(q)])
        else:
            P.op("pool", lambda e, q=q: e.tensor_tensor(out=sq[:, 4 * q:4 * q + 4, :], in0=xt[:, 4 * q:4 * q + 4, :],
                                                      in1=xt[:, 4 * q:4 * q + 4, :], op=ALU.mult),
                 reads=xkeys, writes=[sqk + str(q)])

    def mm(e):
        ins = None
        for kt in range(KT):
            ins = e.matmul(nbank[:, :], lhsT=C.ones_bf[:], rhs=sq[:, kt, :], start=(kt == 0), stop=(kt == KT - 1))
        return ins
    P.op("pe", mm, reads=["ones_bf"] + [sqk + str(q) for q in range(4)], writes=["ps7"])
    rstd = scr["rstd"]
    P.op("act", lambda e: e.activation(out=rstd[:], in_=nbank[:, :], func=AF.Sqrt, bias=scr["eps"][:, 0:1], scale=1.0 / D),
         reads=["ps7", "eps"], writes=["rstd"])
    P.op("dve", lambda e: e.reciprocal(out=rstd[:], in_=rstd[:]), reads=["rstd"], writes=["rstd"])
    for kt in range(KT):
        tmp = scr["tmp"][kt % 2]
        tk = "tmp%d" % (kt % 2)
        P.op("dve", lambda e, kt=kt, tmp=tmp: e.tensor_tensor(out=tmp[:], in0=xt[:, kt, :], in1=rstd[:], op=ALU.mult),
             reads=xkeys + ["rstd"], writes=[tk])
        P.op("act", lambda e, kt=kt, tmp=tmp: e.activation(out=ht[:, kt, :], in_=tmp[:], func=AF.Identity,
                                                         bias=sh_col[:, kt:kt + 1], scale=a_col[:, kt:kt + 1]),
             reads=[tk, "acol", "modt"], writes=[hk])


def build_stageC(KO, TOK=2048):
    nc = new_nc()
    xT = dram_in(nc, "xT", [D, TOK])
    oT = dram_in(nc, "oT", [KO, TOK])
    wo = dram_in(nc, "wo", [KO, D])
    w1 = dram_in(nc, "w1", [D, 4 * D])
    w2 = dram_in(nc, "w2", [4 * D, D])
    nw = dram_in(nc, "nw", [128, 16])
    modl = dram_in(nc, "modl", [128, 96])
    out = dram_out(nc, "x2T", [D, TOK])
    ntt = TOK // TT
    nko = KO // 128
    with ExitStack() as st:
        P = Prog(nc, st)
        C = setup_common(P)
        C.rot_w = 0
        scr = make_scr(P)
        modt = P.sb("modt", [128, 96])
        nwt = P.sb("nwt", [128, 16])
        P.dma_op("sp", modt[:], modl, writes=["modt"])
        P.dma_op("sp", nwt[:], nw, writes=["nwt1"])
        acol, shcol = emit_mod_cols(P, modt, nwt, 1)
        g1 = modt[:, 32:48]
        g2 = modt[:, 80:96]
        xt = P.sb("xt", [128, 16, TT])
        ob = P.sb("ob", [128, max(nko, 16), TT], BF16)
        aT = P.sb("aT", [128, 64, TT], BF16)
        wbufs = [P.sb("wb0", [128, 8192], BF16), P.sb("wb1", [128, 8192], BF16)]
        ot = [P.sb("ot0", [128, TT]), P.sb("ot1", [128, TT])]
        rl = [P.sb("rl0", [128, TT]), P.sb("rl1", [128, TT])]
        xsrc = xT.rearrange("(k p) t -> p k t", p=128)
        osrc = oT.rearrange("(k p) t -> p k t", p=128)
        odst = out.rearrange("(k p) t -> p k t", p=128)
        for tt in range(ntt):
            tsl = slice(tt * TT, (tt + 1) * TT)
            P.dma_op("sp", xt[:], xsrc[:, :, tsl], writes=["xt"])
            P.dma_op("pool", ob[:, 0:nko, :], osrc[:, :, tsl], writes=["ob"])

            def evac1(m, msz, t_, pst, pk):
                P.op("dve", lambda e: e.scalar_tensor_tensor(out=xt[:, m, :], in0=pst[:, :], scalar=g1[:, m:m + 1], in1=xt[:, m, :],
                                                             op0=ALU.mult, op1=ALU.add),
                     reads=[pk, "modt", "xt"], writes=["x1_%d" % m])
            emit_dense(P, C, wo, KO, D, lambda kt, t_: ob[:, kt, :], lambda t_: ["ob"], 1, evac1, wbufs)
            x1keys = ["x1_%d" % m for m in range(16)]
            emit_adaln2(P, C, xt, x1keys, ob[:, 0:16, :], "ob", acol, shcol, aT[:, 0:16, :], "aT", scr)

            def evac2(f, msz, t_, pst, pk):
                r = rl[f % 2]
                rk = "rl%d" % (f % 2)
                P.op("act", lambda e: e.activation(out=r[:], in_=pst[:, :], func=AF.Square), reads=[pk], writes=[rk])
                P.op("dve", lambda e: e.scalar_tensor_tensor(out=aT[:, f, :], in0=pst[:, :], scalar=0.0, in1=r[:],
                                                             op0=ALU.is_gt, op1=ALU.mult),
                     reads=[pk, rk], writes=["aT_%d" % f])
            emit_dense(P, C, w1, D, 4 * D, lambda kt, t_: ob[:, kt, :], lambda t_: ["ob"], 1, evac2, wbufs)
            akeys = ["aT_%d" % f for f in range(64)]

            def evac3(m, msz, t_, pst, pk):
                o = ot[m % 2]
                okey = "ot%d" % (m % 2)
                P.op("dve", lambda e: e.scalar_tensor_tensor(out=o[:], in0=pst[:, :], scalar=g2[:, m:m + 1], in1=xt[:, m, :],
                                                             op0=ALU.mult, op1=ALU.add),
                     reads=[pk, "modt", "x1_%d" % m], writes=[okey])
                P.dma_op("sp", odst[:, m, tsl], o[:], reads=[okey], is_output=True)
            emit_dense(P, C, w2, 4 * D, D, lambda kt, t_: aT[:, kt, :], lambda t_: akeys, 1, evac3, wbufs)
            _fold(P, "xt", x1keys)
            _fold(P, "aT0", akeys); _fold(P, "aT1", akeys); _fold(P, "aT2", akeys); _fold(P, "aT3", akeys)
        P.finish()
        print("stageC ops", P.n_ops())
    return nc


def _fold(P, dst, keys):
    d = P._st(dst)
    for k in keys:
        s = P._st(k)
        if s["w"] is not None:
            d["r"][("w", s["w"][1])] = s["w"] if ("w", s["w"][1]) not in d["r"] or d["r"][("w", s["w"][1])][2] < s["w"][2] else d["r"][("w", s["w"][1])]
        for sid, ev in s["r"].items():
            key = ("r", ev[1])
            if key not in d["r"] or d["r"][key][2] < ev[2]:
                d["r"][key] = ev


def build_stageC2(KO, TOK=2048, CT=1024):
    nc = new_nc()
    xT = dram_in(nc, "xT", [D, TOK])
    oT = dram_in(nc, "oT", [KO, TOK])
    wo = dram_in(nc, "wo", [KO, D])
    w1 = dram_in(nc, "w1", [D, 4 * D])
    w2 = dram_in(nc, "w2", [4 * D, D])
    nw = dram_in(nc, "nw", [128, 16])
    modl = dram_in(nc, "modl", [128, 96])
    out = dram_out(nc, "x2T", [D, TOK])
    nct = TOK // CT
    nsub = CT // TT
    nkh = KO // D
    with ExitStack() as st:
        P = Prog(nc, st)
        C = setup_common(P)
        C.rot_w = 0
        scr = make_scr(P)
        modt = P.sb("modt", [128, 96])
        nwt = P.sb("nwt", [128, 16])
        P.dma_op("sp", modt[:], modl, writes=["modt"])
        P.dma_op("sp", nwt[:], nw, writes=["nwt1"])
        acol, shcol = emit_mod_cols(P, modt, nwt, 1)
        g1 = modt[:, 32:48]
        g2 = modt[:, 80:96]
        xt = P.sb("xt", [128, 16, CT])
        ob = P.sb("ob", [128, 16, CT], BF16)
        aT = P.sb("aT", [128, 16, CT], BF16)
        wbufs = [P.sb("wb0", [128, 8192], BF16), P.sb("wb1", [128, 8192], BF16)]
        rl = [P.sb("rl0", [128, TT]), P.sb("rl1", [128, TT])]
        xsrc = xT.rearrange("(k p) t -> p k t", p=128)
        osrc = oT.rearrange("(k p) t -> p k t", p=128)
        odst = out.rearrange("(k p) t -> p k t", p=128)
        for ct in range(nct):
            csl = slice(ct * CT, (ct + 1) * CT)
            for hh in range(nsub):
                P.dma_op("sp", xt[:, :, hh * TT:(hh + 1) * TT], xsrc[:, :, ct * CT + hh * TT:ct * CT + (hh + 1) * TT], writes=["xt%d" % hh])
            xkeys = [["x_%d_%d" % (m, hh) for m in range(16)] for hh in range(nsub)]
            for hh in range(nsub):
                for m in range(16):
                    P.tiles["x_%d_%d" % (m, hh)] = {"w": P._st("xt%d" % hh)["w"], "r": {}}
            for kh in range(nkh):
                for hh in range(nsub):
                    P.dma_op("pool", ob[:, :, hh * TT:(hh + 1) * TT], osrc[:, kh * 16:(kh + 1) * 16, ct * CT + hh * TT:ct * CT + (hh + 1) * TT], writes=["ob%d" % hh])

                def evac1(m, msz, hh, pst, pk):
                    hs = slice(hh * TT, (hh + 1) * TT)
                    P.op("dve", lambda e: e.scalar_tensor_tensor(out=xt[:, m, hs], in0=pst[:, :], scalar=g1[:, m:m + 1], in1=xt[:, m, hs],
                                                                 op0=ALU.mult, op1=ALU.add),
                         reads=[pk, "modt", "x_%d_%d" % (m, hh)], writes=["x_%d_%d" % (m, hh)])
                emit_dense(P, C, wo[kh * D:(kh + 1) * D, :], D, D, lambda kt, hh: ob[:, kt, hh * TT:(hh + 1) * TT], lambda hh: ["ob%d" % hh], nsub, evac1, wbufs)
            for hh in range(nsub):
                hs = slice(hh * TT, (hh + 1) * TT)
                emit_adaln2(P, C, xt[:, :, hs], xkeys[hh], ob[:, :, hs], "ob%d" % hh, acol, shcol, aT[:, :, hs], "aTsq%d_" % hh, scr)
            for fq in range(4):
                def evac2(f, msz, hh, pst, pk):
                    hs = slice(hh * TT, (hh + 1) * TT)
                    r = rl[C.ev % 2]
                    rk = "rl%d" % (C.ev % 2)
                    C.ev += 1
                    wr = ["aT_%d_%d" % (f, hh)]
                    if f < 4:
                        wr = wr + ["aTsq%d_%d" % (hh, q) for q in range(4)]
                    P.op("act", lambda e: e.activation(out=r[:], in_=pst[:, :], func=AF.Square), reads=[pk], writes=[rk])
                    P.op("dve", lambda e: e.scalar_tensor_tensor(out=aT[:, f, hs], in0=pst[:, :], scalar=0.0, in1=r[:],
                                                                 op0=ALU.is_gt, op1=ALU.mult),
                         reads=[pk, rk], writes=wr)
                emit_dense(P, C, w1, D, D, lambda kt, hh: ob[:, kt, hh * TT:(hh + 1) * TT], lambda hh: ["ob%d" % hh], nsub, evac2, wbufs, col0=fq * D)

                def evac3(m, msz, hh, pst, pk):
                    hs = slice(hh * TT, (hh + 1) * TT)
                    P.op("dve", lambda e: e.scalar_tensor_tensor(out=xt[:, m, hs], in0=pst[:, :], scalar=g2[:, m:m + 1], in1=xt[:, m, hs],
                                                                 op0=ALU.mult, op1=ALU.add),
                         reads=[pk, "modt", "x_%d_%d" % (m, hh)], writes=["x_%d_%d" % (m, hh)])
                    if fq == 3:
                        P.dma_op("sp", odst[:, m, ct * CT + hh * TT:ct * CT + (hh + 1) * TT], xt[:, m, hs], reads=["x_%d_%d" % (m, hh)], is_output=True)
                emit_dense(P, C, w2[fq * D:(fq + 1) * D, :], D, D, lambda kt, hh: aT[:, kt, hh * TT:(hh + 1) * TT],
                           lambda hh: ["aT_%d_%d" % (f, hh) for f in range(16)], nsub, evac3, wbufs)
            for hh in range(nsub):
                _fold(P, "xt%d" % hh, xkeys[hh])
                for q in range(4):
                    _fold(P, "aTsq%d_%d" % (hh, q), ["aT_%d_%d" % (f, hh) for f in range(16)])
        P.finish()
        print("stageC2 ops", P.n_ops())
    return nc


PH = 99
SUB = 0
ST = 512
NEG = -30000.0


def host_consts():
    c = {}
    c["ident"] = np.eye(128, dtype=np.float32)
    i = np.arange(64)[:, None]
    j = np.arange(64)[None, :]
    sl = (i > j).astype(np.float32)
    su = (j > i).astype(np.float32)
    ui = (j >= i).astype(np.float32)
    c["m_sl"] = np.tile(sl, (1, 8))
    c["m_su"] = np.tile(su, (1, 8))
    c["m_ui"] = np.tile(ui, (1, 8))
    c["ibig"] = np.tile(np.eye(64, dtype=np.float32), (1, 8))
    rm = np.ones((1, 512), np.float32)
    rm[0, ::64] = 0.0
    c["rmask"] = rm
    return c


def rel_bias_toeplitz(rb_h):
    q = np.arange(128)[:, None]
    cc = np.arange(640)[None, :]
    rel = cc - 512 - q
    return np.ascontiguousarray(rb_h[np.clip(rel, -256, 256) + 256]).astype(np.float32)


class AttnState:
    pass


def attn_setup(P, nc, C):
    A = AttnState()
    A.qw = P.sb("a_qw", [128, 1]); A.kw = P.sb("a_kw", [128, 1])
    A.bias = P.sb("a_bias", [128, 640])
    A.kbuf = P.sb("a_kbuf", [128, 1024], BF16)
    A.vtp = P.sb("a_vtp", [128, 8, 128], BF16)
    A.qn = P.sb("a_qn", [128, ST], BF16)
    A.q = P.sb("a_q", [128, ST]); A.k = P.sb("a_k", [128, ST]); A.v = P.sb("a_v", [128, ST])
    A.sq = P.sb("a_sq", [128, ST])
    A.rs = P.sb("a_rs", [128, ST])
    A.sc = P.sb("a_sc", [128, 640])
    A.pt = P.sb("a_pt", [128, 640], BF16)
    A.ob = P.sb("a_ob", [128, ST])
    A.mx = P.sb("a_mx", [128, 1]); A.sm = P.sb("a_sm", [128, 1])
    return A


def attn_init(P, C, A, d):
    P.dma_op("sp", A.qw[:], d["qw"], writes=["a_qw"])
    P.dma_op("sp", A.kw[:], d["kw"], writes=["a_kw"])
    P.dma_op("sp", A.bias[:], d["bias"], writes=["a_bias"])
    P.op("pool", lambda e: e.memset(A.bias[0:64, 576:640], NEG), reads=[], writes=["a_bias"])
    P.op("pool", lambda e: e.memset(A.bias[64:128, 0:64], NEG), reads=[], writes=["a_bias"])
    P.op("pool", lambda e: e.memset(A.kbuf[:, 0:512], 0.0), writes=["a_kbuf"])
    P.op("pool", lambda e: e.memset(A.vtp[:, 0:4, :], 0.0), writes=["a_vtp"])


def attn_supertile(P, C, A, s, pT, out, T):
    t0 = s * ST
    tsl = slice(t0, t0 + ST)
    P.dma_op("sp", A.q[:], pT[512:640, tsl], writes=["a_q"])
    P.dma_op("sp", A.k[:], pT[640:768, tsl], writes=["a_k"])
    P.dma_op("sp", A.v[:], pT[768:896, tsl], writes=["a_v"])
    psB = C.psB
    for nm, x, w, dst, dk, sc_, bs_ in (("q", A.q, A.qw, A.q, "a_q", 1.0, "eps128"), ("k", A.k, A.kw, A.kbuf, "a_kbuf", 1.0 / 128, "eps")):
        P.op("act", lambda e, x=x: e.activation(out=A.sq[:], in_=x[:], func=AF.Square), reads=["a_" + nm], writes=["a_sq"])
        P.op("pe", lambda e: e.matmul(psB[:, :], lhsT=C.ones_f[:], rhs=A.sq[:], start=True, stop=True), reads=["ones_f", "a_sq"], writes=["psB"])
        P.op("act", lambda e, sc_=sc_, bs_=bs_: e.activation(out=A.rs[:], in_=psB[:, :], func=AF.Sqrt, bias=C.cst[bs_][:, 0:1], scale=sc_),
             reads=["psB", "cst"], writes=["a_rs"])
        P.op("dve", lambda e: e.reciprocal(out=A.rs[:], in_=A.rs[:]), reads=["a_rs"], writes=["a_rs"])
        if nm == "q":
            P.op("dve", lambda e: e.scalar_tensor_tensor(out=A.qn[:], in0=A.q[:], scalar=A.qw[:, 0:1], in1=A.rs[:], op0=ALU.mult, op1=ALU.mult),
                 reads=["a_q", "a_qw", "a_rs"], writes=["a_qn"])
        else:
            P.op("dve", lambda e: e.scalar_tensor_tensor(out=A.kbuf[:, 512:1024], in0=A.k[:], scalar=A.kw[:, 0:1], in1=A.rs[:], op0=ALU.mult, op1=ALU.mult),
                 reads=["a_k", "a_kw", "a_rs"], writes=["a_kbuf"])
    yield
    def tr(e):
        ins = None
        for b in range(4):
            ins = e.transpose(psB[:, b * 128:(b + 1) * 128], A.v[:, b * 128:(b + 1) * 128], C.ident[:])
        return ins
    P.op("pe", tr, reads=["a_v", "ident"], writes=["psB"])
    P.op("act", lambda e: e.activation(out=A.vtp[:, 4:8, :], in_=psB[:, :].rearrange("p (b e) -> p b e", b=4), func=AF.Copy),
         reads=["psB"], writes=["a_vtp"])
    yield
    psA = C.psA
    for j in range(4):
        def mm(e, j=j):
            e.matmul(psA[:, 0:512], lhsT=A.qn[:, 128 * j:128 * j + 128], rhs=A.kbuf[:, 128 * j:128 * j + 512], start=True, stop=True)
            return e.matmul(psA[:, 512:640], lhsT=A.qn[:, 128 * j:128 * j + 128], rhs=A.kbuf[:, 128 * j + 512:128 * j + 640], start=True, stop=True)
        P.op("pe", mm, reads=["a_qn", "a_kbuf"], writes=["psA"])
        P.op("dve", lambda e: e.tensor_tensor(out=A.sc[:], in0=psA[:, 0:640], in1=A.bias[:], op=ALU.add), reads=["psA", "a_bias"], writes=["a_sc"])
        if s == 0:
            ninv = 512 - 128 * j
            P.op("dve", lambda e, ninv=ninv: e.memset(A.sc[:, 0:ninv], NEG), writes=["a_sc"])
        yield
        P.op("dve", lambda e: e.tensor_reduce(out=A.mx[:], in_=A.sc[:], axis=AX.X, op=ALU.max), reads=["a_sc"], writes=["a_mx"])
        P.op("dve", lambda e: e.tensor_scalar(out=A.mx[:], in0=A.mx[:], scalar1=-1.0, scalar2=None, op0=ALU.mult), reads=["a_mx"], writes=["a_mx"])
        P.op("act", lambda e: e.activation(out=A.sc[:], in_=A.sc[:], func=AF.Exp, bias=A.mx[:, 0:1], scale=1.0, accum_out=A.sm[:, 0:1]),
             reads=["a_sc", "a_mx"], writes=["a_sc", "a_sm"])
        P.op("dve", lambda e: e.reciprocal(out=A.sm[:], in_=A.sm[:]), reads=["a_sm"], writes=["a_sm"])
        P.op("dve", lambda e: e.tensor_scalar(out=A.sc[:], in0=A.sc[:], scalar1=A.sm[:, 0:1], scalar2=None, op0=ALU.mult), reads=["a_sc", "a_sm"], writes=["a_sc"])

        yield

        def trp(e):
            ins = None
            for kb in range(5):
                ins = e.transpose(psA[:, kb * 128:(kb + 1) * 128], A.sc[:, kb * 128:(kb + 1) * 128], C.ident[:])
            return ins
        P.op("pe", trp, reads=["a_sc", "ident"], writes=["psA"])
        P.op("act", lambda e: e.activation(out=A.pt[:], in_=psA[:, 0:640], func=AF.Copy), reads=["psA"], writes=["a_pt"])

        yield

        def mo(e, j=j):
            ins = None
            for kb in range(5):
                ins = e.matmul(psB[:, 0:128], lhsT=A.vtp[:, j + kb, :], rhs=A.pt[:, kb * 128:(kb + 1) * 128], start=(kb == 0), stop=(kb == 4))
            return ins
        P.op("pe", mo, reads=["a_vtp", "a_pt"], writes=["psB"])
        P.op("dve", lambda e, j=j: e.tensor_copy(out=A.ob[:, 128 * j:128 * j + 128], in_=psB[:, 0:128]), reads=["psB"], writes=["a_ob"])
    P.dma_op("sp", out[128:256, tsl], A.ob[:], reads=["a_ob"], is_output=True)
    P.op("act", lambda e: e.activation(out=A.kbuf[:, 0:512], in_=A.kbuf[:, 512:1024], func=AF.Copy), reads=["a_kbuf"], writes=["a_kbuf"])
    P.op("act", lambda e: e.activation(out=A.vtp[:, 0:4, :], in_=A.vtp[:, 4:8, :], func=AF.Copy), reads=["a_vtp"], writes=["a_vtp"])
    yield


def mixer_common(P, nc, C, cd):
    C.ones_f = P.sb("ones_f", [128, 128])
    P.op("pool", lambda e: e.memset(C.ones_f[:], 1.0), writes=["ones_f"])
    C.ident = P.sb("ident_sb", [128, 128])
    P.dma_op("sp", C.ident[:], cd["ident"], writes=["ident"])
    C.cst = {}
    for nm, v in (("eps", EPS), ("eps128", 128 * EPS), ("one", 1.0), ("eps512", EPS)):
        t = P.sb("cst_" + nm, [128, 1])
        P.op("pool", lambda e, t=t, v=v: e.memset(t[:], v), writes=["cst"])
        C.cst[nm] = t
    C.psA = P.ps("psA", [128, 1024])
    C.psB = P.ps("psB", [128, 512])


def build_attn_only(T):
    nc = new_nc()
    pT = dram_in(nc, "pT", [896, T])
    d = {"qw": dram_in(nc, "qw", [128, 1]), "kw": dram_in(nc, "kw", [128, 1]), "bias": dram_in(nc, "bias", [128, 640])}
    cd = {"ident": dram_in(nc, "ident", [128, 128])}
    out = dram_out(nc, "oT", [256, T])
    with ExitStack() as st:
        P = Prog(nc, st)
        C = Ctx()
        mixer_common(P, nc, C, cd)
        A = attn_setup(P, nc, C)
        attn_init(P, C, A, d)
        for s in range(T // ST):
            rr([attn_supertile(P, C, A, s, pT, out, T)])
        P.finish()
        print("attn ops", P.n_ops())
    return nc


class GS:
    pass


def gdn_setup(P, nc, C):
    G = GS()
    def sb(n, shp):
        return P.sb("g_" + n, shp)
    G.cw = sb("cw", [128, 12]); G.alog = sb("alog", [1, 1]); G.dtb = sb("dtb", [1, 1]); G.nA = sb("nA", [1, 1])
    G.nw = sb("nw", [128, 1])
    G.raw = [sb("raw%d" % i, [128, ST + 3]) for i in range(3)]
    G.z = [sb("z0", [128, ST]), sb("z1", [128, ST])]; G.gb = sb("gb", [1, ST]); G.ga = sb("ga", [1, ST])
    G.sq2 = sb("sq2", [128, ST]); G.rs2 = sb("rs2", [128, ST])
    G.q_bf = P.sb("g_q_bf", [128, ST], BF16); G.k_bf = P.sb("g_k_bf", [128, ST], BF16); G.kb_bf = P.sb("g_kb_bf", [128, ST], BF16)
    G.Yf = sb("Yf", [64, ST])
    G.x = [sb("x%d" % i, [128, ST]) for i in range(3)]
    G.sq = sb("sq", [128, ST]); G.rs = sb("rs", [128, ST])
    G.beta = sb("beta", [1, ST]); G.g = sb("g", [1, ST]); G.gc = sb("gc", [1, ST]); G.ngc = sb("ngc", [1, ST])
    G.rmask = sb("rmask", [1, ST])
    G.beta_bc = sb("beta_bc", [128, ST]); G.gc_bc = sb("gc_bc", [128, ST]); G.egc_bc = [sb("egc_bc0", [128, ST]), sb("egc_bc1", [128, ST])]; G.ekd_bc = sb("ekd_bc", [128, ST])
    G.kb = sb("kb", [128, ST]); G.vb = sb("vb", [128, ST]); G.kbg = sb("kbg", [128, ST]); G.kd = sb("kd", [128, ST]); G.qd = [sb("qd0", [128, ST]), sb("qd1", [128, ST])]
    for n in ("m_sl", "m_su", "m_ui", "ibig", "E1", "E2", "Dl", "Du", "Dq"):
        setattr(G, n, sb(n, [64, ST]))
    for n in ("Pm", "Qm", "Ym"):
        setattr(G, n, P.sb("g_" + n, [64, ST], BF16))
    G.qkT = [sb("qkT0", [64, ST]), sb("qkT1", [64, ST])]
    G.vb_tp = P.sb("g_vb_tp", [64, 1024], BF16); G.kbg_tp = P.sb("g_kbg_tp", [64, 1024], BF16); G.kd_tp = [sb("kd_tp0", [64, 1024]), sb("kd_tp1", [64, 1024])]; G.u = [sb("u0", [64, 1024]), sb("u1", [64, 1024])]
    G.wT = [sb("wT0", [128, ST]), sb("wT1", [128, ST])]; G.S = sb("S", [128, 128]); G.vnew = sb("vnew", [64, 128])
    G.o = sb("o", [128, ST]); G.zs = sb("zs", [128, ST]); G.t1 = sb("t1", [128, ST]); G.res = sb("res", [128, ST])
    C.psC = P.ps("psC", [128, 1024]); C.psD = P.ps("psD", [128, 512]); C.psE = P.ps("psE", [128, 512]); C.psF = P.ps("psF", [128, 512])
    return G


def gdn_init(P, C, G, d, cd):
    for n, t in (("cw", G.cw), ("alog", G.alog), ("dtb", G.dtb), ("nw", G.nw)):
        P.dma_op("sp", t[:], d["g_" + n], writes=["g_" + n])
    for n in ("m_sl", "m_su", "m_ui", "ibig", "rmask"):
        P.dma_op("sp", getattr(G, n)[:], cd[n], writes=["g_" + n])
    P.op("act", lambda e: e.activation(out=G.nA[:], in_=G.alog[:], func=AF.Exp), reads=["g_alog"], writes=["g_nA"])
    P.op("dve", lambda e: e.tensor_scalar(out=G.nA[:], in0=G.nA[:], scalar1=-1.0, scalar2=None, op0=ALU.mult), reads=["g_nA"], writes=["g_nA"])
    P.op("pool", lambda e: e.memset(G.S[:], 0.0), writes=["g_S"])
    for i in range(3):
        P.op("pool", lambda e, i=i: e.memset(G.raw[i][:, 0:3], 0.0), writes=["g_raw%d" % i])


def gdn_pre(P, C, G, s, pT, gates, out, T):
    b = s % 2
    t0 = s * ST
    tsl = slice(t0, t0 + ST)
    psB, psC, psD, psE, psF = C.psB, C.psC, C.psD, C.psE, C.psF
    ones_row = C.ones_f[0:1, 0:64]
    for i in range(3):
        if s == 0:
            P.dma_op("sp", G.raw[i][:, 3:ST + 3], pT[128 * i:128 * i + 128, 0:ST], writes=["g_raw%d" % i])
        else:
            P.dma_op("sp", G.raw[i][:, :], pT[128 * i:128 * i + 128, t0 - 3:t0 + ST], writes=["g_raw%d" % i])
    P.dma_op("sp", G.z[b][:], pT[384:512, tsl], writes=["g_z%d" % b])
    P.dma_op("sp", G.gb[:], gates[0:1, tsl], writes=["g_gb"])
    P.dma_op("sp", G.ga[:], gates[1:2, tsl], writes=["g_ga"])
    yield
    for i in range(3):
        x = G.x[i]; raw = G.raw[i]; xk = "g_x%d" % i; rk = "g_raw%d" % i
        P.op("dve", lambda e, x=x, raw=raw, i=i: e.tensor_scalar(out=x[:], in0=raw[:, 0:ST], scalar1=G.cw[:, 4 * i:4 * i + 1], scalar2=None, op0=ALU.mult),
             reads=[rk, "g_cw"], writes=[xk])
        for j in range(1, 4):
            P.op("dve", lambda e, x=x, raw=raw, i=i, j=j: e.scalar_tensor_tensor(out=x[:], in0=raw[:, j:j + ST], scalar=G.cw[:, 4 * i + j:4 * i + j + 1],
                                                                               in1=x[:], op0=ALU.mult, op1=ALU.add),
                 reads=[rk, "g_cw", xk], writes=[xk])
        P.op("act", lambda e, x=x: e.activation(out=x[:], in_=x[:], func=AF.Silu), reads=[xk], writes=[xk])
    q, k, v = G.x
    yield
    for i, (sc_, bs_) in enumerate(((128.0, "eps128"), (1.0, "eps"))):
        x = G.x[i]; xk = "g_x%d" % i
        P.op("act", lambda e, x=x: e.activation(out=G.sq[:], in_=x[:], func=AF.Square), reads=[xk], writes=["g_sq"])
        P.op("pe", lambda e: e.matmul(psB[:, :], lhsT=C.ones_f[:], rhs=G.sq[:], start=True, stop=True), reads=["ones_f", "g_sq"], writes=["psB"])
        P.op("act", lambda e, sc_=sc_, bs_=bs_: e.activation(out=G.rs[:], in_=psB[:, :], func=AF.Sqrt, bias=C.cst[bs_][:, 0:1], scale=sc_),
             reads=["psB", "cst"], writes=["g_rs"])
        P.op("dve", lambda e: e.reciprocal(out=G.rs[:], in_=G.rs[:]), reads=["g_rs"], writes=["g_rs"])
        P.op("dve", lambda e, x=x: e.tensor_tensor(out=x[:], in0=x[:], in1=G.rs[:], op=ALU.mult), reads=[xk, "g_rs"], writes=[xk])
    yield
    P.op("act", lambda e: e.activation(out=G.beta[:], in_=G.gb[:], func=AF.Exp, scale=-1.0), reads=["g_gb"], writes=["g_beta"])
    P.op("dve", lambda e: e.tensor_scalar(out=G.beta[:], in0=G.beta[:], scalar1=1.0, scalar2=None, op0=ALU.add), reads=["g_beta"], writes=["g_beta"])
    P.op("dve", lambda e: e.reciprocal(out=G.beta[:], in_=G.beta[:]), reads=["g_beta"], writes=["g_beta"])
    P.op("act", lambda e: e.activation(out=G.g[:], in_=G.ga[:], func=AF.Exp, bias=G.dtb[0:1, 0:1], scale=1.0), reads=["g_ga", "g_dtb"], writes=["g_g"])
    P.op("act", lambda e: e.activation(out=G.g[:], in_=G.g[:], func=AF.Ln, bias=C.cst["one"][0:1, 0:1], scale=1.0), reads=["g_g", "cst"], writes=["g_g"])
    P.op("dve", lambda e: e.tensor_scalar(out=G.g[:], in0=G.g[:], scalar1=G.nA[0:1, 0:1], scalar2=None, op0=ALU.mult), reads=["g_g", "g_nA"], writes=["g_g"])
    P.op("dve", lambda e: e.tensor_tensor_scan(out=G.gc[:], data0=G.rmask[:], data1=G.g[:], initial=0.0, op0=ALU.mult, op1=ALU.add),
         reads=["g_rmask", "g_g"], writes=["g_gc"])
    P.op("dve", lambda e: e.tensor_scalar(out=G.ngc[:], in0=G.gc[:], scalar1=-1.0, scalar2=None, op0=ALU.mult), reads=["g_gc"], writes=["g_ngc"])
    yield
    P.op("pe", lambda e: e.matmul(psB[:, :], lhsT=C.ones_f[0:1, :], rhs=G.beta[:], start=True, stop=True), reads=["ones_f", "g_beta"], writes=["psB"])
    P.op("act", lambda e: e.activation(out=G.beta_bc[:], in_=psB[:, :], func=AF.Copy), reads=["psB"], writes=["g_beta_bc"])
    P.op("pe", lambda e: e.matmul(psB[:, :], lhsT=C.ones_f[0:1, :], rhs=G.gc[:], start=True, stop=True), reads=["ones_f", "g_gc"], writes=["psB"])
    P.op("act", lambda e: e.activation(out=G.gc_bc[:], in_=psB[:, :], func=AF.Copy), reads=["psB"], writes=["g_gc_bc"])
    P.op("act", lambda e: e.activation(out=G.egc_bc[b][:], in_=G.gc_bc[:], func=AF.Exp), reads=["g_gc_bc"], writes=["g_egc_bc%d" % b])
    for c in range(8):
        P.op("act", lambda e, c=c: e.activation(out=G.ekd_bc[:, 64 * c:64 * c + 64], in_=G.gc_bc[:, 64 * c:64 * c + 64], func=AF.Exp,
                                               bias=G.gc_bc[:, 64 * c + 63:64 * c + 64], scale=-1.0),
             reads=["g_gc_bc"], writes=["g_ekd_bc"])
    yield
    P.op("dve", lambda e: e.tensor_tensor(out=G.kb[:], in0=k[:], in1=G.beta_bc[:], op=ALU.mult), reads=["g_x1", "g_beta_bc"], writes=["g_kb"])
    P.op("dve", lambda e: e.tensor_tensor(out=G.vb[:], in0=v[:], in1=G.beta_bc[:], op=ALU.mult), reads=["g_x2", "g_beta_bc"], writes=["g_vb"])
    P.op("act", lambda e: e.activation(out=G.kb_bf[:], in_=G.kb[:], func=AF.Copy), reads=["g_kb"], writes=["g_kb_bf"])
    P.op("act", lambda e: e.activation(out=G.k_bf[:], in_=k[:], func=AF.Copy), reads=["g_x1"], writes=["g_k_bf"])
    P.op("act", lambda e: e.activation(out=G.q_bf[:], in_=q[:], func=AF.Copy), reads=["g_x0"], writes=["g_q_bf"])
    P.op("dve", lambda e: e.tensor_tensor(out=G.kbg[:], in0=G.kb[:], in1=G.egc_bc[b][:], op=ALU.mult), reads=["g_kb", "g_egc_bc%d" % b], writes=["g_kbg"])
    P.op("dve", lambda e: e.tensor_tensor(out=G.kd[:], in0=k[:], in1=G.ekd_bc[:], op=ALU.mult), reads=["g_x1", "g_ekd_bc"], writes=["g_kd"])
    P.op("dve", lambda e: e.tensor_tensor(out=G.qd[b][:], in0=q[:], in1=G.egc_bc[b][:], op=ALU.mult), reads=["g_x0", "g_egc_bc%d" % b], writes=["g_qd%d" % b])

    yield
    def dec(e):
        ins = None
        for c in range(8):
            cs = slice(64 * c, 64 * c + 64)
            e.matmul(psC[0:64, 64 * c:64 * c + 64], lhsT=G.gc[0:1, cs], rhs=ones_row, start=True, stop=False)
            e.matmul(psC[0:64, 64 * c:64 * c + 64], lhsT=ones_row, rhs=G.ngc[0:1, cs], start=False, stop=True)
        for c in range(8):
            cs = slice(64 * c, 64 * c + 64)
            e.matmul(psC[0:64, 512 + 64 * c:512 + 64 * c + 64], lhsT=ones_row, rhs=G.gc[0:1, cs], start=True, stop=False)
            ins = e.matmul(psC[0:64, 512 + 64 * c:512 + 64 * c + 64], lhsT=G.ngc[0:1, cs], rhs=ones_row, start=False, stop=True)
        return ins
    P.op("pe", dec, reads=["g_gc", "g_ngc", "ones_f"], writes=["psC"])
    P.op("dve", lambda e: e.tensor_scalar(out=G.E1[:], in0=psC[0:64, 0:512], scalar1=0.0, scalar2=None, op0=ALU.min), reads=["psC"], writes=["g_E1"])
    P.op("dve", lambda e: e.tensor_scalar(out=G.E2[:], in0=psC[0:64, 512:1024], scalar1=0.0, scalar2=None, op0=ALU.min), reads=["psC"], writes=["g_E2"])
    P.op("act", lambda e: e.activation(out=G.E1[:], in_=G.E1[:], func=AF.Exp), reads=["g_E1"], writes=["g_E1"])
    P.op("act", lambda e: e.activation(out=G.E2[:], in_=G.E2[:], func=AF.Exp), reads=["g_E2"], writes=["g_E2"])
    P.op("dve", lambda e: e.tensor_tensor(out=G.Dl[:], in0=G.E1[:], in1=G.m_sl[:], op=ALU.mult), reads=["g_E1", "g_m_sl"], writes=["g_Dl"])
    P.op("dve", lambda e: e.tensor_tensor(out=G.Du[:], in0=G.E2[:], in1=G.m_su[:], op=ALU.mult), reads=["g_E2", "g_m_su"], writes=["g_Du"])
    P.op("dve", lambda e: e.tensor_tensor(out=G.Dq[:], in0=G.E2[:], in1=G.m_ui[:], op=ALU.mult), reads=["g_E2", "g_m_ui"], writes=["g_Dq"])

    yield
    def kk(e):
        ins = None
        for c in range(8):
            cs = slice(64 * c, 64 * c + 64)
            e.matmul(psC[0:64, 64 * c:64 * c + 64], lhsT=G.kb_bf[:, cs], rhs=G.k_bf[:, cs], start=True, stop=True)
        for c in range(8):
            cs = slice(64 * c, 64 * c + 64)
            ins = e.matmul(psC[0:64, 512 + 64 * c:512 + 64 * c + 64], lhsT=G.k_bf[:, cs], rhs=G.kb_bf[:, cs], start=True, stop=True)
        return ins
    P.op("pe", kk, reads=["g_kb_bf", "g_k_bf"], writes=["psC"])

    def qkm(e):
        ins = None
        for c in range(8):
            cs = slice(64 * c, 64 * c + 64)
            ins = e.matmul(psD[0:64, 64 * c:64 * c + 64], lhsT=G.k_bf[:, cs], rhs=G.q_bf[:, cs], start=True, stop=True)
        return ins
    P.op("pe", qkm, reads=["g_q_bf", "g_k_bf"], writes=["psD"])
    P.op("dve", lambda e: e.tensor_tensor(out=G.Pm[:], in0=psC[0:64, 0:512], in1=G.Dl[:], op=ALU.mult), reads=["psC", "g_Dl"], writes=["g_Pm"])
    P.op("dve", lambda e: e.tensor_tensor(out=G.Qm[:], in0=psC[0:64, 512:1024], in1=G.Du[:], op=ALU.mult), reads=["psC", "g_Du"], writes=["g_Qm"])
    P.op("dve", lambda e: e.tensor_tensor(out=G.qkT[b][:], in0=psD[0:64, :], in1=G.Dq[:], op=ALU.mult), reads=["psD", "g_Dq"], writes=["g_qkT%d" % b])
    P.op("dve", lambda e: e.tensor_tensor(out=G.Yf[:], in0=G.ibig[:], in1=G.Qm[:], op=ALU.subtract), reads=["g_ibig", "g_Qm"], writes=["g_Yf"])
    P.op("act", lambda e: e.activation(out=G.Ym[:], in_=G.Yf[:], func=AF.Copy), reads=["g_Yf"], writes=["g_Ym"])
    yield
    yield
    for m in range(5):
        def sqr(e):
            ins = None
            for c in range(8):
                cs = slice(64 * c, 64 * c + 64)
                e.matmul(psC[0:64, 64 * c:64 * c + 64], lhsT=G.Qm[:, cs], rhs=G.Pm[:, cs], start=True, stop=True)
            for c in range(8):
                cs = slice(64 * c, 64 * c + 64)
                ins = e.matmul(psC[0:64, 512 + 64 * c:512 + 64 * c + 64], lhsT=G.Pm[:, cs], rhs=G.Qm[:, cs], start=True, stop=True)
            return ins
        P.op("pe", sqr, reads=["g_Pm", "g_Qm"], writes=["psC"])
        P.op("act", lambda e: e.activation(out=G.Pm[:], in_=psC[0:64, 0:512], func=AF.Copy), reads=["psC"], writes=["g_Pm"])
        P.op("dve", lambda e: e.tensor_copy(out=G.Qm[:], in_=psC[0:64, 512:1024]), reads=["psC"], writes=["g_Qm"])
        yield

        def yup(e):
            ins = None
            for c in range(8):
                cs = slice(64 * c, 64 * c + 64)
                ins = e.matmul(psD[0:64, 64 * c:64 * c + 64], lhsT=G.Pm[:, cs], rhs=G.Ym[:, cs], start=True, stop=True)
            return ins
        P.op("pe", yup, reads=["g_Ym", "g_Pm"], writes=["psD"])
        P.op("dve", lambda e: e.tensor_tensor(out=G.Yf[:], in0=G.Yf[:], in1=psD[0:64, :], op=ALU.add), reads=["psD", "g_Yf"], writes=["g_Yf"])
        P.op("act", lambda e: e.activation(out=G.Ym[:], in_=G.Yf[:], func=AF.Copy), reads=["g_Yf"], writes=["g_Ym"])
        yield
    yield
    for src, dst, sk, dk in ((G.vb, G.vb_tp, "g_vb", "g_vb_tp"), (G.kbg, G.kbg_tp, "g_kbg", "g_kbg_tp"), (G.kd, G.kd_tp[b], "g_kd", "g_kd_tp%d" % b)):
        def trp(e, src=src):
            ins = None
            for c in range(8):
                ins = e.transpose(psC[0:64, 128 * c:128 * c + 128], src[:, 64 * c:64 * c + 64], C.ident[:])
            return ins
        P.op("pe", trp, reads=[sk, "ident"], writes=["psC"])
        P.op("dve", lambda e, dst=dst: e.tensor_copy(out=dst[:], in_=psC[0:64, :]), reads=["psC"], writes=[dk])

    yield
    def um(e):
        ins = None
        for c in range(8):
            ins = e.matmul(psC[0:64, 128 * c:128 * c + 128], lhsT=G.Ym[:, 64 * c:64 * c + 64], rhs=G.vb_tp[:, 128 * c:128 * c + 128], start=True, stop=True)
        return ins
    P.op("pe", um, reads=["g_Ym", "g_vb_tp"], writes=["psC"])
    P.op("act", lambda e: e.activation(out=G.u[b][:], in_=psC[0:64, :], func=AF.Copy), reads=["psC"], writes=["g_u%d" % b])

    def wm(e):
        ins = None
        for c in range(8):
            ins = e.matmul(psD[:, 64 * c:64 * c + 64], lhsT=G.kbg_tp[:, 128 * c:128 * c + 128], rhs=G.Ym[:, 64 * c:64 * c + 64], start=True, stop=True)
        return ins
    P.op("pe", wm, reads=["g_Ym", "g_kbg_tp"], writes=["psD"])
    P.op("dve", lambda e: e.tensor_copy(out=G.wT[b][:], in_=psD[:, :]), reads=["psD"], writes=["g_wT%d" % b])
    yield


def gdn_scan(P, C, G, s, pT, gates, out, T):
    b = s % 2
    t0 = s * ST
    tsl = slice(t0, t0 + ST)
    psB, psC, psD, psE, psF = C.psB, C.psC, C.psD, C.psE, C.psF
    ones_row = C.ones_f[0:1, 0:64]
    for c in range(8):
        cs = slice(64 * c, 64 * c + 64)
        P.op("pe", lambda e, cs=cs: e.matmul(psF[0:64, 0:128], lhsT=G.wT[b][:, cs], rhs=G.S[:], start=True, stop=True), reads=["g_wT%d" % b, "g_S"], writes=["psF_a"])
        P.op("dve", lambda e, c=c: e.tensor_tensor(out=G.vnew[:], in0=G.u[b][:, 128 * c:128 * c + 128], in1=psF[0:64, 0:128], op=ALU.subtract),
             reads=["g_u%d" % b, "psF_a"], writes=["g_vnew"])
        yield

        def om(e, cs=cs, c=c):
            e.matmul(psE[:, cs], lhsT=G.S[:], rhs=G.qd[b][:, cs], start=True, stop=False)
            e.matmul(psE[:, cs], lhsT=G.vnew[:], rhs=G.qkT[b][:, cs], start=False, stop=True)
            return e.matmul(psF[:, 128:256], lhsT=G.kd_tp[b][:, 128 * c:128 * c + 128], rhs=G.vnew[:], start=True, stop=True)
        P.op("pe", om, reads=["g_S", "g_qd%d" % b, "g_vnew", "g_qkT%d" % b, "g_kd_tp%d" % b], writes=["psE", "psF_b"])
        P.op("dve", lambda e, c=c: e.scalar_tensor_tensor(out=G.S[:], in0=G.S[:], scalar=G.egc_bc[b][:, 64 * c + 63:64 * c + 64], in1=psF[:, 128:256],
                                                       op0=ALU.mult, op1=ALU.add),
             reads=["g_S", "g_egc_bc%d" % b, "psF_b"], writes=["g_S"])
        yield
    P.op("act", lambda e: e.activation(out=G.o[:], in_=psE[:, :], func=AF.Copy), reads=["psE"], writes=["g_o"])
    P.op("act", lambda e: e.activation(out=G.sq2[:], in_=G.o[:], func=AF.Square), reads=["g_o"], writes=["g_sq2"])
    P.op("pe", lambda e: e.matmul(psB[:, :], lhsT=C.ones_f[:], rhs=G.sq2[:], start=True, stop=True), reads=["ones_f", "g_sq2"], writes=["psB"])
    P.op("act", lambda e: e.activation(out=G.rs2[:], in_=psB[:, :], func=AF.Sqrt, bias=C.cst["eps"][:, 0:1], scale=1.0 / 128), reads=["psB", "cst"], writes=["g_rs2"])
    P.op("dve", lambda e: e.reciprocal(out=G.rs2[:], in_=G.rs2[:]), reads=["g_rs2"], writes=["g_rs2"])
    yield
    P.op("act", lambda e: e.activation(out=G.zs[:], in_=G.z[b][:], func=AF.Silu), reads=["g_z%d" % b], writes=["g_zs"])
    P.op("dve", lambda e: e.tensor_tensor(out=G.t1[:], in0=G.o[:], in1=G.rs2[:], op=ALU.mult), reads=["g_o", "g_rs2"], writes=["g_t1"])
    P.op("dve", lambda e: e.scalar_tensor_tensor(out=G.res[:], in0=G.t1[:], scalar=G.nw[:, 0:1], in1=G.zs[:], op0=ALU.mult, op1=ALU.mult),
         reads=["g_t1", "g_nw", "g_zs"], writes=["g_res"])
    P.dma_op("sp", out[0:128, tsl], G.res[:], reads=["g_res"], is_output=True)


    yield


def rr(gens):
    gens = [g for g in gens if g is not None]
    while gens:
        for g in list(gens):
            try:
                next(g)
            except StopIteration:
                gens.remove(g)


def build_B0(T, do_attn=True, do_gdn=True):
    nc = new_nc()
    pT = dram_in(nc, "pT", [896, T])
    gates = dram_in(nc, "gates", [2, T])
    d = {"qw": dram_in(nc, "qw", [128, 1]), "kw": dram_in(nc, "kw", [128, 1]), "bias": dram_in(nc, "bias", [128, 640]),
         "g_cw": dram_in(nc, "g_cw", [128, 12]), "g_alog": dram_in(nc, "g_alog", [1, 1]), "g_dtb": dram_in(nc, "g_dtb", [1, 1]),
         "g_nw": dram_in(nc, "g_nw", [128, 1])}
    cd = {"ident": dram_in(nc, "ident", [128, 128])}
    for n in ("m_sl", "m_su", "m_ui", "ibig"):
        cd[n] = dram_in(nc, n, [64, 512])
    cd["rmask"] = dram_in(nc, "rmask", [1, 512])
    out = dram_out(nc, "oT", [256, T])
    with ExitStack() as st:
        P = Prog(nc, st)
        C = Ctx()
        mixer_common(P, nc, C, cd)
        if do_attn:
            A = attn_setup(P, nc, C)
            attn_init(P, C, A, d)
        if do_gdn:
            G = gdn_setup(P, nc, C)
            gdn_init(P, C, G, d, cd)
        nst = T // ST
        if do_gdn:
            rr([gdn_pre(P, C, G, 0, pT, gates, out, T)])
        for s in range(nst):
            rr([gdn_scan(P, C, G, s, pT, gates, out, T) if do_gdn else None,
                attn_supertile(P, C, A, s, pT, out, T) if do_attn else None,
                gdn_pre(P, C, G, s + 1, pT, gates, out, T) if (do_gdn and s + 1 < nst) else None])
        P.finish()
        print("B0 ops", P.n_ops())
    return nc


def ssd_host_consts():
    c = {}
    c["ident"] = np.eye(128, dtype=np.float32)
    sp = np.zeros((8, 4, 128), np.float32)
    for j in range(4):
        sp[2 * j, j, 0:64] = 1.0
        sp[2 * j + 1, j, 64:128] = 1.0
    c["selpair"] = sp.reshape(8, 512)
    so = np.zeros((8, 8, 128), np.float32)
    for e in range(8):
        so[e, e, :] = 1.0
    c["selones"] = so.reshape(8, 1024)
    c["sel8"] = np.eye(8, dtype=np.float32)
    c["ones8"] = np.ones((8, 64), np.float32)
    rm = np.ones((8, 512), np.float32)
    rm[:, ::64] = 0.0
    c["rmask8"] = rm
    m = np.arange(64)[:, None]
    l = np.arange(64)[None, :]
    c["negmask"] = np.tile(np.where(l >= m, 0.0, NEG).astype(np.float32), (1, 8))
    sr = np.zeros((8, 8, 64), np.float32)
    for k in range(8):
        sr[k, k, :] = 1.0
    c["selrow64"] = sr.reshape(8, 512)
    return c


def ssd_setup(P, nc, C):
    S = GS()
    def sb(n, shp):
        return P.sb("s_" + n, shp)
    S.cw = sb("cw", [128, 24]); S.cb = sb("cb", [128, 6]); S.dcol = sb("dcol", [128, 4]); S.nw = sb("nw", [128, 4])
    S.dtb = sb("dtb", [8, 1]); S.alog = sb("alog", [8, 1]); S.acol = sb("acol", [8, 1])
    S.selpair = sb("selpair", [8, 512]); S.selones = sb("selones", [8, 1024]); S.sel8 = sb("sel8", [8, 8]); S.ones8 = sb("ones8", [8, 64])
    S.rmask8 = sb("rmask8", [8, 512]); S.negmask = sb("negmask", [64, 512])
    S.selrow64 = sb("selrow64", [8, 512])
    S.R_hi = P.sb("s_R_hi", [8, 8, ST], BF16); S.R_lo = P.sb("s_R_lo", [8, 8, ST], BF16)
    S.ac_hi = P.sb("s_ac_hi", [8, ST], BF16); S.ac_lo = P.sb("s_ac_lo", [8, ST], BF16); S.nac_hi = P.sb("s_nac_hi", [8, ST], BF16); S.nac_lo = P.sb("s_nac_lo", [8, ST], BF16)
    S.selrow_bf = P.sb("s_selrow_bf", [8, 512], BF16); S.ones8_bf = P.sb("s_ones8_bf", [8, 64], BF16)
    S.selpair_bf = P.sb("s_selpair_bf", [8, 512], BF16); S.selones_bf = P.sb("s_selones_bf", [8, 1024], BF16)
    S.dt_bf = P.sb("s_dt_bf", [8, ST], BF16); S.dtw_bf = P.sb("s_dtw_bf", [8, ST], BF16); S.eac_bf = P.sb("s_eac_bf", [8, ST], BF16)
    S.tmpS = sb("tmpS", [128, ST])
    S.raw = [sb("raw%d" % i, [128, ST + 3]) for i in range(6)]
    S.z = [sb("z%d" % i, [128, ST]) for i in range(4)]
    S.xc = [sb("xc%d" % i, [128, ST]) for i in range(6)]
    S.dt = sb("dt", [8, ST]); S.acum = sb("acum", [8, ST]); S.nacum = sb("nacum", [8, ST]); S.eac = sb("eac", [8, ST])
    S.dtw = sb("dtw", [8, ST])
    S.xdtT = [sb("xdtT%d" % j, [128, ST]) for j in range(4)]
    S.xdtwT = [sb("xdtwT%d" % j, [128, ST]) for j in range(4)]
    S.Cdec = P.sb("s_Cdec", [128, 8, ST], BF16); S.glt = sb("glt", [128, 8, 8]); S.eacl = sb("eacl", [8, 8])
    S.B_tp = P.sb("s_B_tp", [64, 1024], BF16); S.xdt_all = P.sb("s_xdt_all", [64, 8, ST], BF16); S.xdtw_tp = P.sb("s_xdtw_tp", [64, 2, ST], BF16)
    S.ST_bf = P.sb("s_ST_bf", [128, 8, ST], BF16); S.Bc_bf = P.sb("s_Bc_bf", [128, ST], BF16); S.Cc_bf = P.sb("s_Cc_bf", [128, ST], BF16)
    S.E = sb("E", [64, 2, ST]); S.cbd_all = P.sb("s_cbd_all", [64, 8, ST], BF16); S.y_sb = sb("y_sb", [64, 2, ST]); S.yT = sb("yT", [128, 4, ST])
    S.ST_all = sb("ST_all", [128, 9, ST])
    S.zs = sb("zs", [128, ST]); S.rs = sb("rs", [128, ST]); S.res = sb("res", [128, ST])
    S.zs4 = [sb("zs4_%d" % j, [128, ST]) for j in range(4)]; S.res4 = [sb("res4_%d" % j, [128, ST]) for j in range(4)]
    S.sqb = P.sb("s_sqb", [128, 4, ST], BF16); S.ones_bf = P.sb("s_ones_bf", [128, 128], BF16)
    C.psX = P.ps("psX", [128, 512]); C.psY = P.ps("psY", [128, 512]); C.psYY = P.ps("psYY", [128, 512]); C.psS = P.ps("psS", [128, 512])
    C.psV = P.ps("psV", [128, 512])
    return S


def ssd_init(P, C, S, d, cd):
    for n in ("cw", "cb", "dcol", "nw", "dtb", "alog"):
        P.dma_op("sp", getattr(S, n)[:], d["s_" + n], writes=["s_" + n])
    for n in ("selpair", "selones", "sel8", "ones8", "rmask8", "negmask", "selrow64"):
        P.dma_op("sp", getattr(S, n)[:], cd[n], writes=["s_" + n])
    P.op("act", lambda e: e.activation(out=S.acol[:], in_=S.alog[:], func=AF.Exp), reads=["s_alog"], writes=["s_acol"])
    P.op("dve", lambda e: e.tensor_scalar(out=S.acol[:], in0=S.acol[:], scalar1=-1.0, scalar2=None, op0=ALU.mult), reads=["s_acol"], writes=["s_acol"])
    P.op("pool", lambda e: e.memset(S.ST_all[:, 0, :], 0.0), writes=["s_ST0"])
    P.op("pool", lambda e: e.memset(S.ones_bf[:], 1.0), writes=["s_ones_bf"])
    for src_, dst_, k_ in ((S.selrow64, S.selrow_bf, "selrow64"), (S.ones8, S.ones8_bf, "ones8"), (S.selpair, S.selpair_bf, "selpair"), (S.selones, S.selones_bf, "selones")):
        P.op("dve", lambda e, src_=src_, dst_=dst_: e.tensor_copy(out=dst_[:], in_=src_[:]), reads=["s_" + k_], writes=["s_" + k_ + "_bf"])
    for i in range(6):
        P.op("pool", lambda e, i=i: e.memset(S.raw[i][:, 0:3], 0.0), writes=["s_raw%d" % i])


def ssd_supertile(P, C, S, s, pT, dtT, out, T):
    t0 = s * ST
    tsl = slice(t0, t0 + ST)
    psB = C.psB
    A0 = C.psA[:, 0:512]; A1 = C.psA[:, 512:1024]
    bk = {"psB": psB, "psA0": A0, "psA1": A1, "psX": C.psX, "psY": C.psY, "psYY": C.psYY, "psS": C.psS, "psV": C.psV}
    for i in range(6):
        r0 = 512 + 128 * i
        if s == 0:
            P.dma_op("sp", S.raw[i][:, 3:ST + 3], pT[r0:r0 + 128, 0:ST], writes=["s_raw%d" % i])
        else:
            P.dma_op("sp", S.raw[i][:, :], pT[r0:r0 + 128, t0 - 3:t0 + ST], writes=["s_raw%d" % i])
    P.dma_op("sp", S.dt[:], dtT[:, tsl], writes=["s_dt"])
    for j in range(4):
        P.dma_op("sp", S.z[j][:], pT[128 * j:128 * j + 128, tsl], writes=["s_z%d" % j])
    xs = S.xc[0:4]; Bc = S.xc[4]; Cc = S.xc[5]

    def conv_gen():
        for i in range(6):
            x = S.xc[i]; raw = S.raw[i]; xk = "s_xc%d" % i; rk = "s_raw%d" % i
            ce = "dve"
            if ce == "dve":
                P.op("dve", lambda e, x=x, raw=raw, i=i: e.tensor_scalar(out=x[:], in0=raw[:, 0:ST], scalar1=S.cw[:, 4 * i:4 * i + 1], scalar2=None, op0=ALU.mult),
                     reads=[rk, "s_cw"], writes=[xk])
                for j in range(1, 4):
                    P.op("dve", lambda e, x=x, raw=raw, i=i, j=j: e.scalar_tensor_tensor(out=x[:], in0=raw[:, j:j + ST], scalar=S.cw[:, 4 * i + j:4 * i + j + 1],
                                                                                       in1=x[:], op0=ALU.mult, op1=ALU.add),
                         reads=[rk, "s_cw", xk], writes=[xk])
            else:
                P.op("pool", lambda e, x=x, raw=raw, i=i: e.tensor_scalar(out=x[:], in0=raw[:, 0:ST], scalar1=S.cw[:, 4 * i:4 * i + 1], scalar2=None, op0=ALU.mult),
                     reads=[rk, "s_cw"], writes=[xk])
                for j in range(1, 4):
                    P.op("pool", lambda e, raw=raw, i=i, j=j: e.tensor_scalar(out=S.zs[:], in0=raw[:, j:j + ST], scalar1=S.cw[:, 4 * i + j:4 * i + j + 1], scalar2=None, op0=ALU.mult),
                         reads=[rk, "s_cw"], writes=["s_zs"])
                    P.op("pool", lambda e, x=x: e.tensor_tensor(out=x[:], in0=x[:], in1=S.zs[:], op=ALU.add), reads=[xk, "s_zs"], writes=[xk])
            P.op("act", lambda e, x=x, i=i: e.activation(out=x[:], in_=x[:], func=AF.Silu, bias=S.cb[:, i:i + 1], scale=1.0), reads=[xk, "s_cb"], writes=[xk])
            yield

    def gates_gen():
        P.op("act", lambda e: e.activation(out=S.dt[:], in_=S.dt[:], func=AF.Exp, bias=S.dtb[:, 0:1], scale=1.0), reads=["s_dt", "s_dtb"], writes=["s_dt"])
        yield
        P.op("act", lambda e: e.activation(out=S.dt[:], in_=S.dt[:], func=AF.Ln, bias=C.cst["one"][0:8, 0:1], scale=1.0), reads=["s_dt", "cst"], writes=["s_dt"])
        yield
        P.op("dve", lambda e: e.tensor_scalar(out=S.nacum[:], in0=S.dt[:], scalar1=S.acol[:, 0:1], scalar2=None, op0=ALU.mult), reads=["s_dt", "s_acol"], writes=["s_nacum"])
        yield
        P.op("dve", lambda e: e.tensor_tensor_scan(out=S.acum[:], data0=S.rmask8[:], data1=S.nacum[:], initial=0.0, op0=ALU.mult, op1=ALU.add),
             reads=["s_rmask8", "s_nacum"], writes=["s_acum"])
        yield
        P.op("dve", lambda e: e.tensor_scalar(out=S.nacum[:], in0=S.acum[:], scalar1=-1.0, scalar2=None, op0=ALU.mult), reads=["s_acum"], writes=["s_nacum"])
        yield
        P.op("act", lambda e: e.activation(out=S.eac[:], in_=S.acum[:], func=AF.Exp), reads=["s_acum"], writes=["s_eac"])
        yield
        for c in range(8):
            P.op("act", lambda e, c=c: e.activation(out=S.dtw[:, 64 * c:64 * c + 64], in_=S.acum[:, 64 * c:64 * c + 64], func=AF.Exp,
                                                   bias=S.acum[:, 64 * c + 63:64 * c + 64], scale=-1.0), reads=["s_acum"], writes=["s_dtw"])
        yield
        P.op("dve", lambda e: e.tensor_tensor(out=S.dtw[:], in0=S.dt[:], in1=S.dtw[:], op=ALU.mult), reads=["s_dt", "s_dtw"], writes=["s_dtw"])
        yield
        P.op("dve", lambda e: e.tensor_copy(out=S.ac_hi[:], in_=S.acum[:]), reads=["s_acum"], writes=["s_ac_hi"])
        yield
        P.op("dve", lambda e: e.tensor_tensor(out=S.ac_lo[:], in0=S.acum[:], in1=S.ac_hi[:], op=ALU.subtract), reads=["s_acum", "s_ac_hi"], writes=["s_ac_lo"])
        yield
        P.op("dve", lambda e: e.tensor_scalar(out=S.nac_hi[:], in0=S.ac_hi[:], scalar1=-1.0, scalar2=None, op0=ALU.mult), reads=["s_ac_hi"], writes=["s_nac_hi"])
        yield
        P.op("dve", lambda e: e.tensor_scalar(out=S.nac_lo[:], in0=S.ac_lo[:], scalar1=-1.0, scalar2=None, op0=ALU.mult), reads=["s_ac_lo"], writes=["s_nac_lo"])
        yield
        selb = S.sel8[:, :].unsqueeze(2).to_broadcast([8, 8, ST])
        yield
        P.op("dve", lambda e: e.tensor_tensor(out=S.R_hi[:], in0=S.ac_hi[:, :].unsqueeze(1).to_broadcast([8, 8, ST]), in1=selb, op=ALU.mult),
             reads=["s_ac_hi", "s_sel8"], writes=["s_R_hi"])
        yield
        P.op("dve", lambda e: e.tensor_tensor(out=S.R_lo[:], in0=S.ac_lo[:, :].unsqueeze(1).to_broadcast([8, 8, ST]), in1=selb, op=ALU.mult),
             reads=["s_ac_lo", "s_sel8"], writes=["s_R_lo"])
        yield
        P.op("act", lambda e: e.activation(out=S.dt_bf[:], in_=S.dt[:], func=AF.Copy), reads=["s_dt"], writes=["s_dt_bf"])
        yield
        P.op("act", lambda e: e.activation(out=S.dtw_bf[:], in_=S.dtw[:], func=AF.Copy), reads=["s_dtw"], writes=["s_dtw_bf"])
        yield
        P.op("act", lambda e: e.activation(out=S.eac_bf[:], in_=S.eac[:], func=AF.Copy), reads=["s_eac"], writes=["s_eac_bf"])
        yield
        P.op("dve", lambda e: e.tensor_copy(out=S.eacl[:], in_=S.eac[:].rearrange("p (c l) -> p c l", l=64)[:, :, 63]), reads=["s_eac"], writes=["s_eacl"])

        yield
    rr([conv_gen(), gates_gen()])
    P.op("act", lambda e: e.activation(out=S.Bc_bf[:], in_=Bc[:], func=AF.Copy), reads=["s_xc4"], writes=["s_Bc_bf"])
    P.op("act", lambda e: e.activation(out=S.Cc_bf[:], in_=Cc[:], func=AF.Copy), reads=["s_xc5"], writes=["s_Cc_bf"])
    for j in range(4):
        P.op("act", lambda e, j=j: e.activation(out=S.zs4[j][:], in_=S.z[j][:], func=AF.Silu), reads=["s_z%d" % j], writes=["s_zs4_%d" % j])
    rot = ["psB", "psV", "psYY", "psS"]
    ri = 0
    for j in range(4):
        for src_, dst_, dkey in ((S.dt_bf, S.xdtT[j], "s_xdtT%d" % j), (S.dtw_bf, S.xdtwT[j], "s_xdtwT%d" % j)):
            bn = rot[ri % 4]; ri += 1
            P.op("pe", lambda e, j=j, src_=src_, bn=bn: e.matmul(bk[bn][:, :], lhsT=S.selpair_bf[:, 128 * j:128 * j + 128], rhs=src_[:], start=True, stop=True),
                 reads=["s_selpair_bf", "s_dt_bf", "s_dtw_bf"], writes=[bn])
            P.op("dve", lambda e, j=j, dst_=dst_, bn=bn: e.tensor_tensor(out=dst_[:], in0=xs[j][:], in1=bk[bn][:, :], op=ALU.mult), reads=["s_xc%d" % j, bn], writes=[dkey])
    for h in range(8):
        bn = rot[ri % 4]; ri += 1
        P.op("pe", lambda e, h=h, bn=bn: e.matmul(bk[bn][:, :], lhsT=S.selones_bf[:, 128 * h:128 * h + 128], rhs=S.eac_bf[:], start=True, stop=True), reads=["s_selones_bf", "s_eac_bf"], writes=[bn])
        eng = "dve"
        if eng == "dve":
            P.op("dve", lambda e, h=h, bn=bn: e.tensor_tensor(out=S.Cdec[:, h, :], in0=Cc[:], in1=bk[bn][:, :], op=ALU.mult), reads=["s_xc5", bn], writes=["s_Cdec%d" % h])
        else:
            P.op("act", lambda e, bn=bn: e.activation(out=S.res[:], in_=bk[bn][:, :], func=AF.Copy), reads=[bn], writes=["s_res"])
            P.op("pool", lambda e, h=h: e.tensor_tensor(out=S.Cdec[:, h, :], in0=Cc[:], in1=S.res[:], op=ALU.mult), reads=["s_xc5", "s_res"], writes=["s_Cdec%d" % h])

    def glm(e):
        ins = None
        for h in range(8):
            ins = e.matmul(C.psX[:, 8 * h:8 * h + 8], lhsT=S.selones[:, 128 * h:128 * h + 128], rhs=S.eacl[:], start=True, stop=True)
        return ins
    P.op("pe", glm, reads=["s_selones", "s_eacl"], writes=["psX"])
    P.op("act", lambda e: e.activation(out=S.glt[:].rearrange("p h c -> p (h c)"), in_=C.psX[:, 0:64], func=AF.Copy), reads=["psX"], writes=["s_glt"])
    for hf in range(2):
        bn = "psA%d" % hf

        def btr(e, hf=hf, bn=bn):
            ins = None
            for c in range(4):
                cc = 4 * hf + c
                ins = e.transpose(bk[bn][0:64, 128 * c:128 * c + 128], Bc[:, 64 * cc:64 * cc + 64], C.ident[:])
            return ins
        P.op("pe", btr, reads=["s_xc4", "ident"], writes=[bn])
        P.op("dve", lambda e, hf=hf, bn=bn: e.tensor_copy(out=S.B_tp[:, 512 * hf:512 * hf + 512], in_=bk[bn][0:64, :]), reads=[bn], writes=["s_B_tp"])
    def p1_head(c):
        cs = slice(64 * c, 64 * c + 64)
        p = c % 2
        b1 = "psA%d" % p
        b2 = ["psX", "psY"][p]

        def xtr(e, cs=cs, b1=b1, b2=b2):
            ins = None
            for j in range(4):
                e.transpose(bk[b1][0:64, 128 * j:128 * j + 128], S.xdtT[j][:, cs], C.ident[:])
            for j in range(4):
                ins = e.transpose(bk[b2][0:64, 128 * j:128 * j + 128], S.xdtwT[j][:, cs], C.ident[:])
            return ins
        P.op("pe", xtr, reads=["s_xdtT%d" % j for j in range(4)] + ["s_xdtwT%d" % j for j in range(4)] + ["ident"], writes=[b1, b2])
        P.op("act", lambda e, c=c, b1=b1: e.activation(out=S.xdt_all[:, c, :], in_=bk[b1][0:64, :], func=AF.Copy), reads=[b1], writes=["s_xdt_all%d" % c])
        P.op("dve", lambda e, p=p, b2=b2: e.tensor_copy(out=S.xdtw_tp[:, p, :], in_=bk[b2][0:64, :]), reads=[b2], writes=["s_xdtw_tp%d" % p])

    def p1_tail(c):
        p = c % 2
        b3 = ["psYY", "psS"][p]
        P.op("pe", lambda e, c=c, p=p, b3=b3: e.matmul(bk[b3][:, :], lhsT=S.B_tp[:, 128 * c:128 * c + 128], rhs=S.xdtw_tp[:, p, :], start=True, stop=True),
             reads=["s_B_tp", "s_xdtw_tp%d" % p], writes=[b3])
        P.op("dve", lambda e, c=c: e.tensor_tensor(out=S.tmpS[:].rearrange("s (h p) -> s h p", h=8), in0=S.ST_all[:, c, :].rearrange("s (h p) -> s h p", h=8),
                                                   in1=S.glt[:, :, c:c + 1].to_broadcast([128, 8, 64]), op=ALU.mult),
             reads=["s_ST%d" % c, "s_glt"], writes=["s_tmpS"])
        P.op("dve", lambda e, c=c, b3=b3: e.tensor_tensor(out=S.ST_all[:, c + 1, :], in0=S.tmpS[:], in1=bk[b3][:, :], op=ALU.add),
             reads=["s_tmpS", b3], writes=["s_ST%d" % (c + 1)])
    for t in range(9):
        if t < 8:
            p1_head(t)
        if t >= 1:
            p1_tail(t - 1)

    def p2_head(c):
        cs = slice(64 * c, 64 * c + 64)
        p = c % 2
        bx = ["psV", "psB"][p]
        by = "psA%d" % p

        def dm(e, cs=cs, bx=bx, by=by):
            e.matmul(bk[bx][0:64, :], lhsT=S.ones8_bf[:, :], rhs=S.R_hi[:, :, cs], start=True, stop=False)
            e.matmul(bk[bx][0:64, :], lhsT=S.ones8_bf[:, :], rhs=S.R_lo[:, :, cs], start=False, stop=False)
            e.matmul(bk[bx][0:64, :], lhsT=S.nac_hi[:, cs], rhs=S.selrow_bf[:, :], start=False, stop=False)
            e.matmul(bk[bx][0:64, :], lhsT=S.nac_lo[:, cs], rhs=S.selrow_bf[:, :], start=False, stop=True)
            return e.matmul(bk[by][0:64, 0:64], lhsT=S.Bc_bf[:, cs], rhs=S.Cc_bf[:, cs], start=True, stop=True)
        P.op("pe", dm, reads=["s_ones8_bf", "s_R_hi", "s_R_lo", "s_nac_hi", "s_nac_lo", "s_selrow64_bf", "s_Bc_bf", "s_Cc_bf"], writes=[bx, by])
        P.op("dve", lambda e, p=p, bx=bx: e.scalar_tensor_tensor(out=S.E[:, p, :], in0=bk[bx][0:64, :], scalar=0.0, in1=S.negmask[:], op0=ALU.min, op1=ALU.add),
             reads=[bx, "s_negmask"], writes=["s_E%d" % p])
        P.op("act", lambda e, p=p: e.activation(out=S.E[:, p, :], in_=S.E[:, p, :], func=AF.Exp), reads=["s_E%d" % p], writes=["s_E%d" % p])

    def p2_tail(c):
        p = c % 2
        by = "psA%d" % p
        P.op("dve", lambda e, c=c, p=p, by=by: e.tensor_tensor(out=S.cbd_all[:, c, :].rearrange("m (h l) -> m h l", h=8),
                                                               in0=S.E[:, p, :].rearrange("m (h l) -> m h l", h=8),
                                                               in1=bk[by][0:64, 0:64].unsqueeze(1).to_broadcast([64, 8, 64]), op=ALU.mult),
             reads=[by, "s_E%d" % p], writes=["s_cbd%d" % c])
    for t in range(9):
        if t < 8:
            p2_head(t)
        if t >= 1:
            p2_tail(t - 1)

    def p3_head(c):
        cs = slice(64 * c, 64 * c + 64)
        p = c % 2
        b1 = ["psX", "psY"][p]

        def ym(e, cs=cs, c=c, b1=b1):
            ins = None
            for h in range(8):
                hs = slice(64 * h, 64 * h + 64)
                e.matmul(bk[b1][0:64, hs], lhsT=S.Cdec[:, h, cs], rhs=S.ST_bf[:, c, hs], start=True, stop=False)
                ins = e.matmul(bk[b1][0:64, hs], lhsT=S.cbd_all[:, c, hs], rhs=S.xdt_all[:, c, hs], start=False, stop=True)
            return ins
        P.op("act", lambda e, c=c: e.activation(out=S.ST_bf[:, c, :], in_=S.ST_all[:, c, :], func=AF.Copy), reads=["s_ST%d" % c], writes=["s_STbf%d" % c])
        P.op("pe", ym, reads=["s_Cdec%d" % h for h in range(8)] + ["s_STbf%d" % c, "s_cbd%d" % c, "s_xdt_all%d" % c], writes=[b1])
        P.op("act", lambda e, p=p, b1=b1: e.activation(out=S.y_sb[:, p, :], in_=bk[b1][0:64, :], func=AF.Copy), reads=[b1], writes=["s_y_sb%d" % p])

    def p3_tail(c):
        cs = slice(64 * c, 64 * c + 64)
        p = c % 2
        b2 = ["psYY", "psS"][p]

        def ytr(e, p=p, b2=b2):
            ins = None
            for j in range(4):
                ins = e.transpose(bk[b2][:, 64 * j:64 * j + 64], S.y_sb[:, p, 128 * j:128 * j + 128], C.ident[0:64, 0:64])
            return ins
        P.op("pe", ytr, reads=["s_y_sb%d" % p, "ident"], writes=[b2])
        if c % 2 == 0:
            P.op("act", lambda e, cs=cs, b2=b2: e.activation(out=S.yT[:, :, cs], in_=bk[b2][:, 0:256].rearrange("p (j l) -> p j l", j=4), func=AF.Copy),
                 reads=[b2], writes=["s_yT"])
        else:
            P.op("dve", lambda e, cs=cs, b2=b2: e.tensor_copy(out=S.yT[:, :, cs], in_=bk[b2][:, 0:256].rearrange("p (j l) -> p j l", j=4)),
                 reads=[b2], writes=["s_yT"])
    for t in range(9):
        if t < 8:
            p3_head(t)
        if t >= 1:
            p3_tail(t - 1)
    P.op("act", lambda e: e.activation(out=S.ST_all[:, 0, :], in_=S.ST_all[:, 8, :], func=AF.Copy), reads=["s_ST8"], writes=["s_ST0"])
    for j in range(4):
        gt = S.xdtT[j]; gk = "s_xdtT%d" % j
        P.op("dve", lambda e, j=j, gt=gt: e.scalar_tensor_tensor(out=gt[:], in0=xs[j][:], scalar=S.dcol[:, j:j + 1], in1=S.yT[:, j, :], op0=ALU.mult, op1=ALU.add),
             reads=["s_xc%d" % j, "s_dcol", "s_yT"], writes=[gk])
        P.op("dve", lambda e, j=j, gt=gt: e.tensor_tensor(out=gt[:], in0=gt[:], in1=S.zs4[j][:], op=ALU.mult), reads=[gk, "s_zs4_%d" % j], writes=[gk])
        P.op("act", lambda e, j=j, gt=gt: e.activation(out=S.sqb[:, j, :], in_=gt[:], func=AF.Square), reads=[gk], writes=["s_sqb%d" % j])

    def nm(e):
        ins = None
        for j in range(4):
            ins = e.matmul(psB[:, :], lhsT=S.ones_bf[:], rhs=S.sqb[:, j, :], start=(j == 0), stop=(j == 3))
        return ins
    P.op("pe", nm, reads=["s_ones_bf"] + ["s_sqb%d" % j for j in range(4)], writes=["psB"])
    P.op("act", lambda e: e.activation(out=S.rs[:], in_=psB[:, :], func=AF.Sqrt, bias=C.cst["eps"][:, 0:1], scale=1.0 / 512), reads=["psB", "cst"], writes=["s_rs"])
    P.op("dve", lambda e: e.reciprocal(out=S.rs[:], in_=S.rs[:]), reads=["s_rs"], writes=["s_rs"])
    for j in range(4):
        P.op("dve", lambda e, j=j: e.scalar_tensor_tensor(out=S.res4[j][:], in0=S.xdtT[j][:], scalar=S.nw[:, j:j + 1], in1=S.rs[:], op0=ALU.mult, op1=ALU.mult),
             reads=["s_xdtT%d" % j, "s_nw", "s_rs"], writes=["s_res4_%d" % j])
        P.dma_op("sp", out[128 * j:128 * j + 128, tsl], S.res4[j][:], reads=["s_res4_%d" % j], is_output=True)


def build_B1(T):
    nc = new_nc()
    pT = dram_in(nc, "pT", [1280, T])
    dtT = dram_in(nc, "dtT", [8, T])
    d = {"s_cw": dram_in(nc, "s_cw", [128, 24]), "s_cb": dram_in(nc, "s_cb", [128, 6]), "s_dcol": dram_in(nc, "s_dcol", [128, 4]),
         "s_nw": dram_in(nc, "s_nw", [128, 4]), "s_dtb": dram_in(nc, "s_dtb", [8, 1]), "s_alog": dram_in(nc, "s_alog", [8, 1])}
    cd = {"ident": dram_in(nc, "ident", [128, 128]), "selpair": dram_in(nc, "selpair", [8, 512]), "selones": dram_in(nc, "selones", [8, 1024]),
          "sel8": dram_in(nc, "sel8", [8, 8]), "ones8": dram_in(nc, "ones8", [8, 64]), "rmask8": dram_in(nc, "rmask8", [8, 512]),
          "negmask": dram_in(nc, "negmask", [64, 512]), "selrow64": dram_in(nc, "selrow64", [8, 512])}
    out = dram_out(nc, "yT", [512, T])
    with ExitStack() as st:
        P = Prog(nc, st)
        C = Ctx()
        mixer_common(P, nc, C, cd)
        S = ssd_setup(P, nc, C)
        ssd_init(P, C, S, d, cd)
        for s in range(T // ST):
            ssd_supertile(P, C, S, s, pT, dtT, out, T)
        P.finish()
        print("B1 ops", P.n_ops())
    return nc

NCORES = 8
SEQ = 16384
TOKC = SEQ // NCORES

OFF = {"gq": 0, "gk": 1024, "gv": 2048, "gz": 3072, "gb": 4096, "ga": 4104, "aq": 4112, "ak": 5136, "av": 6160}


def mod_layout(modrow):
    return np.ascontiguousarray(modrow.reshape(6, 16, 128).transpose(2, 0, 1).reshape(128, 96))


def col_layout(v):
    return np.ascontiguousarray(v.reshape(16, 128).T)


def build_mod():
    nc = new_nc()
    c_in = dram_in(nc, "c_in", [128, 16])
    w_in = dram_in(nc, "w_in", [2, 2048, 1536])
    b_in = dram_in(nc, "b_in", [128, 24])
    out = dram_out(nc, "out", [128, 24])
    with ExitStack() as st:
        P = Prog(nc, st)
        ct = P.sb("ct", [128, 16]); ca = P.sb("ca", [128, 16])
        bt = P.sb("bt", [128, 24]); ot = P.sb("ot", [128, 24])
        wt = P.sb("wt", [128, 16, 1536])
        acc = P.ps("acc", [128, 512])
        P.dma_op("sp", ct[:], c_in, writes=["ct"])
        P.dma_op("sp", bt[:], b_in, writes=["bt"])
        P.op("act", lambda e: e.activation(out=ca[:], in_=ct[:], func=AF.Silu), reads=["ct"], writes=["ca"])
        for l in range(2):
            P.dma_op("sp", wt[:], w_in[l].rearrange("(kt p) n -> p kt n", p=128), writes=["wt"])

            def mm(e, l=l):
                ins = None
                for j in range(12):
                    for kt in range(16):
                        ins = e.matmul(acc[:, l * 12 + j:l * 12 + j + 1], lhsT=wt[:, kt, j * 128:(j + 1) * 128], rhs=ca[:, kt:kt + 1],
                                       start=(kt == 0), stop=(kt == 15))
                return ins
            P.op("pe", mm, reads=["wt", "ca"], writes=["acc"])
        P.op("dve", lambda e: e.tensor_tensor(out=ot[:], in0=acc[:, 0:24], in1=bt[:], op=ALU.add), reads=["acc", "bt"], writes=["ot"])
        P.dma_op("sp", out, ot[:], reads=["ot"], is_output=True)
        P.finish()
    return nc


def run(nc, in_maps):
    res = run_bass_kernel_spmd(nc, in_maps, core_ids=list(range(NCORES)))
    return res.results


def kernel(x, c, mod_w, mod_b, norm_mix_w, norm_mlp_w, mlp_w1, mlp_w2,
           ab_w_in, gdn_conv_w, gdn_a_log, gdn_dt_bias, gdn_norm_w,
           attn_q_norm_w, attn_k_norm_w, attn_rel_bias, ab_w_out,
           ssd_w_in, ssd_conv_w, ssd_conv_b, ssd_dt_bias, ssd_a_log, ssd_d,
           ssd_norm_w, ssd_w_out):
    f = lambda a: np.ascontiguousarray(np.asarray(a, dtype=np.float32))
    x = f(x); c = f(c); mod_w = f(mod_w); mod_b = f(mod_b)
    cl = np.ascontiguousarray(c.reshape(16, 128).T)
    ims = []
    for core in range(NCORES):
        sl = slice(core * 1536, (core + 1) * 1536)
        b = np.stack([mod_b[l, sl].reshape(12, 128).T for l in range(2)], axis=1).reshape(128, 24)
        ims.append({"c_in": cl, "w_in": np.ascontiguousarray(mod_w[:, :, sl]), "b_in": np.ascontiguousarray(b)})
    r = run(build_mod(), ims)
    mod = np.zeros((2, 12288), np.float32)
    for core in range(NCORES):
        o = r[core]["out"].reshape(128, 2, 12)
        for l in range(2):
            mod[l, core * 1536:(core + 1) * 1536] = o[:, l, :].T.reshape(-1)
    modl = [mod_layout(mod[l]) for l in range(2)]
    xT = [np.ascontiguousarray(x[0, cc * TOKC:(cc + 1) * TOKC].T) for cc in range(NCORES)]

    perm0 = np.concatenate([np.concatenate([OFF[n] + 128 * i + np.arange(128) for n in ("gq", "gk", "gv", "gz", "aq", "ak", "av")]) for i in range(8)]
                           + [np.arange(4096, 4112)])
    w0 = np.ascontiguousarray(f(ab_w_in)[0][:, perm0])
    nw0 = col_layout(f(norm_mix_w)[0])
    r = run(build_stageA(7184, TOKC), [{"xT": xT[cc], "w": w0, "nw": nw0, "modl": modl[0]} for cc in range(NCORES)])
    projT = [r[cc]["projT"] for cc in range(NCORES)]
    hc = host_consts()
    cw = f(gdn_conv_w)[0]
    ims = []
    for i in range(8):
        pT = np.ascontiguousarray(np.concatenate([projT[cc][896 * i:896 * (i + 1)] for cc in range(NCORES)], axis=1))
        gates = np.ascontiguousarray(np.concatenate([projT[cc][[7168 + i, 7176 + i]] for cc in range(NCORES)], axis=1))
        cwh = np.concatenate([cw[:, 1024 * j + 128 * i: 1024 * j + 128 * (i + 1)].T for j in range(3)], axis=1)
        im = {"pT": pT, "gates": gates,
              "qw": f(attn_q_norm_w)[0].reshape(128, 1).copy(), "kw": f(attn_k_norm_w)[0].reshape(128, 1).copy(),
              "bias": rel_bias_toeplitz(f(attn_rel_bias)[0, i]),
              "g_cw": np.ascontiguousarray(cwh), "g_alog": f(gdn_a_log)[0, i].reshape(1, 1).copy(), "g_dtb": f(gdn_dt_bias)[0, i].reshape(1, 1).copy(),
              "g_nw": f(gdn_norm_w)[0].reshape(128, 1).copy()}
        im.update(hc)
        ims.append(im)
    del projT
    r = run(build_B0(SEQ), ims)
    oTh = [r[i]["oT"] for i in range(8)]
    ims = []
    wo0 = f(ab_w_out)[0]; w1 = f(mlp_w1); w2 = f(mlp_w2)
    for cc in range(NCORES):
        ts = slice(cc * TOKC, (cc + 1) * TOKC)
        oT = np.ascontiguousarray(np.concatenate([oTh[i][0:128, ts] for i in range(8)] + [oTh[i][128:256, ts] for i in range(8)], axis=0))
        ims.append({"xT": xT[cc], "oT": oT, "wo": wo0, "w1": w1[0], "w2": w2[0], "nw": col_layout(f(norm_mlp_w)[0]), "modl": modl[0]})
    r = run(build_stageC2(2048, TOKC), ims)
    xT = [r[cc]["x2T"] for cc in range(NCORES)]

    perm1 = np.concatenate([np.concatenate([512 * g + np.arange(512), 4096 + 512 * g + np.arange(512), 8192 + 128 * g + np.arange(128),
                                            9216 + 128 * g + np.arange(128)]) for g in range(8)] + [10240 + np.arange(64)])
    ws = np.ascontiguousarray(f(ssd_w_in)[0][:, perm1])
    nw1 = col_layout(f(norm_mix_w)[1])
    r = run(build_stageA(10304, TOKC), [{"xT": xT[cc], "w": ws, "nw": nw1, "modl": modl[1]} for cc in range(NCORES)])
    projT = [r[cc]["projT"] for cc in range(NCORES)]
    shc = ssd_host_consts()
    scw = f(ssd_conv_w)[0]; scb = f(ssd_conv_b)[0]
    ims = []
    for g in range(8):
        pT = np.ascontiguousarray(np.concatenate([projT[cc][1280 * g:1280 * (g + 1)] for cc in range(NCORES)], axis=1))
        dtT = np.ascontiguousarray(np.concatenate([projT[cc][10240 + 8 * g:10240 + 8 * (g + 1)] for cc in range(NCORES)], axis=1))
        chans = [np.arange(512 * g + 128 * j, 512 * g + 128 * (j + 1)) for j in range(4)] + [np.arange(4096 + 128 * g, 4096 + 128 * (g + 1)),
                                                                                           np.arange(5120 + 128 * g, 5120 + 128 * (g + 1))]
        cwl = np.concatenate([scw[:, ch].T for ch in chans], axis=1)
        cbl = np.stack([scb[ch] for ch in chans], axis=1)
        dsk = f(ssd_d)[0][8 * g:8 * (g + 1)]
        dcol = np.stack([np.repeat(dsk[2 * j:2 * j + 2], 64) for j in range(4)], axis=1)
        nwg = f(ssd_norm_w)[0][512 * g:512 * (g + 1)].reshape(4, 128).T
        im = {"pT": pT, "dtT": dtT, "s_cw": np.ascontiguousarray(cwl), "s_cb": np.ascontiguousarray(cbl), "s_dcol": np.ascontiguousarray(dcol),
              "s_nw": np.ascontiguousarray(nwg), "s_dtb": f(ssd_dt_bias)[0][8 * g:8 * (g + 1)].reshape(8, 1).copy(),
              "s_alog": f(ssd_a_log)[0][8 * g:8 * (g + 1)].reshape(8, 1).copy()}
        im.update(shc)
        ims.append(im)
    del projT
    r = run(build_B1(SEQ), ims)
    yTg = [r[g]["yT"] for g in range(8)]
    ims = []
    wo1 = f(ssd_w_out)[0]
    for cc in range(NCORES):
        ts = slice(cc * TOKC, (cc + 1) * TOKC)
        oT = np.ascontiguousarray(np.concatenate([yTg[g][:, ts] for g in range(8)], axis=0))
        ims.append({"xT": xT[cc], "oT": oT, "wo": wo1, "w1": w1[1], "w2": w2[1], "nw": col_layout(f(norm_mlp_w)[1]), "modl": modl[1]})
    r = run(build_stageC2(4096, TOKC), ims)
    out = np.concatenate([r[cc]["x2T"].T for cc in range(NCORES)], axis=0)[None]
    return np.ascontiguousarray(out.astype(np.float32))
```
